# Optimizing a Trainium2 kernel written in Bass

```python
import math
import jax
import jax.numpy as jnp
from jax import lax
import numpy as np

D_MODEL = 2048
BATCH = 2
SEQ = 8192
DEPTH = 4

GRID_W = 64
CTX_LEN = 256
N_MIXERS = 3
EPS = 1e-6

POOL_WINDOWS = (2, 4, 8, 16)
N_POOL_GROUPS = len(POOL_WINDOWS)
POOL_GROUP = D_MODEL // N_POOL_GROUPS

RWKV_HEAD = 64
RWKV_HEADS = D_MODEL // RWKV_HEAD
RWKV_DECAY_LORA = max(32, int(round(1.8 * D_MODEL ** 0.5 / 32)) * 32)
RWKV_AAA_LORA = max(32, int(round(1.8 * D_MODEL ** 0.5 / 32)) * 32)
RWKV_GATE_LORA = max(32, int(round(0.6 * D_MODEL ** 0.8 / 32)) * 32)
LN_X_EPS = 64e-5

DIFF_HEAD = 128
DIFF_HEADS = D_MODEL // (2 * DIFF_HEAD)
ROPE_BASE = 10000.0
Q_BLOCK = 128

FFN_HIDDEN = -(-(8 * D_MODEL) // (3 * 256)) * 256

f32 = jnp.float32

kernel_name = 'hybrid_pool_rwkv7_diffattn_dit_trunk'


def n_layers_of_kind(kind):
    return len(range(kind, DEPTH, N_MIXERS))


def rms_norm(x, g):
    xf = x.astype(f32)
    return (xf * lax.rsqrt(jnp.mean(xf * xf, axis=-1, keepdims=True) + EPS)).astype(x.dtype) * g


def modulate(x, shift, scale):
    return x * (1 + scale) + shift


def swiglu(h, w_in, w_out):
    gate, up = jnp.split(h @ w_in, 2, axis=-1)
    return (jax.nn.silu(gate) * up) @ w_out


def pool_mix(h, w_grp, ls):
    B, S, D = h.shape
    hf = h.astype(f32)
    cs = jnp.pad(jnp.cumsum(hf, axis=1), ((0, 0), (1, 0), (0, 0)))
    t = jnp.arange(S)
    groups = []
    for g, win in enumerate(POOL_WINDOWS):
        lo = jnp.clip(t - win // 2, 0, S)
        hi = jnp.clip(t + win - win // 2, 0, S)
        csg = cs[..., g * POOL_GROUP:(g + 1) * POOL_GROUP]
        mean = (csg[:, hi] - csg[:, lo]) / (hi - lo).astype(f32)[None, :, None]
        groups.append(mean - hf[..., g * POOL_GROUP:(g + 1) * POOL_GROUP])
    p = jnp.stack(groups, axis=2).astype(h.dtype)
    y = jnp.einsum('bsgc,gce->bsge', p, w_grp).reshape(B, S, D)
    return y * ls


def centred_shift(x):
    xp = jnp.pad(x, ((0, 0), (1, 1), (0, 0)))
    return 0.5 * (xp[:, :-2] + xp[:, 2:]) - x


def rwkv_stream(h, mu, w_rkv, dir_vec, w_la, w_lb, a_la, a_lb):
    B, S, D = h.shape
    heads = lambda t: t.reshape(B, S, RWKV_HEADS, RWKV_HEAD)
    xx = centred_shift(h)
    xr, xw, xk, xv, xa, xg = (h + xx * mu[m] for m in range(6))
    r = heads(xr @ w_rkv[0]).astype(f32)
    k = xk @ w_rkv[1]
    v = heads(xv @ w_rkv[2]).astype(f32)
    dirs = []
    for d in range(2):
        w0, a0, k_k, k_a = dir_vec[d]
        w_log = -jax.nn.softplus(-(w0 + jnp.tanh(xw @ w_la[d]) @ w_lb[d])) - 0.5
        decay = heads(jnp.exp(-jnp.exp(w_log.astype(f32))))
        a = jax.nn.sigmoid(a0 + (xa @ a_la[d]) @ a_lb[d])
        kk = heads(k * k_k).astype(f32)
        kk = kk / jnp.maximum(jnp.linalg.norm(kk, axis=-1, keepdims=True), 1e-12)
        k_d = heads(k * (1 + (a - 1) * k_a)).astype(f32)
        a_h = heads(a).astype(f32)
        dirs.append((decay, k_d, -kk, kk * a_h))
    return r, v, xg, dirs


def wkv7_scan(state0, r, v, decay, k, a, b, reverse):
    def step(state, inp):
        r_t, v_t, w_t, k_t, a_t, b_t = inp
        sa = jnp.einsum('bhvk,bhk->bhv', state, a_t)
        state = (state * w_t[:, :, None, :] + sa[..., None] * b_t[:, :, None, :]
                 + v_t[..., None] * k_t[:, :, None, :])
        return state, jnp.einsum('bhvk,bhk->bhv', state, r_t)
    xs = tuple(jnp.swapaxes(t, 0, 1) for t in (r, v, decay, k, a, b))
    state, ys = lax.scan(step, state0, xs, reverse=reverse)
    return state, jnp.swapaxes(ys, 0, 1)


def rwkv_output(ys, r, v, ks, xg, r_k, ln_x, g_la, g_lb, w_o):
    y = ys[0] + ys[1]
    B, S, H, N = y.shape
    mean = jnp.mean(y, axis=-1, keepdims=True)
    var = jnp.mean(jnp.square(y - mean), axis=-1, keepdims=True)
    yn = ((y - mean) * lax.rsqrt(var + LN_X_EPS)).reshape(B, S, H * N)
    rk = r_k.astype(f32)
    bonus = (jnp.sum(r * ks[0] * rk, axis=-1, keepdims=True)
             + jnp.sum(r * ks[1] * rk, axis=-1, keepdims=True)) * v
    o = yn * ln_x[0] + ln_x[1] + bonus.reshape(B, S, H * N)
    g = jax.nn.sigmoid(xg @ g_la) @ g_lb
    return (o.astype(xg.dtype) * g) @ w_o


def rwkv_mix(h_ctx, h_lat, mu, w_rkv, w_o, dir_vec, w_la, w_lb, a_la, a_lb, g_la, g_lb, r_k, ln_x, ctx_out):
    stream_args = (mu, w_rkv, dir_vec, w_la, w_lb, a_la, a_lb)
    rc, vc, xgc, dirs_c = rwkv_stream(h_ctx, *stream_args)
    rl, vl, xgl, dirs_l = rwkv_stream(h_lat, *stream_args)
    zero = jnp.zeros((h_lat.shape[0], RWKV_HEADS, RWKV_HEAD, RWKV_HEAD), f32)
    yc_dirs, yl_dirs = [], []
    for d, reverse in enumerate((False, True)):
        s_ctx, yc = wkv7_scan(zero, rc, vc, *dirs_c[d], reverse=reverse)
        _, yl = wkv7_scan(s_ctx, rl, vl, *dirs_l[d], reverse=reverse)
        yc_dirs.append(yc)
        yl_dirs.append(yl)
    out_args = (r_k, ln_x, g_la, g_lb, w_o)
    y_lat = rwkv_output(yl_dirs, rl, vl, [dl[1] for dl in dirs_l], xgl, *out_args)
    y_ctx = rwkv_output(yc_dirs, rc, vc, [dc[1] for dc in dirs_c], xgc, *out_args) if ctx_out else None
    return y_ctx, y_lat


def axial_rope(L):
    rows = L // GRID_W
    row = jnp.repeat(jnp.arange(rows, dtype=f32), GRID_W)
    col = jnp.tile(jnp.arange(GRID_W, dtype=f32), rows)
    n_freq = DIFF_HEAD // 4
    inv = ROPE_BASE ** (-jnp.arange(n_freq, dtype=f32) / n_freq)
    ang = jnp.concatenate([row[:, None] * inv, col[:, None] * inv], axis=-1)
    return jnp.cos(ang), jnp.sin(ang)


def apply_rope(x, cos, sin):
    xp = x.reshape(x.shape[:-1] + (DIFF_HEAD // 2, 2))
    x1, x2 = xp[..., 0], xp[..., 1]
    c, s = cos[:, None, None, :], sin[:, None, None, :]
    out = jnp.stack([x1 * c - x2 * s, x1 * s + x2 * c], axis=-1)
    return out.reshape(x.shape).astype(x.dtype)


def diff_attend(q, k, v, lam):
    s = jnp.einsum('bqhid,bkhid->bhiqk', q, k).astype(f32) * DIFF_HEAD ** -0.5
    p = jax.nn.softmax(s, axis=-1)
    p = p[:, :, 0] - lam * p[:, :, 1]
    return jnp.einsum('bhqk,bkhe->bqhe', p.astype(v.dtype), v)


def diff_mix(h_ctx, h_lat, w_qkv, w_o, qk_g, lam_vec, subln_g, lambda_init, ctx_out):
    B, L, D = h_lat.shape
    H, d = DIFF_HEADS, DIFF_HEAD
    lv = lam_vec.astype(f32)
    lam = jnp.exp(jnp.sum(lv[0] * lv[1])) - jnp.exp(jnp.sum(lv[2] * lv[3])) + lambda_init
    qk_heads = lambda t, g: rms_norm(t.reshape(t.shape[0], t.shape[1], H, 2, d), g)
    v_heads = lambda t: t.reshape(t.shape[0], t.shape[1], H, 2 * d)
    q_l, k_l, v_l = jnp.split(h_lat @ w_qkv, 3, axis=-1)
    cos, sin = axial_rope(L)
    q_l = apply_rope(qk_heads(q_l, qk_g[0]), cos, sin)
    k_l = apply_rope(qk_heads(k_l, qk_g[1]), cos, sin)
    k_c, v_c = jnp.split(h_ctx @ w_qkv[:, D_MODEL:], 2, axis=-1)
    k_c = qk_heads(k_c, qk_g[1])
    v_c = v_heads(v_c)
    k_all = jnp.concatenate([k_c, k_l], axis=1)
    v_all = jnp.concatenate([v_c, v_heads(v_l)], axis=1)
    nb = L // Q_BLOCK
    q_blocks = jnp.moveaxis(q_l.reshape(B, nb, Q_BLOCK, H, 2, d), 1, 0)
    o_blocks = lax.map(lambda qb: diff_attend(qb, k_all, v_all, lam), q_blocks)
    o_lat = jnp.moveaxis(o_blocks, 0, 1).reshape(B, L, H, 2 * d)

    def finish(o):
        o = rms_norm(o, subln_g) * (1.0 - lambda_init)
        return o.reshape(o.shape[0], o.shape[1], D_MODEL) @ w_o

    y_lat = finish(o_lat)
    y_ctx = None
    if ctx_out:
        q_c = qk_heads(h_ctx @ w_qkv[:, :D_MODEL], qk_g[0])
        y_ctx = finish(diff_attend(q_c, k_c, v_c, lam))
    return y_ctx, y_lat


def setup_inputs(seed: int = 0) -> dict:
    key = jax.random.key(seed)
    keys = iter(list(jax.random.split(key, 40)))

    def nrm(shape, scale):
        return jax.random.normal(next(keys), shape, f32) * scale

    def unif(shape, lo, hi):
        return jax.random.uniform(next(keys), shape, f32, lo, hi)

    D, F = D_MODEL, FFN_HIDDEN
    NP, NR, ND = n_layers_of_kind(0), n_layers_of_kind(1), n_layers_of_kind(2)
    H, N, dh = RWKV_HEADS, RWKV_HEAD, DIFF_HEAD
    inputs = {}
    inputs['x'] = nrm((BATCH, SEQ, D), 1.0)
    inputs['c'] = nrm((BATCH, D), 1.0)
    inputs['ctx'] = nrm((BATCH, CTX_LEN, D), 1.0)
    inputs['c_ctx'] = nrm((D,), 1.0)
    inputs['ada_w'] = nrm((DEPTH, D, 6 * D), 0.5 * D ** -0.5)
    inputs['ada_b'] = nrm((DEPTH, 6 * D), 0.02)
    inputs['norm_g'] = 1.0 + nrm((DEPTH, 2, D), 0.1)
    inputs['ffn_w_in'] = nrm((DEPTH, D, 2 * F), D ** -0.5)
    inputs['ffn_w_out'] = nrm((DEPTH, F, D), F ** -0.5)
    inputs['pool_w'] = nrm((NP, N_POOL_GROUPS, POOL_GROUP, POOL_GROUP), POOL_GROUP ** -0.5)
    inputs['pool_scale'] = 0.5 + nrm((NP, D), 0.1)
    inputs['rwkv_mu'] = unif((NR, 6, D), 0.0, 1.0)
    inputs['rwkv_w_rkv'] = nrm((NR, 3, D, D), D ** -0.5)
    inputs['rwkv_w_o'] = nrm((NR, D, D), D ** -0.5)
    inputs['rwkv_dir_vec'] = jnp.stack([unif((NR, 2, D), -6.0, -1.0), nrm((NR, 2, D), 0.1),
                                        0.85 + nrm((NR, 2, D), 0.05), 1.0 + nrm((NR, 2, D), 0.05)], axis=2)
    inputs['rwkv_w_lora_a'] = nrm((NR, 2, D, RWKV_DECAY_LORA), D ** -0.5)
    inputs['rwkv_w_lora_b'] = nrm((NR, 2, RWKV_DECAY_LORA, D), 0.5 * RWKV_DECAY_LORA ** -0.5)
    inputs['rwkv_a_lora_a'] = nrm((NR, 2, D, RWKV_AAA_LORA), D ** -0.5)
    inputs['rwkv_a_lora_b'] = nrm((NR, 2, RWKV_AAA_LORA, D), 0.5 * RWKV_AAA_LORA ** -0.5)
    inputs['rwkv_g_lora_a'] = nrm((NR, D, RWKV_GATE_LORA), D ** -0.5)
    inputs['rwkv_g_lora_b'] = nrm((NR, RWKV_GATE_LORA, D), RWKV_GATE_LORA ** -0.5)
    inputs['rwkv_r_k'] = nrm((NR, H, N), 0.1)
    inputs['rwkv_ln_x'] = jnp.stack([1.0 + nrm((NR, D), 0.1), nrm((NR, D), 0.02)], axis=1)
    inputs['diff_w_qkv'] = nrm((ND, D, 3 * D), D ** -0.5)
    inputs['diff_w_o'] = nrm((ND, D, D), D ** -0.5)
    inputs['diff_qk_g'] = 1.0 + nrm((ND, 2, dh), 0.1)
    inputs['diff_lambda'] = nrm((ND, 4, dh), 0.1)
    inputs['diff_subln_g'] = 1.0 + nrm((ND, 2 * dh), 0.1)
    return inputs


def reference(x, c, ctx, c_ctx, ada_w, ada_b, norm_g, ffn_w_in, ffn_w_out, pool_w, pool_scale,
              rwkv_mu, rwkv_w_rkv, rwkv_w_o, rwkv_dir_vec, rwkv_w_lora_a, rwkv_w_lora_b,
              rwkv_a_lora_a, rwkv_a_lora_b, rwkv_g_lora_a, rwkv_g_lora_b, rwkv_r_k, rwkv_ln_x,
              diff_w_qkv, diff_w_o, diff_qk_g, diff_lambda, diff_subln_g):
    s_lat = jax.nn.silu(c)
    s_ctx = jax.nn.silu(c_ctx)
    last_reader = max((i for i in range(DEPTH) if i % N_MIXERS != 0), default=-1)
    for i in range(DEPTH):
        kind, j = i % N_MIXERS, i // N_MIXERS
        ctx_in = i <= last_reader
        ctx_out = i < last_reader
        sh1, sc1, g1, sh2, sc2, g2 = jnp.split(s_lat @ ada_w[i] + ada_b[i], 6, axis=-1)
        h = modulate(rms_norm(x, norm_g[i, 0]), sh1[:, None], sc1[:, None])
        hc = None
        if ctx_in:
            csh1, csc1, cg1, csh2, csc2, cg2 = jnp.split(s_ctx @ ada_w[i] + ada_b[i], 6, axis=-1)
            hc = modulate(rms_norm(ctx, norm_g[i, 0]), csh1, csc1)
        if kind == 0:
            y = pool_mix(h, pool_w[j], pool_scale[j])
            yc = pool_mix(hc, pool_w[j], pool_scale[j]) if ctx_out else None
        elif kind == 1:
            yc, y = rwkv_mix(hc, h, rwkv_mu[j], rwkv_w_rkv[j], rwkv_w_o[j], rwkv_dir_vec[j],
                             rwkv_w_lora_a[j], rwkv_w_lora_b[j], rwkv_a_lora_a[j], rwkv_a_lora_b[j],
                             rwkv_g_lora_a[j], rwkv_g_lora_b[j], rwkv_r_k[j], rwkv_ln_x[j], ctx_out)
        else:
            lambda_init = 0.8 - 0.6 * math.exp(-0.3 * i)
            yc, y = diff_mix(hc, h, diff_w_qkv[j], diff_w_o[j], diff_qk_g[j], diff_lambda[j],
                             diff_subln_g[j], lambda_init, ctx_out)
        x = x + g1[:, None] * y
        h = modulate(rms_norm(x, norm_g[i, 1]), sh2[:, None], sc2[:, None])
        x = x + g2[:, None] * swiglu(h, ffn_w_in[i], ffn_w_out[i])
        if ctx_out:
            ctx = ctx + cg1 * yc
            hc = modulate(rms_norm(ctx, norm_g[i, 1]), csh2, csc2)
            ctx = ctx + cg2 * swiglu(hc, ffn_w_in[i], ffn_w_out[i])
    return x
```

```python
import math


import numpy as np
import concourse.bass as bass
import concourse.mybir as mybir
from concourse.bass_utils import run_bass_kernel_spmd

F32 = mybir.dt.float32
BF16 = mybir.dt.bfloat16
ALU = mybir.AluOpType
AF = mybir.ActivationFunctionType
AX = mybir.AxisListType

N_DMA_SEMS = 24


class Prog:
    ENGS = ("pe", "act", "dve", "pool", "sp")

    def __init__(self, nc):
        self.nc = nc
        self.q = {e: [] for e in self.ENGS}
        self.n = {e: 0 for e in self.ENGS}
        self.waited = {e: {} for e in self.ENGS}
        self.lastw = {}
        self.readers = {}
        self.dma_rr = 0
        self.dma_cnt = [0] * N_DMA_SEMS
        self.dma_last = [None] * N_DMA_SEMS
        self.ctx = []
        self.sems = {}

    def enter(self, cm):
        v = cm.__enter__()
        self.ctx.append(cm)
        return v

    def sbuf(self, name, shape, dt):
        return self.enter(self.nc.sbuf_tensor(name, list(shape), dt))

    def psum(self, name, shape, dt=F32):
        return self.enter(self.nc.psum_tensor(name, list(shape), dt))

    def _deps(self, reads, writes):
        toks = []
        for r in reads:
            t = self.lastw.get(r)
            if t is not None:
                toks.append(t)
        for w in writes:
            t = self.lastw.get(w)
            if t is not None:
                toks.append(t)
            toks.extend(self.readers.get(w, ()))
        return toks

    def _commit(self, tok, reads, writes):
        for r in reads:
            self.readers.setdefault(r, []).append(tok)
        for w in writes:
            self.lastw[w] = tok
            self.readers[w] = []

    def _waits(self, eng, toks):
        need = {}
        for (k, v) in toks:
            if v > need.get(k, 0):
                need[k] = v
        out = []
        wd = self.waited[eng]
        for k, v in need.items():
            if wd.get(k, 0) >= v:
                continue
            wd[k] = v
            out.append((k, v))
        return out

    def op(self, eng, fn, reads=(), writes=()):
        toks = self._deps(reads, writes)
        if eng == "pe":
            toks = [t for t in toks if t[0] != "pe"]
        waits = self._waits(eng, toks)
        self.n[eng] += 1
        tok = (eng, self.n[eng])
        self.q[eng].append((fn, waits, ("self", eng, 1)))
        self._commit(tok, reads, writes)
        return tok

    def dma(self, eng, out, in_, reads=(), writes=(), **kw):
        toks = self._deps(reads, writes)
        s = self.dma_rr
        self.dma_rr = (self.dma_rr + 1) % N_DMA_SEMS
        if self.dma_last[s] is not None:
            toks.append(self.dma_last[s])
        waits = self._waits(eng, toks)
        self.dma_cnt[s] += 1
        tok = (("dma", s), 16 * self.dma_cnt[s])
        self.dma_last[s] = tok
        self.q[eng].append((lambda e: e.dma_start(out=out, in_=in_, **kw), waits, ("dma", s, 16)))
        self._commit(tok, reads, writes)
        return tok

    def final_wait(self, eng, toks):
        waits = self._waits(eng, toks)
        self.q[eng].append((None, waits, None))

    def build(self):
        nc = self.nc
        semobjs = {}
        for e in self.ENGS:
            if e != "sp":
                semobjs[e] = self.enter(nc.semaphore("prog_" + e))
        for s in range(N_DMA_SEMS):
            semobjs[("dma", s)] = self.enter(nc.semaphore("dma%d" % s))
        q = self.q

        def emit(engname, e):
            for fn, waits, inc in q[engname]:
                for k, v in waits:
                    e.wait_ge(semobjs[k], v)
                if fn is None:
                    continue
                ins = fn(e)
                if inc[0] == "self":
                    ins.then_inc(semobjs[inc[1]], 1)
                else:
                    ins.then_inc(semobjs[("dma", inc[1])], 16)

        with nc.Block() as block:
            @block.tensor
            def _(e):
                emit("pe", e)

            @block.scalar
            def _(e):
                emit("act", e)

            @block.vector
            def _(e):
                emit("dve", e)

            @block.gpsimd
            def _(e):
                emit("pool", e)

            @block.sync
            def _(e):
                emit("sp", e)
        for cm in reversed(self.ctx):
            cm.__exit__(None, None, None)
        self.ctx = []
        return nc


D = 2048
NL = 4
NM = 6 * D
COLS_PER_CORE = NM // 8
L0_NB = COLS_PER_CORE // 512

CAST_CH = 8192


def build_l0(ncast=0):
    nc = bass.Bass("TRN2", target_bir_lowering=False)
    cT = nc.dram_tensor("cT", [128, 16, 3], F32, kind="ExternalInput").ap()
    w = nc.dram_tensor("w", [NL, D, COLS_PER_CORE], F32, kind="ExternalInput").ap()
    b = nc.dram_tensor("b", [NL, COLS_PER_CORE], F32, kind="ExternalInput").ap()
    out = nc.dram_tensor("out", [3, NL * COLS_PER_CORE], F32, kind="ExternalOutput").ap()
    if ncast:
        cin = nc.dram_tensor("cin", [128, ncast * CAST_CH], F32, kind="ExternalInput").ap()
        cout = nc.dram_tensor("cout", [128, ncast * CAST_CH], BF16, kind="ExternalOutput").ap()
    P = Prog(nc)
    c_sb = P.sbuf("c_sb", [128, 16, 3], F32)
    s_sb = P.sbuf("s_sb", [128, 16, 3], F32)
    b_sb = P.sbuf("b_sb", [3, NL * COLS_PER_CORE], F32)
    o_sb = P.sbuf("o_sb", [3, NL * COLS_PER_CORE], F32)
    wt = [P.sbuf("wt%d" % i, [128, 16, 512], F32) for i in range(2)]
    ps = [P.psum("ps%d" % i, [128, 512]) for i in range(2)]
    P.dma("sp", c_sb[:], cT, writes=["c"])
    for l in range(NL):
        P.dma("sp", b_sb[:, l * COLS_PER_CORE:(l + 1) * COLS_PER_CORE],
              b[l, :].partition_broadcast(3), writes=[("b", l)])
    P.op("act", lambda e: e.activation(out=s_sb[:], in_=c_sb[:], func=AF.Silu), reads=["c"], writes=["s"])
    k = 0
    for l in range(NL):
        for nb in range(L0_NB):
            wb = wt[k % 2]
            pb = ps[k % 2]
            src = w[l, :, nb * 512:(nb + 1) * 512].rearrange("(c p) n -> p c n", p=128)
            P.dma("sp" if k % 2 == 0 else "pool", wb[:], src, writes=[("wt", k % 2)])
            for c in range(16):
                P.op("pe", lambda e, wb=wb, pb=pb, c=c: e.matmul(pb[0:3, :], lhsT=s_sb[:, c, :], rhs=wb[:, c, :],
                                                                   start=(c == 0), stop=(c == 15)),
                     reads=["s", ("wt", k % 2)], writes=[("ps", k % 2)])
            col = l * COLS_PER_CORE + nb * 512
            P.op("dve", lambda e, pb=pb, col=col: e.tensor_tensor(out=o_sb[:, col:col + 512], in0=pb[0:3, :],
                                                                   in1=b_sb[:, col:col + 512], op=ALU.add),
                 reads=[("ps", k % 2), ("b", l)], writes=[("o", k)])
            k += 1
    t = P.dma("sp", out, o_sb[:], reads=[("o", i) for i in range(k)], writes=["out"])
    toks = [t]
    if ncast:
        cb = [P.sbuf("cb%d" % i, [128, CAST_CH], BF16) for i in range(3)]
        for i in range(ncast):
            sl = slice(i * CAST_CH, (i + 1) * CAST_CH)
            P.dma("pool", cb[i % 3][:], cin[:, sl], writes=[("cb", i % 3)])
            toks.append(P.dma("act", cout[:, sl], cb[i % 3][:], reads=[("cb", i % 3)], writes=[("cout", i)]))
    P.final_wait("sp", toks)
    return P.build()

def blocked_weights(inputs):
    perm = np.concatenate([np.arange(0, 128, 2), np.arange(1, 128, 2)])
    out = {}
    for l in range(4):
        wi = inputs["ffn_w_in"][l].reshape(16, 128, 2, 44, 128)
        out["ffn_in%d" % l] = np.ascontiguousarray(wi.transpose(3, 2, 1, 0, 4))
        wo = inputs["ffn_w_out"][l].reshape(44, 128, 16, 128)
        out["ffn_out%d" % l] = np.ascontiguousarray(wo.transpose(2, 1, 0, 3))
    sq = lambda w: np.ascontiguousarray(w.reshape(16, 128, -1, 128).transpose(2, 1, 0, 3))
    out["rwkv_wo"] = sq(inputs["rwkv_w_o"][0])
    out["diff_wo"] = sq(inputs["diff_w_o"][0])
    out["rwkv_rkv"] = np.stack([sq(inputs["rwkv_w_rkv"][0][i]) for i in range(3)])
    wq = inputs["diff_w_qkv"][0]
    cols = (np.arange(32)[:, None] * 128 + perm[None, :]).reshape(-1)
    wq = np.concatenate([wq[:, cols], wq[:, 4096:]], axis=1)
    out["diff_qkv"] = sq(wq)
    return out


def run_l0(inputs, cast=True):
    c = np.concatenate([inputs["c"], inputs["c_ctx"][None]], 0)
    cT = np.ascontiguousarray(c.reshape(3, 16, 128).transpose(2, 1, 0))
    blk = blocked_weights(inputs) if cast else {}
    names = list(blk)
    total = sum(blk[n].size for n in names)
    per = 8 * 128 * CAST_CH
    ncast = (total + per - 1) // per
    nc = build_l0(ncast)
    if ncast:
        flat = np.zeros(ncast * per, np.float32)
        o = 0
        for n in names:
            flat[o:o + blk[n].size] = blk[n].reshape(-1)
            o += blk[n].size
        flat = flat.reshape(8, 128, ncast * CAST_CH)
    maps = []
    for i in range(8):
        sl = slice(i * COLS_PER_CORE, (i + 1) * COLS_PER_CORE)
        m = {"cT": cT, "w": np.ascontiguousarray(inputs["ada_w"][:, :, sl]),
             "b": np.ascontiguousarray(inputs["ada_b"][:, sl])}
        if ncast:
            m["cin"] = flat[i]
        maps.append(m)
    res = run_bass_kernel_spmd(nc, maps, core_ids=list(range(8)))
    outs = [r["out"].reshape(3, NL, COLS_PER_CORE) for r in res.results]
    mods = np.concatenate(outs, axis=2)
    wb = {}
    if ncast:
        cf = np.concatenate([np.asarray(r["cout"]).reshape(-1) for r in res.results])
        o = 0
        for n in names:
            wb[n] = cf[o:o + blk[n].size].reshape(blk[n].shape)
            o += blk[n].size
    return mods, wb


D = 2048
F = 5632
NC16 = 16
NJ = F // 128
EPS = 1e-6
HALO = 8
TBG = 512
WINS = (2, 4, 8, 16)


class Common:
    def __init__(self, P, TBMAX=512, halo=HALO):
        self.P = P
        W = TBMAX + 2 * halo
        self.W = W
        self.ones = P.sbuf("ones_bf", [128, 128], BF16)
        self.rs = P.sbuf("rs", [128, W], F32)
        self.sqc = [P.sbuf("sqc%d" % i, [128, W], BF16) for i in range(2)]
        self.tmp = [P.sbuf("ntmp%d" % i, [128, W], F32) for i in range(2)]
        self.psb = [P.psum("psb%d" % i, [128, 512]) for i in range(8)]
        P.op("dve", lambda e: e.memset(self.ones[:], 1.0), writes=["ones"])
        self.k = 0

    def norm_mod(self, xb, ncols, G, SH, dest, dest_keys, xkeys, stat_bank=0, post=None):
        P = self
        P = self.P
        pst = self.psb[stat_bank]
        pkey = ("psb", stat_bank)
        n2 = ncols
        halves = [(0, min(512, n2))]
        if n2 > 512:
            halves.append((512, n2))
        for c in range(16):
            sq = self.sqc[c % 2]
            P.op("act", lambda e, sq=sq, c=c: e.activation(out=sq[:, :n2], in_=xb[:, c, :], func=AF.Square),
                 reads=[xkeys[c]], writes=[("sqc", c % 2)])
            for hi, (a, b) in enumerate(halves):
                bank = self.psb[stat_bank + hi]
                P.op("pe", lambda e, sq=sq, c=c, a=a, b=b, bank=bank: e.matmul(
                    bank[:, 0:b - a], lhsT=self.ones[:], rhs=sq[:, a:b], start=(c == 0), stop=(c == 15)),
                    reads=[("sqc", c % 2), "ones"], writes=[("psb", stat_bank + hi)])
        for hi, (a, b) in enumerate(halves):
            bank = self.psb[stat_bank + hi]
            P.op("act", lambda e, a=a, b=b, bank=bank: e.activation(
                out=self.rs[:, a:b], in_=bank[:, 0:b - a], func=AF.Sqrt, scale=1.0 / D, bias=self.epsb[:, 0:1]),
                reads=[("psb", stat_bank + hi), "epsb"], writes=[("rs", hi)])
            P.op("dve", lambda e, a=a, b=b: e.reciprocal(out=self.rs[:, a:b], in_=self.rs[:, a:b]),
                 reads=[("rs", hi)], writes=[("rs", hi)])
        Gt, Gk = G
        St, Sk = SH
        for c in range(16):
            tmp = self.tmp[c % 2]
            P.op("dve", lambda e, tmp=tmp, c=c: e.tensor_tensor(out=tmp[:, :n2], in0=xb[:, c, :], in1=self.rs[:, :n2],
                                                                 op=ALU.mult),
                 reads=[xkeys[c], ("rs", 0), ("rs", 1)], writes=[("ntmp", c % 2)])
            P.op("act", lambda e, tmp=tmp, c=c: e.activation(out=dest(c), in_=tmp[:, :n2], func=AF.Identity,
                                                             scale=Gt[:, c:c + 1], bias=St[:, c:c + 1]),
                 reads=[("ntmp", c % 2), Gk, Sk], writes=[dest_keys[c]])
            if post is not None:
                post(c)

    def setup_eps(self):
        P = self.P
        self.epsb = P.sbuf("epsb", [128, 1], F32)
        P.op("dve", lambda e: e.memset(self.epsb[:], EPS), writes=["epsb"])


class FFN:
    def __init__(self, P, cm, w_in, w_out, TB=512, nsplit=1, WC=256):
        self.P, self.cm = P, cm
        self.w_in, self.w_out = w_in, w_out
        WC = 128
        self.nsplit, self.WC = nsplit, WC
        self.NJS = NJ // nsplit
        self.actT = P.sbuf("actT", [128, self.NJS, TB], BF16)
        self.wg = [P.sbuf("wg%d" % i, [128, 16, WC], BF16) for i in range(2)]
        self.wu = [P.sbuf("wu%d" % i, [128, 16, WC], BF16) for i in range(2)]
        self.wo = [P.sbuf("wo%d" % i, [128, self.NJS, 128], BF16) for i in range(2)]
        self.silu = [P.sbuf("silu%d" % i, [128, TB], F32) for i in range(2)]
        self.kin = 0
        self.kout = 0
        self.kj = 0
        self.km = 0

    def emit(self, hT, hkeys, n, xb, xoff, xkeys, g2):
        P, cm = self.P, self.cm
        g2t, g2k = g2
        WC, NJS = self.WC, self.NJS
        per = WC // 128
        for sp in range(self.nsplit):
            j0 = sp * NJS
            for jb in range(NJS // per):
                s = self.kin % 2
                self.kin += 1
                wg, wu = self.wg[s], self.wu[s]
                jg = j0 + jb
                P.dma("sp", wg[:], self.w_in[jg, 0], writes=[("wg", s)])
                P.dma("sp", wu[:], self.w_in[jg, 1], writes=[("wu", s)])
                for jj in range(per):
                    jl = jb * per + jj
                    q = self.kj % 2
                    self.kj += 1
                    pg, pu = cm.psb[2 + q], cm.psb[4 + q]
                    for c in range(16):
                        P.op("pe", lambda e, wg=wg, pg=pg, c=c, jj=jj: e.matmul(
                            pg[:, :n], lhsT=wg[:, c, jj * 128:(jj + 1) * 128], rhs=hT[:, c, :n],
                            start=(c == 0), stop=(c == 15)),
                            reads=[("wg", s), hkeys[c]], writes=[("psb", 2 + q)])
                    for c in range(16):
                        P.op("pe", lambda e, wu=wu, pu=pu, c=c, jj=jj: e.matmul(
                            pu[:, :n], lhsT=wu[:, c, jj * 128:(jj + 1) * 128], rhs=hT[:, c, :n],
                            start=(c == 0), stop=(c == 15)),
                            reads=[("wu", s), hkeys[c]], writes=[("psb", 4 + q)])
                    sl = self.silu[q]
                    P.op("act", lambda e, sl=sl, pg=pg: e.activation(out=sl[:, :n], in_=pg[:, :n], func=AF.Silu),
                         reads=[("psb", 2 + q)], writes=[("silu", q)])
                    P.op("dve", lambda e, sl=sl, pu=pu, jl=jl: e.tensor_tensor(
                        out=self.actT[:, jl, :n], in0=sl[:, :n], in1=pu[:, :n], op=ALU.mult),
                        reads=[("silu", q), ("psb", 4 + q)], writes=[("actT", jl)])
            for m in range(16):
                s = self.kout % 2
                self.kout += 1
                wo = self.wo[s]
                P.dma("sp", wo[:], self.w_out[m][:, j0:j0 + NJS, :], writes=[("wo", s)])
                q = self.km % 2
                self.km += 1
                py = cm.psb[6 + q]
                for jl in range(NJS):
                    P.op("pe", lambda e, wo=wo, py=py, jl=jl: e.matmul(
                        py[:, :n], lhsT=wo[:, jl, :], rhs=self.actT[:, jl, :n], start=(jl == 0), stop=(jl == NJS - 1)),
                        reads=[("wo", s), ("actT", jl)], writes=[("psb", 6 + q)])
                P.op("dve", lambda e, py=py, m=m: e.scalar_tensor_tensor(
                    out=xb[:, m, xoff:xoff + n], in0=py[:, :n], scalar=g2t[:, m:m + 1], in1=xb[:, m, xoff:xoff + n],
                    op0=ALU.mult, op1=ALU.add),
                    reads=[("psb", 6 + q), g2k, xkeys[m]], writes=[xkeys[m]])


def build_pool_layer(segs, TB=512, dbg=0):
    nc = bass.Bass("TRN2", target_bir_lowering=False)
    dram = {}
    for name, T in segs:
        dram[name] = dict(
            x=nc.dram_tensor("x_" + name, [128, 16, T + 2 * HALO], F32, kind="ExternalInput").ap(),
            valid=nc.dram_tensor("valid_" + name, [T + 2 * HALO], F32, kind="ExternalInput").ap(),
            invc=nc.dram_tensor("invc_" + name, [4, T], F32, kind="ExternalInput").ap(),
            mod=nc.dram_tensor("mod_" + name, [128, 6, 16], F32, kind="ExternalInput").ap(),
            out=nc.dram_tensor("out_" + name, [128, 16, T], F32, kind="ExternalOutput").ap(),
        )
    normg = nc.dram_tensor("normg", [128, 2, 16], F32, kind="ExternalInput").ap()
    pscale = nc.dram_tensor("pscale", [128, 16], F32, kind="ExternalInput").ap()
    poolw = nc.dram_tensor("poolw", [4, 128, 4, 512], F32, kind="ExternalInput").ap()
    w_in = nc.dram_tensor("w_in", [NJ, 2, 128, 16, 128], BF16, kind="ExternalInput").ap()
    w_out = nc.dram_tensor("w_out", [16, 128, NJ, 128], BF16, kind="ExternalInput").ap()

    P = Prog(nc)
    cm = Common(P, TB)
    cm.setup_eps()
    W = TB + 2 * HALO
    ffn = FFN(P, cm, w_in, w_out, TB)
    xb = P.sbuf("xb", [128, 16, W], F32)
    hT = P.sbuf("hT", [128, 16, W], BF16)
    hc = [P.sbuf("hc%d" % i, [128, W], F32) for i in range(2)]
    pa = [P.sbuf("pa%d" % i, [128, W], F32) for i in range(2)]
    pm = P.sbuf("pm", [128, TB], F32)
    pw = [P.sbuf("pw%d" % i, [128, 4, 512], BF16) for i in range(2)]
    invc = P.sbuf("invc", [128, 4, TB], F32)
    vmask = P.sbuf("vmask", [128, W], F32)
    ng = P.sbuf("ng", [128, 2, 16], F32)
    psc = P.sbuf("psc", [128, 16], F32)
    P.dma("sp", ng[:], normg, writes=["ng"])
    P.dma("sp", psc[:], pscale, writes=["psc"])
    xkeys = [("xb", c) for c in range(16)]
    hkeys = [("hT", c) for c in range(16)]
    out_toks = []
    st = dict(kpw=0, kpy=0)
    def do_block(dr, mod, G1, G2, GL, mk, name, bi, t0, n):
        nw = n + 2 * HALO
        P.dma("sp", xb[:, :, :nw], dr["x"][:, :, t0:t0 + nw], writes=xkeys)
        P.dma("sp", vmask[:, :nw], dr["valid"][t0:t0 + nw].partition_broadcast(128), writes=["vmask"])
        P.dma("sp", invc[:, :, :n], dr["invc"][:, t0:t0 + n].partition_broadcast(128), writes=["invc"])

        def pool_chunk(c, n=n, nw=nw):
            g = c // 4
            w = WINS[g]
            h = hc[c % 2]
            hk = ("hc", c % 2)
            P.op("dve", lambda e: e.tensor_tensor(out=h[:, 0:HALO], in0=h[:, 0:HALO], in1=vmask[:, 0:HALO], op=ALU.mult),
                 reads=[hk, "vmask"], writes=[hk])
            P.op("dve", lambda e: e.tensor_tensor(out=h[:, nw - HALO:nw], in0=h[:, nw - HALO:nw],
                                                  in1=vmask[:, nw - HALO:nw], op=ALU.mult),
                 reads=[hk, "vmask"], writes=[hk])
            cur, curk, ln = h, hk, nw
            s = 1
            i = 0
            while s < w:
                dst = pa[i % 2]
                P.op("dve", lambda e, cur=cur, dst=dst, s=s, ln=ln: e.tensor_tensor(
                    out=dst[:, 0:ln - s], in0=cur[:, 0:ln - s], in1=cur[:, s:ln], op=ALU.add),
                    reads=[curk], writes=[("pa", i % 2)])
                cur, curk, ln = dst, ("pa", i % 2), ln - s
                s *= 2
                i += 1
            o = HALO - w // 2
            P.op("dve", lambda e, cur=cur, o=o, g=g: e.tensor_tensor(
                out=pm[:, :n], in0=cur[:, o:o + n], in1=invc[:, g, :n], op=ALU.mult),
                reads=[curk, "invc"], writes=["pm"])
            P.op("dve", lambda e, c=c: e.tensor_tensor(
                out=hT[:, c, :n], in0=pm[:, :n], in1=h[:, HALO:HALO + n], op=ALU.subtract),
                reads=["pm", hk], writes=[hkeys[c]])

        hck = [("hc", c % 2) for c in range(16)]
        cm.norm_mod(xb[:, :, :nw], nw, (G1, ("G1", name)), (mod[:, 0, :], mk),
                    lambda c, nw=nw: hc[c % 2][:, :nw], hck, xkeys, stat_bank=0, post=pool_chunk)
        for g in range(4):
            s = st["kpw"] % 2
            st["kpw"] += 1
            P.dma("pool", pw[s][:], poolw[g], writes=[("pw", s)])
            for mm in range(4):
                m = 4 * g + mm
                q = st["kpy"] % 2
                st["kpy"] += 1
                py = cm.psb[6 + q]
                for cc in range(4):
                    P.op("pe", lambda e, s=s, py=py, cc=cc, mm=mm, g=g: e.matmul(
                        py[:, :n], lhsT=pw[s][:, cc, mm * 128:(mm + 1) * 128], rhs=hT[:, 4 * g + cc, :n],
                        start=(cc == 0), stop=(cc == 3)),
                        reads=[("pw", s), hkeys[4 * g + cc]], writes=[("psb", 6 + q)])
                P.op("dve", lambda e, py=py, m=m: e.scalar_tensor_tensor(
                    out=xb[:, m, HALO:HALO + n], in0=py[:, :n], scalar=GL[:, m:m + 1], in1=xb[:, m, HALO:HALO + n],
                    op0=ALU.mult, op1=ALU.add),
                    reads=[("psb", 6 + q), ("GL", name), xkeys[m]], writes=[xkeys[m]])
        if dbg == 0:
            cm.norm_mod(xb[:, :, HALO:HALO + n], n, (G2, ("G2", name)), (mod[:, 3, :], mk),
                        lambda c, n=n: hT[:, c, :n], hkeys, xkeys, stat_bank=0)
            ffn.emit(hT, hkeys, n, xb, HALO, xkeys, (mod[:, 5, :], mk))
        t = P.dma("sp", dr["out"][:, :, t0:t0 + n], xb[:, :, HALO:HALO + n], reads=xkeys, writes=[("out", name, bi)])
        out_toks.append(t)

    for si, (name, T) in enumerate(segs):
        dr = dram[name]
        mod = P.sbuf("modsb_" + name, [128, 6, 16], F32)
        G1 = P.sbuf("G1_" + name, [128, 16], F32)
        G2 = P.sbuf("G2_" + name, [128, 16], F32)
        GL = P.sbuf("GL_" + name, [128, 16], F32)
        mk = ("mod", name)
        P.dma("sp", mod[:], dr["mod"], writes=[mk])
        P.op("dve", lambda e, G1=G1, mod=mod: e.scalar_tensor_tensor(
            out=G1[:], in0=mod[:, 1, :], scalar=1.0, in1=ng[:, 0, :], op0=ALU.add, op1=ALU.mult),
            reads=[mk, "ng"], writes=[("G1", name)])
        P.op("dve", lambda e, G2=G2, mod=mod: e.scalar_tensor_tensor(
            out=G2[:], in0=mod[:, 4, :], scalar=1.0, in1=ng[:, 1, :], op0=ALU.add, op1=ALU.mult),
            reads=[mk, "ng"], writes=[("G2", name)])
        P.op("dve", lambda e, GL=GL, mod=mod: e.tensor_tensor(out=GL[:], in0=mod[:, 2, :], in1=psc[:], op=ALU.mult),
             reads=[mk, "psc"], writes=[("GL", name)])
        nblk = (T + TB - 1) // TB
        for bi in range(nblk):
            do_block(dr, mod, G1, G2, GL, mk, name, bi, bi * TB, min(TB, T - bi * TB))
    P.final_wait("sp", out_toks)
    return P.build()


def to_fm(a):
    T = a.shape[0]
    return np.ascontiguousarray(a.reshape(T, 16, 128).transpose(2, 1, 0))


def from_fm(a):
    T = a.shape[2]
    return np.ascontiguousarray(a.transpose(2, 1, 0).reshape(T, 2048))


def vec_fm(v):
    lead = v.shape[:-1]
    r = v.reshape(lead + (16, 128))
    return np.ascontiguousarray(np.moveaxis(r, -1, 0))


def seg_shards(seq, T):
    S = seq.shape[0]
    pad = np.zeros((S + 2 * HALO, 2048), np.float32)
    pad[HALO:HALO + S] = seq
    t = np.arange(S)
    inv = np.zeros((4, S), np.float32)
    for g, w in enumerate(WINS):
        lo = np.clip(t - w // 2, 0, S)
        hi = np.clip(t + w - w // 2, 0, S)
        inv[g] = 1.0 / (hi - lo)
    valid = np.zeros(S + 2 * HALO, np.float32)
    valid[HALO:HALO + S] = 1.0
    out = []
    for s0 in range(0, S, T):
        out.append(dict(x=to_fm(pad[s0:s0 + T + 2 * HALO]), valid=np.ascontiguousarray(valid[s0:s0 + T + 2 * HALO]),
                        invc=np.ascontiguousarray(inv[:, s0:s0 + T])))
    return out


def run_pool_layer(x, ctx, mods_l, normg_l, pscale, poolw, w_in, w_out, with_ctx, dbg=0):
    segs = [("lat", 2048)] + ([("ctx", 64)] if with_ctx else [])
    nc = build_pool_layer(segs, TB=TBG, dbg=dbg)
    maps = []
    pw_l = np.ascontiguousarray(poolw.reshape(4, 4, 128, 512).transpose(0, 2, 1, 3))
    lat = [seg_shards(x[b], 2048) for b in range(2)]
    cs = [seg_shards(ctx[b], 64) for b in range(2)] if with_ctx else None
    for i in range(8):
        b, k = i // 4, i % 4
        m = {"normg": vec_fm(normg_l), "pscale": vec_fm(pscale), "poolw": pw_l, "w_in": w_in, "w_out": w_out}
        sh = lat[b][k]
        m.update({"x_lat": sh["x"], "valid_lat": sh["valid"], "invc_lat": sh["invc"],
                  "mod_lat": vec_fm(mods_l[b].reshape(6, 2048))})
        if with_ctx:
            sh = cs[b][k]
            m.update({"x_ctx": sh["x"], "valid_ctx": sh["valid"], "invc_ctx": sh["invc"],
                      "mod_ctx": vec_fm(mods_l[2].reshape(6, 2048))})
        maps.append(m)
    res = run_bass_kernel_spmd(nc, maps, core_ids=list(range(8)))
    xo = np.zeros_like(x)
    co = np.zeros_like(ctx) if with_ctx else None
    for i in range(8):
        b, k = i // 4, i % 4
        xo[b, k * 2048:(k + 1) * 2048] = from_fm(res.results[i]["out_lat"])
        if with_ctx:
            co[b, k * 64:(k + 1) * 64] = from_fm(res.results[i]["out_ctx"])
    return xo, co


DH = 128
NHEAD = 8
GRID_W = 64
CTX = 256
TLAT = 2048
TCTX = 64


def build_a1():
    nc = bass.Bass("TRN2", target_bir_lowering=False)
    x_lat = nc.dram_tensor("x_lat", [128, 16, TLAT], F32, kind="ExternalInput").ap()
    x_ctx = nc.dram_tensor("x_ctx", [128, 16, TCTX], F32, kind="ExternalInput").ap()
    mod_lat = nc.dram_tensor("mod_lat", [128, 6, 16], F32, kind="ExternalInput").ap()
    mod_ctx = nc.dram_tensor("mod_ctx", [128, 6, 16], F32, kind="ExternalInput").ap()
    normg = nc.dram_tensor("normg", [128, 2, 16], F32, kind="ExternalInput").ap()
    wqkv = nc.dram_tensor("wqkv", [48, 128, 16, 128], BF16, kind="ExternalInput").ap()
    qkg = nc.dram_tensor("qkg", [128, 2], F32, kind="ExternalInput").ap()
    cs_d = nc.dram_tensor("cs", [128, TLAT], F32, kind="ExternalInput").ap()
    sn_d = nc.dram_tensor("sn", [128, TLAT], F32, kind="ExternalInput").ap()
    qT = nc.dram_tensor("qT", [16, 128, TLAT], BF16, kind="ExternalOutput").ap()
    kT = nc.dram_tensor("kT", [16, 128, TLAT], BF16, kind="ExternalOutput").ap()
    vT = nc.dram_tensor("vT", [16, 128, TLAT], BF16, kind="ExternalOutput").ap()
    kcT = nc.dram_tensor("kcT", [16, 128, TCTX], BF16, kind="ExternalOutput").ap()
    vcT = nc.dram_tensor("vcT", [16, 128, TCTX], BF16, kind="ExternalOutput").ap()

    P = Prog(nc)
    cm = Common(P, 512, halo=0)
    cm.setup_eps()
    TALL = TLAT + TCTX
    xb = P.sbuf("xb", [128, 16, 512], F32)
    hT = P.sbuf("hT", [128, 16, TALL], BF16)
    wt = [P.sbuf("wt%d" % i, [128, 16, 128], BF16) for i in range(3)]
    cs = P.sbuf("cs_sb", [128, TLAT], F32)
    sn = P.sbuf("sn_sb", [128, TLAT], F32)
    ng = P.sbuf("ng", [128, 2, 16], F32)
    g_sb = P.sbuf("qkg_sb", [128, 2], F32)
    qn = [P.sbuf("qn%d" % i, [128, 512], F32) for i in range(2)]
    sw = [P.sbuf("sw%d" % i, [128, 512], F32) for i in range(2)]
    tm = [P.sbuf("tm%d" % i, [128, 512], F32) for i in range(2)]
    oo = [P.sbuf("oo%d" % i, [128, 512], F32) for i in range(2)]
    sq = [P.sbuf("sq%d" % i, [128, 512], BF16) for i in range(2)]
    rr = [P.sbuf("rr%d" % i, [128, 512], F32) for i in range(2)]
    stg = [P.sbuf("stg%d" % i, [128, TLAT], BF16) for i in range(2)]
    stgc = [P.sbuf("stgc%d" % i, [128, TCTX], BF16) for i in range(2)]
    P.dma("sp", ng[:], normg, writes=["ng"])
    P.dma("sp", g_sb[:], qkg, writes=["qkg"])
    P.dma("sp", cs[:], cs_d, writes=["cs"])
    P.dma("sp", sn[:], sn_d, writes=["sn"])
    xkeys = [("xb", c) for c in range(16)]
    segs = [("lat", x_lat, mod_lat, TLAT, 0), ("ctx", x_ctx, mod_ctx, TCTX, TLAT)]
    for name, xd, md, T, off in segs:
        mod = P.sbuf("modsb_" + name, [128, 6, 16], F32)
        G1 = P.sbuf("G1_" + name, [128, 16], F32)
        mk = ("mod", name)
        P.dma("sp", mod[:], md, writes=[mk])
        P.op("dve", lambda e, G1=G1, mod=mod: e.scalar_tensor_tensor(
            out=G1[:], in0=mod[:, 1, :], scalar=1.0, in1=ng[:, 0, :], op0=ALU.add, op1=ALU.mult),
            reads=[mk, "ng"], writes=[("G1", name)])
        for t0 in range(0, T, 512):
            n = min(512, T - t0)
            P.dma("sp", xb[:, :, :n], xd[:, :, t0:t0 + n], writes=xkeys)
            hk = [("hT", c, (off + t0) // 512) for c in range(16)]
            cm.norm_mod(xb[:, :, :n], n, (G1, ("G1", name)), (mod[:, 0, :], mk),
                        lambda c, o=off + t0, n=n: hT[:, c, o:o + n], hk, xkeys, stat_bank=0)
    blocks = [(i * 512, 512, i) for i in range(4)] + [(TLAT, TCTX, 4)]
    st = dict(k=0, ps=0, ss=0, t=0)
    out_toks = []

    def proj(m, wtile, wkey, jj, t0, n, bi, kind):
        pb = 2 + st["ps"] % 2
        st["ps"] += 1
        ps = cm.psb[pb]
        for c in range(16):
            P.op("pe", lambda e, c=c: e.matmul(ps[:, :n], lhsT=wtile[:, c, jj * 128:(jj + 1) * 128], rhs=hT[:, c, t0:t0 + n],
                                                start=(c == 0), stop=(c == 15)),
                 reads=[wkey, ("hT", c, bi)], writes=[("psb", pb)])
        return ps, ("psb", pb)

    for mb in range(48):
        s = st["k"] % 3
        st["k"] += 1
        P.dma("sp", wt[s][:], wqkv[mb], writes=[("wt", s)])
        for jj in range(1):
            m = mb
            kind = "q" if m < 16 else ("k" if m < 32 else "v")
            mi = m % 16
            sg = m % 2
            gcol = 0 if kind == "q" else 1
            for (t0, n, bi) in blocks:
                is_ctx = bi == 4
                if is_ctx and kind == "q":
                    continue
                ps, pk = proj(m, wt[s], ("wt", s), jj, t0, n, bi, kind)
                dst = stgc[sg][:, :n] if is_ctx else stg[sg][:, t0:t0 + n]
                dkey = ("stgc", sg) if is_ctx else ("stg", sg, bi)
                if kind == "v":
                    P.op("act", lambda e, ps=ps, dst=dst, n=n: e.activation(out=dst, in_=ps[:, :n], func=AF.Copy),
                         reads=[pk], writes=[dkey])
                    continue
                u = st["t"] % 2
                st["t"] += 1
                sb = 4 + st["ss"] % 2
                st["ss"] += 1
                pss = cm.psb[sb]
                P.op("act", lambda e, ps=ps, u=u, n=n: e.activation(out=sq[u][:, :n], in_=ps[:, :n], func=AF.Square),
                     reads=[pk], writes=[("sq", u)])
                P.op("pe", lambda e, pss=pss, u=u, n=n: e.matmul(pss[:, :n], lhsT=cm.ones[:], rhs=sq[u][:, :n], start=True, stop=True),
                     reads=[("sq", u), "ones"], writes=[("psb", sb)])
                P.op("act", lambda e, pss=pss, u=u, n=n: e.activation(out=rr[u][:, :n], in_=pss[:, :n], func=AF.Sqrt,
                                                                    scale=1.0 / DH, bias=cm.epsb[:, 0:1]),
                     reads=[("psb", sb), "epsb"], writes=[("rr", u)])
                P.op("dve", lambda e, u=u, n=n: e.reciprocal(out=rr[u][:, :n], in_=rr[u][:, :n]),
                     reads=[("rr", u)], writes=[("rr", u)])
                if is_ctx:
                    P.op("dve", lambda e, ps=ps, u=u, n=n, dst=dst, gcol=gcol: e.scalar_tensor_tensor(
                        out=dst, in0=ps[:, :n], scalar=g_sb[:, gcol:gcol + 1], in1=rr[u][:, :n], op0=ALU.mult, op1=ALU.mult),
                        reads=[pk, "qkg", ("rr", u)], writes=[dkey])
                    continue
                P.op("dve", lambda e, ps=ps, u=u, n=n, gcol=gcol: e.scalar_tensor_tensor(
                    out=qn[u][:, :n], in0=ps[:, :n], scalar=g_sb[:, gcol:gcol + 1], in1=rr[u][:, :n], op0=ALU.mult, op1=ALU.mult),
                    reads=[pk, "qkg", ("rr", u)], writes=[("qn", u)])
                P.op("act", lambda e, u=u, n=n: e.activation(out=sw[u][0:64, :n], in_=qn[u][64:128, :n], func=AF.Copy),
                     reads=[("qn", u)], writes=[("sw", u, 0)])
                P.op("act", lambda e, u=u, n=n: e.activation(out=sw[u][64:128, :n], in_=qn[u][0:64, :n], func=AF.Copy),
                     reads=[("qn", u)], writes=[("sw", u, 1)])
                P.op("pool", lambda e, u=u, n=n, t0=t0: e.tensor_tensor(out=tm[u][:, :n], in0=sw[u][:, :n], in1=sn[:, t0:t0 + n], op=ALU.mult),
                     reads=[("sw", u, 0), ("sw", u, 1), "sn"], writes=[("tm", u)])
                P.op("dve", lambda e, u=u, n=n, t0=t0: e.tensor_tensor(out=oo[u][:, :n], in0=qn[u][:, :n], in1=cs[:, t0:t0 + n], op=ALU.mult),
                     reads=[("qn", u), "cs"], writes=[("oo", u)])
                P.op("dve", lambda e, u=u, n=n, dst=dst: e.tensor_tensor(out=dst, in0=oo[u][:, :n], in1=tm[u][:, :n], op=ALU.add),
                     reads=[("oo", u), ("tm", u)], writes=[dkey])
            od = {"q": qT, "k": kT, "v": vT}[kind]
            out_toks.append(P.dma("sp", od[mi], stg[sg][:], reads=[("stg", sg, bi) for bi in range(4)], writes=[("o", kind, mi)]))
            if kind != "q":
                oc = {"k": kcT, "v": vcT}[kind]
                out_toks.append(P.dma("sp", oc[mi], stgc[sg][:], reads=[("stgc", sg)], writes=[("oc", kind, mi)]))
    P.final_wait("sp", out_toks)
    return P.build()


def rope_perm():
    return np.concatenate([np.arange(0, 128, 2), np.arange(1, 128, 2)])


def rope_tables(tok0, n):
    t = np.arange(tok0, tok0 + n)
    row = (t // GRID_W).astype(np.float32)
    col = (t % GRID_W).astype(np.float32)
    nf = DH // 4
    inv = (10000.0 ** (-np.arange(nf, dtype=np.float32) / nf)).astype(np.float32)
    ang = np.concatenate([row[:, None] * inv, col[:, None] * inv], -1)
    c = np.cos(ang).astype(np.float32).T
    s = np.sin(ang).astype(np.float32).T
    cs = np.concatenate([c, c], 0)
    sn = np.concatenate([-s, s], 0)
    return np.ascontiguousarray(cs), np.ascontiguousarray(sn)


def run_a1(x, ctx, mods_l, normg_l, w_qkv, qk_g):
    nc = build_a1()
    perm = rope_perm()
    w = w_qkv
    qkg = np.ascontiguousarray(qk_g[:, perm].T)
    maps = []
    for i in range(8):
        b, k = i // 4, i % 4
        cs, sn = rope_tables(k * TLAT, TLAT)
        maps.append({"x_lat": to_fm(x[b, k * TLAT:(k + 1) * TLAT]), "x_ctx": to_fm(ctx[b, k * TCTX:(k + 1) * TCTX]),
                     "mod_lat": vec_fm(mods_l[b].reshape(6, 2048)), "mod_ctx": vec_fm(mods_l[2].reshape(6, 2048)),
                     "normg": vec_fm(normg_l), "wqkv": w, "qkg": qkg, "cs": cs, "sn": sn})
    res = run_bass_kernel_spmd(nc, maps, core_ids=list(range(8)))
    return res.results


NKT = (CTX + 8192) // 128
KH = NKT // 2


def build_a2(lambda_init, dbg=0):
    nc = bass.Bass("TRN2", target_bir_lowering=False)
    qT = nc.dram_tensor("qT", [16, 128, TLAT], BF16, kind="ExternalInput").ap()
    kT = nc.dram_tensor("kT", [16, 128, NKT * 128], BF16, kind="ExternalInput").ap()
    vv = nc.dram_tensor("vv", [8, 128, NKT, 256], BF16, kind="ExternalInput").ap()
    x_lat = nc.dram_tensor("x_lat", [128, 16, TLAT], F32, kind="ExternalInput").ap()
    mod_lat = nc.dram_tensor("mod_lat", [128, 6, 16], F32, kind="ExternalInput").ap()
    normg = nc.dram_tensor("normg", [128, 2, 16], F32, kind="ExternalInput").ap()
    lamv = nc.dram_tensor("lamv", [128, 4], F32, kind="ExternalInput").ap()
    sublng = nc.dram_tensor("sublng", [128, 2], F32, kind="ExternalInput").ap()
    w_o = nc.dram_tensor("w_o", [16, 128, 16, 128], BF16, kind="ExternalInput").ap()
    w_in = nc.dram_tensor("w_in", [NJ, 2, 128, 16, 128], BF16, kind="ExternalInput").ap()
    w_out = nc.dram_tensor("w_out", [16, 128, NJ, 128], BF16, kind="ExternalInput").ap()
    out = nc.dram_tensor("out_lat", [128, 16, TLAT], F32, kind="ExternalOutput").ap()

    P = Prog(nc)
    cm = Common(P, 512, halo=0)
    cm.setup_eps()
    ffn = FFN(P, cm, w_in, w_out, 512, nsplit=4, WC=128)
    xb = P.sbuf("xb", [128, 16, 512], F32)
    hT = P.sbuf("hT", [128, 16, 512], BF16)
    ring = [dict(k=P.sbuf("rk%d" % s, [128, 2, KH * 128], BF16), v=P.sbuf("rv%d" % s, [128, KH, 256], BF16)) for s in range(2)]
    qsb = [P.sbuf("qsb%d" % s, [128, 2, 512], BF16) for s in range(2)]
    pT = [P.sbuf("pT%d" % s, [128, 512], BF16) for s in range(3)]
    osb = [P.sbuf("osb%d" % i, [128, 2, 512], F32) for i in range(2)]
    rden = [P.sbuf("rden%d" % i, [128, 512], F32) for i in range(2)]
    dacc = [P.sbuf("dacc%d" % i, [128, 512], F32) for i in range(2)]
    dif = P.sbuf("dif", [128, 2, 512], F32)
    sqd = P.sbuf("sqd", [128, 2, 512], BF16)
    rst = P.sbuf("rst", [128, 512], F32)
    ng = P.sbuf("ng", [128, 2, 16], F32)
    mod = P.sbuf("modsb", [128, 6, 16], F32)
    G2 = P.sbuf("G2", [128, 16], F32)
    lv = P.sbuf("lv", [128, 4], F32)
    lpr = P.sbuf("lpr", [128, 2], F32)
    lex = P.sbuf("lex", [128, 2], F32)
    nlam = P.sbuf("nlam", [128, 1], F32)
    sg = P.sbuf("sg", [128, 2], F32)
    ones32 = P.sbuf("ones32", [128, 128], F32)
    eps256 = cm.epsb
    P.dma("sp", ng[:], normg, writes=["ng"])
    P.dma("sp", mod[:], mod_lat, writes=["mod"])
    P.dma("sp", lv[:], lamv, writes=["lv"])
    P.dma("sp", sg[:], sublng, writes=["sg"])
    P.op("dve", lambda e: e.memset(ones32[:], 1.0), writes=["ones32"])
    P.op("dve", lambda e: e.scalar_tensor_tensor(out=G2[:], in0=mod[:, 4, :], scalar=1.0, in1=ng[:, 1, :],
                                                 op0=ALU.add, op1=ALU.mult), reads=["mod", "ng"], writes=["G2"])
    P.op("dve", lambda e: e.tensor_tensor(out=lpr[:, 0:1], in0=lv[:, 0:1], in1=lv[:, 1:2], op=ALU.mult), reads=["lv"], writes=["lpr0"])
    P.op("dve", lambda e: e.tensor_tensor(out=lpr[:, 1:2], in0=lv[:, 2:3], in1=lv[:, 3:4], op=ALU.mult), reads=["lv"], writes=["lpr1"])
    P.op("pe", lambda e: e.matmul(cm.psb[0][:, 0:2], lhsT=ones32[:], rhs=lpr[:], start=True, stop=True),
         reads=["ones32", "lpr0", "lpr1"], writes=[("psb", 0)])
    P.op("act", lambda e: e.activation(out=lex[:], in_=cm.psb[0][:, 0:2], func=AF.Exp), reads=[("psb", 0)], writes=["lex"])
    P.op("dve", lambda e: e.tensor_tensor(out=nlam[:], in0=lex[:, 1:2], in1=lex[:, 0:1], op=ALU.subtract), reads=["lex"], writes=["nlam"])
    P.op("dve", lambda e: e.tensor_scalar(out=nlam[:], in0=nlam[:], scalar1=-float(lambda_init), scalar2=None, op0=ALU.add),
         reads=["nlam"], writes=["nlam"])
    P.op("dve", lambda e: e.tensor_scalar(out=sg[:], in0=sg[:], scalar1=float(1.0 - lambda_init), scalar2=None, op0=ALU.mult),
         reads=["sg"], writes=["sg"])
    xkeys = [("xb", c) for c in range(16)]
    hkeys = [("hT", c) for c in range(16)]
    out_toks = []
    st = dict(ring=0, q=0, pt=0, sT=0, wo=0, py=0)
    SCALE = float(DH) ** -0.5

    def attn_head(qb, h):
        qs = st["q"] % 2
        st["q"] += 1
        for i in range(2):
            P.dma("sp", qsb[qs][:, i, :], qT[2 * h + i][:, qb * 512:(qb + 1) * 512], writes=[("qsb", qs, i)])
        for half in range(2):
            rs_ = st["ring"] % 2
            st["ring"] += 1
            rg = ring[rs_]
            for i in range(2):
                P.dma("sp", rg["k"][:, i, :], kT[2 * h + i][:, half * KH * 128:(half + 1) * KH * 128], writes=[("rk", rs_, i)])
            P.dma("sp", rg["v"][:], vv[h][:, half * KH:(half + 1) * KH, :], writes=[("rv", rs_)])
            steps = [(i, kt) for i in range(2) for kt in range(KH)]

            def emit_s(i, kt, rg=rg, rs_=rs_):
                sb = st["sT"] % 2
                st["sT"] += 1
                pss = cm.psb[sb]
                P.op("pe", lambda e, pss=pss, rg=rg, i=i, kt=kt: e.matmul(
                    pss[:, :], lhsT=rg["k"][:, i, kt * 128:(kt + 1) * 128], rhs=qsb[qs][:, i, :], start=True, stop=True),
                    reads=[("rk", rs_, i), ("qsb", qs, i)], writes=[("psb", sb)])
                pi = st["pt"] % 3
                st["pt"] += 1
                P.op("act", lambda e, pss=pss, pi=pi: e.activation(out=pT[pi][:], in_=pss[:, :], func=AF.Exp, scale=SCALE),
                     reads=[("psb", sb)], writes=[("pT", pi)])
                return pi

            def emit_pv(i, kt, pi, rg=rg, rs_=rs_, half=half):
                first = (half == 0 and kt == 0)
                last = (half == 1 and kt == KH - 1)
                for ec in range(2):
                    P.op("pe", lambda e, rg=rg, kt=kt, ec=ec, pi=pi, i=i, first=first, last=last: e.matmul(
                        cm.psb[2 + 2 * i + ec][:, :], lhsT=rg["v"][:, kt, ec * 128:(ec + 1) * 128], rhs=pT[pi][:],
                        start=first, stop=last),
                        reads=[("rv", rs_), ("pT", pi)], writes=[("psb", 2 + 2 * i + ec)])
                if first:
                    P.op("dve", lambda e, pi=pi, i=i: e.tensor_copy(out=dacc[i][:], in_=pT[pi][:]),
                         reads=[("pT", pi)], writes=[("dacc", i)])
                else:
                    P.op("dve", lambda e, pi=pi, i=i: e.tensor_tensor(out=dacc[i][:], in0=dacc[i][:], in1=pT[pi][:], op=ALU.add),
                         reads=[("pT", pi), ("dacc", i)], writes=[("dacc", i)])

            pend = emit_s(*steps[0])
            for j in range(len(steps)):
                nxt = emit_s(*steps[j + 1]) if j + 1 < len(steps) else None
                emit_pv(steps[j][0], steps[j][1], pend)
                pend = nxt
        for i in range(2):
            P.op("pe", lambda e, i=i: e.matmul(cm.psb[6 + i][:, :], lhsT=ones32[:], rhs=dacc[i][:], start=True, stop=True),
                 reads=["ones32", ("dacc", i)], writes=[("psb", 6 + i)])
            P.op("dve", lambda e, i=i: e.reciprocal(out=rden[i][:], in_=cm.psb[6 + i][:, :]),
                 reads=[("psb", 6 + i)], writes=[("rden", i)])
            for ec in range(2):
                P.op("dve", lambda e, i=i, ec=ec: e.tensor_tensor(out=osb[i][:, ec, :], in0=cm.psb[2 + 2 * i + ec][:, :],
                                                                  in1=rden[i][:], op=ALU.mult),
                     reads=[("psb", 2 + 2 * i + ec), ("rden", i)], writes=[("osb", i, ec)])
        for ec in range(2):
            P.op("dve", lambda e, ec=ec: e.scalar_tensor_tensor(out=dif[:, ec, :], in0=osb[1][:, ec, :], scalar=nlam[:, 0:1],
                                                                in1=osb[0][:, ec, :], op0=ALU.mult, op1=ALU.add),
                 reads=[("osb", 1, ec), ("osb", 0, ec), "nlam"], writes=[("dif", ec)])
            P.op("act", lambda e, ec=ec: e.activation(out=sqd[:, ec, :], in_=dif[:, ec, :], func=AF.Square),
                 reads=[("dif", ec)], writes=[("sqd", ec)])
        for ec in range(2):
            P.op("pe", lambda e, ec=ec: e.matmul(cm.psb[0][:, :], lhsT=cm.ones[:], rhs=sqd[:, ec, :], start=(ec == 0), stop=(ec == 1)),
                 reads=["ones", ("sqd", ec)], writes=[("psb", 0)])
        P.op("act", lambda e: e.activation(out=rst[:], in_=cm.psb[0][:, :], func=AF.Sqrt, scale=1.0 / 256.0, bias=cm.epsb[:, 0:1]),
             reads=[("psb", 0), "epsb"], writes=["rst"])
        P.op("dve", lambda e: e.reciprocal(out=rst[:], in_=rst[:]), reads=["rst"], writes=["rst"])
        for ec in range(2):
            P.op("dve", lambda e, ec=ec: e.scalar_tensor_tensor(out=hT[:, 2 * h + ec, :], in0=dif[:, ec, :], scalar=sg[:, ec:ec + 1],
                                                                in1=rst[:], op0=ALU.mult, op1=ALU.mult),
                 reads=[("dif", ec), "sg", "rst"], writes=[hkeys[2 * h + ec]])

    def do_block(qb):
        for h in range(NHEAD):
            attn_head(qb, h)
        P.dma("sp", xb[:], x_lat[:, :, qb * 512:(qb + 1) * 512], writes=xkeys)
        for m in range(16):
            s = ffn.kin % 2
            ffn.kin += 1
            wt = ffn.wg[s]
            P.dma("sp", wt[:], w_o[m], writes=[("wg", s)])
            q = ffn.km % 2
            ffn.km += 1
            py = cm.psb[6 + q]
            for c in range(16):
                P.op("pe", lambda e, wt=wt, py=py, c=c: e.matmul(py[:, :], lhsT=wt[:, c, :], rhs=hT[:, c, :],
                                                                  start=(c == 0), stop=(c == 15)),
                     reads=[("wg", s), hkeys[c]], writes=[("psb", 6 + q)])
            P.op("dve", lambda e, py=py, m=m: e.scalar_tensor_tensor(
                out=xb[:, m, :], in0=py[:, :], scalar=mod[:, 2, m:m + 1], in1=xb[:, m, :], op0=ALU.mult, op1=ALU.add),
                reads=[("psb", 6 + q), "mod", xkeys[m]], writes=[xkeys[m]])
        if dbg == 0:
            cm.norm_mod(xb[:, :, :], 512, (G2, "G2"), (mod[:, 3, :], "mod"), lambda c: hT[:, c, :], hkeys, xkeys, stat_bank=0)
            ffn.emit(hT, hkeys, 512, xb, 0, xkeys, (mod[:, 5, :], "mod"))
        out_toks.append(P.dma("sp", out[:, :, qb * 512:(qb + 1) * 512], xb[:], reads=xkeys, writes=[("out", qb)]))

    for qb in range(4):
        do_block(qb)
    P.final_wait("sp", out_toks)
    return P.build()


def run_a2(a1res, x, mods_l, normg_l, lam_vec, subln_g, w_o, w_in, w_out, lambda_init, dbg=0):
    nc = build_a2(lambda_init, dbg)
    maps = []
    kv = []
    for b in range(2):
        kparts = [a1res[b * 4 + k]["kcT"] for k in range(4)] + [a1res[b * 4 + k]["kT"] for k in range(4)]
        kall = np.ascontiguousarray(np.concatenate(kparts, axis=2))
        vparts = [a1res[b * 4 + k]["vcT"] for k in range(4)] + [a1res[b * 4 + k]["vT"] for k in range(4)]
        vall = np.concatenate(vparts, axis=2)
        v5 = vall.reshape(8, 2, 128, NKT, 128)
        v5 = np.ascontiguousarray(v5.transpose(0, 4, 3, 1, 2).reshape(8, 128, NKT, 256))
        kv.append((kall, v5))
    for i in range(8):
        b, k = i // 4, i % 4
        maps.append({"qT": a1res[i]["qT"], "kT": kv[b][0], "vv": kv[b][1], "x_lat": to_fm(x[b, k * TLAT:(k + 1) * TLAT]),
                     "mod_lat": vec_fm(mods_l[b].reshape(6, 2048)), "normg": vec_fm(normg_l),
                     "lamv": np.ascontiguousarray(lam_vec.T), "sublng": np.ascontiguousarray(subln_g.reshape(2, 128).T),
                     "w_o": w_o, "w_in": w_in, "w_out": w_out})
    res = run_bass_kernel_spmd(nc, maps, core_ids=list(range(8)))
    xo = np.zeros_like(x)
    for i in range(8):
        b, k = i // 4, i % 4
        xo[b, k * TLAT:(k + 1) * TLAT] = from_fm(res.results[i]["out_lat"])
    return xo


LW = 96
NSET = 1
LG = 256
C0 = 0.6065306597126334
R1_TCTX = 128


def build_r1(segs=(("lat", TLAT), ("ctx", R1_TCTX)), NB=384):
    nc = bass.Bass("TRN2", target_bir_lowering=False)
    dr = {}
    for name, T in segs:
        dr[name] = dict(
            x=nc.dram_tensor("x_" + name, [128, 16, T + 2], F32, kind="ExternalInput").ap(),
            valid=nc.dram_tensor("valid_" + name, [T + 2], F32, kind="ExternalInput").ap(),
            mod=nc.dram_tensor("mod_" + name, [128, 6, 16], F32, kind="ExternalInput").ap(),
            ot=nc.dram_tensor("ot_" + name, [2, 4, 16, 128, T], BF16, kind="ExternalOutput").ap(),
            pc=nc.dram_tensor("pc_" + name, [128, 2, 16, T // 128], F32, kind="ExternalOutput").ap(),
            v=nc.dram_tensor("v_" + name, [16, 128, T], BF16, kind="ExternalOutput").ap(),
            bonus=nc.dram_tensor("bonus_" + name, [16, 128, T], F32, kind="ExternalOutput").ap(),
            g=nc.dram_tensor("g_" + name, [16, 128, T], F32, kind="ExternalOutput").ap(),
        )
    normg = nc.dram_tensor("normg", [128, 2, 16], F32, kind="ExternalInput").ap()
    mu_d = nc.dram_tensor("mu", [128, 6, 16], F32, kind="ExternalInput").ap()
    w_rkv = nc.dram_tensor("w_rkv", [3, 16, 128, 16, 128], BF16, kind="ExternalInput").ap()
    w_la = nc.dram_tensor("w_la", [2, D, LW], F32, kind="ExternalInput").ap()
    w_lb = nc.dram_tensor("w_lb", [2, LW, D], F32, kind="ExternalInput").ap()
    a_la = nc.dram_tensor("a_la", [2, D, LW], F32, kind="ExternalInput").ap()
    a_lb = nc.dram_tensor("a_lb", [2, LW, D], F32, kind="ExternalInput").ap()
    g_la = nc.dram_tensor("g_la", [D, LG], F32, kind="ExternalInput").ap()
    g_lb = nc.dram_tensor("g_lb", [LG, D], F32, kind="ExternalInput").ap()
    dirvec = nc.dram_tensor("dirvec", [128, 2, 4, 16], F32, kind="ExternalInput").ap()
    rk_d = nc.dram_tensor("r_k", [128, 16], F32, kind="ExternalInput").ap()
    rmask_d = nc.dram_tensor("rmask", [128, NB], F32, kind="ExternalInput").ap()

    P = Prog(nc)
    cm = Common(P, NB, halo=1)
    cm.setup_eps()
    W = NB + 2
    xb = P.sbuf("xb", [128, 16, W], F32)
    xx = P.sbuf("xx", [128, 16, NB], BF16)
    xm = [P.sbuf("xm%d" % i, [128, 16, NB], BF16) for i in range(3)]
    vmask = P.sbuf("vmask", [128, W], F32)
    ng = P.sbuf("ng", [128, 2, 16], F32)
    mu = P.sbuf("mu_sb", [128, 6, 16], F32)
    dv = P.sbuf("dv_sb", [128, 2, 4, 16], F32)
    rk = P.sbuf("rk_sb", [128, 16], F32)
    rmask = P.sbuf("rmask_sb", [128, NB], F32)
    bd = P.sbuf("bd_bf", [128, 128], BF16)
    wla = [P.sbuf("wla%d" % d, [128, 16, LW], BF16) for d in range(2)]
    ala = [P.sbuf("ala%d" % d, [128, 16, LW], BF16) for d in range(2)]
    gla = P.sbuf("gla", [128, 16, LG], BF16)
    wlb = [P.sbuf("wlb%d" % d, [LW, D], BF16) for d in range(2)]
    alb = [P.sbuf("alb%d" % d, [LW, D], BF16) for d in range(2)]
    glb = P.sbuf("glb", [128, 2, D], BF16)
    tw = [P.sbuf("tw%d" % d, [LW, NB], BF16) for d in range(2)]
    al = [P.sbuf("al%d" % d, [LW, NB], BF16) for d in range(2)]
    sgl = P.sbuf("sgl", [128, 2, NB], BF16)
    wt = [P.sbuf("wt%d" % i, [128, 16, 128], BF16) for i in range(6)]
    T2 = {}

    def tmp(name, dt=F32, n=NB):
        if name not in T2:
            T2[name] = P.sbuf("t_" + name, [128, n], dt)
        return T2[name]

    P.dma("sp", ng[:], normg, writes=["ng"])
    P.dma("sp", mu[:], mu_d, writes=["mu"])
    P.dma("sp", dv[:], dirvec, writes=["dv"])
    P.dma("sp", rk[:], rk_d, writes=["rk"])
    P.dma("sp", rmask[:], rmask_d, writes=["rmask"])
    P.op("pool", lambda e: e.memset(bd[:], 0.0), writes=["bd"])
    P.op("pool", lambda e: e.memset(bd[0:64, 0:64], 1.0), writes=["bd"])
    P.op("pool", lambda e: e.memset(bd[64:128, 64:128], 1.0), writes=["bd"])
    for d in range(2):
        P.dma("pool", wla[d][:], w_la[d].rearrange("(c p) n -> p c n", p=128), writes=[("wla", d)])
        P.dma("pool", ala[d][:], a_la[d].rearrange("(c p) n -> p c n", p=128), writes=[("ala", d)])
        P.dma("pool", wlb[d][:], w_lb[d], writes=[("wlb", d)])
        P.dma("pool", alb[d][:], a_lb[d], writes=[("alb", d)])
    P.dma("pool", gla[:], g_la.rearrange("(c p) n -> p c n", p=128), writes=["gla"])
    P.dma("pool", glb[:], g_lb.rearrange("(k p) n -> p k n", p=128), writes=["glb"])

    xkeys = [("xb", c) for c in range(16)]
    bank_rr = [0]

    def nb_():
        b_ = bank_rr[0] % 5 + 2
        bank_rr[0] += 1
        return cm.psb[b_], ("psb", b_)

    out_toks = []
    st = dict(wt=0, stg=0)

    def do_block(name, T, t0, n, mod, G1, pcs_all):
        d_ = dr[name]
        nw = n + 2
        nj = n // 128
        mk = ("mod", name)
        P.dma("sp", xb[:, :, :nw], d_["x"][:, :, t0:t0 + nw], writes=xkeys)
        P.dma("sp", vmask[:, :nw], d_["valid"][t0:t0 + nw].partition_broadcast(128), writes=["vmask"])
        cm.norm_mod(xb[:, :, :nw], nw, (G1, ("G1", name)), (mod[:, 0, :], mk),
                    lambda c: xb[:, c, :nw], xkeys, xkeys, stat_bank=0)
        for col in (0, nw - 1):
            P.op("dve", lambda e, col=col: e.tensor_tensor(
                out=xb[:, :, col:col + 1], in0=xb[:, :, col:col + 1],
                in1=vmask[:, col:col + 1].unsqueeze(1).broadcast_to([128, 16, 1]), op=ALU.mult),
                reads=xkeys + ["vmask"], writes=xkeys)
        for c in range(16):
            tq = tmp("xs%d" % (c % 2))
            P.op("pool", lambda e, c=c, tq=tq: e.tensor_tensor(out=tq[:, :n], in0=xb[:, c, 0:n], in1=xb[:, c, 2:n + 2], op=ALU.add),
                 reads=[xkeys[c]], writes=[("xs", c % 2)])
            P.op("dve", lambda e, c=c, tq=tq: e.scalar_tensor_tensor(out=xx[:, c, :n], in0=tq[:, :n], scalar=0.5, in1=xb[:, c, 1:n + 1],
                                                                     op0=ALU.mult, op1=ALU.subtract),
                 reads=[("xs", c % 2), xkeys[c]], writes=[("xx", c)])

        def mix(m, buf):
            for c in range(16):
                P.op("dve", lambda e, c=c: e.scalar_tensor_tensor(out=xm[buf][:, c, :n], in0=xx[:, c, :n], scalar=mu[:, m, c:c + 1],
                                                                  in1=xb[:, c, 1:n + 1], op0=ALU.mult, op1=ALU.add),
                     reads=[("xx", c), "mu", xkeys[c]], writes=[("xm", buf, c)])

        mix(1, 0)
        for d in range(2):
            bank, bk = nb_()
            for c in range(16):
                P.op("pe", lambda e, c=c, d=d, bank=bank: e.matmul(bank[0:LW, :n], lhsT=wla[d][:, c, :], rhs=xm[0][:, c, :n],
                                                                    start=(c == 0), stop=(c == 15)),
                     reads=[("wla", d), ("xm", 0, c)], writes=[bk])
            P.op("act", lambda e, d=d, bank=bank: e.activation(out=tw[d][:, :n], in_=bank[0:LW, :n], func=AF.Tanh),
                 reads=[bk], writes=[("tw", d)])
        mix(4, 1)
        for d in range(2):
            bank, bk = nb_()
            for c in range(16):
                P.op("pe", lambda e, c=c, d=d, bank=bank: e.matmul(bank[0:LW, :n], lhsT=ala[d][:, c, :], rhs=xm[1][:, c, :n],
                                                                    start=(c == 0), stop=(c == 15)),
                     reads=[("ala", d), ("xm", 1, c)], writes=[bk])
            P.op("act", lambda e, d=d, bank=bank: e.activation(out=al[d][:, :n], in_=bank[0:LW, :n], func=AF.Copy),
                 reads=[bk], writes=[("al", d)])
        mix(5, 2)
        for kc in range(2):
            bank, bk = nb_()
            for c in range(16):
                P.op("pe", lambda e, c=c, kc=kc, bank=bank: e.matmul(bank[:, :n], lhsT=gla[:, c, kc * 128:(kc + 1) * 128], rhs=xm[2][:, c, :n],
                                                                      start=(c == 0), stop=(c == 15)),
                     reads=["gla", ("xm", 2, c)], writes=[bk])
            P.op("act", lambda e, kc=kc, bank=bank: e.activation(out=sgl[:, kc, :n], in_=bank[:, :n], func=AF.Sigmoid),
                 reads=[bk], writes=[("sgl", kc)])
        mix(0, 0)
        mix(2, 1)
        mix(3, 2)
        for c in range(16):
            outs = []
            for wi in range(3):
                s = st["wt"] % 6
                st["wt"] += 1
                P.dma("sp", wt[s][:], w_rkv[wi, c], writes=[("wt", s)])
                bank, bk = nb_()
                for kc in range(16):
                    P.op("pe", lambda e, kc=kc, s=s, wi=wi, bank=bank: e.matmul(bank[:, :n], lhsT=wt[s][:, kc, :], rhs=xm[wi][:, kc, :n],
                                                                                 start=(kc == 0), stop=(kc == 15)),
                         reads=[("wt", s), ("xm", wi, kc)], writes=[bk])
                dst = tmp(("rc", "kc", "vc")[wi] + str(c % 2))
                P.op("act", lambda e, dst=dst, bank=bank: e.activation(out=dst[:, :n], in_=bank[:, :n], func=AF.Copy),
                     reads=[bk], writes=[("rkv", wi, c % 2)])
                outs.append(dst)
            rc_, kc_, vc_ = outs
            vst = tmp("vst%d" % (c % 2), BF16)
            P.op("pool", lambda e, vst=vst, vc_=vc_: e.tensor_copy(out=vst[:, :n], in_=vc_[:, :n]), reads=[("rkv", 2, c % 2)], writes=[("vst", c % 2)])
            out_toks.append(P.dma("pool", d_["v"][c][:, t0:t0 + n], vst[:, :n], reads=[("vst", c % 2)], writes=[("ov", name, c, t0)]))
            bank, bk = nb_()
            for kc in range(2):
                P.op("pe", lambda e, kc=kc, c=c, bank=bank: e.matmul(bank[:, :n], lhsT=glb[:, kc, c * 128:(c + 1) * 128], rhs=sgl[:, kc, :n],
                                                                      start=(kc == 0), stop=(kc == 1)),
                     reads=["glb", ("sgl", 0), ("sgl", 1)], writes=[bk])
            gst = tmp("gst%d" % (c % 2))
            P.op("act", lambda e, gst=gst, bank=bank: e.activation(out=gst[:, :n], in_=bank[:, :n], func=AF.Copy), reads=[bk], writes=[("gst", c % 2)])
            out_toks.append(P.dma("act", d_["g"][c][:, t0:t0 + n], gst[:, :n], reads=[("gst", c % 2)], writes=[("og", name, c, t0)]))
            cbank, cbk = cm.psb[7], ("psb", 7)
            for d in range(2):
                sid = (2 * c + d) % NSET
                w0 = dv[:, d, 0, c:c + 1]
                a0 = dv[:, d, 1, c:c + 1]
                kk_ = dv[:, d, 2, c:c + 1]
                ka_ = dv[:, d, 3, c:c + 1]
                bank, bk = nb_()
                P.op("pe", lambda e, d=d, c=c, bank=bank: e.matmul(bank[:, :n], lhsT=wlb[d][:, c * 128:(c + 1) * 128], rhs=tw[d][:, :n], start=True, stop=True),
                     reads=[("wlb", d), ("tw", d)], writes=[bk])
                sg_ = tmp("sg_%d" % sid)
                P.op("act", lambda e, bank=bank, sg_=sg_, w0=w0: e.activation(out=sg_[:, :n], in_=bank[:, :n], func=AF.Sigmoid, bias=w0),
                     reads=[bk, "dv"], writes=[("sg", sid)])
                bank, bk = nb_()
                P.op("pe", lambda e, d=d, c=c, bank=bank: e.matmul(bank[:, :n], lhsT=alb[d][:, c * 128:(c + 1) * 128], rhs=al[d][:, :n], start=True, stop=True),
                     reads=[("alb", d), ("al", d)], writes=[bk])
                ag = tmp("ag_%d" % sid)
                P.op("act", lambda e, bank=bank, ag=ag, a0=a0: e.activation(out=ag[:, :n], in_=bank[:, :n], func=AF.Sigmoid, bias=a0),
                     reads=[bk, "dv"], writes=[("ag", sid)])
                sq = tmp("sqk_%d" % sid, BF16)
                P.op("act", lambda e, sq=sq, kc_=kc_, kk_=kk_: e.activation(out=sq[:, :n], in_=kc_[:, :n], func=AF.Square, scale=kk_),
                     reads=[("rkv", 1, c % 2), "dv"], writes=[("sqk", sid)])
                bank, bk = nb_()
                P.op("pe", lambda e, sq=sq, bank=bank: e.matmul(bank[:, :n], lhsT=bd[:], rhs=sq[:, :n], start=True, stop=True),
                     reads=["bd", ("sqk", sid)], writes=[bk])
                rn = tmp("rn_%d" % sid)
                P.op("act", lambda e, rn=rn, bank=bank: e.activation(out=rn[:, :n], in_=bank[:, :n], func=AF.Sqrt), reads=[bk], writes=[("rn", sid)])
                P.op("dve", lambda e, rn=rn: e.tensor_scalar(out=rn[:, :n], in0=rn[:, :n], scalar1=1e-12, scalar2=None, op0=ALU.max),
                     reads=[("rn", sid)], writes=[("rn", sid)])
                P.op("dve", lambda e, rn=rn: e.reciprocal(out=rn[:, :n], in_=rn[:, :n]), reads=[("rn", sid)], writes=[("rn", sid)])
                kkn = tmp("kkn_%d" % sid)
                P.op("dve", lambda e, kkn=kkn, kc_=kc_, rn=rn, kk_=kk_: e.scalar_tensor_tensor(out=kkn[:, :n], in0=kc_[:, :n], scalar=kk_, in1=rn[:, :n],
                                                                                       op0=ALU.mult, op1=ALU.mult),
                     reads=[("rkv", 1, c % 2), ("rn", sid), "dv"], writes=[("kkn", sid)])
                t1 = tmp("t1_%d" % sid)
                P.op("pool", lambda e, t1=t1, ag=ag, ka_=ka_: e.tensor_scalar(out=t1[:, :n], in0=ag[:, :n], scalar1=-1.0, scalar2=ka_, op0=ALU.add, op1=ALU.mult),
                     reads=[("ag", sid), "dv"], writes=[("t1", sid)])
                kd = tmp("kd_%d" % sid)
                P.op("dve", lambda e, kd=kd, t1=t1, kc_=kc_: e.scalar_tensor_tensor(out=kd[:, :n], in0=t1[:, :n], scalar=1.0, in1=kc_[:, :n],
                                                                                op0=ALU.add, op1=ALU.mult),
                     reads=[("t1", sid), ("rkv", 1, c % 2)], writes=[("kd", sid)])
                bs = tmp("bs_%d" % sid)
                P.op("pool", lambda e, bs=bs, kkn=kkn, ag=ag: e.tensor_tensor(out=bs[:, :n], in0=kkn[:, :n], in1=ag[:, :n], op=ALU.mult),
                     reads=[("kkn", sid), ("ag", sid)], writes=[("bs", sid)])
                Lf = tmp("Lf_%d" % sid)
                P.op("dve", lambda e, Lf=Lf, sg_=sg_: e.tensor_tensor_scan(out=Lf[:, :n], data0=rmask[:, :n], data1=sg_[:, :n], initial=0.0,
                                                                          op0=ALU.mult, op1=ALU.add),
                     reads=["rmask", ("sg", sid)], writes=[("Lf", sid)])
                Li = tmp("Li_%d" % sid)
                Lx = tmp("Lx_%d" % sid)
                Lf3 = Lf[:, :n].rearrange("p (j t) -> p j t", t=128)
                tot = Lf3[:, :, 127:128].broadcast_to([128, nj, 128])
                if d == 0:
                    P.op("pool", lambda e, Lx=Lx, Lf=Lf, sg_=sg_: e.tensor_tensor(out=Lx[:, :n], in0=Lf[:, :n], in1=sg_[:, :n], op=ALU.subtract),
                         reads=[("Lf", sid), ("sg", sid)], writes=[("Lx", sid)])
                    Li = Lf
                    lik = ("Lf", sid)
                else:
                    P.op("pool", lambda e, Lx=Lx, Lf3=Lf3, tot=tot: e.tensor_tensor(out=Lx[:, :n].rearrange("p (j t) -> p j t", t=128), in0=tot, in1=Lf3,
                                                                                    op=ALU.subtract),
                         reads=[("Lf", sid)], writes=[("Lx", sid)])
                    P.op("pool", lambda e, Li=Li, Lx=Lx, sg_=sg_: e.tensor_tensor(out=Li[:, :n], in0=Lx[:, :n], in1=sg_[:, :n], op=ALU.add),
                         reads=[("Lx", sid), ("sg", sid)], writes=[("Li", sid)])
                    lik = ("Li", sid)
                ep = tmp("ep_%d" % sid)
                en = tmp("en_%d" % sid)
                ex = tmp("ex_%d" % sid)
                P.op("act", lambda e, ep=ep, Li=Li: e.activation(out=ep[:, :n], in_=Li[:, :n], func=AF.Exp, scale=-C0), reads=[lik], writes=[("ep", sid)])
                P.op("act", lambda e, en=en, Li=Li: e.activation(out=en[:, :n], in_=Li[:, :n], func=AF.Exp, scale=C0), reads=[lik], writes=[("en", sid)])
                P.op("act", lambda e, ex=ex, Lx=Lx: e.activation(out=ex[:, :n], in_=Lx[:, :n], func=AF.Exp, scale=-C0), reads=[("Lx", sid)], writes=[("ex", sid)])
                P.op("act", lambda e, d=d, c=c, Lf3=Lf3: e.activation(out=pcs_all[:, d, c, t0 // 128:t0 // 128 + nj], in_=Lf3[:, :, 127], func=AF.Exp, scale=-C0),
                     reads=[("Lf", sid)], writes=[("pcs", name)])
                sgi = st["stg"] % 2
                st["stg"] += 1
                stg = tmp("stg%d" % sgi, BF16, 4 * NB)
                sk = ("stg", sgi)
                P.op("dve", lambda e, stg=stg, kkn=kkn, ex=ex: e.scalar_tensor_tensor(out=stg[:, 0:n], in0=kkn[:, :n], scalar=-1.0, in1=ex[:, :n],
                                                                                  op0=ALU.mult, op1=ALU.mult),
                     reads=[("kkn", sid), ("ex", sid)], writes=[sk])
                P.op("pool", lambda e, stg=stg, bs=bs, en=en: e.tensor_tensor(out=stg[:, NB:NB + n], in0=bs[:, :n], in1=en[:, :n], op=ALU.mult),
                     reads=[("bs", sid), ("en", sid)], writes=[sk])
                P.op("pool", lambda e, stg=stg, kd=kd, en=en: e.tensor_tensor(out=stg[:, 2 * NB:2 * NB + n], in0=kd[:, :n], in1=en[:, :n], op=ALU.mult),
                     reads=[("kd", sid), ("en", sid)], writes=[sk])
                P.op("pool", lambda e, stg=stg, rc_=rc_, ep=ep: e.tensor_tensor(out=stg[:, 3 * NB:3 * NB + n], in0=rc_[:, :n], in1=ep[:, :n], op=ALU.mult),
                     reads=[("rkv", 0, c % 2), ("ep", sid)], writes=[sk])
                out_toks.append(P.dma("pool", d_["ot"][d][:, c, :, t0:t0 + n].rearrange("o p n -> p o n"),
                                      stg[:].rearrange("p (o n) -> p o n", o=4)[:, :, :n], reads=[sk], writes=[("oo", name, d, c, t0)]))
                rkq = tmp("rkq_%d" % sid, BF16)
                P.op("dve", lambda e, rkq=rkq, rc_=rc_, kd=kd, c=c: e.scalar_tensor_tensor(out=rkq[:, :n], in0=rc_[:, :n], scalar=rk[:, c:c + 1], in1=kd[:, :n],
                                                                                       op0=ALU.mult, op1=ALU.mult),
                     reads=[("rkv", 0, c % 2), ("kd", sid), "rk"], writes=[("rkq", sid)])
                P.op("pe", lambda e, rkq=rkq, d=d, cbank=cbank: e.matmul(cbank[:, :n], lhsT=bd[:], rhs=rkq[:, :n], start=(d == 0), stop=(d == 1)),
                     reads=["bd", ("rkq", sid)], writes=[cbk])
            bst = tmp("bst%d" % (c % 2))
            P.op("dve", lambda e, bst=bst, vc_=vc_, cbank=cbank: e.tensor_tensor(out=bst[:, :n], in0=cbank[:, :n], in1=vc_[:, :n], op=ALU.mult),
                 reads=[cbk, ("rkv", 2, c % 2)], writes=[("bst", c % 2)])
            out_toks.append(P.dma("act", d_["bonus"][c][:, t0:t0 + n], bst[:, :n], reads=[("bst", c % 2)], writes=[("ob", name, c, t0)]))

    for name, T in segs:
        mod = P.sbuf("modsb_" + name, [128, 6, 16], F32)
        G1 = P.sbuf("G1_" + name, [128, 16], F32)
        mk = ("mod", name)
        P.dma("sp", mod[:], dr[name]["mod"], writes=[mk])
        P.op("dve", lambda e, G1=G1, mod=mod: e.scalar_tensor_tensor(
            out=G1[:], in0=mod[:, 1, :], scalar=1.0, in1=ng[:, 0, :], op0=ALU.add, op1=ALU.mult),
            reads=[mk, "ng"], writes=[("G1", name)])
        pcs_all = P.sbuf("pcs_" + name, [128, 2, 16, T // 128], F32)
        for t0 in range(0, T, NB):
            do_block(name, T, t0, min(NB, T - t0), mod, G1, pcs_all)
        out_toks.append(P.dma("sp", dr[name]["pc"], pcs_all[:], reads=[("pcs", name)], writes=[("opc", name)]))
    P.final_wait("sp", out_toks)
    return P.build()


NCH = 66
NFC = 8
OPA, OPB, OPK, OPR = 0, 1, 2, 3


def build_r2(nch=NCH):
    nc = bass.Bass("TRN2", target_bir_lowering=False)
    fm = nc.dram_tensor("fm", [4, nch, 128, NFC, 128], BF16, kind="ExternalInput").ap()
    tmj = nc.dram_tensor("tm", [3, nch, 128, NFC, 128], BF16, kind="ExternalInput").ap()
    pcd = nc.dram_tensor("pc", [nch, 128, NFC], F32, kind="ExternalInput").ap()
    m4d = nc.dram_tensor("m4", [128, 2, 512], F32, kind="ExternalInput").ap()
    mld = nc.dram_tensor("ml", [128, 512], F32, kind="ExternalInput").ap()
    idd = nc.dram_tensor("idm", [128, 2, 128], F32, kind="ExternalInput").ap()
    mbd = nc.dram_tensor("mbd", [128, 512], F32, kind="ExternalInput").ap()
    yout = nc.dram_tensor("y", [nch, 128, NFC, 128], F32, kind="ExternalOutput").ap()

    P = Prog(nc)
    psb = [P.psum("psb%d" % i, [128, 512]) for i in range(8)]
    m4 = P.sbuf("m4s", [128, 2, 512], F32)
    ml = P.sbuf("mls", [128, 512], F32)
    idm = P.sbuf("ids", [128, 2, 128], F32)
    mbd4 = P.sbuf("mbds", [128, 512], F32)
    P.dma("sp", m4[:], m4d, writes=["m4"])
    P.dma("sp", ml[:], mld, writes=["ml"])
    P.dma("sp", idm[:], idd, writes=["idm"])
    P.dma("sp", mbd4[:], mbd, writes=["mbd4"])
    slots = []
    for s in range(2):
        d = dict(
            fa=P.sbuf("fa%d" % s, [128, NFC, 128], BF16), fb=P.sbuf("fb%d" % s, [128, NFC, 128], BF16),
            fr=P.sbuf("fr%d" % s, [128, NFC, 128], BF16),
            pa=[P.sbuf("pa%d_%d" % (s, h), [128, NFC, 128], BF16) for h in range(2)],
            pb=[P.sbuf("pb%d_%d" % (s, h), [128, NFC, 128], BF16) for h in range(2)],
            pk=[P.sbuf("pk%d_%d" % (s, h), [128, NFC, 128], BF16) for h in range(2)],
            tB=P.sbuf("tB%d" % s, [128, NFC, 128], BF16), tK=P.sbuf("tK%d" % s, [128, NFC, 128], BF16),
            tV=P.sbuf("tV%d" % s, [128, NFC, 128], BF16), pc=P.sbuf("pcs%d" % s, [128, NFC], F32),
        )
        for nm in ("pa", "pb", "pk"):
            for h in range(2):
                P.op("pool", lambda e, t=d[nm][h]: e.memset(t[:], 0.0), writes=[(nm, s, h)])
        slots.append(d)
    Hf = P.sbuf("Hf", [128, NFC, 128], F32)
    Hb = P.sbuf("Hb", [128, NFC, 128], BF16)
    P.op("dve", lambda e: e.memset(Hf[:], 0.0), writes=[("Hf", f) for f in range(NFC // 4)])
    P.op("dve", lambda e: e.memset(Hb[:], 0.0), writes=[("Hb", f) for f in range(NFC // 4)])
    A4p = [[P.sbuf("A4_%d_%d" % (f, par), [128, 2, 512], BF16) for f in range(NFC)] for par in range(2)]
    NN = [P.sbuf("NN_%d" % f, [128, 4, 128], F32) for f in range(NFC)]
    X = [[P.sbuf("X%d_%d" % (f, i), [128, 512], F32) for i in range(2)] for f in range(NFC // 2)]
    XT = [[P.sbuf("XT%d_%d" % (f, i), [128, 512], F32) for i in range(2)] for f in range(NFC // 2)]
    TTf = [[P.sbuf("TTf%d_%d" % (f, i), [128, 2, 128], F32) for i in range(2)] for f in range(NFC)]
    TTb = [[P.sbuf("TTb%d_%d" % (f, par), [128, 2, 128], BF16) for f in range(NFC)] for par in range(2)]
    Wsb = [P.sbuf("W%d" % f, [128, 512], BF16) for f in range(NFC // 4)]
    Usb = [P.sbuf("U%d" % f, [128, 512], BF16) for f in range(NFC // 4)]
    Ht = [P.sbuf("Ht%d" % i, [128, 512], F32) for i in range(NFC // 4)]
    yst = [P.sbuf("yst%d" % i, [128, NFC, 128], F32) for i in range(2)]
    out_toks = []

    def load(t):
        s = t % 2
        d = slots[s]
        P.dma("sp", d["fa"][:], fm[OPA, t], writes=[("fa", s)])
        P.dma("sp", d["fb"][:], fm[OPB, t], writes=[("fb", s)])
        P.dma("sp", d["fr"][:], fm[OPR, t], writes=[("fr", s)])
        for h in range(2):
            hp = slice(64 * h, 64 * h + 64)
            P.dma("sp", d["pa"][h][hp, :, :], fm[OPA, t, hp], writes=[("pa", s, h)])
            P.dma("sp", d["pb"][h][hp, :, :], fm[OPB, t, hp], writes=[("pb", s, h)])
            P.dma("sp", d["pk"][h][hp, :, :], fm[OPK, t, hp], writes=[("pk", s, h)])
        P.dma("sp", d["tB"][:], tmj[0, t], writes=[("tB", s)])
        P.dma("sp", d["tK"][:], tmj[1, t], writes=[("tK", s)])
        P.dma("sp", d["tV"][:], tmj[2, t], writes=[("tV", s)])
        P.dma("sp", d["pc"][:], pcd[t], writes=[("pc", s)])

    bank_rr = [0]

    def nb():
        b_ = bank_rr[0] % 8
        bank_rr[0] += 1
        return psb[b_], ("psb", b_)

    def stage_A(t):
        par = t % 2
        A4 = A4p[par]
        s = t % 2
        d = slots[s]
        for f in range(NFC):
            for h in range(2):
                bank, bk = nb()
                specs = [(d["pb"][h], ("pb", s, h), d["fa"], ("fa", s)),
                         (d["pk"][h], ("pk", s, h), d["fa"], ("fa", s)),
                         (d["pb"][h], ("pb", s, h), d["fr"], ("fr", s)),
                         (d["pk"][h], ("pk", s, h), d["fr"], ("fr", s))]
                for q, (lt, lk, rt, rk) in enumerate(specs):
                    P.op("pe", lambda e, bank=bank, q=q, lt=lt, rt=rt, f=f: e.matmul(
                        bank[:, q * 128:(q + 1) * 128], lhsT=lt[:, f, :], rhs=rt[:, f, :], start=True, stop=True),
                        reads=[lk, rk], writes=[bk])
                P.op("dve", lambda e, bank=bank, f=f, h=h: e.tensor_tensor(out=A4[f][:, h, :], in0=bank[:, :], in1=m4[:, h, :], op=ALU.mult),
                     reads=[bk, "m4"], writes=[("A4", par, f, h)])
            bank, bk = nb()
            for h in range(2):
                P.op("pe", lambda e, h=h, f=f, bank=bank: e.matmul(
                    bank[:, h * 128:(h + 1) * 128], lhsT=d["pb"][h][:, f, :], rhs=d["fa"][:, f, :], start=True, stop=True),
                    reads=[("pb", s, h), ("fa", s)], writes=[bk])
            for h in range(2):
                P.op("pe", lambda e, h=h, f=f, bank=bank: e.matmul(
                    bank[:, 256 + h * 128:256 + (h + 1) * 128], lhsT=d["pa"][h][:, f, :], rhs=d["fb"][:, f, :], start=True, stop=True),
                    reads=[("pa", s, h), ("fb", s)], writes=[bk])
            P.op("dve", lambda e, f=f, bank=bank: e.tensor_tensor(out=NN[f][:].rearrange("p q c -> p (q c)"), in0=bank[:, :], in1=ml[:], op=ALU.mult),
                 reads=[bk, "ml"], writes=[("NN", f)])
            P.op("pool", lambda e, f=f: e.tensor_tensor(out=TTf[f][0][:], in0=NN[f][:, 0:2, :], in1=idm[:], op=ALU.add),
                 reads=[("NN", f), "idm"], writes=[("TTf", f, 0)])

    def stage_D(t):
        par = t % 2
        for lvl in range(1, 7):
            pi, po = (lvl - 1) % 2, lvl % 2
            for fp in range(NFC // 2):
                def xin(ff, h, fp=fp, pi=pi, lvl=lvl):
                    if lvl == 1:
                        return NN[2 * fp + ff][:, 2 + h, :]
                    return X[fp][pi][:, ff * 256 + h * 128: ff * 256 + (h + 1) * 128]

                def xtin(ff, h, fp=fp, pi=pi, lvl=lvl):
                    if lvl == 1:
                        return NN[2 * fp + ff][:, h, :]
                    return XT[fp][pi][:, ff * 256 + h * 128: ff * 256 + (h + 1) * 128]
                if lvl == 1:
                    xk = [("NN", 2 * fp), ("NN", 2 * fp + 1)]
                    xtk = []
                else:
                    xk = [("X", fp, pi)]
                    xtk = [("XT", fp, pi)]
                bank, bk = nb()
                for ff in range(2):
                    for h in range(2):
                        P.op("pe", lambda e, h=h, ff=ff, xin=xin, xtin=xtin, bank=bank: e.matmul(
                            bank[:, ff * 256 + h * 128: ff * 256 + (h + 1) * 128], lhsT=xtin(ff, h), rhs=xin(ff, h), start=True, stop=True),
                            reads=xk + xtk, writes=[bk])
                P.op("act", lambda e, fp=fp, po=po, bank=bank: e.activation(out=X[fp][po][:], in_=bank[:, :], func=AF.Copy),
                     reads=[bk], writes=[("X", fp, po)])
                if lvl < 6:
                    bank2, bk2 = nb()
                    for ff in range(2):
                        for h in range(2):
                            P.op("pe", lambda e, h=h, ff=ff, xin=xin, xtin=xtin, bank2=bank2: e.matmul(
                                bank2[:, ff * 256 + h * 128: ff * 256 + (h + 1) * 128], lhsT=xin(ff, h), rhs=xtin(ff, h), start=True, stop=True),
                                reads=xk + xtk, writes=[bk2])
                    P.op("act", lambda e, fp=fp, po=po, bank2=bank2: e.activation(out=XT[fp][po][:], in_=bank2[:, :], func=AF.Copy),
                         reads=[bk2], writes=[("XT", fp, po)])
            for fp in range(NFC // 2):
                bank, bk = nb()
                for ff in range(2):
                    f = 2 * fp + ff
                    for h in range(2):
                        P.op("pe", lambda e, h=h, f=f, ff=ff, fp=fp, po=po, pi=pi, bank=bank: e.matmul(
                            bank[:, ff * 256 + h * 128: ff * 256 + (h + 1) * 128],
                            lhsT=X[fp][po][:, ff * 256 + h * 128: ff * 256 + (h + 1) * 128], rhs=TTf[f][pi][:, h, :], start=True, stop=True),
                            reads=[("X", fp, po), ("TTf", f, pi)], writes=[bk])
                for ff in range(2):
                    f = 2 * fp + ff
                    if lvl < 6:
                        P.op("dve", lambda e, f=f, ff=ff, po=po, pi=pi, bank=bank: e.tensor_tensor(
                            out=TTf[f][po][:].rearrange("p h c -> p (h c)"), in0=bank[:, ff * 256:(ff + 1) * 256],
                            in1=TTf[f][pi][:].rearrange("p h c -> p (h c)"), op=ALU.add),
                            reads=[bk, ("TTf", f, pi)], writes=[("TTf", f, po)])
                    else:
                        P.op("dve", lambda e, f=f, ff=ff, po=po, pi=pi, bank=bank: e.tensor_tensor(
                            out=TTb[par][f][:].rearrange("p h c -> p (h c)"), in0=bank[:, ff * 256:(ff + 1) * 256],
                            in1=TTf[f][pi][:].rearrange("p h c -> p (h c)"), op=ALU.add),
                            reads=[bk, ("TTf", f, pi)], writes=[("TTb", par, f)])

    def stage_S(t):
        par = t % 2
        A4 = A4p[par]
        s = t % 2
        d = slots[s]
        ys = yst[t % 2]
        NG = NFC // 4
        for g in range(NG):
            bank, bk = nb()
            for fi in range(4):
                f = 4 * g + fi
                off = fi * 128
                P.op("pe", lambda e, f=f, off=off, bank=bank: e.matmul(bank[:, off:off + 128], lhsT=d["fa"][:, f, :], rhs=Hb[:, f, :], start=True, stop=False),
                     reads=[("fa", s), ("Hb", g)], writes=[bk])
                for h in range(2):
                    P.op("pe", lambda e, f=f, h=h, off=off, bank=bank: e.matmul(bank[:, off + 64 * h: off + 64 * h + 64], lhsT=A4[f][:, h, 128:256],
                                                                                 rhs=d["tV"][:, f, 64 * h:64 * h + 64], start=False, stop=(h == 1)),
                         reads=[("A4", par, f, h), ("tV", s)], writes=[bk])
            P.op("act", lambda e, g=g, bank=bank: e.activation(out=Wsb[g][:], in_=bank[:, :], func=AF.Copy),
                 reads=[bk], writes=[("W", g)])
        for g in range(NG):
            bank, bk = nb()
            for fi in range(4):
                f = 4 * g + fi
                off = fi * 128
                for h in range(2):
                    P.op("pe", lambda e, f=f, g=g, h=h, off=off, bank=bank: e.matmul(bank[:, off + 64 * h: off + 64 * h + 64], lhsT=TTb[par][f][:, h, :],
                                                                                      rhs=Wsb[g][:, off + 64 * h: off + 64 * h + 64], start=True, stop=True),
                         reads=[("TTb", par, f), ("W", g)], writes=[bk])
            P.op("act", lambda e, g=g, bank=bank: e.activation(out=Usb[g][:], in_=bank[:, :], func=AF.Copy),
                 reads=[bk], writes=[("U", g)])
        for g in range(NG):
            bank, bk = nb()
            for fi in range(4):
                f = 4 * g + fi
                off = fi * 128
                P.op("pe", lambda e, f=f, off=off, bank=bank: e.matmul(bank[:, off:off + 128], lhsT=d["fr"][:, f, :], rhs=Hb[:, f, :], start=True, stop=False),
                     reads=[("fr", s), ("Hb", g)], writes=[bk])
                for h in range(2):
                    P.op("pe", lambda e, f=f, g=g, h=h, off=off, bank=bank: e.matmul(bank[:, off + 64 * h: off + 64 * h + 64], lhsT=A4[f][:, h, 256:384],
                                                                                      rhs=Usb[g][:, off + 64 * h: off + 64 * h + 64], start=False, stop=False),
                         reads=[("A4", par, f, h), ("U", g)], writes=[bk])
                    P.op("pe", lambda e, f=f, h=h, off=off, bank=bank: e.matmul(bank[:, off + 64 * h: off + 64 * h + 64], lhsT=A4[f][:, h, 384:512],
                                                                                 rhs=d["tV"][:, f, 64 * h:64 * h + 64], start=False, stop=(h == 1)),
                         reads=[("A4", par, f, h), ("tV", s)], writes=[bk])
            P.op("act", lambda e, g=g, bank=bank, ys=ys: e.activation(out=ys[:, 4 * g:4 * g + 4, :].rearrange("p f c -> p (f c)"), in_=bank[:, :], func=AF.Copy),
                 reads=[bk], writes=[("yst", t % 2, g)])
        for g in range(NG):
            bank, bk = nb()
            for fi in range(4):
                f = 4 * g + fi
                off = fi * 128
                P.op("pe", lambda e, f=f, g=g, off=off, bank=bank: e.matmul(bank[:, off:off + 128], lhsT=d["tB"][:, f, :], rhs=Usb[g][:, off:off + 128], start=True, stop=False),
                     reads=[("tB", s), ("U", g)], writes=[bk])
                P.op("pe", lambda e, f=f, off=off, bank=bank: e.matmul(bank[:, off:off + 128], lhsT=d["tK"][:, f, :], rhs=d["tV"][:, f, :], start=False, stop=True),
                     reads=[("tK", s), ("tV", s)], writes=[bk])
            hf_g = Hf[:, 4 * g:4 * g + 4, :]
            P.op("dve", lambda e, g=g, bank=bank: e.tensor_tensor(out=Ht[g][:], in0=bank[:, :], in1=mbd4[:], op=ALU.mult),
                 reads=[bk, "mbd4"], writes=[("Ht", g)])
            P.op("pool", lambda e, g=g, hf_g=hf_g: e.tensor_tensor(out=hf_g, in0=Ht[g][:].rearrange("p (f c) -> p f c", f=4), in1=hf_g, op=ALU.add),
                 reads=[("Ht", g), ("Hf", g)], writes=[("Hf", g)])
            P.op("pool", lambda e, g=g, hf_g=hf_g: e.tensor_tensor(out=hf_g, in0=hf_g, in1=d["pc"][:, 4 * g:4 * g + 4].unsqueeze(2).broadcast_to([128, 4, 128]), op=ALU.mult),
                 reads=[("Hf", g), ("pc", s)], writes=[("Hf", g)])
            P.op("act", lambda e, g=g, hf_g=hf_g: e.activation(out=Hb[:, 4 * g:4 * g + 4, :], in_=hf_g, func=AF.Copy),
                 reads=[("Hf", g)], writes=[("Hb", g)])
        out_toks.append(P.dma("sp", yout[t], ys[:], reads=[("yst", t % 2, g) for g in range(NG)], writes=[("yo", t)]))

    load(0)
    stage_A(0)
    stage_D(0)
    for t in range(nch):
        if t + 1 < nch:
            load(t + 1)
            stage_A(t + 1)
            stage_D(t + 1)
        stage_S(t)
    P.final_wait("sp", out_toks)
    return P.build()


def r2_consts():
    i = np.arange(128)[:, None]
    c = np.arange(128)[None, :]
    su = (c > i).astype(np.float32)
    ue = (c >= i).astype(np.float32)
    m4h = np.concatenate([su, su, ue, ue], 1)
    m4 = np.stack([m4h, m4h], 1)
    sl = (c < i).astype(np.float32)
    ml = np.concatenate([su, su, sl, sl], 1)
    idm = np.stack([np.eye(128, dtype=np.float32)] * 2, 1)
    bd = np.zeros((128, 128), np.float32)
    bd[:64, :64] = 1
    bd[64:, 64:] = 1
    bd = np.concatenate([bd] * 4, 1)
    return dict(m4=np.ascontiguousarray(m4), ml=np.ascontiguousarray(ml), idm=np.ascontiguousarray(idm), mbd=bd)


LN_X_EPS = 64e-5


def build_r3(segs=(("lat", 2048), ("ctx", 64)), TB=512):
    nc = bass.Bass("TRN2", target_bir_lowering=False)
    dr = {}
    for name, T in segs:
        dr[name] = dict(
            x=nc.dram_tensor("x_" + name, [128, 16, T], F32, kind="ExternalInput").ap(),
            y=nc.dram_tensor("y_" + name, [2, 16, 128, T], F32, kind="ExternalInput").ap(),
            bonus=nc.dram_tensor("bonus_" + name, [16, 128, T], F32, kind="ExternalInput").ap(),
            g=nc.dram_tensor("g_" + name, [16, 128, T], F32, kind="ExternalInput").ap(),
            mod=nc.dram_tensor("mod_" + name, [128, 6, 16], F32, kind="ExternalInput").ap(),
            out=nc.dram_tensor("out_" + name, [128, 16, T], F32, kind="ExternalOutput").ap(),
        )
    normg = nc.dram_tensor("normg", [128, 2, 16], F32, kind="ExternalInput").ap()
    lnx_d = nc.dram_tensor("lnx", [128, 2, 16], F32, kind="ExternalInput").ap()
    w_o = nc.dram_tensor("w_o", [16, 128, 16, 128], BF16, kind="ExternalInput").ap()
    w_in = nc.dram_tensor("w_in", [NJ, 2, 128, 16, 128], BF16, kind="ExternalInput").ap()
    w_out = nc.dram_tensor("w_out", [16, 128, NJ, 128], BF16, kind="ExternalInput").ap()

    P = Prog(nc)
    cm = Common(P, TB, halo=0)
    cm.setup_eps()
    ffn = FFN(P, cm, w_in, w_out, TB, nsplit=2, WC=128)
    xb = P.sbuf("xb", [128, 16, TB], F32)
    hT = P.sbuf("hT", [128, 16, TB], BF16)
    ng = P.sbuf("ng", [128, 2, 16], F32)
    lnx = P.sbuf("lnx_sb", [128, 2, 16], F32)
    bd64 = P.sbuf("bd64", [128, 128], F32)
    epsl = P.sbuf("epsl", [128, 1], F32)
    inb = [dict(y0=P.sbuf("y0_%d" % i, [128, TB], F32), y1=P.sbuf("y1_%d" % i, [128, TB], F32),
                bo=P.sbuf("bo_%d" % i, [128, TB], F32), g=P.sbuf("g_%d" % i, [128, TB], F32)) for i in range(2)]
    ysum = P.sbuf("ysum", [128, TB], F32)
    cen = P.sbuf("cen", [128, TB], F32)
    sq = P.sbuf("sq", [128, TB], F32)
    rstd = P.sbuf("rstd", [128, TB], F32)
    yn = P.sbuf("yn", [128, TB], F32)
    oo = P.sbuf("oo", [128, TB], F32)
    P.dma("sp", ng[:], normg, writes=["ng"])
    P.dma("sp", lnx[:], lnx_d, writes=["lnx"])
    P.op("pool", lambda e: e.memset(bd64[:], 0.0), writes=["bd64"])
    P.op("pool", lambda e: e.memset(bd64[0:64, 0:64], 1.0 / 64), writes=["bd64"])
    P.op("pool", lambda e: e.memset(bd64[64:128, 64:128], 1.0 / 64), writes=["bd64"])
    P.op("pool", lambda e: e.memset(epsl[:], LN_X_EPS), writes=["epsl"])
    xkeys = [("xb", c) for c in range(16)]
    hkeys = [("hT", c) for c in range(16)]
    out_toks = []
    st = dict(i=0)

    def do_block(name, t0, n, mod, G2):
        d_ = dr[name]
        mk = ("mod", name)
        P.dma("sp", xb[:, :, :n], d_["x"][:, :, t0:t0 + n], writes=xkeys)
        for c in range(16):
            s = st["i"] % 2
            st["i"] += 1
            ib = inb[s]
            P.dma("sp", ib["y0"][:, :n], d_["y"][0, c][:, t0:t0 + n], writes=[("y0", s)])
            P.dma("sp", ib["y1"][:, :n], d_["y"][1, c][:, t0:t0 + n], writes=[("y1", s)])
            P.dma("sp", ib["bo"][:, :n], d_["bonus"][c][:, t0:t0 + n], writes=[("bo", s)])
            P.dma("sp", ib["g"][:, :n], d_["g"][c][:, t0:t0 + n], writes=[("g", s)])
            P.op("pool", lambda e, ib=ib: e.tensor_tensor(out=ysum[:, :n], in0=ib["y0"][:, :n], in1=ib["y1"][:, :n], op=ALU.add),
                 reads=[("y0", s), ("y1", s)], writes=["ysum"])
            P.op("pe", lambda e: e.matmul(cm.psb[2][:, :n], lhsT=bd64[:], rhs=ysum[:, :n], start=True, stop=True),
                 reads=["bd64", "ysum"], writes=[("psb", 2)])
            P.op("dve", lambda e: e.tensor_tensor(out=cen[:, :n], in0=ysum[:, :n], in1=cm.psb[2][:, :n], op=ALU.subtract),
                 reads=["ysum", ("psb", 2)], writes=["cen"])
            P.op("act", lambda e: e.activation(out=sq[:, :n], in_=cen[:, :n], func=AF.Square), reads=["cen"], writes=["sq"])
            P.op("pe", lambda e: e.matmul(cm.psb[3][:, :n], lhsT=bd64[:], rhs=sq[:, :n], start=True, stop=True),
                 reads=["bd64", "sq"], writes=[("psb", 3)])
            P.op("act", lambda e: e.activation(out=rstd[:, :n], in_=cm.psb[3][:, :n], func=AF.Sqrt, bias=epsl[:, 0:1]),
                 reads=[("psb", 3), "epsl"], writes=["rstd"])
            P.op("dve", lambda e: e.reciprocal(out=rstd[:, :n], in_=rstd[:, :n]), reads=["rstd"], writes=["rstd"])
            P.op("dve", lambda e: e.tensor_tensor(out=yn[:, :n], in0=cen[:, :n], in1=rstd[:, :n], op=ALU.mult),
                 reads=["cen", "rstd"], writes=["yn"])
            P.op("act", lambda e, c=c: e.activation(out=oo[:, :n], in_=yn[:, :n], func=AF.Identity, scale=lnx[:, 0, c:c + 1], bias=lnx[:, 1, c:c + 1]),
                 reads=["yn", "lnx"], writes=["oo"])
            P.op("pool", lambda e, ib=ib: e.tensor_tensor(out=oo[:, :n], in0=oo[:, :n], in1=ib["bo"][:, :n], op=ALU.add),
                 reads=["oo", ("bo", s)], writes=["oo"])
            P.op("pool", lambda e, ib=ib, c=c: e.tensor_tensor(out=hT[:, c, :n], in0=oo[:, :n], in1=ib["g"][:, :n], op=ALU.mult),
                 reads=["oo", ("g", s)], writes=[hkeys[c]])
        for m in range(16):
            s = ffn.kin % 2
            ffn.kin += 1
            wt = ffn.wg[s]
            P.dma("sp", wt[:], w_o[m], writes=[("wg", s)])
            q = ffn.km % 2
            ffn.km += 1
            py = cm.psb[6 + q]
            for c in range(16):
                P.op("pe", lambda e, wt=wt, py=py, c=c: e.matmul(py[:, :n], lhsT=wt[:, c, :], rhs=hT[:, c, :n],
                                                                  start=(c == 0), stop=(c == 15)),
                     reads=[("wg", s), hkeys[c]], writes=[("psb", 6 + q)])
            P.op("dve", lambda e, py=py, m=m: e.scalar_tensor_tensor(
                out=xb[:, m, :n], in0=py[:, :n], scalar=mod[:, 2, m:m + 1], in1=xb[:, m, :n], op0=ALU.mult, op1=ALU.add),
                reads=[("psb", 6 + q), mk, xkeys[m]], writes=[xkeys[m]])
        cm.norm_mod(xb[:, :, :n], n, (G2, ("G2", name)), (mod[:, 3, :], mk), lambda c: hT[:, c, :n], hkeys, xkeys, stat_bank=0)
        ffn.emit(hT, hkeys, n, xb, 0, xkeys, (mod[:, 5, :], mk))
        out_toks.append(P.dma("sp", d_["out"][:, :, t0:t0 + n], xb[:, :, :n], reads=xkeys, writes=[("out", name, t0)]))

    for name, T in segs:
        mod = P.sbuf("modsb_" + name, [128, 6, 16], F32)
        G2 = P.sbuf("G2_" + name, [128, 16], F32)
        mk = ("mod", name)
        P.dma("sp", mod[:], dr[name]["mod"], writes=[mk])
        P.op("dve", lambda e, G2=G2, mod=mod: e.scalar_tensor_tensor(
            out=G2[:], in0=mod[:, 4, :], scalar=1.0, in1=ng[:, 1, :], op0=ALU.add, op1=ALU.mult),
            reads=[mk, "ng"], writes=[("G2", name)])
        for t0 in range(0, T, TB):
            do_block(name, t0, min(TB, T - t0), mod, G2)
    P.final_wait("sp", out_toks)
    return P.build()


def r2_maps_from_r1(r1, cons):
    maps = []
    for b in range(2):
        cores = [r1[b * 4 + k] for k in range(4)]
        ot_lat = np.concatenate([c["ot_lat"] for c in cores], axis=4)
        ot_ctx = np.concatenate([cores[0]["ot_ctx"], cores[1]["ot_ctx"]], axis=4)
        v_lat = np.concatenate([c["v_lat"] for c in cores], axis=2)
        v_ctx = np.concatenate([cores[0]["v_ctx"], cores[1]["v_ctx"]], axis=2)
        pc_lat = np.concatenate([c["pc_lat"] for c in cores], axis=3)
        pc_ctx = np.concatenate([cores[0]["pc_ctx"], cores[1]["pc_ctx"]], axis=3)
        for d in range(2):
            if d == 0:
                ot = np.concatenate([ot_ctx[d], ot_lat[d]], axis=3)
                vv = np.concatenate([v_ctx, v_lat], axis=2)
                pc = np.concatenate([pc_ctx[:, d], pc_lat[:, d]], axis=2)
            else:
                ot = np.concatenate([ot_ctx[d][..., ::-1], ot_lat[d][..., ::-1]], axis=3)
                vv = np.concatenate([v_ctx[..., ::-1], v_lat[..., ::-1]], axis=2)
                pc = np.concatenate([pc_ctx[:, d][..., ::-1], pc_lat[:, d][..., ::-1]], axis=2)
            for hh in range(2):
                cs = slice(8 * hh, 8 * hh + 8)
                o = ot[:, cs].reshape(4, 8, 128, 66, 128)
                fm = np.ascontiguousarray(o.transpose(0, 3, 2, 1, 4))
                tmB = o[1].transpose(2, 3, 0, 1)
                tmK = o[2].transpose(2, 3, 0, 1)
                tmV = vv[cs].reshape(8, 128, 66, 128).transpose(2, 3, 0, 1)
                tm = np.ascontiguousarray(np.stack([tmB, tmK, tmV]))
                pcl = np.ascontiguousarray(pc[:, cs].transpose(2, 0, 1))
                m = dict(fm=fm, tm=tm, pc=pcl)
                m.update(cons)
                maps.append(((b, d, hh), m))
    maps.sort(key=lambda t: t[0][0] * 4 + t[0][1] * 2 + t[0][2])
    return [m for _, m in maps]


def r3_y_from_r2(r2res):
    yl = [[None, None], [None, None]]
    yc = [[None, None], [None, None]]
    for b in range(2):
        for d in range(2):
            halves = []
            for hh in range(2):
                y = np.asarray(r2res[b * 4 + d * 2 + hh]["y"])
                halves.append(y.reshape(66 * 128, 8 * 128))
            seq = np.concatenate(halves, axis=1)
            c, l = seq[:256], seq[256:]
            if d == 1:
                c, l = c[::-1], l[::-1]
            yc[b][d], yl[b][d] = c, l
    return yl, yc


def _r1_maps(x, ctx, mods_l, inp, wb):
    maps = []
    rmask = np.ones((128, 384), np.float32)
    rmask[:, ::128] = 0.0
    for i in range(8):
        b, k = i // 4, i % 4
        xl = np.zeros((TLAT + 2, 2048), np.float32)
        vl = np.zeros(TLAT + 2, np.float32)
        lo, hi = k * TLAT - 1, (k + 1) * TLAT + 1
        s0, s1 = max(lo, 0), min(hi, 8192)
        xl[s0 - lo:s1 - lo] = x[b, s0:s1]
        vl[s0 - lo:s1 - lo] = 1
        ck = k % 2
        xc = np.zeros((R1_TCTX + 2, 2048), np.float32)
        vc = np.zeros(R1_TCTX + 2, np.float32)
        lo, hi = ck * R1_TCTX - 1, (ck + 1) * R1_TCTX + 1
        s0, s1 = max(lo, 0), min(hi, 256)
        xc[s0 - lo:s1 - lo] = ctx[b, s0:s1]
        vc[s0 - lo:s1 - lo] = 1
        maps.append({"x_lat": to_fm(xl), "valid_lat": vl, "mod_lat": vec_fm(mods_l[b].reshape(6, 2048)),
                     "x_ctx": to_fm(xc), "valid_ctx": vc, "mod_ctx": vec_fm(mods_l[2].reshape(6, 2048)),
                     "normg": vec_fm(inp["norm_g"][1]), "mu": vec_fm(inp["rwkv_mu"][0]), "w_rkv": wb["rwkv_rkv"],
                     "w_la": inp["rwkv_w_lora_a"][0], "w_lb": inp["rwkv_w_lora_b"][0], "a_la": inp["rwkv_a_lora_a"][0],
                     "a_lb": inp["rwkv_a_lora_b"][0], "g_la": inp["rwkv_g_lora_a"][0], "g_lb": inp["rwkv_g_lora_b"][0],
                     "dirvec": vec_fm(inp["rwkv_dir_vec"][0]), "r_k": vec_fm(inp["rwkv_r_k"][0].reshape(2048)), "rmask": rmask})
    return maps


def _r3_maps(x, ctx, yl, yc, r1, mods_l, inp, li, wb):
    maps = []
    fmc = lambda a: np.ascontiguousarray(a.reshape(a.shape[0], 16, 128).transpose(1, 2, 0))
    for i in range(8):
        b, k = i // 4, i % 4
        ls = slice(k * 2048, (k + 1) * 2048)
        cs = slice(k * 64, (k + 1) * 64)
        cc = r1[b * 4 + (k // 2)]
        co = (k % 2) * 64
        maps.append({"x_lat": to_fm(x[b, ls]), "x_ctx": to_fm(ctx[b, cs]),
                     "y_lat": np.stack([fmc(yl[b][0][ls]), fmc(yl[b][1][ls])]),
                     "y_ctx": np.stack([fmc(yc[b][0][cs]), fmc(yc[b][1][cs])]),
                     "bonus_lat": r1[i]["bonus_lat"], "g_lat": r1[i]["g_lat"],
                     "bonus_ctx": np.ascontiguousarray(cc["bonus_ctx"][:, :, co:co + 64]),
                     "g_ctx": np.ascontiguousarray(cc["g_ctx"][:, :, co:co + 64]),
                     "mod_lat": vec_fm(mods_l[b].reshape(6, 2048)), "mod_ctx": vec_fm(mods_l[2].reshape(6, 2048)),
                     "normg": vec_fm(inp["norm_g"][li]), "lnx": vec_fm(inp["rwkv_ln_x"][0]), "w_o": wb["rwkv_wo"],
                     "w_in": wb["ffn_in%d" % li], "w_out": wb["ffn_out%d" % li]})
    return maps


def _run_rwkv_layer(x, ctx, mods_l, inp, li, wb):
    nc1 = build_r1()
    res1 = run_bass_kernel_spmd(nc1, _r1_maps(x, ctx, mods_l, inp, wb), core_ids=list(range(8)))
    r1 = [{k: np.asarray(v) for k, v in r.items()} for r in res1.results]
    nc2 = build_r2(NCH)
    res2 = run_bass_kernel_spmd(nc2, r2_maps_from_r1(r1, r2_consts()), core_ids=list(range(8)))
    r2res = [{"y": np.asarray(r["y"])} for r in res2.results]
    yl, yc = r3_y_from_r2(r2res)
    nc3 = build_r3()
    res3 = run_bass_kernel_spmd(nc3, _r3_maps(x, ctx, yl, yc, r1, mods_l, inp, li, wb), core_ids=list(range(8)))
    xo = np.zeros_like(x)
    co = np.zeros_like(ctx)
    for i in range(8):
        b, k = i // 4, i % 4
        xo[b, k * 2048:(k + 1) * 2048] = from_fm(res3.results[i]["out_lat"])
        co[b, k * 64:(k + 1) * 64] = from_fm(res3.results[i]["out_ctx"])
    return xo, co


def kernel(**inputs):
    inp = {k: np.ascontiguousarray(np.asarray(v, dtype=np.float32)) for k, v in inputs.items()}
    x, ctx = inp["x"], inp["ctx"]
    mods, wb = run_l0(inp)
    x, ctx = run_pool_layer(x, ctx, mods[:, 0], inp["norm_g"][0], inp["pool_scale"][0], inp["pool_w"][0],
                            wb["ffn_in0"], wb["ffn_out0"], True)
    x, ctx = _run_rwkv_layer(x, ctx, mods[:, 1], inp, 1, wb)
    a1 = run_a1(x, ctx, mods[:, 2], inp["norm_g"][2], wb["diff_qkv"], inp["diff_qk_g"][0])
    a1 = [{k: np.asarray(v) for k, v in r.items()} for r in a1]
    lambda_init = 0.8 - 0.6 * math.exp(-0.3 * 2)
    x = run_a2(a1, x, mods[:, 2], inp["norm_g"][2], inp["diff_lambda"][0], inp["diff_subln_g"][0], wb["diff_wo"],
               wb["ffn_in2"], wb["ffn_out2"], lambda_init)
    x, _ = run_pool_layer(x, None, mods[:, 3], inp["norm_g"][3], inp["pool_scale"][1], inp["pool_w"][1],
                          wb["ffn_in3"], wb["ffn_out3"], False)
    return x.astype(np.float32)
```

```python
import math


import numpy as np
import concourse.bass as bass
import concourse.mybir as mybir
from concourse.bass_utils import run_bass_kernel_spmd

F32 = mybir.dt.float32
BF16 = mybir.dt.bfloat16
ALU = mybir.AluOpType
AF = mybir.ActivationFunctionType
AX = mybir.AxisListType

N_DMA_SEMS = 24


class Prog:
    ENGS = ("pe", "act", "dve", "pool", "sp")

    def __init__(self, nc):
        self.nc = nc
        self.q = {e: [] for e in self.ENGS}
        self.n = {e: 0 for e in self.ENGS}
        self.waited = {e: {} for e in self.ENGS}
        self.lastw = {}
        self.readers = {}
        self.dma_rr = 0
        self.dma_cnt = [0] * N_DMA_SEMS
        self.dma_last = [None] * N_DMA_SEMS
        self.ctx = []
        self.sems = {}

    def enter(self, cm):
        v = cm.__enter__()
        self.ctx.append(cm)
        return v

    def sbuf(self, name, shape, dt):
        return self.enter(self.nc.sbuf_tensor(name, list(shape), dt))

    def psum(self, name, shape, dt=F32):
        return self.enter(self.nc.psum_tensor(name, list(shape), dt))

    def _deps(self, reads, writes):
        toks = []
        for r in reads:
            t = self.lastw.get(r)
            if t is not None:
                toks.append(t)
        for w in writes:
            t = self.lastw.get(w)
            if t is not None:
                toks.append(t)
            toks.extend(self.readers.get(w, ()))
        return toks

    def _commit(self, tok, reads, writes):
        for r in reads:
            self.readers.setdefault(r, []).append(tok)
        for w in writes:
            self.lastw[w] = tok
            self.readers[w] = []

    def _waits(self, eng, toks):
        need = {}
        for (k, v) in toks:
            if v > need.get(k, 0):
                need[k] = v
        out = []
        wd = self.waited[eng]
        for k, v in need.items():
            if wd.get(k, 0) >= v:
                continue
            wd[k] = v
            out.append((k, v))
        return out

    def op(self, eng, fn, reads=(), writes=()):
        toks = self._deps(reads, writes)
        if eng == "pe":
            toks = [t for t in toks if t[0] != "pe"]
        waits = self._waits(eng, toks)
        self.n[eng] += 1
        tok = (eng, self.n[eng])
        self.q[eng].append((fn, waits, ("self", eng, 1)))
        self._commit(tok, reads, writes)
        return tok

    def dma(self, eng, out, in_, reads=(), writes=(), **kw):
        toks = self._deps(reads, writes)
        s = self.dma_rr
        self.dma_rr = (self.dma_rr + 1) % N_DMA_SEMS
        if self.dma_last[s] is not None:
            toks.append(self.dma_last[s])
        waits = self._waits(eng, toks)
        self.dma_cnt[s] += 1
        tok = (("dma", s), 16 * self.dma_cnt[s])
        self.dma_last[s] = tok
        self.q[eng].append((lambda e: e.dma_start(out=out, in_=in_, **kw), waits, ("dma", s, 16)))
        self._commit(tok, reads, writes)
        return tok

    def final_wait(self, eng, toks):
        waits = self._waits(eng, toks)
        self.q[eng].append((None, waits, None))

    def build(self):
        nc = self.nc
        semobjs = {}
        for e in self.ENGS:
            if e != "sp":
                semobjs[e] = self.enter(nc.semaphore("prog_" + e))
        for s in range(N_DMA_SEMS):
            semobjs[("dma", s)] = self.enter(nc.semaphore("dma%d" % s))
        q = self.q

        def emit(engname, e):
            for fn, waits, inc in q[engname]:
                for k, v in waits:
                    e.wait_ge(semobjs[k], v)
                if fn is None:
                    continue
                ins = fn(e)
                if inc[0] == "self":
                    ins.then_inc(semobjs[inc[1]], 1)
                else:
                    ins.then_inc(semobjs[("dma", inc[1])], 16)

        with nc.Block() as block:
            @block.tensor
            def _(e):
                emit("pe", e)

            @block.scalar
            def _(e):
                emit("act", e)

            @block.vector
            def _(e):
                emit("dve", e)

            @block.gpsimd
            def _(e):
                emit("pool", e)

            @block.sync
            def _(e):
                emit("sp", e)
        for cm in reversed(self.ctx):
            cm.__exit__(None, None, None)
        self.ctx = []
        return nc


D = 2048
NL = 4
NM = 6 * D
COLS_PER_CORE = NM // 8
L0_NB = COLS_PER_CORE // 512

CAST_CH = 8192


def build_l0(ncast=0):
    nc = bass.Bass("TRN2", target_bir_lowering=False)
    cT = nc.dram_tensor("cT", [128, 16, 3], F32, kind="ExternalInput").ap()
    w = nc.dram_tensor("w", [NL, D, COLS_PER_CORE], F32, kind="ExternalInput").ap()
    b = nc.dram_tensor("b", [NL, COLS_PER_CORE], F32, kind="ExternalInput").ap()
    out = nc.dram_tensor("out", [3, NL * COLS_PER_CORE], F32, kind="ExternalOutput").ap()
    if ncast:
        cin = nc.dram_tensor("cin", [128, ncast * CAST_CH], F32, kind="ExternalInput").ap()
        cout = nc.dram_tensor("cout", [128, ncast * CAST_CH], BF16, kind="ExternalOutput").ap()
    P = Prog(nc)
    c_sb = P.sbuf("c_sb", [128, 16, 3], F32)
    s_sb = P.sbuf("s_sb", [128, 16, 3], F32)
    b_sb = P.sbuf("b_sb", [3, NL * COLS_PER_CORE], F32)
    o_sb = P.sbuf("o_sb", [3, NL * COLS_PER_CORE], F32)
    wt = [P.sbuf("wt%d" % i, [128, 16, 512], F32) for i in range(2)]
    ps = [P.psum("ps%d" % i, [128, 512]) for i in range(2)]
    P.dma("sp", c_sb[:], cT, writes=["c"])
    for l in range(NL):
        P.dma("sp", b_sb[:, l * COLS_PER_CORE:(l + 1) * COLS_PER_CORE],
              b[l, :].partition_broadcast(3), writes=[("b", l)])
    P.op("act", lambda e: e.activation(out=s_sb[:], in_=c_sb[:], func=AF.Silu), reads=["c"], writes=["s"])
    k = 0
    for l in range(NL):
        for nb in range(L0_NB):
            wb = wt[k % 2]
            pb = ps[k % 2]
            src = w[l, :, nb * 512:(nb + 1) * 512].rearrange("(c p) n -> p c n", p=128)
            P.dma("sp" if k % 2 == 0 else "pool", wb[:], src, writes=[("wt", k % 2)])
            for c in range(16):
                P.op("pe", lambda e, wb=wb, pb=pb, c=c: e.matmul(pb[0:3, :], lhsT=s_sb[:, c, :], rhs=wb[:, c, :],
                                                                   start=(c == 0), stop=(c == 15)),
                     reads=["s", ("wt", k % 2)], writes=[("ps", k % 2)])
            col = l * COLS_PER_CORE + nb * 512
            P.op("dve", lambda e, pb=pb, col=col: e.tensor_tensor(out=o_sb[:, col:col + 512], in0=pb[0:3, :],
                                                                   in1=b_sb[:, col:col + 512], op=ALU.add),
                 reads=[("ps", k % 2), ("b", l)], writes=[("o", k)])
            k += 1
    t = P.dma("sp", out, o_sb[:], reads=[("o", i) for i in range(k)], writes=["out"])
    toks = [t]
    if ncast:
        cb = [P.sbuf("cb%d" % i, [128, CAST_CH], BF16) for i in range(3)]
        for i in range(ncast):
            sl = slice(i * CAST_CH, (i + 1) * CAST_CH)
            P.dma("pool", cb[i % 3][:], cin[:, sl], writes=[("cb", i % 3)])
            toks.append(P.dma("act", cout[:, sl], cb[i % 3][:], reads=[("cb", i % 3)], writes=[("cout", i)]))
    P.final_wait("sp", toks)
    return P.build()

def blocked_weights(inputs):
    perm = np.concatenate([np.arange(0, 128, 2), np.arange(1, 128, 2)])
    out = {}
    for l in range(4):
        wi = inputs["ffn_w_in"][l].reshape(16, 128, 2, 44, 128)
        out["ffn_in%d" % l] = np.ascontiguousarray(wi.transpose(3, 2, 1, 0, 4))
        wo = inputs["ffn_w_out"][l].reshape(44, 128, 16, 128)
        out["ffn_out%d" % l] = np.ascontiguousarray(wo.transpose(2, 1, 0, 3))
    sq = lambda w: np.ascontiguousarray(w.reshape(16, 128, -1, 128).transpose(2, 1, 0, 3))
    out["rwkv_wo"] = sq(inputs["rwkv_w_o"][0])
    out["diff_wo"] = sq(inputs["diff_w_o"][0])
    out["rwkv_rkv"] = np.stack([sq(inputs["rwkv_w_rkv"][0][i]) for i in range(3)])
    wq = inputs["diff_w_qkv"][0]
    cols = (np.arange(32)[:, None] * 128 + perm[None, :]).reshape(-1)
    wq = np.concatenate([wq[:, cols], wq[:, 4096:]], axis=1)
    out["diff_qkv"] = sq(wq)
    return out


def run_l0(inputs, cast=True):
    c = np.concatenate([inputs["c"], inputs["c_ctx"][None]], 0)
    cT = np.ascontiguousarray(c.reshape(3, 16, 128).transpose(2, 1, 0))
    blk = blocked_weights(inputs) if cast else {}
    names = list(blk)
    total = sum(blk[n].size for n in names)
    per = 8 * 128 * CAST_CH
    ncast = (total + per - 1) // per
    nc = build_l0(ncast)
    if ncast:
        flat = np.zeros(ncast * per, np.float32)
        o = 0
        for n in names:
            flat[o:o + blk[n].size] = blk[n].reshape(-1)
            o += blk[n].size
        flat = flat.reshape(8, 128, ncast * CAST_CH)
    maps = []
    for i in range(8):
        sl = slice(i * COLS_PER_CORE, (i + 1) * COLS_PER_CORE)
        m = {"cT": cT, "w": np.ascontiguousarray(inputs["ada_w"][:, :, sl]),
             "b": np.ascontiguousarray(inputs["ada_b"][:, sl])}
        if ncast:
            m["cin"] = flat[i]
        maps.append(m)
    res = run_bass_kernel_spmd(nc, maps, core_ids=list(range(8)))
    outs = [r["out"].reshape(3, NL, COLS_PER_CORE) for r in res.results]
    mods = np.concatenate(outs, axis=2)
    wb = {}
    if ncast:
        cf = np.concatenate([np.asarray(r["cout"]).reshape(-1) for r in res.results])
        o = 0
        for n in names:
            wb[n] = cf[o:o + blk[n].size].reshape(blk[n].shape)
            o += blk[n].size
    return mods, wb


D = 2048
F = 5632
NC16 = 16
NJ = F // 128
EPS = 1e-6
HALO = 8
TBG = 512
WINS = (2, 4, 8, 16)


class Common:
    def __init__(self, P, TBMAX=512, halo=HALO):
        self.P = P
        W = TBMAX + 2 * halo
        self.W = W
        self.ones = P.sbuf("ones_bf", [128, 128], BF16)
        self.rs = P.sbuf("rs", [128, W], F32)
        self.sqc = [P.sbuf("sqc%d" % i, [128, W], BF16) for i in range(2)]
        self.tmp = [P.sbuf("ntmp%d" % i, [128, W], F32) for i in range(2)]
        self.psb = [P.psum("psb%d" % i, [128, 512]) for i in range(8)]
        P.op("dve", lambda e: e.memset(self.ones[:], 1.0), writes=["ones"])
        self.k = 0

    def norm_mod(self, xb, ncols, G, SH, dest, dest_keys, xkeys, stat_bank=0, post=None):
        P = self
        P = self.P
        pst = self.psb[stat_bank]
        pkey = ("psb", stat_bank)
        n2 = ncols
        halves = [(0, min(512, n2))]
        if n2 > 512:
            halves.append((512, n2))
        for c in range(16):
            sq = self.sqc[c % 2]
            P.op("act", lambda e, sq=sq, c=c: e.activation(out=sq[:, :n2], in_=xb[:, c, :], func=AF.Square),
                 reads=[xkeys[c]], writes=[("sqc", c % 2)])
            for hi, (a, b) in enumerate(halves):
                bank = self.psb[stat_bank + hi]
                P.op("pe", lambda e, sq=sq, c=c, a=a, b=b, bank=bank: e.matmul(
                    bank[:, 0:b - a], lhsT=self.ones[:], rhs=sq[:, a:b], start=(c == 0), stop=(c == 15)),
                    reads=[("sqc", c % 2), "ones"], writes=[("psb", stat_bank + hi)])
        for hi, (a, b) in enumerate(halves):
            bank = self.psb[stat_bank + hi]
            P.op("act", lambda e, a=a, b=b, bank=bank: e.activation(
                out=self.rs[:, a:b], in_=bank[:, 0:b - a], func=AF.Sqrt, scale=1.0 / D, bias=self.epsb[:, 0:1]),
                reads=[("psb", stat_bank + hi), "epsb"], writes=[("rs", hi)])
            P.op("dve", lambda e, a=a, b=b: e.reciprocal(out=self.rs[:, a:b], in_=self.rs[:, a:b]),
                 reads=[("rs", hi)], writes=[("rs", hi)])
        Gt, Gk = G
        St, Sk = SH
        for c in range(16):
            tmp = self.tmp[c % 2]
            P.op("dve", lambda e, tmp=tmp, c=c: e.tensor_tensor(out=tmp[:, :n2], in0=xb[:, c, :], in1=self.rs[:, :n2],
                                                                 op=ALU.mult),
                 reads=[xkeys[c], ("rs", 0), ("rs", 1)], writes=[("ntmp", c % 2)])
            P.op("act", lambda e, tmp=tmp, c=c: e.activation(out=dest(c), in_=tmp[:, :n2], func=AF.Identity,
                                                             scale=Gt[:, c:c + 1], bias=St[:, c:c + 1]),
                 reads=[("ntmp", c % 2), Gk, Sk], writes=[dest_keys[c]])
            if post is not None:
                post(c)

    def setup_eps(self):
        P = self.P
        self.epsb = P.sbuf("epsb", [128, 1], F32)
        P.op("dve", lambda e: e.memset(self.epsb[:], EPS), writes=["epsb"])


class FFN:
    def __init__(self, P, cm, w_in, w_out, TB=512, nsplit=1, WC=256):
        self.P, self.cm = P, cm
        self.w_in, self.w_out = w_in, w_out
        WC = 128
        self.nsplit, self.WC = nsplit, WC
        self.NJS = NJ // nsplit
        self.actT = P.sbuf("actT", [128, self.NJS, TB], BF16)
        self.wg = [P.sbuf("wg%d" % i, [128, 16, WC], BF16) for i in range(2)]
        self.wu = [P.sbuf("wu%d" % i, [128, 16, WC], BF16) for i in range(2)]
        self.wo = [P.sbuf("wo%d" % i, [128, self.NJS, 128], BF16) for i in range(2)]
        self.silu = [P.sbuf("silu%d" % i, [128, TB], F32) for i in range(2)]
        self.kin = 0
        self.kout = 0
        self.kj = 0
        self.km = 0

    def emit(self, hT, hkeys, n, xb, xoff, xkeys, g2):
        P, cm = self.P, self.cm
        g2t, g2k = g2
        WC, NJS = self.WC, self.NJS
        per = WC // 128
        for sp in range(self.nsplit):
            j0 = sp * NJS
            for jb in range(NJS // per):
                s = self.kin % 2
                self.kin += 1
                wg, wu = self.wg[s], self.wu[s]
                jg = j0 + jb
                P.dma("sp", wg[:], self.w_in[jg, 0], writes=[("wg", s)])
                P.dma("sp", wu[:], self.w_in[jg, 1], writes=[("wu", s)])
                for jj in range(per):
                    jl = jb * per + jj
                    q = self.kj % 2
                    self.kj += 1
                    pg, pu = cm.psb[2 + q], cm.psb[4 + q]
                    for c in range(16):
                        P.op("pe", lambda e, wg=wg, pg=pg, c=c, jj=jj: e.matmul(
                            pg[:, :n], lhsT=wg[:, c, jj * 128:(jj + 1) * 128], rhs=hT[:, c, :n],
                            start=(c == 0), stop=(c == 15)),
                            reads=[("wg", s), hkeys[c]], writes=[("psb", 2 + q)])
                    for c in range(16):
                        P.op("pe", lambda e, wu=wu, pu=pu, c=c, jj=jj: e.matmul(
                            pu[:, :n], lhsT=wu[:, c, jj * 128:(jj + 1) * 128], rhs=hT[:, c, :n],
                            start=(c == 0), stop=(c == 15)),
                            reads=[("wu", s), hkeys[c]], writes=[("psb", 4 + q)])
                    sl = self.silu[q]
                    P.op("act", lambda e, sl=sl, pg=pg: e.activation(out=sl[:, :n], in_=pg[:, :n], func=AF.Silu),
                         reads=[("psb", 2 + q)], writes=[("silu", q)])
                    P.op("dve", lambda e, sl=sl, pu=pu, jl=jl: e.tensor_tensor(
                        out=self.actT[:, jl, :n], in0=sl[:, :n], in1=pu[:, :n], op=ALU.mult),
                        reads=[("silu", q), ("psb", 4 + q)], writes=[("actT", jl)])
            for m in range(16):
                s = self.kout % 2
                self.kout += 1
                wo = self.wo[s]
                P.dma("sp", wo[:], self.w_out[m][:, j0:j0 + NJS, :], writes=[("wo", s)])
                q = self.km % 2
                self.km += 1
                py = cm.psb[6 + q]
                for jl in range(NJS):
                    P.op("pe", lambda e, wo=wo, py=py, jl=jl: e.matmul(
                        py[:, :n], lhsT=wo[:, jl, :], rhs=self.actT[:, jl, :n], start=(jl == 0), stop=(jl == NJS - 1)),
                        reads=[("wo", s), ("actT", jl)], writes=[("psb", 6 + q)])
                P.op("dve", lambda e, py=py, m=m: e.scalar_tensor_tensor(
                    out=xb[:, m, xoff:xoff + n], in0=py[:, :n], scalar=g2t[:, m:m + 1], in1=xb[:, m, xoff:xoff + n],
                    op0=ALU.mult, op1=ALU.add),
                    reads=[("psb", 6 + q), g2k, xkeys[m]], writes=[xkeys[m]])


def build_pool_layer(segs, TB=512, dbg=0):
    nc = bass.Bass("TRN2", target_bir_lowering=False)
    dram = {}
    for name, T in segs:
        dram[name] = dict(
            x=nc.dram_tensor("x_" + name, [128, 16, T + 2 * HALO], F32, kind="ExternalInput").ap(),
            valid=nc.dram_tensor("valid_" + name, [T + 2 * HALO], F32, kind="ExternalInput").ap(),
            invc=nc.dram_tensor("invc_" + name, [4, T], F32, kind="ExternalInput").ap(),
            mod=nc.dram_tensor("mod_" + name, [128, 6, 16], F32, kind="ExternalInput").ap(),
            out=nc.dram_tensor("out_" + name, [128, 16, T], F32, kind="ExternalOutput").ap(),
        )
    normg = nc.dram_tensor("normg", [128, 2, 16], F32, kind="ExternalInput").ap()
    pscale = nc.dram_tensor("pscale", [128, 16], F32, kind="ExternalInput").ap()
    poolw = nc.dram_tensor("poolw", [4, 128, 4, 512], F32, kind="ExternalInput").ap()
    w_in = nc.dram_tensor("w_in", [NJ, 2, 128, 16, 128], BF16, kind="ExternalInput").ap()
    w_out = nc.dram_tensor("w_out", [16, 128, NJ, 128], BF16, kind="ExternalInput").ap()

    P = Prog(nc)
    cm = Common(P, TB)
    cm.setup_eps()
    W = TB + 2 * HALO
    ffn = FFN(P, cm, w_in, w_out, TB)
    xb = P.sbuf("xb", [128, 16, W], F32)
    hT = P.sbuf("hT", [128, 16, W], BF16)
    hc = [P.sbuf("hc%d" % i, [128, W], F32) for i in range(2)]
    pa = [P.sbuf("pa%d" % i, [128, W], F32) for i in range(2)]
    pm = P.sbuf("pm", [128, TB], F32)
    pw = [P.sbuf("pw%d" % i, [128, 4, 512], BF16) for i in range(2)]
    invc = P.sbuf("invc", [128, 4, TB], F32)
    vmask = P.sbuf("vmask", [128, W], F32)
    ng = P.sbuf("ng", [128, 2, 16], F32)
    psc = P.sbuf("psc", [128, 16], F32)
    P.dma("sp", ng[:], normg, writes=["ng"])
    P.dma("sp", psc[:], pscale, writes=["psc"])
    xkeys = [("xb", c) for c in range(16)]
    hkeys = [("hT", c) for c in range(16)]
    out_toks = []
    st = dict(kpw=0, kpy=0)
    def do_block(dr, mod, G1, G2, GL, mk, name, bi, t0, n):
        nw = n + 2 * HALO
        P.dma("sp", xb[:, :, :nw], dr["x"][:, :, t0:t0 + nw], writes=xkeys)
        P.dma("sp", vmask[:, :nw], dr["valid"][t0:t0 + nw].partition_broadcast(128), writes=["vmask"])
        P.dma("sp", invc[:, :, :n], dr["invc"][:, t0:t0 + n].partition_broadcast(128), writes=["invc"])

        def pool_chunk(c, n=n, nw=nw):
            g = c // 4
            w = WINS[g]
            h = hc[c % 2]
            hk = ("hc", c % 2)
            P.op("dve", lambda e: e.tensor_tensor(out=h[:, 0:HALO], in0=h[:, 0:HALO], in1=vmask[:, 0:HALO], op=ALU.mult),
                 reads=[hk, "vmask"], writes=[hk])
            P.op("dve", lambda e: e.tensor_tensor(out=h[:, nw - HALO:nw], in0=h[:, nw - HALO:nw],
                                                  in1=vmask[:, nw - HALO:nw], op=ALU.mult),
                 reads=[hk, "vmask"], writes=[hk])
            cur, curk, ln = h, hk, nw
            s = 1
            i = 0
            while s < w:
                dst = pa[i % 2]
                P.op("dve", lambda e, cur=cur, dst=dst, s=s, ln=ln: e.tensor_tensor(
                    out=dst[:, 0:ln - s], in0=cur[:, 0:ln - s], in1=cur[:, s:ln], op=ALU.add),
                    reads=[curk], writes=[("pa", i % 2)])
                cur, curk, ln = dst, ("pa", i % 2), ln - s
                s *= 2
                i += 1
            o = HALO - w // 2
            P.op("dve", lambda e, cur=cur, o=o, g=g: e.tensor_tensor(
                out=pm[:, :n], in0=cur[:, o:o + n], in1=invc[:, g, :n], op=ALU.mult),
                reads=[curk, "invc"], writes=["pm"])
            P.op("dve", lambda e, c=c: e.tensor_tensor(
                out=hT[:, c, :n], in0=pm[:, :n], in1=h[:, HALO:HALO + n], op=ALU.subtract),
                reads=["pm", hk], writes=[hkeys[c]])

        hck = [("hc", c % 2) for c in range(16)]
        cm.norm_mod(xb[:, :, :nw], nw, (G1, ("G1", name)), (mod[:, 0, :], mk),
                    lambda c, nw=nw: hc[c % 2][:, :nw], hck, xkeys, stat_bank=0, post=pool_chunk)
        for g in range(4):
            s = st["kpw"] % 2
            st["kpw"] += 1
            P.dma("pool", pw[s][:], poolw[g], writes=[("pw", s)])
            for mm in range(4):
                m = 4 * g + mm
                q = st["kpy"] % 2
                st["kpy"] += 1
                py = cm.psb[6 + q]
                for cc in range(4):
                    P.op("pe", lambda e, s=s, py=py, cc=cc, mm=mm, g=g: e.matmul(
                        py[:, :n], lhsT=pw[s][:, cc, mm * 128:(mm + 1) * 128], rhs=hT[:, 4 * g + cc, :n],
                        start=(cc == 0), stop=(cc == 3)),
                        reads=[("pw", s), hkeys[4 * g + cc]], writes=[("psb", 6 + q)])
                P.op("dve", lambda e, py=py, m=m: e.scalar_tensor_tensor(
                    out=xb[:, m, HALO:HALO + n], in0=py[:, :n], scalar=GL[:, m:m + 1], in1=xb[:, m, HALO:HALO + n],
                    op0=ALU.mult, op1=ALU.add),
                    reads=[("psb", 6 + q), ("GL", name), xkeys[m]], writes=[xkeys[m]])
        if dbg == 0:
            cm.norm_mod(xb[:, :, HALO:HALO + n], n, (G2, ("G2", name)), (mod[:, 3, :], mk),
                        lambda c, n=n: hT[:, c, :n], hkeys, xkeys, stat_bank=0)
            ffn.emit(hT, hkeys, n, xb, HALO, xkeys, (mod[:, 5, :], mk))
        t = P.dma("sp", dr["out"][:, :, t0:t0 + n], xb[:, :, HALO:HALO + n], reads=xkeys, writes=[("out", name, bi)])
        out_toks.append(t)

    for si, (name, T) in enumerate(segs):
        dr = dram[name]
        mod = P.sbuf("modsb_" + name, [128, 6, 16], F32)
        G1 = P.sbuf("G1_" + name, [128, 16], F32)
        G2 = P.sbuf("G2_" + name, [128, 16], F32)
        GL = P.sbuf("GL_" + name, [128, 16], F32)
        mk = ("mod", name)
        P.dma("sp", mod[:], dr["mod"], writes=[mk])
        P.op("dve", lambda e, G1=G1, mod=mod: e.scalar_tensor_tensor(
            out=G1[:], in0=mod[:, 1, :], scalar=1.0, in1=ng[:, 0, :], op0=ALU.add, op1=ALU.mult),
            reads=[mk, "ng"], writes=[("G1", name)])
        P.op("dve", lambda e, G2=G2, mod=mod: e.scalar_tensor_tensor(
            out=G2[:], in0=mod[:, 4, :], scalar=1.0, in1=ng[:, 1, :], op0=ALU.add, op1=ALU.mult),
            reads=[mk, "ng"], writes=[("G2", name)])
        P.op("dve", lambda e, GL=GL, mod=mod: e.tensor_tensor(out=GL[:], in0=mod[:, 2, :], in1=psc[:], op=ALU.mult),
             reads=[mk, "psc"], writes=[("GL", name)])
        nblk = (T + TB - 1) // TB
        for bi in range(nblk):
            do_block(dr, mod, G1, G2, GL, mk, name, bi, bi * TB, min(TB, T - bi * TB))
    P.final_wait("sp", out_toks)
    return P.build()


def to_fm(a):
    T = a.shape[0]
    return np.ascontiguousarray(a.reshape(T, 16, 128).transpose(2, 1, 0))


def from_fm(a):
    T = a.shape[2]
    return np.ascontiguousarray(a.transpose(2, 1, 0).reshape(T, 2048))


def vec_fm(v):
    lead = v.shape[:-1]
    r = v.reshape(lead + (16, 128))
    return np.ascontiguousarray(np.moveaxis(r, -1, 0))


def seg_shards(seq, T):
    S = seq.shape[0]
    pad = np.zeros((S + 2 * HALO, 2048), np.float32)
    pad[HALO:HALO + S] = seq
    t = np.arange(S)
    inv = np.zeros((4, S), np.float32)
    for g, w in enumerate(WINS):
        lo = np.clip(t - w // 2, 0, S)
        hi = np.clip(t + w - w // 2, 0, S)
        inv[g] = 1.0 / (hi - lo)
    valid = np.zeros(S + 2 * HALO, np.float32)
    valid[HALO:HALO + S] = 1.0
    out = []
    for s0 in range(0, S, T):
        out.append(dict(x=to_fm(pad[s0:s0 + T + 2 * HALO]), valid=np.ascontiguousarray(valid[s0:s0 + T + 2 * HALO]),
                        invc=np.ascontiguousarray(inv[:, s0:s0 + T])))
    return out


def run_pool_layer(x, ctx, mods_l, normg_l, pscale, poolw, w_in, w_out, with_ctx, dbg=0):
    segs = [("lat", 2048)] + ([("ctx", 64)] if with_ctx else [])
    nc = build_pool_layer(segs, TB=TBG, dbg=dbg)
    maps = []
    pw_l = np.ascontiguousarray(poolw.reshape(4, 4, 128, 512).transpose(0, 2, 1, 3))
    lat = [seg_shards(x[b], 2048) for b in range(2)]
    cs = [seg_shards(ctx[b], 64) for b in range(2)] if with_ctx else None
    for i in range(8):
        b, k = i // 4, i % 4
        m = {"normg": vec_fm(normg_l), "pscale": vec_fm(pscale), "poolw": pw_l, "w_in": w_in, "w_out": w_out}
        sh = lat[b][k]
        m.update({"x_lat": sh["x"], "valid_lat": sh["valid"], "invc_lat": sh["invc"],
                  "mod_lat": vec_fm(mods_l[b].reshape(6, 2048))})
        if with_ctx:
            sh = cs[b][k]
            m.update({"x_ctx": sh["x"], "valid_ctx": sh["valid"], "invc_ctx": sh["invc"],
                      "mod_ctx": vec_fm(mods_l[2].reshape(6, 2048))})
        maps.append(m)
    res = run_bass_kernel_spmd(nc, maps, core_ids=list(range(8)))
    xo = np.zeros_like(x)
    co = np.zeros_like(ctx) if with_ctx else None
    for i in range(8):
        b, k = i // 4, i % 4
        xo[b, k * 2048:(k + 1) * 2048] = from_fm(res.results[i]["out_lat"])
        if with_ctx:
            co[b, k * 64:(k + 1) * 64] = from_fm(res.results[i]["out_ctx"])
    return xo, co


DH = 128
NHEAD = 8
GRID_W = 64
CTX = 256
TLAT = 2048
TCTX = 64


def build_a1():
    nc = bass.Bass("TRN2", target_bir_lowering=False)
    x_lat = nc.dram_tensor("x_lat", [128, 16, TLAT], F32, kind="ExternalInput").ap()
    x_ctx = nc.dram_tensor("x_ctx", [128, 16, TCTX], F32, kind="ExternalInput").ap()
    mod_lat = nc.dram_tensor("mod_lat", [128, 6, 16], F32, kind="ExternalInput").ap()
    mod_ctx = nc.dram_tensor("mod_ctx", [128, 6, 16], F32, kind="ExternalInput").ap()
    normg = nc.dram_tensor("normg", [128, 2, 16], F32, kind="ExternalInput").ap()
    wqkv = nc.dram_tensor("wqkv", [48, 128, 16, 128], BF16, kind="ExternalInput").ap()
    qkg = nc.dram_tensor("qkg", [128, 2], F32, kind="ExternalInput").ap()
    cs_d = nc.dram_tensor("cs", [128, TLAT], F32, kind="ExternalInput").ap()
    sn_d = nc.dram_tensor("sn", [128, TLAT], F32, kind="ExternalInput").ap()
    qT = nc.dram_tensor("qT", [16, 128, TLAT], BF16, kind="ExternalOutput").ap()
    kT = nc.dram_tensor("kT", [16, 128, TLAT], BF16, kind="ExternalOutput").ap()
    vT = nc.dram_tensor("vT", [16, 128, TLAT], BF16, kind="ExternalOutput").ap()
    kcT = nc.dram_tensor("kcT", [16, 128, TCTX], BF16, kind="ExternalOutput").ap()
    vcT = nc.dram_tensor("vcT", [16, 128, TCTX], BF16, kind="ExternalOutput").ap()

    P = Prog(nc)
    cm = Common(P, 512, halo=0)
    cm.setup_eps()
    TALL = TLAT + TCTX
    xb = P.sbuf("xb", [128, 16, 512], F32)
    hT = P.sbuf("hT", [128, 16, TALL], BF16)
    wt = [P.sbuf("wt%d" % i, [128, 16, 128], BF16) for i in range(3)]
    cs = P.sbuf("cs_sb", [128, TLAT], F32)
    sn = P.sbuf("sn_sb", [128, TLAT], F32)
    ng = P.sbuf("ng", [128, 2, 16], F32)
    g_sb = P.sbuf("qkg_sb", [128, 2], F32)
    qn = [P.sbuf("qn%d" % i, [128, 512], F32) for i in range(2)]
    sw = [P.sbuf("sw%d" % i, [128, 512], F32) for i in range(2)]
    tm = [P.sbuf("tm%d" % i, [128, 512], F32) for i in range(2)]
    oo = [P.sbuf("oo%d" % i, [128, 512], F32) for i in range(2)]
    sq = [P.sbuf("sq%d" % i, [128, 512], BF16) for i in range(2)]
    rr = [P.sbuf("rr%d" % i, [128, 512], F32) for i in range(2)]
    stg = [P.sbuf("stg%d" % i, [128, TLAT], BF16) for i in range(2)]
    stgc = [P.sbuf("stgc%d" % i, [128, TCTX], BF16) for i in range(2)]
    P.dma("sp", ng[:], normg, writes=["ng"])
    P.dma("sp", g_sb[:], qkg, writes=["qkg"])
    P.dma("sp", cs[:], cs_d, writes=["cs"])
    P.dma("sp", sn[:], sn_d, writes=["sn"])
    xkeys = [("xb", c) for c in range(16)]
    segs = [("lat", x_lat, mod_lat, TLAT, 0), ("ctx", x_ctx, mod_ctx, TCTX, TLAT)]
    for name, xd, md, T, off in segs:
        mod = P.sbuf("modsb_" + name, [128, 6, 16], F32)
        G1 = P.sbuf("G1_" + name, [128, 16], F32)
        mk = ("mod", name)
        P.dma("sp", mod[:], md, writes=[mk])
        P.op("dve", lambda e, G1=G1, mod=mod: e.scalar_tensor_tensor(
            out=G1[:], in0=mod[:, 1, :], scalar=1.0, in1=ng[:, 0, :], op0=ALU.add, op1=ALU.mult),
            reads=[mk, "ng"], writes=[("G1", name)])
        for t0 in range(0, T, 512):
            n = min(512, T - t0)
            P.dma("sp", xb[:, :, :n], xd[:, :, t0:t0 + n], writes=xkeys)
            hk = [("hT", c, (off + t0) // 512) for c in range(16)]
            cm.norm_mod(xb[:, :, :n], n, (G1, ("G1", name)), (mod[:, 0, :], mk),
                        lambda c, o=off + t0, n=n: hT[:, c, o:o + n], hk, xkeys, stat_bank=0)
    blocks = [(i * 512, 512, i) for i in range(4)] + [(TLAT, TCTX, 4)]
    st = dict(k=0, ps=0, ss=0, t=0)
    out_toks = []

    def proj(m, wtile, wkey, jj, t0, n, bi, kind):
        pb = 2 + st["ps"] % 2
        st["ps"] += 1
        ps = cm.psb[pb]
        for c in range(16):
            P.op("pe", lambda e, c=c: e.matmul(ps[:, :n], lhsT=wtile[:, c, jj * 128:(jj + 1) * 128], rhs=hT[:, c, t0:t0 + n],
                                                start=(c == 0), stop=(c == 15)),
                 reads=[wkey, ("hT", c, bi)], writes=[("psb", pb)])
        return ps, ("psb", pb)

    for mb in range(48):
        s = st["k"] % 3
        st["k"] += 1
        P.dma("sp", wt[s][:], wqkv[mb], writes=[("wt", s)])
        for jj in range(1):
            m = mb
            kind = "q" if m < 16 else ("k" if m < 32 else "v")
            mi = m % 16
            sg = m % 2
            gcol = 0 if kind == "q" else 1
            for (t0, n, bi) in blocks:
                is_ctx = bi == 4
                if is_ctx and kind == "q":
                    continue
                ps, pk = proj(m, wt[s], ("wt", s), jj, t0, n, bi, kind)
                dst = stgc[sg][:, :n] if is_ctx else stg[sg][:, t0:t0 + n]
                dkey = ("stgc", sg) if is_ctx else ("stg", sg, bi)
                if kind == "v":
                    P.op("act", lambda e, ps=ps, dst=dst, n=n: e.activation(out=dst, in_=ps[:, :n], func=AF.Copy),
                         reads=[pk], writes=[dkey])
                    continue
                u = st["t"] % 2
                st["t"] += 1
                sb = 4 + st["ss"] % 2
                st["ss"] += 1
                pss = cm.psb[sb]
                P.op("act", lambda e, ps=ps, u=u, n=n: e.activation(out=sq[u][:, :n], in_=ps[:, :n], func=AF.Square),
                     reads=[pk], writes=[("sq", u)])
                P.op("pe", lambda e, pss=pss, u=u, n=n: e.matmul(pss[:, :n], lhsT=cm.ones[:], rhs=sq[u][:, :n], start=True, stop=True),
                     reads=[("sq", u), "ones"], writes=[("psb", sb)])
                P.op("act", lambda e, pss=pss, u=u, n=n: e.activation(out=rr[u][:, :n], in_=pss[:, :n], func=AF.Sqrt,
                                                                    scale=1.0 / DH, bias=cm.epsb[:, 0:1]),
                     reads=[("psb", sb), "epsb"], writes=[("rr", u)])
                P.op("dve", lambda e, u=u, n=n: e.reciprocal(out=rr[u][:, :n], in_=rr[u][:, :n]),
                     reads=[("rr", u)], writes=[("rr", u)])
                if is_ctx:
                    P.op("dve", lambda e, ps=ps, u=u, n=n, dst=dst, gcol=gcol: e.scalar_tensor_tensor(
                        out=dst, in0=ps[:, :n], scalar=g_sb[:, gcol:gcol + 1], in1=rr[u][:, :n], op0=ALU.mult, op1=ALU.mult),
                        reads=[pk, "qkg", ("rr", u)], writes=[dkey])
                    continue
                P.op("dve", lambda e, ps=ps, u=u, n=n, gcol=gcol: e.scalar_tensor_tensor(
                    out=qn[u][:, :n], in0=ps[:, :n], scalar=g_sb[:, gcol:gcol + 1], in1=rr[u][:, :n], op0=ALU.mult, op1=ALU.mult),
                    reads=[pk, "qkg", ("rr", u)], writes=[("qn", u)])
                P.op("act", lambda e, u=u, n=n: e.activation(out=sw[u][0:64, :n], in_=qn[u][64:128, :n], func=AF.Copy),
                     reads=[("qn", u)], writes=[("sw", u, 0)])
                P.op("act", lambda e, u=u, n=n: e.activation(out=sw[u][64:128, :n], in_=qn[u][0:64, :n], func=AF.Copy),
                     reads=[("qn", u)], writes=[("sw", u, 1)])
                P.op("pool", lambda e, u=u, n=n, t0=t0: e.tensor_tensor(out=tm[u][:, :n], in0=sw[u][:, :n], in1=sn[:, t0:t0 + n], op=ALU.mult),
                     reads=[("sw", u, 0), ("sw", u, 1), "sn"], writes=[("tm", u)])
                P.op("dve", lambda e, u=u, n=n, t0=t0: e.tensor_tensor(out=oo[u][:, :n], in0=qn[u][:, :n], in1=cs[:, t0:t0 + n], op=ALU.mult),
                     reads=[("qn", u), "cs"], writes=[("oo", u)])
                P.op("dve", lambda e, u=u, n=n, dst=dst: e.tensor_tensor(out=dst, in0=oo[u][:, :n], in1=tm[u][:, :n], op=ALU.add),
                     reads=[("oo", u), ("tm", u)], writes=[dkey])
            od = {"q": qT, "k": kT, "v": vT}[kind]
            out_toks.append(P.dma("sp", od[mi], stg[sg][:], reads=[("stg", sg, bi) for bi in range(4)], writes=[("o", kind, mi)]))
            if kind != "q":
                oc = {"k": kcT, "v": vcT}[kind]
                out_toks.append(P.dma("sp", oc[mi], stgc[sg][:], reads=[("stgc", sg)], writes=[("oc", kind, mi)]))
    P.final_wait("sp", out_toks)
    return P.build()


def rope_perm():
    return np.concatenate([np.arange(0, 128, 2), np.arange(1, 128, 2)])


def rope_tables(tok0, n):
    t = np.arange(tok0, tok0 + n)
    row = (t // GRID_W).astype(np.float32)
    col = (t % GRID_W).astype(np.float32)
    nf = DH // 4
    inv = (10000.0 ** (-np.arange(nf, dtype=np.float32) / nf)).astype(np.float32)
    ang = np.concatenate([row[:, None] * inv, col[:, None] * inv], -1)
    c = np.cos(ang).astype(np.float32).T
    s = np.sin(ang).astype(np.float32).T
    cs = np.concatenate([c, c], 0)
    sn = np.concatenate([-s, s], 0)
    return np.ascontiguousarray(cs), np.ascontiguousarray(sn)


def run_a1(x, ctx, mods_l, normg_l, w_qkv, qk_g):
    nc = build_a1()
    perm = rope_perm()
    w = w_qkv
    qkg = np.ascontiguousarray(qk_g[:, perm].T)
    maps = []
    for i in range(8):
        b, k = i // 4, i % 4
        cs, sn = rope_tables(k * TLAT, TLAT)
        maps.append({"x_lat": to_fm(x[b, k * TLAT:(k + 1) * TLAT]), "x_ctx": to_fm(ctx[b, k * TCTX:(k + 1) * TCTX]),
                     "mod_lat": vec_fm(mods_l[b].reshape(6, 2048)), "mod_ctx": vec_fm(mods_l[2].reshape(6, 2048)),
                     "normg": vec_fm(normg_l), "wqkv": w, "qkg": qkg, "cs": cs, "sn": sn})
    res = run_bass_kernel_spmd(nc, maps, core_ids=list(range(8)))
    return res.results


NKT = (CTX + 8192) // 128
KH = NKT // 2


def build_a2(lambda_init, dbg=0):
    nc = bass.Bass("TRN2", target_bir_lowering=False)
    qT = nc.dram_tensor("qT", [16, 128, TLAT], BF16, kind="ExternalInput").ap()
    kT = nc.dram_tensor("kT", [16, 128, NKT * 128], BF16, kind="ExternalInput").ap()
    vv = nc.dram_tensor("vv", [8, 128, NKT, 256], BF16, kind="ExternalInput").ap()
    x_lat = nc.dram_tensor("x_lat", [128, 16, TLAT], F32, kind="ExternalInput").ap()
    mod_lat = nc.dram_tensor("mod_lat", [128, 6, 16], F32, kind="ExternalInput").ap()
    normg = nc.dram_tensor("normg", [128, 2, 16], F32, kind="ExternalInput").ap()
    lamv = nc.dram_tensor("lamv", [128, 4], F32, kind="ExternalInput").ap()
    sublng = nc.dram_tensor("sublng", [128, 2], F32, kind="ExternalInput").ap()
    w_o = nc.dram_tensor("w_o", [16, 128, 16, 128], BF16, kind="ExternalInput").ap()
    w_in = nc.dram_tensor("w_in", [NJ, 2, 128, 16, 128], BF16, kind="ExternalInput").ap()
    w_out = nc.dram_tensor("w_out", [16, 128, NJ, 128], BF16, kind="ExternalInput").ap()
    out = nc.dram_tensor("out_lat", [128, 16, TLAT], F32, kind="ExternalOutput").ap()

    P = Prog(nc)
    cm = Common(P, 512, halo=0)
    cm.setup_eps()
    ffn = FFN(P, cm, w_in, w_out, 512, nsplit=4, WC=128)
    xb = P.sbuf("xb", [128, 16, 512], F32)
    hT = P.sbuf("hT", [128, 16, 512], BF16)
    ring = [dict(k=P.sbuf("rk%d" % s, [128, 2, KH * 128], BF16), v=P.sbuf("rv%d" % s, [128, KH, 256], BF16)) for s in range(2)]
    qsb = [P.sbuf("qsb%d" % s, [128, 2, 512], BF16) for s in range(2)]
    pT = [P.sbuf("pT%d" % s, [128, 512], BF16) for s in range(3)]
    osb = [P.sbuf("osb%d" % i, [128, 2, 512], F32) for i in range(2)]
    rden = [P.sbuf("rden%d" % i, [128, 512], F32) for i in range(2)]
    dacc = [P.sbuf("dacc%d" % i, [128, 512], F32) for i in range(2)]
    dif = P.sbuf("dif", [128, 2, 512], F32)
    sqd = P.sbuf("sqd", [128, 2, 512], BF16)
    rst = P.sbuf("rst", [128, 512], F32)
    ng = P.sbuf("ng", [128, 2, 16], F32)
    mod = P.sbuf("modsb", [128, 6, 16], F32)
    G2 = P.sbuf("G2", [128, 16], F32)
    lv = P.sbuf("lv", [128, 4], F32)
    lpr = P.sbuf("lpr", [128, 2], F32)
    lex = P.sbuf("lex", [128, 2], F32)
    nlam = P.sbuf("nlam", [128, 1], F32)
    sg = P.sbuf("sg", [128, 2], F32)
    ones32 = P.sbuf("ones32", [128, 128], F32)
    eps256 = cm.epsb
    P.dma("sp", ng[:], normg, writes=["ng"])
    P.dma("sp", mod[:], mod_lat, writes=["mod"])
    P.dma("sp", lv[:], lamv, writes=["lv"])
    P.dma("sp", sg[:], sublng, writes=["sg"])
    P.op("dve", lambda e: e.memset(ones32[:], 1.0), writes=["ones32"])
    P.op("dve", lambda e: e.scalar_tensor_tensor(out=G2[:], in0=mod[:, 4, :], scalar=1.0, in1=ng[:, 1, :],
                                                 op0=ALU.add, op1=ALU.mult), reads=["mod", "ng"], writes=["G2"])
    P.op("dve", lambda e: e.tensor_tensor(out=lpr[:, 0:1], in0=lv[:, 0:1], in1=lv[:, 1:2], op=ALU.mult), reads=["lv"], writes=["lpr0"])
    P.op("dve", lambda e: e.tensor_tensor(out=lpr[:, 1:2], in0=lv[:, 2:3], in1=lv[:, 3:4], op=ALU.mult), reads=["lv"], writes=["lpr1"])
    P.op("pe", lambda e: e.matmul(cm.psb[0][:, 0:2], lhsT=ones32[:], rhs=lpr[:], start=True, stop=True),
         reads=["ones32", "lpr0", "lpr1"], writes=[("psb", 0)])
    P.op("act", lambda e: e.activation(out=lex[:], in_=cm.psb[0][:, 0:2], func=AF.Exp), reads=[("psb", 0)], writes=["lex"])
    P.op("dve", lambda e: e.tensor_tensor(out=nlam[:], in0=lex[:, 1:2], in1=lex[:, 0:1], op=ALU.subtract), reads=["lex"], writes=["nlam"])
    P.op("dve", lambda e: e.tensor_scalar(out=nlam[:], in0=nlam[:], scalar1=-float(lambda_init), scalar2=None, op0=ALU.add),
         reads=["nlam"], writes=["nlam"])
    P.op("dve", lambda e: e.tensor_scalar(out=sg[:], in0=sg[:], scalar1=float(1.0 - lambda_init), scalar2=None, op0=ALU.mult),
         reads=["sg"], writes=["sg"])
    xkeys = [("xb", c) for c in range(16)]
    hkeys = [("hT", c) for c in range(16)]
    out_toks = []
    st = dict(ring=0, q=0, pt=0, sT=0, wo=0, py=0)
    SCALE = float(DH) ** -0.5

    def attn_head(qb, h):
        qs = st["q"] % 2
        st["q"] += 1
        for i in range(2):
            P.dma("sp", qsb[qs][:, i, :], qT[2 * h + i][:, qb * 512:(qb + 1) * 512], writes=[("qsb", qs, i)])
        for half in range(2):
            rs_ = st["ring"] % 2
            st["ring"] += 1
            rg = ring[rs_]
            for i in range(2):
                P.dma("sp", rg["k"][:, i, :], kT[2 * h + i][:, half * KH * 128:(half + 1) * KH * 128], writes=[("rk", rs_, i)])
            P.dma("sp", rg["v"][:], vv[h][:, half * KH:(half + 1) * KH, :], writes=[("rv", rs_)])
            steps = [(i, kt) for i in range(2) for kt in range(KH)]

            def emit_s(i, kt, rg=rg, rs_=rs_):
                sb = st["sT"] % 2
                st["sT"] += 1
                pss = cm.psb[sb]
                P.op("pe", lambda e, pss=pss, rg=rg, i=i, kt=kt: e.matmul(
                    pss[:, :], lhsT=rg["k"][:, i, kt * 128:(kt + 1) * 128], rhs=qsb[qs][:, i, :], start=True, stop=True),
                    reads=[("rk", rs_, i), ("qsb", qs, i)], writes=[("psb", sb)])
                pi = st["pt"] % 3
                st["pt"] += 1
                P.op("act", lambda e, pss=pss, pi=pi: e.activation(out=pT[pi][:], in_=pss[:, :], func=AF.Exp, scale=SCALE),
                     reads=[("psb", sb)], writes=[("pT", pi)])
                return pi

            def emit_pv(i, kt, pi, rg=rg, rs_=rs_, half=half):
                first = (half == 0 and kt == 0)
                last = (half == 1 and kt == KH - 1)
                for ec in range(2):
                    P.op("pe", lambda e, rg=rg, kt=kt, ec=ec, pi=pi, i=i, first=first, last=last: e.matmul(
                        cm.psb[2 + 2 * i + ec][:, :], lhsT=rg["v"][:, kt, ec * 128:(ec + 1) * 128], rhs=pT[pi][:],
                        start=first, stop=last),
                        reads=[("rv", rs_), ("pT", pi)], writes=[("psb", 2 + 2 * i + ec)])
                if first:
                    P.op("dve", lambda e, pi=pi, i=i: e.tensor_copy(out=dacc[i][:], in_=pT[pi][:]),
                         reads=[("pT", pi)], writes=[("dacc", i)])
                else:
                    P.op("dve", lambda e, pi=pi, i=i: e.tensor_tensor(out=dacc[i][:], in0=dacc[i][:], in1=pT[pi][:], op=ALU.add),
                         reads=[("pT", pi), ("dacc", i)], writes=[("dacc", i)])

            pend = emit_s(*steps[0])
            for j in range(len(steps)):
                nxt = emit_s(*steps[j + 1]) if j + 1 < len(steps) else None
                emit_pv(steps[j][0], steps[j][1], pend)
                pend = nxt
        for i in range(2):
            P.op("pe", lambda e, i=i: e.matmul(cm.psb[6 + i][:, :], lhsT=ones32[:], rhs=dacc[i][:], start=True, stop=True),
                 reads=["ones32", ("dacc", i)], writes=[("psb", 6 + i)])
            P.op("dve", lambda e, i=i: e.reciprocal(out=rden[i][:], in_=cm.psb[6 + i][:, :]),
                 reads=[("psb", 6 + i)], writes=[("rden", i)])
            for ec in range(2):
                P.op("dve", lambda e, i=i, ec=ec: e.tensor_tensor(out=osb[i][:, ec, :], in0=cm.psb[2 + 2 * i + ec][:, :],
                                                                  in1=rden[i][:], op=ALU.mult),
                     reads=[("psb", 2 + 2 * i + ec), ("rden", i)], writes=[("osb", i, ec)])
        for ec in range(2):
            P.op("dve", lambda e, ec=ec: e.scalar_tensor_tensor(out=dif[:, ec, :], in0=osb[1][:, ec, :], scalar=nlam[:, 0:1],
                                                                in1=osb[0][:, ec, :], op0=ALU.mult, op1=ALU.add),
                 reads=[("osb", 1, ec), ("osb", 0, ec), "nlam"], writes=[("dif", ec)])
            P.op("act", lambda e, ec=ec: e.activation(out=sqd[:, ec, :], in_=dif[:, ec, :], func=AF.Square),
                 reads=[("dif", ec)], writes=[("sqd", ec)])
        for ec in range(2):
            P.op("pe", lambda e, ec=ec: e.matmul(cm.psb[0][:, :], lhsT=cm.ones[:], rhs=sqd[:, ec, :], start=(ec == 0), stop=(ec == 1)),
                 reads=["ones", ("sqd", ec)], writes=[("psb", 0)])
        P.op("act", lambda e: e.activation(out=rst[:], in_=cm.psb[0][:, :], func=AF.Sqrt, scale=1.0 / 256.0, bias=cm.epsb[:, 0:1]),
             reads=[("psb", 0), "epsb"], writes=["rst"])
        P.op("dve", lambda e: e.reciprocal(out=rst[:], in_=rst[:]), reads=["rst"], writes=["rst"])
        for ec in range(2):
            P.op("dve", lambda e, ec=ec: e.scalar_tensor_tensor(out=hT[:, 2 * h + ec, :], in0=dif[:, ec, :], scalar=sg[:, ec:ec + 1],
                                                                in1=rst[:], op0=ALU.mult, op1=ALU.mult),
                 reads=[("dif", ec), "sg", "rst"], writes=[hkeys[2 * h + ec]])

    def do_block(qb):
        for h in range(NHEAD):
            attn_head(qb, h)
        P.dma("sp", xb[:], x_lat[:, :, qb * 512:(qb + 1) * 512], writes=xkeys)
        for m in range(16):
            s = ffn.kin % 2
            ffn.kin += 1
            wt = ffn.wg[s]
            P.dma("sp", wt[:], w_o[m], writes=[("wg", s)])
            q = ffn.km % 2
            ffn.km += 1
            py = cm.psb[6 + q]
            for c in range(16):
                P.op("pe", lambda e, wt=wt, py=py, c=c: e.matmul(py[:, :], lhsT=wt[:, c, :], rhs=hT[:, c, :],
                                                                  start=(c == 0), stop=(c == 15)),
                     reads=[("wg", s), hkeys[c]], writes=[("psb", 6 + q)])
            P.op("dve", lambda e, py=py, m=m: e.scalar_tensor_tensor(
                out=xb[:, m, :], in0=py[:, :], scalar=mod[:, 2, m:m + 1], in1=xb[:, m, :], op0=ALU.mult, op1=ALU.add),
                reads=[("psb", 6 + q), "mod", xkeys[m]], writes=[xkeys[m]])
        if dbg == 0:
            cm.norm_mod(xb[:, :, :], 512, (G2, "G2"), (mod[:, 3, :], "mod"), lambda c: hT[:, c, :], hkeys, xkeys, stat_bank=0)
            ffn.emit(hT, hkeys, 512, xb, 0, xkeys, (mod[:, 5, :], "mod"))
        out_toks.append(P.dma("sp", out[:, :, qb * 512:(qb + 1) * 512], xb[:], reads=xkeys, writes=[("out", qb)]))

    for qb in range(4):
        do_block(qb)
    P.final_wait("sp", out_toks)
    return P.build()


def run_a2(a1res, x, mods_l, normg_l, lam_vec, subln_g, w_o, w_in, w_out, lambda_init, dbg=0):
    nc = build_a2(lambda_init, dbg)
    maps = []
    kv = []
    for b in range(2):
        kparts = [a1res[b * 4 + k]["kcT"] for k in range(4)] + [a1res[b * 4 + k]["kT"] for k in range(4)]
        kall = np.ascontiguousarray(np.concatenate(kparts, axis=2))
        vparts = [a1res[b * 4 + k]["vcT"] for k in range(4)] + [a1res[b * 4 + k]["vT"] for k in range(4)]
        vall = np.concatenate(vparts, axis=2)
        v5 = vall.reshape(8, 2, 128, NKT, 128)
        v5 = np.ascontiguousarray(v5.transpose(0, 4, 3, 1, 2).reshape(8, 128, NKT, 256))
        kv.append((kall, v5))
    for i in range(8):
        b, k = i // 4, i % 4
        maps.append({"qT": a1res[i]["qT"], "kT": kv[b][0], "vv": kv[b][1], "x_lat": to_fm(x[b, k * TLAT:(k + 1) * TLAT]),
                     "mod_lat": vec_fm(mods_l[b].reshape(6, 2048)), "normg": vec_fm(normg_l),
                     "lamv": np.ascontiguousarray(lam_vec.T), "sublng": np.ascontiguousarray(subln_g.reshape(2, 128).T),
                     "w_o": w_o, "w_in": w_in, "w_out": w_out})
    res = run_bass_kernel_spmd(nc, maps, core_ids=list(range(8)))
    xo = np.zeros_like(x)
    for i in range(8):
        b, k = i // 4, i % 4
        xo[b, k * TLAT:(k + 1) * TLAT] = from_fm(res.results[i]["out_lat"])
    return xo


LW = 96
NSET = 4
LG = 256
C0 = 0.6065306597126334
R1_TCTX = 128


def build_r1(segs=(("lat", TLAT), ("ctx", R1_TCTX)), NB=256):
    nc = bass.Bass("TRN2", target_bir_lowering=False)
    dr = {}
    for name, T in segs:
        dr[name] = dict(
            x=nc.dram_tensor("x_" + name, [128, 16, T + 2], F32, kind="ExternalInput").ap(),
            valid=nc.dram_tensor("valid_" + name, [T + 2], F32, kind="ExternalInput").ap(),
            mod=nc.dram_tensor("mod_" + name, [128, 6, 16], F32, kind="ExternalInput").ap(),
            ot=nc.dram_tensor("ot_" + name, [2, 4, 16, 128, T], BF16, kind="ExternalOutput").ap(),
            pc=nc.dram_tensor("pc_" + name, [128, 2, 16, T // 128], F32, kind="ExternalOutput").ap(),
            v=nc.dram_tensor("v_" + name, [16, 128, T], BF16, kind="ExternalOutput").ap(),
            bonus=nc.dram_tensor("bonus_" + name, [16, 128, T], F32, kind="ExternalOutput").ap(),
            g=nc.dram_tensor("g_" + name, [16, 128, T], F32, kind="ExternalOutput").ap(),
        )
    normg = nc.dram_tensor("normg", [128, 2, 16], F32, kind="ExternalInput").ap()
    mu_d = nc.dram_tensor("mu", [128, 6, 16], F32, kind="ExternalInput").ap()
    w_rkv = nc.dram_tensor("w_rkv", [3, 16, 128, 16, 128], BF16, kind="ExternalInput").ap()
    w_la = nc.dram_tensor("w_la", [2, D, LW], F32, kind="ExternalInput").ap()
    w_lb = nc.dram_tensor("w_lb", [2, LW, D], F32, kind="ExternalInput").ap()
    a_la = nc.dram_tensor("a_la", [2, D, LW], F32, kind="ExternalInput").ap()
    a_lb = nc.dram_tensor("a_lb", [2, LW, D], F32, kind="ExternalInput").ap()
    g_la = nc.dram_tensor("g_la", [D, LG], F32, kind="ExternalInput").ap()
    g_lb = nc.dram_tensor("g_lb", [LG, D], F32, kind="ExternalInput").ap()
    dirvec = nc.dram_tensor("dirvec", [128, 2, 4, 16], F32, kind="ExternalInput").ap()
    rk_d = nc.dram_tensor("r_k", [128, 16], F32, kind="ExternalInput").ap()
    rmask_d = nc.dram_tensor("rmask", [128, NB], F32, kind="ExternalInput").ap()

    P = Prog(nc)
    cm = Common(P, NB, halo=1)
    cm.setup_eps()
    W = NB + 2
    xb = P.sbuf("xb", [128, 16, W], F32)
    xx = P.sbuf("xx", [128, 16, NB], BF16)
    xm = [P.sbuf("xm%d" % i, [128, 16, NB], BF16) for i in range(3)]
    vmask = P.sbuf("vmask", [128, W], F32)
    ng = P.sbuf("ng", [128, 2, 16], F32)
    mu = P.sbuf("mu_sb", [128, 6, 16], F32)
    dv = P.sbuf("dv_sb", [128, 2, 4, 16], F32)
    rk = P.sbuf("rk_sb", [128, 16], F32)
    rmask = P.sbuf("rmask_sb", [128, NB], F32)
    bd = P.sbuf("bd_bf", [128, 128], BF16)
    wla = [P.sbuf("wla%d" % d, [128, 16, LW], BF16) for d in range(2)]
    ala = [P.sbuf("ala%d" % d, [128, 16, LW], BF16) for d in range(2)]
    gla = P.sbuf("gla", [128, 16, LG], BF16)
    wlb = [P.sbuf("wlb%d" % d, [LW, D], BF16) for d in range(2)]
    alb = [P.sbuf("alb%d" % d, [LW, D], BF16) for d in range(2)]
    glb = P.sbuf("glb", [128, 2, D], BF16)
    tw = [P.sbuf("tw%d" % d, [LW, NB], BF16) for d in range(2)]
    al = [P.sbuf("al%d" % d, [LW, NB], BF16) for d in range(2)]
    sgl = P.sbuf("sgl", [128, 2, NB], BF16)
    wt = [P.sbuf("wt%d" % i, [128, 16, 128], BF16) for i in range(6)]
    T2 = {}

    def tmp(name, dt=F32, n=NB):
        if name not in T2:
            T2[name] = P.sbuf("t_" + name, [128, n], dt)
        return T2[name]

    P.dma("sp", ng[:], normg, writes=["ng"])
    P.dma("sp", mu[:], mu_d, writes=["mu"])
    P.dma("sp", dv[:], dirvec, writes=["dv"])
    P.dma("sp", rk[:], rk_d, writes=["rk"])
    P.dma("sp", rmask[:], rmask_d, writes=["rmask"])
    P.op("pool", lambda e: e.memset(bd[:], 0.0), writes=["bd"])
    P.op("pool", lambda e: e.memset(bd[0:64, 0:64], 1.0), writes=["bd"])
    P.op("pool", lambda e: e.memset(bd[64:128, 64:128], 1.0), writes=["bd"])
    for d in range(2):
        P.dma("pool", wla[d][:], w_la[d].rearrange("(c p) n -> p c n", p=128), writes=[("wla", d)])
        P.dma("pool", ala[d][:], a_la[d].rearrange("(c p) n -> p c n", p=128), writes=[("ala", d)])
        P.dma("pool", wlb[d][:], w_lb[d], writes=[("wlb", d)])
        P.dma("pool", alb[d][:], a_lb[d], writes=[("alb", d)])
    P.dma("pool", gla[:], g_la.rearrange("(c p) n -> p c n", p=128), writes=["gla"])
    P.dma("pool", glb[:], g_lb.rearrange("(k p) n -> p k n", p=128), writes=["glb"])

    xkeys = [("xb", c) for c in range(16)]
    bank_rr = [0]

    def nb_():
        b_ = bank_rr[0] % 4 + 2
        bank_rr[0] += 1
        return cm.psb[b_], ("psb", b_)

    out_toks = []
    st = dict(wt=0, stg=0)

    def do_block(name, T, t0, n, mod, G1, pcs_all):
        d_ = dr[name]
        nw = n + 2
        nj = n // 128
        mk = ("mod", name)
        P.dma("sp", xb[:, :, :nw], d_["x"][:, :, t0:t0 + nw], writes=xkeys)
        P.dma("sp", vmask[:, :nw], d_["valid"][t0:t0 + nw].partition_broadcast(128), writes=["vmask"])
        cm.norm_mod(xb[:, :, :nw], nw, (G1, ("G1", name)), (mod[:, 0, :], mk),
                    lambda c: xb[:, c, :nw], xkeys, xkeys, stat_bank=0)
        for col in (0, nw - 1):
            P.op("dve", lambda e, col=col: e.tensor_tensor(
                out=xb[:, :, col:col + 1], in0=xb[:, :, col:col + 1],
                in1=vmask[:, col:col + 1].unsqueeze(1).broadcast_to([128, 16, 1]), op=ALU.mult),
                reads=xkeys + ["vmask"], writes=xkeys)
        for c in range(16):
            tq = tmp("xs%d" % (c % 2))
            P.op("pool", lambda e, c=c, tq=tq: e.tensor_tensor(out=tq[:, :n], in0=xb[:, c, 0:n], in1=xb[:, c, 2:n + 2], op=ALU.add),
                 reads=[xkeys[c]], writes=[("xs", c % 2)])
            P.op("dve", lambda e, c=c, tq=tq: e.scalar_tensor_tensor(out=xx[:, c, :n], in0=tq[:, :n], scalar=0.5, in1=xb[:, c, 1:n + 1],
                                                                     op0=ALU.mult, op1=ALU.subtract),
                 reads=[("xs", c % 2), xkeys[c]], writes=[("xx", c)])

        def mix(m, buf):
            for c in range(16):
                P.op("dve", lambda e, c=c: e.scalar_tensor_tensor(out=xm[buf][:, c, :n], in0=xx[:, c, :n], scalar=mu[:, m, c:c + 1],
                                                                  in1=xb[:, c, 1:n + 1], op0=ALU.mult, op1=ALU.add),
                     reads=[("xx", c), "mu", xkeys[c]], writes=[("xm", buf, c)])

        mix(1, 0)
        for d in range(2):
            bank, bk = nb_()
            for c in range(16):
                P.op("pe", lambda e, c=c, d=d, bank=bank: e.matmul(bank[0:LW, :n], lhsT=wla[d][:, c, :], rhs=xm[0][:, c, :n],
                                                                    start=(c == 0), stop=(c == 15)),
                     reads=[("wla", d), ("xm", 0, c)], writes=[bk])
            P.op("act", lambda e, d=d, bank=bank: e.activation(out=tw[d][:, :n], in_=bank[0:LW, :n], func=AF.Tanh),
                 reads=[bk], writes=[("tw", d)])
        mix(4, 1)
        for d in range(2):
            bank, bk = nb_()
            for c in range(16):
                P.op("pe", lambda e, c=c, d=d, bank=bank: e.matmul(bank[0:LW, :n], lhsT=ala[d][:, c, :], rhs=xm[1][:, c, :n],
                                                                    start=(c == 0), stop=(c == 15)),
                     reads=[("ala", d), ("xm", 1, c)], writes=[bk])
            P.op("act", lambda e, d=d, bank=bank: e.activation(out=al[d][:, :n], in_=bank[0:LW, :n], func=AF.Copy),
                 reads=[bk], writes=[("al", d)])
        mix(5, 2)
        for kc in range(2):
            bank, bk = nb_()
            for c in range(16):
                P.op("pe", lambda e, c=c, kc=kc, bank=bank: e.matmul(bank[:, :n], lhsT=gla[:, c, kc * 128:(kc + 1) * 128], rhs=xm[2][:, c, :n],
                                                                      start=(c == 0), stop=(c == 15)),
                     reads=["gla", ("xm", 2, c)], writes=[bk])
            P.op("act", lambda e, kc=kc, bank=bank: e.activation(out=sgl[:, kc, :n], in_=bank[:, :n], func=AF.Sigmoid),
                 reads=[bk], writes=[("sgl", kc)])
        mix(0, 0)
        mix(2, 1)
        mix(3, 2)
        def chain(c, d, rc_, kc_, vc_, cbank, cbk):
            sid = (2 * c + d) % NSET
            w0 = dv[:, d, 0, c:c + 1]
            a0 = dv[:, d, 1, c:c + 1]
            kk_ = dv[:, d, 2, c:c + 1]
            ka_ = dv[:, d, 3, c:c + 1]
            bank, bk = nb_()
            P.op("pe", lambda e, d=d, c=c, bank=bank: e.matmul(bank[:, :n], lhsT=wlb[d][:, c * 128:(c + 1) * 128], rhs=tw[d][:, :n], start=True, stop=True),
                 reads=[("wlb", d), ("tw", d)], writes=[bk])
            yield
            sg_ = tmp("sg_%d" % sid)
            P.op("act", lambda e, bank=bank, sg_=sg_, w0=w0: e.activation(out=sg_[:, :n], in_=bank[:, :n], func=AF.Sigmoid, bias=w0),
                 reads=[bk, "dv"], writes=[("sg", sid)])
            yield
            bank, bk = nb_()
            P.op("pe", lambda e, d=d, c=c, bank=bank: e.matmul(bank[:, :n], lhsT=alb[d][:, c * 128:(c + 1) * 128], rhs=al[d][:, :n], start=True, stop=True),
                 reads=[("alb", d), ("al", d)], writes=[bk])
            yield
            ag = tmp("ag_%d" % sid)
            P.op("act", lambda e, bank=bank, ag=ag, a0=a0: e.activation(out=ag[:, :n], in_=bank[:, :n], func=AF.Sigmoid, bias=a0),
                 reads=[bk, "dv"], writes=[("ag", sid)])
            yield
            sq = tmp("sqk_%d" % sid, BF16)
            P.op("act", lambda e, sq=sq, kc_=kc_, kk_=kk_: e.activation(out=sq[:, :n], in_=kc_[:, :n], func=AF.Square, scale=kk_),
                 reads=[("rkv", 1, c % 2), "dv"], writes=[("sqk", sid)])
            yield
            bank, bk = nb_()
            P.op("pe", lambda e, sq=sq, bank=bank: e.matmul(bank[:, :n], lhsT=bd[:], rhs=sq[:, :n], start=True, stop=True),
                 reads=["bd", ("sqk", sid)], writes=[bk])
            yield
            rn = tmp("rn_%d" % sid)
            P.op("act", lambda e, rn=rn, bank=bank: e.activation(out=rn[:, :n], in_=bank[:, :n], func=AF.Sqrt), reads=[bk], writes=[("rn", sid)])
            yield
            P.op("dve", lambda e, rn=rn: e.tensor_scalar(out=rn[:, :n], in0=rn[:, :n], scalar1=1e-12, scalar2=None, op0=ALU.max),
                 reads=[("rn", sid)], writes=[("rn", sid)])
            yield
            P.op("dve", lambda e, rn=rn: e.reciprocal(out=rn[:, :n], in_=rn[:, :n]), reads=[("rn", sid)], writes=[("rn", sid)])
            yield
            kkn = tmp("kkn_%d" % sid)
            P.op("dve", lambda e, kkn=kkn, kc_=kc_, rn=rn, kk_=kk_: e.scalar_tensor_tensor(out=kkn[:, :n], in0=kc_[:, :n], scalar=kk_, in1=rn[:, :n],
                                                                                   op0=ALU.mult, op1=ALU.mult),
                 reads=[("rkv", 1, c % 2), ("rn", sid), "dv"], writes=[("kkn", sid)])
            yield
            t1 = tmp("t1_%d" % sid)
            P.op("pool", lambda e, t1=t1, ag=ag, ka_=ka_: e.tensor_scalar(out=t1[:, :n], in0=ag[:, :n], scalar1=-1.0, scalar2=ka_, op0=ALU.add, op1=ALU.mult),
                 reads=[("ag", sid), "dv"], writes=[("t1", sid)])
            yield
            kd = tmp("kd_%d" % sid)
            P.op("dve", lambda e, kd=kd, t1=t1, kc_=kc_: e.scalar_tensor_tensor(out=kd[:, :n], in0=t1[:, :n], scalar=1.0, in1=kc_[:, :n],
                                                                            op0=ALU.add, op1=ALU.mult),
                 reads=[("t1", sid), ("rkv", 1, c % 2)], writes=[("kd", sid)])
            yield
            bs = tmp("bs_%d" % sid)
            P.op("pool", lambda e, bs=bs, kkn=kkn, ag=ag: e.tensor_tensor(out=bs[:, :n], in0=kkn[:, :n], in1=ag[:, :n], op=ALU.mult),
                 reads=[("kkn", sid), ("ag", sid)], writes=[("bs", sid)])
            yield
            Lf = tmp("Lf_%d" % sid)
            P.op("dve", lambda e, Lf=Lf, sg_=sg_: e.tensor_tensor_scan(out=Lf[:, :n], data0=rmask[:, :n], data1=sg_[:, :n], initial=0.0,
                                                                      op0=ALU.mult, op1=ALU.add),
                 reads=["rmask", ("sg", sid)], writes=[("Lf", sid)])
            yield
            Li = tmp("Li_%d" % sid)
            Lx = tmp("Lx_%d" % sid)
            Lf3 = Lf[:, :n].rearrange("p (j t) -> p j t", t=128)
            tot = Lf3[:, :, 127:128].broadcast_to([128, nj, 128])
            if d == 0:
                P.op("pool", lambda e, Lx=Lx, Lf=Lf, sg_=sg_: e.tensor_tensor(out=Lx[:, :n], in0=Lf[:, :n], in1=sg_[:, :n], op=ALU.subtract),
                     reads=[("Lf", sid), ("sg", sid)], writes=[("Lx", sid)])
                yield
                Li = Lf
                lik = ("Lf", sid)
            else:
                P.op("pool", lambda e, Lx=Lx, Lf3=Lf3, tot=tot: e.tensor_tensor(out=Lx[:, :n].rearrange("p (j t) -> p j t", t=128), in0=tot, in1=Lf3,
                                                                                op=ALU.subtract),
                     reads=[("Lf", sid)], writes=[("Lx", sid)])
                yield
                P.op("pool", lambda e, Li=Li, Lx=Lx, sg_=sg_: e.tensor_tensor(out=Li[:, :n], in0=Lx[:, :n], in1=sg_[:, :n], op=ALU.add),
                     reads=[("Lx", sid), ("sg", sid)], writes=[("Li", sid)])
                yield
                lik = ("Li", sid)
            ep = tmp("ep_%d" % sid)
            en = tmp("en_%d" % sid)
            ex = tmp("ex_%d" % sid)
            P.op("act", lambda e, ep=ep, Li=Li: e.activation(out=ep[:, :n], in_=Li[:, :n], func=AF.Exp, scale=-C0), reads=[lik], writes=[("ep", sid)])
            yield
            P.op("act", lambda e, en=en, Li=Li: e.activation(out=en[:, :n], in_=Li[:, :n], func=AF.Exp, scale=C0), reads=[lik], writes=[("en", sid)])
            yield
            P.op("act", lambda e, ex=ex, Lx=Lx: e.activation(out=ex[:, :n], in_=Lx[:, :n], func=AF.Exp, scale=-C0), reads=[("Lx", sid)], writes=[("ex", sid)])
            yield
            P.op("act", lambda e, d=d, c=c, Lf3=Lf3: e.activation(out=pcs_all[:, d, c, t0 // 128:t0 // 128 + nj], in_=Lf3[:, :, 127], func=AF.Exp, scale=-C0),
                 reads=[("Lf", sid)], writes=[("pcs", name)])
            yield
            stg = tmp("stg%d" % sid, BF16, 4 * NB)
            sk = ("stg", sid)
            P.op("dve", lambda e, stg=stg, kkn=kkn, ex=ex: e.scalar_tensor_tensor(out=stg[:, 0:n], in0=kkn[:, :n], scalar=-1.0, in1=ex[:, :n],
                                                                              op0=ALU.mult, op1=ALU.mult),
                 reads=[("kkn", sid), ("ex", sid)], writes=[sk])
            yield
            P.op("pool", lambda e, stg=stg, bs=bs, en=en: e.tensor_tensor(out=stg[:, NB:NB + n], in0=bs[:, :n], in1=en[:, :n], op=ALU.mult),
                 reads=[("bs", sid), ("en", sid)], writes=[sk])
            yield
            P.op("pool", lambda e, stg=stg, kd=kd, en=en: e.tensor_tensor(out=stg[:, 2 * NB:2 * NB + n], in0=kd[:, :n], in1=en[:, :n], op=ALU.mult),
                 reads=[("kd", sid), ("en", sid)], writes=[sk])
            yield
            P.op("pool", lambda e, stg=stg, rc_=rc_, ep=ep: e.tensor_tensor(out=stg[:, 3 * NB:3 * NB + n], in0=rc_[:, :n], in1=ep[:, :n], op=ALU.mult),
                 reads=[("rkv", 0, c % 2), ("ep", sid)], writes=[sk])
            yield
            out_toks.append(P.dma("pool", d_["ot"][d][:, c, :, t0:t0 + n].rearrange("o p n -> p o n"),
                                  stg[:].rearrange("p (o n) -> p o n", o=4)[:, :, :n], reads=[sk], writes=[("oo", name, d, c, t0)]))
            yield
            rkq = tmp("rkq_%d" % sid, BF16)
            P.op("dve", lambda e, rkq=rkq, rc_=rc_, kd=kd, c=c: e.scalar_tensor_tensor(out=rkq[:, :n], in0=rc_[:, :n], scalar=rk[:, c:c + 1], in1=kd[:, :n],
                                                                                   op0=ALU.mult, op1=ALU.mult),
                 reads=[("rkv", 0, c % 2), ("kd", sid), "rk"], writes=[("rkq", sid)])
            yield
            P.op("pe", lambda e, rkq=rkq, d=d, cbank=cbank: e.matmul(cbank[:, :n], lhsT=bd[:], rhs=rkq[:, :n], start=(d == 0), stop=(d == 1)),
                 reads=["bd", ("rkq", sid)], writes=[cbk])
            yield


        def prologue(c):
            outs = []
            for wi in range(3):
                s = st["wt"] % 6
                st["wt"] += 1
                P.dma("sp", wt[s][:], w_rkv[wi, c], writes=[("wt", s)])
                bank, bk = nb_()
                for kc in range(16):
                    P.op("pe", lambda e, kc=kc, s=s, wi=wi, bank=bank: e.matmul(bank[:, :n], lhsT=wt[s][:, kc, :], rhs=xm[wi][:, kc, :n],
                                                                                 start=(kc == 0), stop=(kc == 15)),
                         reads=[("wt", s), ("xm", wi, kc)], writes=[bk])
                dst = tmp(("rc", "kc", "vc")[wi] + str(c % 2))
                P.op("act", lambda e, dst=dst, bank=bank: e.activation(out=dst[:, :n], in_=bank[:, :n], func=AF.Copy),
                     reads=[bk], writes=[("rkv", wi, c % 2)])
                outs.append(dst)
            rc_, kc_, vc_ = outs
            vst = tmp("vst%d" % (c % 2), BF16)
            P.op("pool", lambda e, vst=vst, vc_=vc_: e.tensor_copy(out=vst[:, :n], in_=vc_[:, :n]), reads=[("rkv", 2, c % 2)], writes=[("vst", c % 2)])
            out_toks.append(P.dma("pool", d_["v"][c][:, t0:t0 + n], vst[:, :n], reads=[("vst", c % 2)], writes=[("ov", name, c, t0)]))
            bank, bk = nb_()
            for kc in range(2):
                P.op("pe", lambda e, kc=kc, c=c, bank=bank: e.matmul(bank[:, :n], lhsT=glb[:, kc, c * 128:(c + 1) * 128], rhs=sgl[:, kc, :n],
                                                                      start=(kc == 0), stop=(kc == 1)),
                     reads=["glb", ("sgl", 0), ("sgl", 1)], writes=[bk])
            gst = tmp("gst%d" % (c % 2))
            P.op("act", lambda e, gst=gst, bank=bank: e.activation(out=gst[:, :n], in_=bank[:, :n], func=AF.Copy), reads=[bk], writes=[("gst", c % 2)])
            out_toks.append(P.dma("act", d_["g"][c][:, t0:t0 + n], gst[:, :n], reads=[("gst", c % 2)], writes=[("og", name, c, t0)]))
            cb_ = 6 + (c % 2)
            cbank, cbk = cm.psb[cb_], ("psb", cb_)
            return rc_, kc_, vc_, cbank, cbk

        def epilogue(c, rc_, kc_, vc_, cbank, cbk):
            bst = tmp("bst%d" % (c % 2))
            P.op("dve", lambda e, bst=bst, vc_=vc_, cbank=cbank: e.tensor_tensor(out=bst[:, :n], in0=cbank[:, :n], in1=vc_[:, :n], op=ALU.mult),
                 reads=[cbk, ("rkv", 2, c % 2)], writes=[("bst", c % 2)])
            out_toks.append(P.dma("act", d_["bonus"][c][:, t0:t0 + n], bst[:, :n], reads=[("bst", c % 2)], writes=[("ob", name, c, t0)]))

        for cp in range(0, 16, 2):
            units = []
            for c in (cp, cp + 1):
                units.append((c,) + prologue(c))
            gens = [chain(u[0], d, *u[1:]) for u in units for d in range(2)]
            while gens:
                alive = []
                for g_ in gens:
                    try:
                        next(g_)
                        alive.append(g_)
                    except StopIteration:
                        pass
                gens = alive
            for u in units:
                epilogue(*u)

    for name, T in segs:
        mod = P.sbuf("modsb_" + name, [128, 6, 16], F32)
        G1 = P.sbuf("G1_" + name, [128, 16], F32)
        mk = ("mod", name)
        P.dma("sp", mod[:], dr[name]["mod"], writes=[mk])
        P.op("dve", lambda e, G1=G1, mod=mod: e.scalar_tensor_tensor(
            out=G1[:], in0=mod[:, 1, :], scalar=1.0, in1=ng[:, 0, :], op0=ALU.add, op1=ALU.mult),
            reads=[mk, "ng"], writes=[("G1", name)])
        pcs_all = P.sbuf("pcs_" + name, [128, 2, 16, T // 128], F32)
        for t0 in range(0, T, NB):
            do_block(name, T, t0, min(NB, T - t0), mod, G1, pcs_all)
        out_toks.append(P.dma("sp", dr[name]["pc"], pcs_all[:], reads=[("pcs", name)], writes=[("opc", name)]))
    P.final_wait("sp", out_toks)
    return P.build()


NCH = 66
NFC = 8
OPA, OPB, OPK, OPR = 0, 1, 2, 3


def build_r2(nch=NCH):
    nc = bass.Bass("TRN2", target_bir_lowering=False)
    fm = nc.dram_tensor("fm", [4, nch, 128, NFC, 128], BF16, kind="ExternalInput").ap()
    tmj = nc.dram_tensor("tm", [3, nch, 128, NFC, 128], BF16, kind="ExternalInput").ap()
    pcd = nc.dram_tensor("pc", [nch, 128, NFC], F32, kind="ExternalInput").ap()
    m4d = nc.dram_tensor("m4", [128, 2, 512], F32, kind="ExternalInput").ap()
    mld = nc.dram_tensor("ml", [128, 512], F32, kind="ExternalInput").ap()
    idd = nc.dram_tensor("idm", [128, 2, 128], F32, kind="ExternalInput").ap()
    mbd = nc.dram_tensor("mbd", [128, 512], F32, kind="ExternalInput").ap()
    yout = nc.dram_tensor("y", [nch, 128, NFC, 128], F32, kind="ExternalOutput").ap()

    P = Prog(nc)
    psb = [P.psum("psb%d" % i, [128, 512]) for i in range(8)]
    m4 = P.sbuf("m4s", [128, 2, 512], F32)
    ml = P.sbuf("mls", [128, 512], F32)
    idm = P.sbuf("ids", [128, 2, 128], F32)
    mbd4 = P.sbuf("mbds", [128, 512], F32)
    P.dma("sp", m4[:], m4d, writes=["m4"])
    P.dma("sp", ml[:], mld, writes=["ml"])
    P.dma("sp", idm[:], idd, writes=["idm"])
    P.dma("sp", mbd4[:], mbd, writes=["mbd4"])
    slots = []
    for s in range(2):
        d = dict(
            fa=P.sbuf("fa%d" % s, [128, NFC, 128], BF16), fb=P.sbuf("fb%d" % s, [128, NFC, 128], BF16),
            fr=P.sbuf("fr%d" % s, [128, NFC, 128], BF16),
            pa=[P.sbuf("pa%d_%d" % (s, h), [128, NFC, 128], BF16) for h in range(2)],
            pb=[P.sbuf("pb%d_%d" % (s, h), [128, NFC, 128], BF16) for h in range(2)],
            pk=[P.sbuf("pk%d_%d" % (s, h), [128, NFC, 128], BF16) for h in range(2)],
            tB=P.sbuf("tB%d" % s, [128, NFC, 128], BF16), tK=P.sbuf("tK%d" % s, [128, NFC, 128], BF16),
            tV=P.sbuf("tV%d" % s, [128, NFC, 128], BF16), pc=P.sbuf("pcs%d" % s, [128, NFC], F32),
        )
        for nm in ("pa", "pb", "pk"):
            for h in range(2):
                P.op("pool", lambda e, t=d[nm][h]: e.memset(t[:], 0.0), writes=[(nm, s, h)])
        slots.append(d)
    Hf = P.sbuf("Hf", [128, NFC, 128], F32)
    Hb = P.sbuf("Hb", [128, NFC, 128], BF16)
    P.op("dve", lambda e: e.memset(Hf[:], 0.0), writes=[("Hf", f) for f in range(NFC // 4)])
    P.op("dve", lambda e: e.memset(Hb[:], 0.0), writes=[("Hb", f) for f in range(NFC // 4)])
    A4p = [[P.sbuf("A4_%d_%d" % (f, par), [128, 2, 512], BF16) for f in range(NFC)] for par in range(2)]
    NN = [P.sbuf("NN_%d" % f, [128, 4, 128], F32) for f in range(NFC)]
    X = [[P.sbuf("X%d_%d" % (f, i), [128, 512], F32) for i in range(2)] for f in range(NFC // 2)]
    XT = [[P.sbuf("XT%d_%d" % (f, i), [128, 512], F32) for i in range(2)] for f in range(NFC // 2)]
    TTf = [[P.sbuf("TTf%d_%d" % (f, i), [128, 2, 128], F32) for i in range(2)] for f in range(NFC)]
    TTb = [[P.sbuf("TTb%d_%d" % (f, par), [128, 2, 128], BF16) for f in range(NFC)] for par in range(2)]
    Wsb = [P.sbuf("W%d" % f, [128, 512], BF16) for f in range(NFC // 4)]
    Usb = [P.sbuf("U%d" % f, [128, 512], BF16) for f in range(NFC // 4)]
    Ht = [P.sbuf("Ht%d" % i, [128, 512], F32) for i in range(NFC // 4)]
    yst = [P.sbuf("yst%d" % i, [128, NFC, 128], F32) for i in range(2)]
    out_toks = []

    def load(t):
        s = t % 2
        d = slots[s]
        P.dma("sp", d["fa"][:], fm[OPA, t], writes=[("fa", s)])
        P.dma("sp", d["fb"][:], fm[OPB, t], writes=[("fb", s)])
        P.dma("sp", d["fr"][:], fm[OPR, t], writes=[("fr", s)])
        for h in range(2):
            hp = slice(64 * h, 64 * h + 64)
            P.dma("sp", d["pa"][h][hp, :, :], fm[OPA, t, hp], writes=[("pa", s, h)])
            P.dma("sp", d["pb"][h][hp, :, :], fm[OPB, t, hp], writes=[("pb", s, h)])
            P.dma("sp", d["pk"][h][hp, :, :], fm[OPK, t, hp], writes=[("pk", s, h)])
        P.dma("sp", d["tB"][:], tmj[0, t], writes=[("tB", s)])
        P.dma("sp", d["tK"][:], tmj[1, t], writes=[("tK", s)])
        P.dma("sp", d["tV"][:], tmj[2, t], writes=[("tV", s)])
        P.dma("sp", d["pc"][:], pcd[t], writes=[("pc", s)])

    bank_rr = [0]

    def nb():
        b_ = bank_rr[0] % 8
        bank_rr[0] += 1
        return psb[b_], ("psb", b_)

    def stage_A(t):
        par = t % 2
        A4 = A4p[par]
        s = t % 2
        d = slots[s]
        for f in range(NFC):
            for h in range(2):
                bank, bk = nb()
                specs = [(d["pb"][h], ("pb", s, h), d["fa"], ("fa", s)),
                         (d["pk"][h], ("pk", s, h), d["fa"], ("fa", s)),
                         (d["pb"][h], ("pb", s, h), d["fr"], ("fr", s)),
                         (d["pk"][h], ("pk", s, h), d["fr"], ("fr", s))]
                for q, (lt, lk, rt, rk) in enumerate(specs):
                    P.op("pe", lambda e, bank=bank, q=q, lt=lt, rt=rt, f=f: e.matmul(
                        bank[:, q * 128:(q + 1) * 128], lhsT=lt[:, f, :], rhs=rt[:, f, :], start=True, stop=True),
                        reads=[lk, rk], writes=[bk])
                P.op("dve", lambda e, bank=bank, f=f, h=h: e.tensor_tensor(out=A4[f][:, h, :], in0=bank[:, :], in1=m4[:, h, :], op=ALU.mult),
                     reads=[bk, "m4"], writes=[("A4", par, f, h)])
            bank, bk = nb()
            for h in range(2):
                P.op("pe", lambda e, h=h, f=f, bank=bank: e.matmul(
                    bank[:, h * 128:(h + 1) * 128], lhsT=d["pb"][h][:, f, :], rhs=d["fa"][:, f, :], start=True, stop=True),
                    reads=[("pb", s, h), ("fa", s)], writes=[bk])
            for h in range(2):
                P.op("pe", lambda e, h=h, f=f, bank=bank: e.matmul(
                    bank[:, 256 + h * 128:256 + (h + 1) * 128], lhsT=d["pa"][h][:, f, :], rhs=d["fb"][:, f, :], start=True, stop=True),
                    reads=[("pa", s, h), ("fb", s)], writes=[bk])
            P.op("dve", lambda e, f=f, bank=bank: e.tensor_tensor(out=NN[f][:].rearrange("p q c -> p (q c)"), in0=bank[:, :], in1=ml[:], op=ALU.mult),
                 reads=[bk, "ml"], writes=[("NN", f)])
            P.op("pool", lambda e, f=f: e.tensor_tensor(out=TTf[f][0][:], in0=NN[f][:, 0:2, :], in1=idm[:], op=ALU.add),
                 reads=[("NN", f), "idm"], writes=[("TTf", f, 0)])

    def stage_D(t):
        par = t % 2
        for lvl in range(1, 7):
            pi, po = (lvl - 1) % 2, lvl % 2
            for fp in range(NFC // 2):
                def xin(ff, h, fp=fp, pi=pi, lvl=lvl):
                    if lvl == 1:
                        return NN[2 * fp + ff][:, 2 + h, :]
                    return X[fp][pi][:, ff * 256 + h * 128: ff * 256 + (h + 1) * 128]

                def xtin(ff, h, fp=fp, pi=pi, lvl=lvl):
                    if lvl == 1:
                        return NN[2 * fp + ff][:, h, :]
                    return XT[fp][pi][:, ff * 256 + h * 128: ff * 256 + (h + 1) * 128]
                if lvl == 1:
                    xk = [("NN", 2 * fp), ("NN", 2 * fp + 1)]
                    xtk = []
                else:
                    xk = [("X", fp, pi)]
                    xtk = [("XT", fp, pi)]
                bank, bk = nb()
                for ff in range(2):
                    for h in range(2):
                        P.op("pe", lambda e, h=h, ff=ff, xin=xin, xtin=xtin, bank=bank: e.matmul(
                            bank[:, ff * 256 + h * 128: ff * 256 + (h + 1) * 128], lhsT=xtin(ff, h), rhs=xin(ff, h), start=True, stop=True),
                            reads=xk + xtk, writes=[bk])
                P.op("act", lambda e, fp=fp, po=po, bank=bank: e.activation(out=X[fp][po][:], in_=bank[:, :], func=AF.Copy),
                     reads=[bk], writes=[("X", fp, po)])
                if lvl < 6:
                    bank2, bk2 = nb()
                    for ff in range(2):
                        for h in range(2):
                            P.op("pe", lambda e, h=h, ff=ff, xin=xin, xtin=xtin, bank2=bank2: e.matmul(
                                bank2[:, ff * 256 + h * 128: ff * 256 + (h + 1) * 128], lhsT=xin(ff, h), rhs=xtin(ff, h), start=True, stop=True),
                                reads=xk + xtk, writes=[bk2])
                    P.op("act", lambda e, fp=fp, po=po, bank2=bank2: e.activation(out=XT[fp][po][:], in_=bank2[:, :], func=AF.Copy),
                         reads=[bk2], writes=[("XT", fp, po)])
            for fp in range(NFC // 2):
                bank, bk = nb()
                for ff in range(2):
                    f = 2 * fp + ff
                    for h in range(2):
                        P.op("pe", lambda e, h=h, f=f, ff=ff, fp=fp, po=po, pi=pi, bank=bank: e.matmul(
                            bank[:, ff * 256 + h * 128: ff * 256 + (h + 1) * 128],
                            lhsT=X[fp][po][:, ff * 256 + h * 128: ff * 256 + (h + 1) * 128], rhs=TTf[f][pi][:, h, :], start=True, stop=True),
                            reads=[("X", fp, po), ("TTf", f, pi)], writes=[bk])
                for ff in range(2):
                    f = 2 * fp + ff
                    if lvl < 6:
                        P.op("dve", lambda e, f=f, ff=ff, po=po, pi=pi, bank=bank: e.tensor_tensor(
                            out=TTf[f][po][:].rearrange("p h c -> p (h c)"), in0=bank[:, ff * 256:(ff + 1) * 256],
                            in1=TTf[f][pi][:].rearrange("p h c -> p (h c)"), op=ALU.add),
                            reads=[bk, ("TTf", f, pi)], writes=[("TTf", f, po)])
                    else:
                        P.op("dve", lambda e, f=f, ff=ff, po=po, pi=pi, bank=bank: e.tensor_tensor(
                            out=TTb[par][f][:].rearrange("p h c -> p (h c)"), in0=bank[:, ff * 256:(ff + 1) * 256],
                            in1=TTf[f][pi][:].rearrange("p h c -> p (h c)"), op=ALU.add),
                            reads=[bk, ("TTf", f, pi)], writes=[("TTb", par, f)])

    def stage_S(t):
        par = t % 2
        A4 = A4p[par]
        s = t % 2
        d = slots[s]
        ys = yst[t % 2]
        NG = NFC // 4
        for g in range(NG):
            bank, bk = nb()
            for fi in range(4):
                f = 4 * g + fi
                off = fi * 128
                P.op("pe", lambda e, f=f, off=off, bank=bank: e.matmul(bank[:, off:off + 128], lhsT=d["fa"][:, f, :], rhs=Hb[:, f, :], start=True, stop=False),
                     reads=[("fa", s), ("Hb", g)], writes=[bk])
                for h in range(2):
                    P.op("pe", lambda e, f=f, h=h, off=off, bank=bank: e.matmul(bank[:, off + 64 * h: off + 64 * h + 64], lhsT=A4[f][:, h, 128:256],
                                                                                 rhs=d["tV"][:, f, 64 * h:64 * h + 64], start=False, stop=(h == 1)),
                         reads=[("A4", par, f, h), ("tV", s)], writes=[bk])
            P.op("act", lambda e, g=g, bank=bank: e.activation(out=Wsb[g][:], in_=bank[:, :], func=AF.Copy),
                 reads=[bk], writes=[("W", g)])
        for g in range(NG):
            bank, bk = nb()
            for fi in range(4):
                f = 4 * g + fi
                off = fi * 128
                for h in range(2):
                    P.op("pe", lambda e, f=f, g=g, h=h, off=off, bank=bank: e.matmul(bank[:, off + 64 * h: off + 64 * h + 64], lhsT=TTb[par][f][:, h, :],
                                                                                      rhs=Wsb[g][:, off + 64 * h: off + 64 * h + 64], start=True, stop=True),
                         reads=[("TTb", par, f), ("W", g)], writes=[bk])
            P.op("act", lambda e, g=g, bank=bank: e.activation(out=Usb[g][:], in_=bank[:, :], func=AF.Copy),
                 reads=[bk], writes=[("U", g)])
        for g in range(NG):
            bank, bk = nb()
            for fi in range(4):
                f = 4 * g + fi
                off = fi * 128
                P.op("pe", lambda e, f=f, off=off, bank=bank: e.matmul(bank[:, off:off + 128], lhsT=d["fr"][:, f, :], rhs=Hb[:, f, :], start=True, stop=False),
                     reads=[("fr", s), ("Hb", g)], writes=[bk])
                for h in range(2):
                    P.op("pe", lambda e, f=f, g=g, h=h, off=off, bank=bank: e.matmul(bank[:, off + 64 * h: off + 64 * h + 64], lhsT=A4[f][:, h, 256:384],
                                                                                      rhs=Usb[g][:, off + 64 * h: off + 64 * h + 64], start=False, stop=False),
                         reads=[("A4", par, f, h), ("U", g)], writes=[bk])
                    P.op("pe", lambda e, f=f, h=h, off=off, bank=bank: e.matmul(bank[:, off + 64 * h: off + 64 * h + 64], lhsT=A4[f][:, h, 384:512],
                                                                                 rhs=d["tV"][:, f, 64 * h:64 * h + 64], start=False, stop=(h == 1)),
                         reads=[("A4", par, f, h), ("tV", s)], writes=[bk])
            P.op("act", lambda e, g=g, bank=bank, ys=ys: e.activation(out=ys[:, 4 * g:4 * g + 4, :].rearrange("p f c -> p (f c)"), in_=bank[:, :], func=AF.Copy),
                 reads=[bk], writes=[("yst", t % 2, g)])
        for g in range(NG):
            bank, bk = nb()
            for fi in range(4):
                f = 4 * g + fi
                off = fi * 128
                P.op("pe", lambda e, f=f, g=g, off=off, bank=bank: e.matmul(bank[:, off:off + 128], lhsT=d["tB"][:, f, :], rhs=Usb[g][:, off:off + 128], start=True, stop=False),
                     reads=[("tB", s), ("U", g)], writes=[bk])
                P.op("pe", lambda e, f=f, off=off, bank=bank: e.matmul(bank[:, off:off + 128], lhsT=d["tK"][:, f, :], rhs=d["tV"][:, f, :], start=False, stop=True),
                     reads=[("tK", s), ("tV", s)], writes=[bk])
            hf_g = Hf[:, 4 * g:4 * g + 4, :]
            P.op("dve", lambda e, g=g, bank=bank: e.tensor_tensor(out=Ht[g][:], in0=bank[:, :], in1=mbd4[:], op=ALU.mult),
                 reads=[bk, "mbd4"], writes=[("Ht", g)])
            P.op("pool", lambda e, g=g, hf_g=hf_g: e.tensor_tensor(out=hf_g, in0=Ht[g][:].rearrange("p (f c) -> p f c", f=4), in1=hf_g, op=ALU.add),
                 reads=[("Ht", g), ("Hf", g)], writes=[("Hf", g)])
            P.op("pool", lambda e, g=g, hf_g=hf_g: e.tensor_tensor(out=hf_g, in0=hf_g, in1=d["pc"][:, 4 * g:4 * g + 4].unsqueeze(2).broadcast_to([128, 4, 128]), op=ALU.mult),
                 reads=[("Hf", g), ("pc", s)], writes=[("Hf", g)])
            P.op("act", lambda e, g=g, hf_g=hf_g: e.activation(out=Hb[:, 4 * g:4 * g + 4, :], in_=hf_g, func=AF.Copy),
                 reads=[("Hf", g)], writes=[("Hb", g)])
        out_toks.append(P.dma("sp", yout[t], ys[:], reads=[("yst", t % 2, g) for g in range(NG)], writes=[("yo", t)]))

    load(0)
    stage_A(0)
    stage_D(0)
    for t in range(nch):
        if t + 1 < nch:
            load(t + 1)
            stage_A(t + 1)
            stage_D(t + 1)
        stage_S(t)
    P.final_wait("sp", out_toks)
    return P.build()


def r2_consts():
    i = np.arange(128)[:, None]
    c = np.arange(128)[None, :]
    su = (c > i).astype(np.float32)
    ue = (c >= i).astype(np.float32)
    m4h = np.concatenate([su, su, ue, ue], 1)
    m4 = np.stack([m4h, m4h], 1)
    sl = (c < i).astype(np.float32)
    ml = np.concatenate([su, su, sl, sl], 1)
    idm = np.stack([np.eye(128, dtype=np.float32)] * 2, 1)
    bd = np.zeros((128, 128), np.float32)
    bd[:64, :64] = 1
    bd[64:, 64:] = 1
    bd = np.concatenate([bd] * 4, 1)
    return dict(m4=np.ascontiguousarray(m4), ml=np.ascontiguousarray(ml), idm=np.ascontiguousarray(idm), mbd=bd)


LN_X_EPS = 64e-5


def build_r3(segs=(("lat", 2048), ("ctx", 64)), TB=512):
    nc = bass.Bass("TRN2", target_bir_lowering=False)
    dr = {}
    for name, T in segs:
        dr[name] = dict(
            x=nc.dram_tensor("x_" + name, [128, 16, T], F32, kind="ExternalInput").ap(),
            y=nc.dram_tensor("y_" + name, [2, 16, 128, T], F32, kind="ExternalInput").ap(),
            bonus=nc.dram_tensor("bonus_" + name, [16, 128, T], F32, kind="ExternalInput").ap(),
            g=nc.dram_tensor("g_" + name, [16, 128, T], F32, kind="ExternalInput").ap(),
            mod=nc.dram_tensor("mod_" + name, [128, 6, 16], F32, kind="ExternalInput").ap(),
            out=nc.dram_tensor("out_" + name, [128, 16, T], F32, kind="ExternalOutput").ap(),
        )
    normg = nc.dram_tensor("normg", [128, 2, 16], F32, kind="ExternalInput").ap()
    lnx_d = nc.dram_tensor("lnx", [128, 2, 16], F32, kind="ExternalInput").ap()
    w_o = nc.dram_tensor("w_o", [16, 128, 16, 128], BF16, kind="ExternalInput").ap()
    w_in = nc.dram_tensor("w_in", [NJ, 2, 128, 16, 128], BF16, kind="ExternalInput").ap()
    w_out = nc.dram_tensor("w_out", [16, 128, NJ, 128], BF16, kind="ExternalInput").ap()

    P = Prog(nc)
    cm = Common(P, TB, halo=0)
    cm.setup_eps()
    ffn = FFN(P, cm, w_in, w_out, TB, nsplit=2, WC=128)
    xb = P.sbuf("xb", [128, 16, TB], F32)
    hT = P.sbuf("hT", [128, 16, TB], BF16)
    ng = P.sbuf("ng", [128, 2, 16], F32)
    lnx = P.sbuf("lnx_sb", [128, 2, 16], F32)
    bd64 = P.sbuf("bd64", [128, 128], F32)
    epsl = P.sbuf("epsl", [128, 1], F32)
    inb = [dict(y0=P.sbuf("y0_%d" % i, [128, TB], F32), y1=P.sbuf("y1_%d" % i, [128, TB], F32),
                bo=P.sbuf("bo_%d" % i, [128, TB], F32), g=P.sbuf("g_%d" % i, [128, TB], F32)) for i in range(2)]
    ysum = P.sbuf("ysum", [128, TB], F32)
    cen = P.sbuf("cen", [128, TB], F32)
    sq = P.sbuf("sq", [128, TB], F32)
    rstd = P.sbuf("rstd", [128, TB], F32)
    yn = P.sbuf("yn", [128, TB], F32)
    oo = P.sbuf("oo", [128, TB], F32)
    P.dma("sp", ng[:], normg, writes=["ng"])
    P.dma("sp", lnx[:], lnx_d, writes=["lnx"])
    P.op("pool", lambda e: e.memset(bd64[:], 0.0), writes=["bd64"])
    P.op("pool", lambda e: e.memset(bd64[0:64, 0:64], 1.0 / 64), writes=["bd64"])
    P.op("pool", lambda e: e.memset(bd64[64:128, 64:128], 1.0 / 64), writes=["bd64"])
    P.op("pool", lambda e: e.memset(epsl[:], LN_X_EPS), writes=["epsl"])
    xkeys = [("xb", c) for c in range(16)]
    hkeys = [("hT", c) for c in range(16)]
    out_toks = []
    st = dict(i=0)

    def do_block(name, t0, n, mod, G2):
        d_ = dr[name]
        mk = ("mod", name)
        P.dma("sp", xb[:, :, :n], d_["x"][:, :, t0:t0 + n], writes=xkeys)
        for c in range(16):
            s = st["i"] % 2
            st["i"] += 1
            ib = inb[s]
            P.dma("sp", ib["y0"][:, :n], d_["y"][0, c][:, t0:t0 + n], writes=[("y0", s)])
            P.dma("sp", ib["y1"][:, :n], d_["y"][1, c][:, t0:t0 + n], writes=[("y1", s)])
            P.dma("sp", ib["bo"][:, :n], d_["bonus"][c][:, t0:t0 + n], writes=[("bo", s)])
            P.dma("sp", ib["g"][:, :n], d_["g"][c][:, t0:t0 + n], writes=[("g", s)])
            P.op("pool", lambda e, ib=ib: e.tensor_tensor(out=ysum[:, :n], in0=ib["y0"][:, :n], in1=ib["y1"][:, :n], op=ALU.add),
                 reads=[("y0", s), ("y1", s)], writes=["ysum"])
            P.op("pe", lambda e: e.matmul(cm.psb[2][:, :n], lhsT=bd64[:], rhs=ysum[:, :n], start=True, stop=True),
                 reads=["bd64", "ysum"], writes=[("psb", 2)])
            P.op("dve", lambda e: e.tensor_tensor(out=cen[:, :n], in0=ysum[:, :n], in1=cm.psb[2][:, :n], op=ALU.subtract),
                 reads=["ysum", ("psb", 2)], writes=["cen"])
            P.op("act", lambda e: e.activation(out=sq[:, :n], in_=cen[:, :n], func=AF.Square), reads=["cen"], writes=["sq"])
            P.op("pe", lambda e: e.matmul(cm.psb[3][:, :n], lhsT=bd64[:], rhs=sq[:, :n], start=True, stop=True),
                 reads=["bd64", "sq"], writes=[("psb", 3)])
            P.op("act", lambda e: e.activation(out=rstd[:, :n], in_=cm.psb[3][:, :n], func=AF.Sqrt, bias=epsl[:, 0:1]),
                 reads=[("psb", 3), "epsl"], writes=["rstd"])
            P.op("dve", lambda e: e.reciprocal(out=rstd[:, :n], in_=rstd[:, :n]), reads=["rstd"], writes=["rstd"])
            P.op("dve", lambda e: e.tensor_tensor(out=yn[:, :n], in0=cen[:, :n], in1=rstd[:, :n], op=ALU.mult),
                 reads=["cen", "rstd"], writes=["yn"])
            P.op("act", lambda e, c=c: e.activation(out=oo[:, :n], in_=yn[:, :n], func=AF.Identity, scale=lnx[:, 0, c:c + 1], bias=lnx[:, 1, c:c + 1]),
                 reads=["yn", "lnx"], writes=["oo"])
            P.op("pool", lambda e, ib=ib: e.tensor_tensor(out=oo[:, :n], in0=oo[:, :n], in1=ib["bo"][:, :n], op=ALU.add),
                 reads=["oo", ("bo", s)], writes=["oo"])
            P.op("pool", lambda e, ib=ib, c=c: e.tensor_tensor(out=hT[:, c, :n], in0=oo[:, :n], in1=ib["g"][:, :n], op=ALU.mult),
                 reads=["oo", ("g", s)], writes=[hkeys[c]])
        for m in range(16):
            s = ffn.kin % 2
            ffn.kin += 1
            wt = ffn.wg[s]
            P.dma("sp", wt[:], w_o[m], writes=[("wg", s)])
            q = ffn.km % 2
            ffn.km += 1
            py = cm.psb[6 + q]
            for c in range(16):
                P.op("pe", lambda e, wt=wt, py=py, c=c: e.matmul(py[:, :n], lhsT=wt[:, c, :], rhs=hT[:, c, :n],
                                                                  start=(c == 0), stop=(c == 15)),
                     reads=[("wg", s), hkeys[c]], writes=[("psb", 6 + q)])
            P.op("dve", lambda e, py=py, m=m: e.scalar_tensor_tensor(
                out=xb[:, m, :n], in0=py[:, :n], scalar=mod[:, 2, m:m + 1], in1=xb[:, m, :n], op0=ALU.mult, op1=ALU.add),
                reads=[("psb", 6 + q), mk, xkeys[m]], writes=[xkeys[m]])
        cm.norm_mod(xb[:, :, :n], n, (G2, ("G2", name)), (mod[:, 3, :], mk), lambda c: hT[:, c, :n], hkeys, xkeys, stat_bank=0)
        ffn.emit(hT, hkeys, n, xb, 0, xkeys, (mod[:, 5, :], mk))
        out_toks.append(P.dma("sp", d_["out"][:, :, t0:t0 + n], xb[:, :, :n], reads=xkeys, writes=[("out", name, t0)]))

    for name, T in segs:
        mod = P.sbuf("modsb_" + name, [128, 6, 16], F32)
        G2 = P.sbuf("G2_" + name, [128, 16], F32)
        mk = ("mod", name)
        P.dma("sp", mod[:], dr[name]["mod"], writes=[mk])
        P.op("dve", lambda e, G2=G2, mod=mod: e.scalar_tensor_tensor(
            out=G2[:], in0=mod[:, 4, :], scalar=1.0, in1=ng[:, 1, :], op0=ALU.add, op1=ALU.mult),
            reads=[mk, "ng"], writes=[("G2", name)])
        for t0 in range(0, T, TB):
            do_block(name, t0, min(TB, T - t0), mod, G2)
    P.final_wait("sp", out_toks)
    return P.build()


def r2_maps_from_r1(r1, cons):
    maps = []
    for b in range(2):
        cores = [r1[b * 4 + k] for k in range(4)]
        ot_lat = np.concatenate([c["ot_lat"] for c in cores], axis=4)
        ot_ctx = np.concatenate([cores[0]["ot_ctx"], cores[1]["ot_ctx"]], axis=4)
        v_lat = np.concatenate([c["v_lat"] for c in cores], axis=2)
        v_ctx = np.concatenate([cores[0]["v_ctx"], cores[1]["v_ctx"]], axis=2)
        pc_lat = np.concatenate([c["pc_lat"] for c in cores], axis=3)
        pc_ctx = np.concatenate([cores[0]["pc_ctx"], cores[1]["pc_ctx"]], axis=3)
        for d in range(2):
            if d == 0:
                ot = np.concatenate([ot_ctx[d], ot_lat[d]], axis=3)
                vv = np.concatenate([v_ctx, v_lat], axis=2)
                pc = np.concatenate([pc_ctx[:, d], pc_lat[:, d]], axis=2)
            else:
                ot = np.concatenate([ot_ctx[d][..., ::-1], ot_lat[d][..., ::-1]], axis=3)
                vv = np.concatenate([v_ctx[..., ::-1], v_lat[..., ::-1]], axis=2)
                pc = np.concatenate([pc_ctx[:, d][..., ::-1], pc_lat[:, d][..., ::-1]], axis=2)
            for hh in range(2):
                cs = slice(8 * hh, 8 * hh + 8)
                o = ot[:, cs].reshape(4, 8, 128, 66, 128)
                fm = np.ascontiguousarray(o.transpose(0, 3, 2, 1, 4))
                tmB = o[1].transpose(2, 3, 0, 1)
                tmK = o[2].transpose(2, 3, 0, 1)
                tmV = vv[cs].reshape(8, 128, 66, 128).transpose(2, 3, 0, 1)
                tm = np.ascontiguousarray(np.stack([tmB, tmK, tmV]))
                pcl = np.ascontiguousarray(pc[:, cs].transpose(2, 0, 1))
                m = dict(fm=fm, tm=tm, pc=pcl)
                m.update(cons)
                maps.append(((b, d, hh), m))
    maps.sort(key=lambda t: t[0][0] * 4 + t[0][1] * 2 + t[0][2])
    return [m for _, m in maps]


def r3_y_from_r2(r2res):
    yl = [[None, None], [None, None]]
    yc = [[None, None], [None, None]]
    for b in range(2):
        for d in range(2):
            halves = []
            for hh in range(2):
                y = np.asarray(r2res[b * 4 + d * 2 + hh]["y"])
                halves.append(y.reshape(66 * 128, 8 * 128))
            seq = np.concatenate(halves, axis=1)
            c, l = seq[:256], seq[256:]
            if d == 1:
                c, l = c[::-1], l[::-1]
            yc[b][d], yl[b][d] = c, l
    return yl, yc


def _r1_maps(x, ctx, mods_l, inp, wb):
    maps = []
    rmask = np.ones((128, 256), np.float32)
    rmask[:, ::128] = 0.0
    for i in range(8):
        b, k = i // 4, i % 4
        xl = np.zeros((TLAT + 2, 2048), np.float32)
        vl = np.zeros(TLAT + 2, np.float32)
        lo, hi = k * TLAT - 1, (k + 1) * TLAT + 1
        s0, s1 = max(lo, 0), min(hi, 8192)
        xl[s0 - lo:s1 - lo] = x[b, s0:s1]
        vl[s0 - lo:s1 - lo] = 1
        ck = k % 2
        xc = np.zeros((R1_TCTX + 2, 2048), np.float32)
        vc = np.zeros(R1_TCTX + 2, np.float32)
        lo, hi = ck * R1_TCTX - 1, (ck + 1) * R1_TCTX + 1
        s0, s1 = max(lo, 0), min(hi, 256)
        xc[s0 - lo:s1 - lo] = ctx[b, s0:s1]
        vc[s0 - lo:s1 - lo] = 1
        maps.append({"x_lat": to_fm(xl), "valid_lat": vl, "mod_lat": vec_fm(mods_l[b].reshape(6, 2048)),
                     "x_ctx": to_fm(xc), "valid_ctx": vc, "mod_ctx": vec_fm(mods_l[2].reshape(6, 2048)),
                     "normg": vec_fm(inp["norm_g"][1]), "mu": vec_fm(inp["rwkv_mu"][0]), "w_rkv": wb["rwkv_rkv"],
                     "w_la": inp["rwkv_w_lora_a"][0], "w_lb": inp["rwkv_w_lora_b"][0], "a_la": inp["rwkv_a_lora_a"][0],
                     "a_lb": inp["rwkv_a_lora_b"][0], "g_la": inp["rwkv_g_lora_a"][0], "g_lb": inp["rwkv_g_lora_b"][0],
                     "dirvec": vec_fm(inp["rwkv_dir_vec"][0]), "r_k": vec_fm(inp["rwkv_r_k"][0].reshape(2048)), "rmask": rmask})
    return maps


def _r3_maps(x, ctx, yl, yc, r1, mods_l, inp, li, wb):
    maps = []
    fmc = lambda a: np.ascontiguousarray(a.reshape(a.shape[0], 16, 128).transpose(1, 2, 0))
    for i in range(8):
        b, k = i // 4, i % 4
        ls = slice(k * 2048, (k + 1) * 2048)
        cs = slice(k * 64, (k + 1) * 64)
        cc = r1[b * 4 + (k // 2)]
        co = (k % 2) * 64
        maps.append({"x_lat": to_fm(x[b, ls]), "x_ctx": to_fm(ctx[b, cs]),
                     "y_lat": np.stack([fmc(yl[b][0][ls]), fmc(yl[b][1][ls])]),
                     "y_ctx": np.stack([fmc(yc[b][0][cs]), fmc(yc[b][1][cs])]),
                     "bonus_lat": r1[i]["bonus_lat"], "g_lat": r1[i]["g_lat"],
                     "bonus_ctx": np.ascontiguousarray(cc["bonus_ctx"][:, :, co:co + 64]),
                     "g_ctx": np.ascontiguousarray(cc["g_ctx"][:, :, co:co + 64]),
                     "mod_lat": vec_fm(mods_l[b].reshape(6, 2048)), "mod_ctx": vec_fm(mods_l[2].reshape(6, 2048)),
                     "normg": vec_fm(inp["norm_g"][li]), "lnx": vec_fm(inp["rwkv_ln_x"][0]), "w_o": wb["rwkv_wo"],
                     "w_in": wb["ffn_in%d" % li], "w_out": wb["ffn_out%d" % li]})
    return maps


def _run_rwkv_layer(x, ctx, mods_l, inp, li, wb):
    nc1 = build_r1()
    res1 = run_bass_kernel_spmd(nc1, _r1_maps(x, ctx, mods_l, inp, wb), core_ids=list(range(8)))
    r1 = [{k: np.asarray(v) for k, v in r.items()} for r in res1.results]
    nc2 = build_r2(NCH)
    res2 = run_bass_kernel_spmd(nc2, r2_maps_from_r1(r1, r2_consts()), core_ids=list(range(8)))
    r2res = [{"y": np.asarray(r["y"])} for r in res2.results]
    yl, yc = r3_y_from_r2(r2res)
    nc3 = build_r3()
    res3 = run_bass_kernel_spmd(nc3, _r3_maps(x, ctx, yl, yc, r1, mods_l, inp, li, wb), core_ids=list(range(8)))
    xo = np.zeros_like(x)
    co = np.zeros_like(ctx)
    for i in range(8):
        b, k = i // 4, i % 4
        xo[b, k * 2048:(k + 1) * 2048] = from_fm(res3.results[i]["out_lat"])
        co[b, k * 64:(k + 1) * 64] = from_fm(res3.results[i]["out_ctx"])
    return xo, co


def kernel(**inputs):
    inp = {k: np.ascontiguousarray(np.asarray(v, dtype=np.float32)) for k, v in inputs.items()}
    x, ctx = inp["x"], inp["ctx"]
    mods, wb = run_l0(inp)
    x, ctx = run_pool_layer(x, ctx, mods[:, 0], inp["norm_g"][0], inp["pool_scale"][0], inp["pool_w"][0],
                            wb["ffn_in0"], wb["ffn_out0"], True)
    x, ctx = _run_rwkv_layer(x, ctx, mods[:, 1], inp, 1, wb)
    a1 = run_a1(x, ctx, mods[:, 2], inp["norm_g"][2], wb["diff_qkv"], inp["diff_qk_g"][0])
    a1 = [{k: np.asarray(v) for k, v in r.items()} for r in a1]
    lambda_init = 0.8 - 0.6 * math.exp(-0.3 * 2)
    x = run_a2(a1, x, mods[:, 2], inp["norm_g"][2], inp["diff_lambda"][0], inp["diff_subln_g"][0], wb["diff_wo"],
               wb["ffn_in2"], wb["ffn_out2"], lambda_init)
    x, _ = run_pool_layer(x, None, mods[:, 3], inp["norm_g"][3], inp["pool_scale"][1], inp["pool_w"][1],
                          wb["ffn_in3"], wb["ffn_out3"], False)
    return x.astype(np.float32)
```

```python
import math


import numpy as np
import concourse.bass as bass
import concourse.mybir as mybir
from concourse.bass_utils import run_bass_kernel_spmd

F32 = mybir.dt.float32
BF16 = mybir.dt.bfloat16
ALU = mybir.AluOpType
AF = mybir.ActivationFunctionType
AX = mybir.AxisListType

N_DMA_SEMS = 24


class Prog:
    ENGS = ("pe", "act", "dve", "pool", "sp")

    def __init__(self, nc):
        self.nc = nc
        self.q = {e: [] for e in self.ENGS}
        self.n = {e: 0 for e in self.ENGS}
        self.waited = {e: {} for e in self.ENGS}
        self.lastw = {}
        self.readers = {}
        self.dma_rr = 0
        self.dma_cnt = [0] * N_DMA_SEMS
        self.dma_last = [None] * N_DMA_SEMS
        self.ctx = []
        self.sems = {}

    def enter(self, cm):
        v = cm.__enter__()
        self.ctx.append(cm)
        return v

    def sbuf(self, name, shape, dt):
        return self.enter(self.nc.sbuf_tensor(name, list(shape), dt))

    def psum(self, name, shape, dt=F32):
        return self.enter(self.nc.psum_tensor(name, list(shape), dt))

    def _deps(self, reads, writes):
        toks = []
        for r in reads:
            t = self.lastw.get(r)
            if t is not None:
                toks.append(t)
        for w in writes:
            t = self.lastw.get(w)
            if t is not None:
                toks.append(t)
            toks.extend(self.readers.get(w, ()))
        return toks

    def _commit(self, tok, reads, writes):
        for r in reads:
            self.readers.setdefault(r, []).append(tok)
        for w in writes:
            self.lastw[w] = tok
            self.readers[w] = []

    def _waits(self, eng, toks):
        need = {}
        for (k, v) in toks:
            if v > need.get(k, 0):
                need[k] = v
        out = []
        wd = self.waited[eng]
        for k, v in need.items():
            if wd.get(k, 0) >= v:
                continue
            wd[k] = v
            out.append((k, v))
        return out

    def op(self, eng, fn, reads=(), writes=()):
        toks = self._deps(reads, writes)
        if eng == "pe":
            toks = [t for t in toks if t[0] != "pe"]
        waits = self._waits(eng, toks)
        self.n[eng] += 1
        tok = (eng, self.n[eng])
        self.q[eng].append((fn, waits, ("self", eng, 1)))
        self._commit(tok, reads, writes)
        return tok

    def dma(self, eng, out, in_, reads=(), writes=(), **kw):
        toks = self._deps(reads, writes)
        s = self.dma_rr
        self.dma_rr = (self.dma_rr + 1) % N_DMA_SEMS
        if self.dma_last[s] is not None:
            toks.append(self.dma_last[s])
        waits = self._waits(eng, toks)
        self.dma_cnt[s] += 1
        tok = (("dma", s), 16 * self.dma_cnt[s])
        self.dma_last[s] = tok
        self.q[eng].append((lambda e: e.dma_start(out=out, in_=in_, **kw), waits, ("dma", s, 16)))
        self._commit(tok, reads, writes)
        return tok

    def final_wait(self, eng, toks):
        waits = self._waits(eng, toks)
        self.q[eng].append((None, waits, None))

    def build(self):
        nc = self.nc
        semobjs = {}
        for e in self.ENGS:
            if e != "sp":
                semobjs[e] = self.enter(nc.semaphore("prog_" + e))
        for s in range(N_DMA_SEMS):
            semobjs[("dma", s)] = self.enter(nc.semaphore("dma%d" % s))
        q = self.q

        def emit(engname, e):
            for fn, waits, inc in q[engname]:
                for k, v in waits:
                    e.wait_ge(semobjs[k], v)
                if fn is None:
                    continue
                ins = fn(e)
                if inc[0] == "self":
                    ins.then_inc(semobjs[inc[1]], 1)
                else:
                    ins.then_inc(semobjs[("dma", inc[1])], 16)

        with nc.Block() as block:
            @block.tensor
            def _(e):
                emit("pe", e)

            @block.scalar
            def _(e):
                emit("act", e)

            @block.vector
            def _(e):
                emit("dve", e)

            @block.gpsimd
            def _(e):
                emit("pool", e)

            @block.sync
            def _(e):
                emit("sp", e)
        for cm in reversed(self.ctx):
            cm.__exit__(None, None, None)
        self.ctx = []
        return nc


D = 2048
NL = 4
NM = 6 * D
COLS_PER_CORE = NM // 8
L0_NB = COLS_PER_CORE // 512

CAST_CH = 8192


def build_l0(ncast=0):
    nc = bass.Bass("TRN2", target_bir_lowering=False)
    cT = nc.dram_tensor("cT", [128, 16, 3], F32, kind="ExternalInput").ap()
    w = nc.dram_tensor("w", [NL, D, COLS_PER_CORE], F32, kind="ExternalInput").ap()
    b = nc.dram_tensor("b", [NL, COLS_PER_CORE], F32, kind="ExternalInput").ap()
    out = nc.dram_tensor("out", [3, NL * COLS_PER_CORE], F32, kind="ExternalOutput").ap()
    if ncast:
        cin = nc.dram_tensor("cin", [128, ncast * CAST_CH], F32, kind="ExternalInput").ap()
        cout = nc.dram_tensor("cout", [128, ncast * CAST_CH], BF16, kind="ExternalOutput").ap()
    P = Prog(nc)
    c_sb = P.sbuf("c_sb", [128, 16, 3], F32)
    s_sb = P.sbuf("s_sb", [128, 16, 3], F32)
    b_sb = P.sbuf("b_sb", [3, NL * COLS_PER_CORE], F32)
    o_sb = P.sbuf("o_sb", [3, NL * COLS_PER_CORE], F32)
    wt = [P.sbuf("wt%d" % i, [128, 16, 512], F32) for i in range(2)]
    ps = [P.psum("ps%d" % i, [128, 512]) for i in range(2)]
    P.dma("sp", c_sb[:], cT, writes=["c"])
    for l in range(NL):
        P.dma("sp", b_sb[:, l * COLS_PER_CORE:(l + 1) * COLS_PER_CORE],
              b[l, :].partition_broadcast(3), writes=[("b", l)])
    P.op("act", lambda e: e.activation(out=s_sb[:], in_=c_sb[:], func=AF.Silu), reads=["c"], writes=["s"])
    k = 0
    for l in range(NL):
        for nb in range(L0_NB):
            wb = wt[k % 2]
            pb = ps[k % 2]
            src = w[l, :, nb * 512:(nb + 1) * 512].rearrange("(c p) n -> p c n", p=128)
            P.dma("sp" if k % 2 == 0 else "pool", wb[:], src, writes=[("wt", k % 2)])
            for c in range(16):
                P.op("pe", lambda e, wb=wb, pb=pb, c=c: e.matmul(pb[0:3, :], lhsT=s_sb[:, c, :], rhs=wb[:, c, :],
                                                                   start=(c == 0), stop=(c == 15)),
                     reads=["s", ("wt", k % 2)], writes=[("ps", k % 2)])
            col = l * COLS_PER_CORE + nb * 512
            P.op("dve", lambda e, pb=pb, col=col: e.tensor_tensor(out=o_sb[:, col:col + 512], in0=pb[0:3, :],
                                                                   in1=b_sb[:, col:col + 512], op=ALU.add),
                 reads=[("ps", k % 2), ("b", l)], writes=[("o", k)])
            k += 1
    t = P.dma("sp", out, o_sb[:], reads=[("o", i) for i in range(k)], writes=["out"])
    toks = [t]
    if ncast:
        cb = [P.sbuf("cb%d" % i, [128, CAST_CH], BF16) for i in range(3)]
        for i in range(ncast):
            sl = slice(i * CAST_CH, (i + 1) * CAST_CH)
            P.dma("pool", cb[i % 3][:], cin[:, sl], writes=[("cb", i % 3)])
            toks.append(P.dma("act", cout[:, sl], cb[i % 3][:], reads=[("cb", i % 3)], writes=[("cout", i)]))
    P.final_wait("sp", toks)
    return P.build()

def blocked_weights(inputs):
    perm = np.concatenate([np.arange(0, 128, 2), np.arange(1, 128, 2)])
    out = {}
    for l in range(4):
        wi = inputs["ffn_w_in"][l].reshape(16, 128, 2, 44, 128)
        out["ffn_in%d" % l] = np.ascontiguousarray(wi.transpose(3, 2, 1, 0, 4))
        wo = inputs["ffn_w_out"][l].reshape(44, 128, 16, 128)
        out["ffn_out%d" % l] = np.ascontiguousarray(wo.transpose(2, 1, 0, 3))
    sq = lambda w: np.ascontiguousarray(w.reshape(16, 128, -1, 128).transpose(2, 1, 0, 3))
    out["rwkv_wo"] = sq(inputs["rwkv_w_o"][0])
    out["diff_wo"] = sq(inputs["diff_w_o"][0])
    out["rwkv_rkv"] = np.stack([sq(inputs["rwkv_w_rkv"][0][i]) for i in range(3)])
    wq = inputs["diff_w_qkv"][0]
    cols = (np.arange(32)[:, None] * 128 + perm[None, :]).reshape(-1)
    wq = np.concatenate([wq[:, cols], wq[:, 4096:]], axis=1)
    out["diff_qkv"] = sq(wq)
    return out


def run_l0(inputs, cast=True):
    c = np.concatenate([inputs["c"], inputs["c_ctx"][None]], 0)
    cT = np.ascontiguousarray(c.reshape(3, 16, 128).transpose(2, 1, 0))
    blk = blocked_weights(inputs) if cast else {}
    names = list(blk)
    total = sum(blk[n].size for n in names)
    per = 8 * 128 * CAST_CH
    ncast = (total + per - 1) // per
    nc = build_l0(ncast)
    if ncast:
        flat = np.zeros(ncast * per, np.float32)
        o = 0
        for n in names:
            flat[o:o + blk[n].size] = blk[n].reshape(-1)
            o += blk[n].size
        flat = flat.reshape(8, 128, ncast * CAST_CH)
    maps = []
    for i in range(8):
        sl = slice(i * COLS_PER_CORE, (i + 1) * COLS_PER_CORE)
        m = {"cT": cT, "w": np.ascontiguousarray(inputs["ada_w"][:, :, sl]),
             "b": np.ascontiguousarray(inputs["ada_b"][:, sl])}
        if ncast:
            m["cin"] = flat[i]
        maps.append(m)
    res = run_bass_kernel_spmd(nc, maps, core_ids=list(range(8)))
    outs = [r["out"].reshape(3, NL, COLS_PER_CORE) for r in res.results]
    mods = np.concatenate(outs, axis=2)
    wb = {}
    if ncast:
        cf = np.concatenate([np.asarray(r["cout"]).reshape(-1) for r in res.results])
        o = 0
        for n in names:
            wb[n] = cf[o:o + blk[n].size].reshape(blk[n].shape)
            o += blk[n].size
    return mods, wb


D = 2048
F = 5632
NC16 = 16
NJ = F // 128
EPS = 1e-6
HALO = 8
TBG = 512
WINS = (2, 4, 8, 16)


class Common:
    def __init__(self, P, TBMAX=512, halo=HALO):
        self.P = P
        W = TBMAX + 2 * halo
        self.W = W
        self.ones = P.sbuf("ones_bf", [128, 128], BF16)
        self.rs = P.sbuf("rs", [128, W], F32)
        self.sqc = [P.sbuf("sqc%d" % i, [128, W], BF16) for i in range(2)]
        self.tmp = [P.sbuf("ntmp%d" % i, [128, W], F32) for i in range(2)]
        self.psb = [P.psum("psb%d" % i, [128, 512]) for i in range(8)]
        P.op("dve", lambda e: e.memset(self.ones[:], 1.0), writes=["ones"])
        self.k = 0

    def norm_mod(self, xb, ncols, G, SH, dest, dest_keys, xkeys, stat_bank=0, post=None):
        P = self
        P = self.P
        pst = self.psb[stat_bank]
        pkey = ("psb", stat_bank)
        n2 = ncols
        halves = [(0, min(512, n2))]
        if n2 > 512:
            halves.append((512, n2))
        for c in range(16):
            sq = self.sqc[c % 2]
            P.op("act", lambda e, sq=sq, c=c: e.activation(out=sq[:, :n2], in_=xb[:, c, :], func=AF.Square),
                 reads=[xkeys[c]], writes=[("sqc", c % 2)])
            for hi, (a, b) in enumerate(halves):
                bank = self.psb[stat_bank + hi]
                P.op("pe", lambda e, sq=sq, c=c, a=a, b=b, bank=bank: e.matmul(
                    bank[:, 0:b - a], lhsT=self.ones[:], rhs=sq[:, a:b], start=(c == 0), stop=(c == 15)),
                    reads=[("sqc", c % 2), "ones"], writes=[("psb", stat_bank + hi)])
        for hi, (a, b) in enumerate(halves):
            bank = self.psb[stat_bank + hi]
            P.op("act", lambda e, a=a, b=b, bank=bank: e.activation(
                out=self.rs[:, a:b], in_=bank[:, 0:b - a], func=AF.Sqrt, scale=1.0 / D, bias=self.epsb[:, 0:1]),
                reads=[("psb", stat_bank + hi), "epsb"], writes=[("rs", hi)])
            P.op("dve", lambda e, a=a, b=b: e.reciprocal(out=self.rs[:, a:b], in_=self.rs[:, a:b]),
                 reads=[("rs", hi)], writes=[("rs", hi)])
        Gt, Gk = G
        St, Sk = SH
        for c in range(16):
            tmp = self.tmp[c % 2]
            P.op("dve", lambda e, tmp=tmp, c=c: e.tensor_tensor(out=tmp[:, :n2], in0=xb[:, c, :], in1=self.rs[:, :n2],
                                                                 op=ALU.mult),
                 reads=[xkeys[c], ("rs", 0), ("rs", 1)], writes=[("ntmp", c % 2)])
            P.op("act", lambda e, tmp=tmp, c=c: e.activation(out=dest(c), in_=tmp[:, :n2], func=AF.Identity,
                                                             scale=Gt[:, c:c + 1], bias=St[:, c:c + 1]),
                 reads=[("ntmp", c % 2), Gk, Sk], writes=[dest_keys[c]])
            if post is not None:
                post(c)

    def setup_eps(self):
        P = self.P
        self.epsb = P.sbuf("epsb", [128, 1], F32)
        P.op("dve", lambda e: e.memset(self.epsb[:], EPS), writes=["epsb"])


class FFN:
    def __init__(self, P, cm, w_in, w_out, TB=512, nsplit=1, WC=256):
        self.P, self.cm = P, cm
        self.w_in, self.w_out = w_in, w_out
        WC = 128
        self.nsplit, self.WC = nsplit, WC
        self.NJS = NJ // nsplit
        self.actT = P.sbuf("actT", [128, self.NJS, TB], BF16)
        self.wg = [P.sbuf("wg%d" % i, [128, 16, WC], BF16) for i in range(2)]
        self.wu = [P.sbuf("wu%d" % i, [128, 16, WC], BF16) for i in range(2)]
        self.wo = [P.sbuf("wo%d" % i, [128, self.NJS, 128], BF16) for i in range(2)]
        self.silu = [P.sbuf("silu%d" % i, [128, TB], F32) for i in range(2)]
        self.kin = 0
        self.kout = 0
        self.kj = 0
        self.km = 0

    def emit(self, hT, hkeys, n, xb, xoff, xkeys, g2):
        P, cm = self.P, self.cm
        g2t, g2k = g2
        WC, NJS = self.WC, self.NJS
        per = WC // 128
        for sp in range(self.nsplit):
            j0 = sp * NJS
            for jb in range(NJS // per):
                s = self.kin % 2
                self.kin += 1
                wg, wu = self.wg[s], self.wu[s]
                jg = j0 + jb
                P.dma("sp", wg[:], self.w_in[jg, 0], writes=[("wg", s)])
                P.dma("sp", wu[:], self.w_in[jg, 1], writes=[("wu", s)])
                for jj in range(per):
                    jl = jb * per + jj
                    q = self.kj % 2
                    self.kj += 1
                    pg, pu = cm.psb[2 + q], cm.psb[4 + q]
                    for c in range(16):
                        P.op("pe", lambda e, wg=wg, pg=pg, c=c, jj=jj: e.matmul(
                            pg[:, :n], lhsT=wg[:, c, jj * 128:(jj + 1) * 128], rhs=hT[:, c, :n],
                            start=(c == 0), stop=(c == 15)),
                            reads=[("wg", s), hkeys[c]], writes=[("psb", 2 + q)])
                    for c in range(16):
                        P.op("pe", lambda e, wu=wu, pu=pu, c=c, jj=jj: e.matmul(
                            pu[:, :n], lhsT=wu[:, c, jj * 128:(jj + 1) * 128], rhs=hT[:, c, :n],
                            start=(c == 0), stop=(c == 15)),
                            reads=[("wu", s), hkeys[c]], writes=[("psb", 4 + q)])
                    sl = self.silu[q]
                    P.op("act", lambda e, sl=sl, pg=pg: e.activation(out=sl[:, :n], in_=pg[:, :n], func=AF.Silu),
                         reads=[("psb", 2 + q)], writes=[("silu", q)])
                    P.op("dve", lambda e, sl=sl, pu=pu, jl=jl: e.tensor_tensor(
                        out=self.actT[:, jl, :n], in0=sl[:, :n], in1=pu[:, :n], op=ALU.mult),
                        reads=[("silu", q), ("psb", 4 + q)], writes=[("actT", jl)])
            for m in range(16):
                s = self.kout % 2
                self.kout += 1
                wo = self.wo[s]
                P.dma("sp", wo[:], self.w_out[m][:, j0:j0 + NJS, :], writes=[("wo", s)])
                q = self.km % 2
                self.km += 1
                py = cm.psb[6 + q]
                for jl in range(NJS):
                    P.op("pe", lambda e, wo=wo, py=py, jl=jl: e.matmul(
                        py[:, :n], lhsT=wo[:, jl, :], rhs=self.actT[:, jl, :n], start=(jl == 0), stop=(jl == NJS - 1)),
                        reads=[("wo", s), ("actT", jl)], writes=[("psb", 6 + q)])
                P.op("dve", lambda e, py=py, m=m: e.scalar_tensor_tensor(
                    out=xb[:, m, xoff:xoff + n], in0=py[:, :n], scalar=g2t[:, m:m + 1], in1=xb[:, m, xoff:xoff + n],
                    op0=ALU.mult, op1=ALU.add),
                    reads=[("psb", 6 + q), g2k, xkeys[m]], writes=[xkeys[m]])


def build_pool_layer(segs, TB=512, dbg=0):
    nc = bass.Bass("TRN2", target_bir_lowering=False)
    dram = {}
    for name, T in segs:
        dram[name] = dict(
            x=nc.dram_tensor("x_" + name, [128, 16, T + 2 * HALO], F32, kind="ExternalInput").ap(),
            valid=nc.dram_tensor("valid_" + name, [T + 2 * HALO], F32, kind="ExternalInput").ap(),
            invc=nc.dram_tensor("invc_" + name, [4, T], F32, kind="ExternalInput").ap(),
            mod=nc.dram_tensor("mod_" + name, [128, 6, 16], F32, kind="ExternalInput").ap(),
            out=nc.dram_tensor("out_" + name, [128, 16, T], F32, kind="ExternalOutput").ap(),
        )
    normg = nc.dram_tensor("normg", [128, 2, 16], F32, kind="ExternalInput").ap()
    pscale = nc.dram_tensor("pscale", [128, 16], F32, kind="ExternalInput").ap()
    poolw = nc.dram_tensor("poolw", [4, 128, 4, 512], F32, kind="ExternalInput").ap()
    w_in = nc.dram_tensor("w_in", [NJ, 2, 128, 16, 128], BF16, kind="ExternalInput").ap()
    w_out = nc.dram_tensor("w_out", [16, 128, NJ, 128], BF16, kind="ExternalInput").ap()

    P = Prog(nc)
    cm = Common(P, TB)
    cm.setup_eps()
    W = TB + 2 * HALO
    ffn = FFN(P, cm, w_in, w_out, TB)
    xb = P.sbuf("xb", [128, 16, W], F32)
    hT = P.sbuf("hT", [128, 16, W], BF16)
    hc = [P.sbuf("hc%d" % i, [128, W], F32) for i in range(2)]
    pa = [P.sbuf("pa%d" % i, [128, W], F32) for i in range(2)]
    pm = P.sbuf("pm", [128, TB], F32)
    pw = [P.sbuf("pw%d" % i, [128, 4, 512], BF16) for i in range(2)]
    invc = P.sbuf("invc", [128, 4, TB], F32)
    vmask = P.sbuf("vmask", [128, W], F32)
    ng = P.sbuf("ng", [128, 2, 16], F32)
    psc = P.sbuf("psc", [128, 16], F32)
    P.dma("sp", ng[:], normg, writes=["ng"])
    P.dma("sp", psc[:], pscale, writes=["psc"])
    xkeys = [("xb", c) for c in range(16)]
    hkeys = [("hT", c) for c in range(16)]
    out_toks = []
    st = dict(kpw=0, kpy=0)
    def do_block(dr, mod, G1, G2, GL, mk, name, bi, t0, n):
        nw = n + 2 * HALO
        P.dma("sp", xb[:, :, :nw], dr["x"][:, :, t0:t0 + nw], writes=xkeys)
        P.dma("sp", vmask[:, :nw], dr["valid"][t0:t0 + nw].partition_broadcast(128), writes=["vmask"])
        P.dma("sp", invc[:, :, :n], dr["invc"][:, t0:t0 + n].partition_broadcast(128), writes=["invc"])

        def pool_chunk(c, n=n, nw=nw):
            g = c // 4
            w = WINS[g]
            h = hc[c % 2]
            hk = ("hc", c % 2)
            P.op("dve", lambda e: e.tensor_tensor(out=h[:, 0:HALO], in0=h[:, 0:HALO], in1=vmask[:, 0:HALO], op=ALU.mult),
                 reads=[hk, "vmask"], writes=[hk])
            P.op("dve", lambda e: e.tensor_tensor(out=h[:, nw - HALO:nw], in0=h[:, nw - HALO:nw],
                                                  in1=vmask[:, nw - HALO:nw], op=ALU.mult),
                 reads=[hk, "vmask"], writes=[hk])
            cur, curk, ln = h, hk, nw
            s = 1
            i = 0
            while s < w:
                dst = pa[i % 2]
                P.op("dve", lambda e, cur=cur, dst=dst, s=s, ln=ln: e.tensor_tensor(
                    out=dst[:, 0:ln - s], in0=cur[:, 0:ln - s], in1=cur[:, s:ln], op=ALU.add),
                    reads=[curk], writes=[("pa", i % 2)])
                cur, curk, ln = dst, ("pa", i % 2), ln - s
                s *= 2
                i += 1
            o = HALO - w // 2
            P.op("dve", lambda e, cur=cur, o=o, g=g: e.tensor_tensor(
                out=pm[:, :n], in0=cur[:, o:o + n], in1=invc[:, g, :n], op=ALU.mult),
                reads=[curk, "invc"], writes=["pm"])
            P.op("dve", lambda e, c=c: e.tensor_tensor(
                out=hT[:, c, :n], in0=pm[:, :n], in1=h[:, HALO:HALO + n], op=ALU.subtract),
                reads=["pm", hk], writes=[hkeys[c]])

        hck = [("hc", c % 2) for c in range(16)]
        cm.norm_mod(xb[:, :, :nw], nw, (G1, ("G1", name)), (mod[:, 0, :], mk),
                    lambda c, nw=nw: hc[c % 2][:, :nw], hck, xkeys, stat_bank=0, post=pool_chunk)
        for g in range(4):
            s = st["kpw"] % 2
            st["kpw"] += 1
            P.dma("pool", pw[s][:], poolw[g], writes=[("pw", s)])
            for mm in range(4):
                m = 4 * g + mm
                q = st["kpy"] % 2
                st["kpy"] += 1
                py = cm.psb[6 + q]
                for cc in range(4):
                    P.op("pe", lambda e, s=s, py=py, cc=cc, mm=mm, g=g: e.matmul(
                        py[:, :n], lhsT=pw[s][:, cc, mm * 128:(mm + 1) * 128], rhs=hT[:, 4 * g + cc, :n],
                        start=(cc == 0), stop=(cc == 3)),
                        reads=[("pw", s), hkeys[4 * g + cc]], writes=[("psb", 6 + q)])
                P.op("dve", lambda e, py=py, m=m: e.scalar_tensor_tensor(
                    out=xb[:, m, HALO:HALO + n], in0=py[:, :n], scalar=GL[:, m:m + 1], in1=xb[:, m, HALO:HALO + n],
                    op0=ALU.mult, op1=ALU.add),
                    reads=[("psb", 6 + q), ("GL", name), xkeys[m]], writes=[xkeys[m]])
        if dbg == 0:
            cm.norm_mod(xb[:, :, HALO:HALO + n], n, (G2, ("G2", name)), (mod[:, 3, :], mk),
                        lambda c, n=n: hT[:, c, :n], hkeys, xkeys, stat_bank=0)
            ffn.emit(hT, hkeys, n, xb, HALO, xkeys, (mod[:, 5, :], mk))
        t = P.dma("sp", dr["out"][:, :, t0:t0 + n], xb[:, :, HALO:HALO + n], reads=xkeys, writes=[("out", name, bi)])
        out_toks.append(t)

    for si, (name, T) in enumerate(segs):
        dr = dram[name]
        mod = P.sbuf("modsb_" + name, [128, 6, 16], F32)
        G1 = P.sbuf("G1_" + name, [128, 16], F32)
        G2 = P.sbuf("G2_" + name, [128, 16], F32)
        GL = P.sbuf("GL_" + name, [128, 16], F32)
        mk = ("mod", name)
        P.dma("sp", mod[:], dr["mod"], writes=[mk])
        P.op("dve", lambda e, G1=G1, mod=mod: e.scalar_tensor_tensor(
            out=G1[:], in0=mod[:, 1, :], scalar=1.0, in1=ng[:, 0, :], op0=ALU.add, op1=ALU.mult),
            reads=[mk, "ng"], writes=[("G1", name)])
        P.op("dve", lambda e, G2=G2, mod=mod: e.scalar_tensor_tensor(
            out=G2[:], in0=mod[:, 4, :], scalar=1.0, in1=ng[:, 1, :], op0=ALU.add, op1=ALU.mult),
            reads=[mk, "ng"], writes=[("G2", name)])
        P.op("dve", lambda e, GL=GL, mod=mod: e.tensor_tensor(out=GL[:], in0=mod[:, 2, :], in1=psc[:], op=ALU.mult),
             reads=[mk, "psc"], writes=[("GL", name)])
        nblk = (T + TB - 1) // TB
        for bi in range(nblk):
            do_block(dr, mod, G1, G2, GL, mk, name, bi, bi * TB, min(TB, T - bi * TB))
    P.final_wait("sp", out_toks)
    return P.build()


def to_fm(a):
    T = a.shape[0]
    return np.ascontiguousarray(a.reshape(T, 16, 128).transpose(2, 1, 0))


def from_fm(a):
    T = a.shape[2]
    return np.ascontiguousarray(a.transpose(2, 1, 0).reshape(T, 2048))


def vec_fm(v):
    lead = v.shape[:-1]
    r = v.reshape(lead + (16, 128))
    return np.ascontiguousarray(np.moveaxis(r, -1, 0))


def seg_shards(seq, T):
    S = seq.shape[0]
    pad = np.zeros((S + 2 * HALO, 2048), np.float32)
    pad[HALO:HALO + S] = seq
    t = np.arange(S)
    inv = np.zeros((4, S), np.float32)
    for g, w in enumerate(WINS):
        lo = np.clip(t - w // 2, 0, S)
        hi = np.clip(t + w - w // 2, 0, S)
        inv[g] = 1.0 / (hi - lo)
    valid = np.zeros(S + 2 * HALO, np.float32)
    valid[HALO:HALO + S] = 1.0
    out = []
    for s0 in range(0, S, T):
        out.append(dict(x=to_fm(pad[s0:s0 + T + 2 * HALO]), valid=np.ascontiguousarray(valid[s0:s0 + T + 2 * HALO]),
                        invc=np.ascontiguousarray(inv[:, s0:s0 + T])))
    return out


def run_pool_layer(x, ctx, mods_l, normg_l, pscale, poolw, w_in, w_out, with_ctx, dbg=0):
    segs = [("lat", 2048)] + ([("ctx", 64)] if with_ctx else [])
    nc = build_pool_layer(segs, TB=TBG, dbg=dbg)
    maps = []
    pw_l = np.ascontiguousarray(poolw.reshape(4, 4, 128, 512).transpose(0, 2, 1, 3))
    lat = [seg_shards(x[b], 2048) for b in range(2)]
    cs = [seg_shards(ctx[b], 64) for b in range(2)] if with_ctx else None
    for i in range(8):
        b, k = i // 4, i % 4
        m = {"normg": vec_fm(normg_l), "pscale": vec_fm(pscale), "poolw": pw_l, "w_in": w_in, "w_out": w_out}
        sh = lat[b][k]
        m.update({"x_lat": sh["x"], "valid_lat": sh["valid"], "invc_lat": sh["invc"],
                  "mod_lat": vec_fm(mods_l[b].reshape(6, 2048))})
        if with_ctx:
            sh = cs[b][k]
            m.update({"x_ctx": sh["x"], "valid_ctx": sh["valid"], "invc_ctx": sh["invc"],
                      "mod_ctx": vec_fm(mods_l[2].reshape(6, 2048))})
        maps.append(m)
    res = run_bass_kernel_spmd(nc, maps, core_ids=list(range(8)))
    xo = np.zeros_like(x)
    co = np.zeros_like(ctx) if with_ctx else None
    for i in range(8):
        b, k = i // 4, i % 4
        xo[b, k * 2048:(k + 1) * 2048] = from_fm(res.results[i]["out_lat"])
        if with_ctx:
            co[b, k * 64:(k + 1) * 64] = from_fm(res.results[i]["out_ctx"])
    return xo, co


DH = 128
NHEAD = 8
GRID_W = 64
CTX = 256
TLAT = 2048
TCTX = 64


def build_a1():
    nc = bass.Bass("TRN2", target_bir_lowering=False)
    x_lat = nc.dram_tensor("x_lat", [128, 16, TLAT], F32, kind="ExternalInput").ap()
    x_ctx = nc.dram_tensor("x_ctx", [128, 16, TCTX], F32, kind="ExternalInput").ap()
    mod_lat = nc.dram_tensor("mod_lat", [128, 6, 16], F32, kind="ExternalInput").ap()
    mod_ctx = nc.dram_tensor("mod_ctx", [128, 6, 16], F32, kind="ExternalInput").ap()
    normg = nc.dram_tensor("normg", [128, 2, 16], F32, kind="ExternalInput").ap()
    wqkv = nc.dram_tensor("wqkv", [48, 128, 16, 128], BF16, kind="ExternalInput").ap()
    qkg = nc.dram_tensor("qkg", [128, 2], F32, kind="ExternalInput").ap()
    cs_d = nc.dram_tensor("cs", [128, TLAT], F32, kind="ExternalInput").ap()
    sn_d = nc.dram_tensor("sn", [128, TLAT], F32, kind="ExternalInput").ap()
    qT = nc.dram_tensor("qT", [16, 128, TLAT], BF16, kind="ExternalOutput").ap()
    kT = nc.dram_tensor("kT", [16, 128, TLAT], BF16, kind="ExternalOutput").ap()
    vT = nc.dram_tensor("vT", [16, 128, TLAT], BF16, kind="ExternalOutput").ap()
    kcT = nc.dram_tensor("kcT", [16, 128, TCTX], BF16, kind="ExternalOutput").ap()
    vcT = nc.dram_tensor("vcT", [16, 128, TCTX], BF16, kind="ExternalOutput").ap()

    P = Prog(nc)
    cm = Common(P, 512, halo=0)
    cm.setup_eps()
    TALL = TLAT + TCTX
    xb = P.sbuf("xb", [128, 16, 512], F32)
    hT = P.sbuf("hT", [128, 16, TALL], BF16)
    wt = [P.sbuf("wt%d" % i, [128, 16, 128], BF16) for i in range(3)]
    cs = P.sbuf("cs_sb", [128, TLAT], F32)
    sn = P.sbuf("sn_sb", [128, TLAT], F32)
    ng = P.sbuf("ng", [128, 2, 16], F32)
    g_sb = P.sbuf("qkg_sb", [128, 2], F32)
    qn = [P.sbuf("qn%d" % i, [128, 512], F32) for i in range(2)]
    sw = [P.sbuf("sw%d" % i, [128, 512], F32) for i in range(2)]
    tm = [P.sbuf("tm%d" % i, [128, 512], F32) for i in range(2)]
    oo = [P.sbuf("oo%d" % i, [128, 512], F32) for i in range(2)]
    sq = [P.sbuf("sq%d" % i, [128, 512], BF16) for i in range(2)]
    rr = [P.sbuf("rr%d" % i, [128, 512], F32) for i in range(2)]
    stg = [P.sbuf("stg%d" % i, [128, TLAT], BF16) for i in range(2)]
    stgc = [P.sbuf("stgc%d" % i, [128, TCTX], BF16) for i in range(2)]
    P.dma("sp", ng[:], normg, writes=["ng"])
    P.dma("sp", g_sb[:], qkg, writes=["qkg"])
    P.dma("sp", cs[:], cs_d, writes=["cs"])
    P.dma("sp", sn[:], sn_d, writes=["sn"])
    xkeys = [("xb", c) for c in range(16)]
    segs = [("lat", x_lat, mod_lat, TLAT, 0), ("ctx", x_ctx, mod_ctx, TCTX, TLAT)]
    for name, xd, md, T, off in segs:
        mod = P.sbuf("modsb_" + name, [128, 6, 16], F32)
        G1 = P.sbuf("G1_" + name, [128, 16], F32)
        mk = ("mod", name)
        P.dma("sp", mod[:], md, writes=[mk])
        P.op("dve", lambda e, G1=G1, mod=mod: e.scalar_tensor_tensor(
            out=G1[:], in0=mod[:, 1, :], scalar=1.0, in1=ng[:, 0, :], op0=ALU.add, op1=ALU.mult),
            reads=[mk, "ng"], writes=[("G1", name)])
        for t0 in range(0, T, 512):
            n = min(512, T - t0)
            P.dma("sp", xb[:, :, :n], xd[:, :, t0:t0 + n], writes=xkeys)
            hk = [("hT", c, (off + t0) // 512) for c in range(16)]
            cm.norm_mod(xb[:, :, :n], n, (G1, ("G1", name)), (mod[:, 0, :], mk),
                        lambda c, o=off + t0, n=n: hT[:, c, o:o + n], hk, xkeys, stat_bank=0)
    blocks = [(i * 512, 512, i) for i in range(4)] + [(TLAT, TCTX, 4)]
    st = dict(k=0, ps=0, ss=0, t=0)
    out_toks = []

    def proj(m, wtile, wkey, jj, t0, n, bi, kind):
        pb = 2 + st["ps"] % 2
        st["ps"] += 1
        ps = cm.psb[pb]
        for c in range(16):
            P.op("pe", lambda e, c=c: e.matmul(ps[:, :n], lhsT=wtile[:, c, jj * 128:(jj + 1) * 128], rhs=hT[:, c, t0:t0 + n],
                                                start=(c == 0), stop=(c == 15)),
                 reads=[wkey, ("hT", c, bi)], writes=[("psb", pb)])
        return ps, ("psb", pb)

    for mb in range(48):
        s = st["k"] % 3
        st["k"] += 1
        P.dma("sp", wt[s][:], wqkv[mb], writes=[("wt", s)])
        for jj in range(1):
            m = mb
            kind = "q" if m < 16 else ("k" if m < 32 else "v")
            mi = m % 16
            sg = m % 2
            gcol = 0 if kind == "q" else 1
            for (t0, n, bi) in blocks:
                is_ctx = bi == 4
                if is_ctx and kind == "q":
                    continue
                ps, pk = proj(m, wt[s], ("wt", s), jj, t0, n, bi, kind)
                dst = stgc[sg][:, :n] if is_ctx else stg[sg][:, t0:t0 + n]
                dkey = ("stgc", sg) if is_ctx else ("stg", sg, bi)
                if kind == "v":
                    P.op("act", lambda e, ps=ps, dst=dst, n=n: e.activation(out=dst, in_=ps[:, :n], func=AF.Copy),
                         reads=[pk], writes=[dkey])
                    continue
                u = st["t"] % 2
                st["t"] += 1
                sb = 4 + st["ss"] % 2
                st["ss"] += 1
                pss = cm.psb[sb]
                P.op("act", lambda e, ps=ps, u=u, n=n: e.activation(out=sq[u][:, :n], in_=ps[:, :n], func=AF.Square),
                     reads=[pk], writes=[("sq", u)])
                P.op("pe", lambda e, pss=pss, u=u, n=n: e.matmul(pss[:, :n], lhsT=cm.ones[:], rhs=sq[u][:, :n], start=True, stop=True),
                     reads=[("sq", u), "ones"], writes=[("psb", sb)])
                P.op("act", lambda e, pss=pss, u=u, n=n: e.activation(out=rr[u][:, :n], in_=pss[:, :n], func=AF.Sqrt,
                                                                    scale=1.0 / DH, bias=cm.epsb[:, 0:1]),
                     reads=[("psb", sb), "epsb"], writes=[("rr", u)])
                P.op("dve", lambda e, u=u, n=n: e.reciprocal(out=rr[u][:, :n], in_=rr[u][:, :n]),
                     reads=[("rr", u)], writes=[("rr", u)])
                if is_ctx:
                    P.op("dve", lambda e, ps=ps, u=u, n=n, dst=dst, gcol=gcol: e.scalar_tensor_tensor(
                        out=dst, in0=ps[:, :n], scalar=g_sb[:, gcol:gcol + 1], in1=rr[u][:, :n], op0=ALU.mult, op1=ALU.mult),
                        reads=[pk, "qkg", ("rr", u)], writes=[dkey])
                    continue
                P.op("dve", lambda e, ps=ps, u=u, n=n, gcol=gcol: e.scalar_tensor_tensor(
                    out=qn[u][:, :n], in0=ps[:, :n], scalar=g_sb[:, gcol:gcol + 1], in1=rr[u][:, :n], op0=ALU.mult, op1=ALU.mult),
                    reads=[pk, "qkg", ("rr", u)], writes=[("qn", u)])
                P.op("act", lambda e, u=u, n=n: e.activation(out=sw[u][0:64, :n], in_=qn[u][64:128, :n], func=AF.Copy),
                     reads=[("qn", u)], writes=[("sw", u, 0)])
                P.op("act", lambda e, u=u, n=n: e.activation(out=sw[u][64:128, :n], in_=qn[u][0:64, :n], func=AF.Copy),
                     reads=[("qn", u)], writes=[("sw", u, 1)])
                P.op("pool", lambda e, u=u, n=n, t0=t0: e.tensor_tensor(out=tm[u][:, :n], in0=sw[u][:, :n], in1=sn[:, t0:t0 + n], op=ALU.mult),
                     reads=[("sw", u, 0), ("sw", u, 1), "sn"], writes=[("tm", u)])
                P.op("dve", lambda e, u=u, n=n, t0=t0: e.tensor_tensor(out=oo[u][:, :n], in0=qn[u][:, :n], in1=cs[:, t0:t0 + n], op=ALU.mult),
                     reads=[("qn", u), "cs"], writes=[("oo", u)])
                P.op("dve", lambda e, u=u, n=n, dst=dst: e.tensor_tensor(out=dst, in0=oo[u][:, :n], in1=tm[u][:, :n], op=ALU.add),
                     reads=[("oo", u), ("tm", u)], writes=[dkey])
            od = {"q": qT, "k": kT, "v": vT}[kind]
            out_toks.append(P.dma("sp", od[mi], stg[sg][:], reads=[("stg", sg, bi) for bi in range(4)], writes=[("o", kind, mi)]))
            if kind != "q":
                oc = {"k": kcT, "v": vcT}[kind]
                out_toks.append(P.dma("sp", oc[mi], stgc[sg][:], reads=[("stgc", sg)], writes=[("oc", kind, mi)]))
    P.final_wait("sp", out_toks)
    return P.build()


def rope_perm():
    return np.concatenate([np.arange(0, 128, 2), np.arange(1, 128, 2)])


def rope_tables(tok0, n):
    t = np.arange(tok0, tok0 + n)
    row = (t // GRID_W).astype(np.float32)
    col = (t % GRID_W).astype(np.float32)
    nf = DH // 4
    inv = (10000.0 ** (-np.arange(nf, dtype=np.float32) / nf)).astype(np.float32)
    ang = np.concatenate([row[:, None] * inv, col[:, None] * inv], -1)
    c = np.cos(ang).astype(np.float32).T
    s = np.sin(ang).astype(np.float32).T
    cs = np.concatenate([c, c], 0)
    sn = np.concatenate([-s, s], 0)
    return np.ascontiguousarray(cs), np.ascontiguousarray(sn)


def run_a1(x, ctx, mods_l, normg_l, w_qkv, qk_g):
    nc = build_a1()
    perm = rope_perm()
    w = w_qkv
    qkg = np.ascontiguousarray(qk_g[:, perm].T)
    maps = []
    for i in range(8):
        b, k = i // 4, i % 4
        cs, sn = rope_tables(k * TLAT, TLAT)
        maps.append({"x_lat": to_fm(x[b, k * TLAT:(k + 1) * TLAT]), "x_ctx": to_fm(ctx[b, k * TCTX:(k + 1) * TCTX]),
                     "mod_lat": vec_fm(mods_l[b].reshape(6, 2048)), "mod_ctx": vec_fm(mods_l[2].reshape(6, 2048)),
                     "normg": vec_fm(normg_l), "wqkv": w, "qkg": qkg, "cs": cs, "sn": sn})
    res = run_bass_kernel_spmd(nc, maps, core_ids=list(range(8)))
    return res.results


NKT = (CTX + 8192) // 128
KH = NKT // 2


def build_a2(lambda_init, dbg=0):
    nc = bass.Bass("TRN2", target_bir_lowering=False)
    qT = nc.dram_tensor("qT", [16, 128, TLAT], BF16, kind="ExternalInput").ap()
    kT = nc.dram_tensor("kT", [16, 128, NKT * 128], BF16, kind="ExternalInput").ap()
    vv = nc.dram_tensor("vv", [8, 128, NKT, 256], BF16, kind="ExternalInput").ap()
    x_lat = nc.dram_tensor("x_lat", [128, 16, TLAT], F32, kind="ExternalInput").ap()
    mod_lat = nc.dram_tensor("mod_lat", [128, 6, 16], F32, kind="ExternalInput").ap()
    normg = nc.dram_tensor("normg", [128, 2, 16], F32, kind="ExternalInput").ap()
    lamv = nc.dram_tensor("lamv", [128, 4], F32, kind="ExternalInput").ap()
    sublng = nc.dram_tensor("sublng", [128, 2], F32, kind="ExternalInput").ap()
    w_o = nc.dram_tensor("w_o", [16, 128, 16, 128], BF16, kind="ExternalInput").ap()
    w_in = nc.dram_tensor("w_in", [NJ, 2, 128, 16, 128], BF16, kind="ExternalInput").ap()
    w_out = nc.dram_tensor("w_out", [16, 128, NJ, 128], BF16, kind="ExternalInput").ap()
    out = nc.dram_tensor("out_lat", [128, 16, TLAT], F32, kind="ExternalOutput").ap()

    P = Prog(nc)
    cm = Common(P, 512, halo=0)
    cm.setup_eps()
    ffn = FFN(P, cm, w_in, w_out, 512, nsplit=4, WC=128)
    xb = P.sbuf("xb", [128, 16, 512], F32)
    hT = P.sbuf("hT", [128, 16, 512], BF16)
    ring = [dict(k=P.sbuf("rk%d" % s, [128, 2, KH * 128], BF16), v=P.sbuf("rv%d" % s, [128, KH, 256], BF16)) for s in range(2)]
    qsb = [P.sbuf("qsb%d" % s, [128, 2, 512], BF16) for s in range(2)]
    pT = [P.sbuf("pT%d" % s, [128, 512], BF16) for s in range(3)]
    osb = [P.sbuf("osb%d" % i, [128, 2, 512], F32) for i in range(2)]
    rden = [P.sbuf("rden%d" % i, [128, 512], F32) for i in range(2)]
    dacc = [P.sbuf("dacc%d" % i, [128, 512], F32) for i in range(2)]
    dif = P.sbuf("dif", [128, 2, 512], F32)
    sqd = P.sbuf("sqd", [128, 2, 512], BF16)
    rst = P.sbuf("rst", [128, 512], F32)
    ng = P.sbuf("ng", [128, 2, 16], F32)
    mod = P.sbuf("modsb", [128, 6, 16], F32)
    G2 = P.sbuf("G2", [128, 16], F32)
    lv = P.sbuf("lv", [128, 4], F32)
    lpr = P.sbuf("lpr", [128, 2], F32)
    lex = P.sbuf("lex", [128, 2], F32)
    nlam = P.sbuf("nlam", [128, 1], F32)
    sg = P.sbuf("sg", [128, 2], F32)
    ones32 = P.sbuf("ones32", [128, 128], F32)
    eps256 = cm.epsb
    P.dma("sp", ng[:], normg, writes=["ng"])
    P.dma("sp", mod[:], mod_lat, writes=["mod"])
    P.dma("sp", lv[:], lamv, writes=["lv"])
    P.dma("sp", sg[:], sublng, writes=["sg"])
    P.op("dve", lambda e: e.memset(ones32[:], 1.0), writes=["ones32"])
    P.op("dve", lambda e: e.scalar_tensor_tensor(out=G2[:], in0=mod[:, 4, :], scalar=1.0, in1=ng[:, 1, :],
                                                 op0=ALU.add, op1=ALU.mult), reads=["mod", "ng"], writes=["G2"])
    P.op("dve", lambda e: e.tensor_tensor(out=lpr[:, 0:1], in0=lv[:, 0:1], in1=lv[:, 1:2], op=ALU.mult), reads=["lv"], writes=["lpr0"])
    P.op("dve", lambda e: e.tensor_tensor(out=lpr[:, 1:2], in0=lv[:, 2:3], in1=lv[:, 3:4], op=ALU.mult), reads=["lv"], writes=["lpr1"])
    P.op("pe", lambda e: e.matmul(cm.psb[0][:, 0:2], lhsT=ones32[:], rhs=lpr[:], start=True, stop=True),
         reads=["ones32", "lpr0", "lpr1"], writes=[("psb", 0)])
    P.op("act", lambda e: e.activation(out=lex[:], in_=cm.psb[0][:, 0:2], func=AF.Exp), reads=[("psb", 0)], writes=["lex"])
    P.op("dve", lambda e: e.tensor_tensor(out=nlam[:], in0=lex[:, 1:2], in1=lex[:, 0:1], op=ALU.subtract), reads=["lex"], writes=["nlam"])
    P.op("dve", lambda e: e.tensor_scalar(out=nlam[:], in0=nlam[:], scalar1=-float(lambda_init), scalar2=None, op0=ALU.add),
         reads=["nlam"], writes=["nlam"])
    P.op("dve", lambda e: e.tensor_scalar(out=sg[:], in0=sg[:], scalar1=float(1.0 - lambda_init), scalar2=None, op0=ALU.mult),
         reads=["sg"], writes=["sg"])
    xkeys = [("xb", c) for c in range(16)]
    hkeys = [("hT", c) for c in range(16)]
    out_toks = []
    st = dict(ring=0, q=0, pt=0, sT=0, wo=0, py=0)
    SCALE = float(DH) ** -0.5

    def attn_head(qb, h):
        qs = st["q"] % 2
        st["q"] += 1
        for i in range(2):
            P.dma("sp", qsb[qs][:, i, :], qT[2 * h + i][:, qb * 512:(qb + 1) * 512], writes=[("qsb", qs, i)])
        for half in range(2):
            rs_ = st["ring"] % 2
            st["ring"] += 1
            rg = ring[rs_]
            for i in range(2):
                P.dma("sp", rg["k"][:, i, :], kT[2 * h + i][:, half * KH * 128:(half + 1) * KH * 128], writes=[("rk", rs_, i)])
            P.dma("sp", rg["v"][:], vv[h][:, half * KH:(half + 1) * KH, :], writes=[("rv", rs_)])
            steps = [(i, kt) for i in range(2) for kt in range(KH)]

            def emit_s(i, kt, rg=rg, rs_=rs_):
                sb = st["sT"] % 2
                st["sT"] += 1
                pss = cm.psb[sb]
                P.op("pe", lambda e, pss=pss, rg=rg, i=i, kt=kt: e.matmul(
                    pss[:, :], lhsT=rg["k"][:, i, kt * 128:(kt + 1) * 128], rhs=qsb[qs][:, i, :], start=True, stop=True),
                    reads=[("rk", rs_, i), ("qsb", qs, i)], writes=[("psb", sb)])
                pi = st["pt"] % 3
                st["pt"] += 1
                P.op("act", lambda e, pss=pss, pi=pi: e.activation(out=pT[pi][:], in_=pss[:, :], func=AF.Exp, scale=SCALE),
                     reads=[("psb", sb)], writes=[("pT", pi)])
                return pi

            def emit_pv(i, kt, pi, rg=rg, rs_=rs_, half=half):
                first = (half == 0 and kt == 0)
                last = (half == 1 and kt == KH - 1)
                for ec in range(2):
                    P.op("pe", lambda e, rg=rg, kt=kt, ec=ec, pi=pi, i=i, first=first, last=last: e.matmul(
                        cm.psb[2 + 2 * i + ec][:, :], lhsT=rg["v"][:, kt, ec * 128:(ec + 1) * 128], rhs=pT[pi][:],
                        start=first, stop=last),
                        reads=[("rv", rs_), ("pT", pi)], writes=[("psb", 2 + 2 * i + ec)])
                if first:
                    P.op("dve", lambda e, pi=pi, i=i: e.tensor_copy(out=dacc[i][:], in_=pT[pi][:]),
                         reads=[("pT", pi)], writes=[("dacc", i)])
                else:
                    P.op("dve", lambda e, pi=pi, i=i: e.tensor_tensor(out=dacc[i][:], in0=dacc[i][:], in1=pT[pi][:], op=ALU.add),
                         reads=[("pT", pi), ("dacc", i)], writes=[("dacc", i)])

            pend = emit_s(*steps[0])
            for j in range(len(steps)):
                nxt = emit_s(*steps[j + 1]) if j + 1 < len(steps) else None
                emit_pv(steps[j][0], steps[j][1], pend)
                pend = nxt
        for i in range(2):
            P.op("pe", lambda e, i=i: e.matmul(cm.psb[6 + i][:, :], lhsT=ones32[:], rhs=dacc[i][:], start=True, stop=True),
                 reads=["ones32", ("dacc", i)], writes=[("psb", 6 + i)])
            P.op("dve", lambda e, i=i: e.reciprocal(out=rden[i][:], in_=cm.psb[6 + i][:, :]),
                 reads=[("psb", 6 + i)], writes=[("rden", i)])
            for ec in range(2):
                P.op("dve", lambda e, i=i, ec=ec: e.tensor_tensor(out=osb[i][:, ec, :], in0=cm.psb[2 + 2 * i + ec][:, :],
                                                                  in1=rden[i][:], op=ALU.mult),
                     reads=[("psb", 2 + 2 * i + ec), ("rden", i)], writes=[("osb", i, ec)])
        for ec in range(2):
            P.op("dve", lambda e, ec=ec: e.scalar_tensor_tensor(out=dif[:, ec, :], in0=osb[1][:, ec, :], scalar=nlam[:, 0:1],
                                                                in1=osb[0][:, ec, :], op0=ALU.mult, op1=ALU.add),
                 reads=[("osb", 1, ec), ("osb", 0, ec), "nlam"], writes=[("dif", ec)])
            P.op("act", lambda e, ec=ec: e.activation(out=sqd[:, ec, :], in_=dif[:, ec, :], func=AF.Square),
                 reads=[("dif", ec)], writes=[("sqd", ec)])
        for ec in range(2):
            P.op("pe", lambda e, ec=ec: e.matmul(cm.psb[0][:, :], lhsT=cm.ones[:], rhs=sqd[:, ec, :], start=(ec == 0), stop=(ec == 1)),
                 reads=["ones", ("sqd", ec)], writes=[("psb", 0)])
        P.op("act", lambda e: e.activation(out=rst[:], in_=cm.psb[0][:, :], func=AF.Sqrt, scale=1.0 / 256.0, bias=cm.epsb[:, 0:1]),
             reads=[("psb", 0), "epsb"], writes=["rst"])
        P.op("dve", lambda e: e.reciprocal(out=rst[:], in_=rst[:]), reads=["rst"], writes=["rst"])
        for ec in range(2):
            P.op("dve", lambda e, ec=ec: e.scalar_tensor_tensor(out=hT[:, 2 * h + ec, :], in0=dif[:, ec, :], scalar=sg[:, ec:ec + 1],
                                                                in1=rst[:], op0=ALU.mult, op1=ALU.mult),
                 reads=[("dif", ec), "sg", "rst"], writes=[hkeys[2 * h + ec]])

    def do_block(qb):
        for h in range(NHEAD):
            attn_head(qb, h)
        P.dma("sp", xb[:], x_lat[:, :, qb * 512:(qb + 1) * 512], writes=xkeys)
        for m in range(16):
            s = ffn.kin % 2
            ffn.kin += 1
            wt = ffn.wg[s]
            P.dma("sp", wt[:], w_o[m], writes=[("wg", s)])
            q = ffn.km % 2
            ffn.km += 1
            py = cm.psb[6 + q]
            for c in range(16):
                P.op("pe", lambda e, wt=wt, py=py, c=c: e.matmul(py[:, :], lhsT=wt[:, c, :], rhs=hT[:, c, :],
                                                                  start=(c == 0), stop=(c == 15)),
                     reads=[("wg", s), hkeys[c]], writes=[("psb", 6 + q)])
            P.op("dve", lambda e, py=py, m=m: e.scalar_tensor_tensor(
                out=xb[:, m, :], in0=py[:, :], scalar=mod[:, 2, m:m + 1], in1=xb[:, m, :], op0=ALU.mult, op1=ALU.add),
                reads=[("psb", 6 + q), "mod", xkeys[m]], writes=[xkeys[m]])
        if dbg == 0:
            cm.norm_mod(xb[:, :, :], 512, (G2, "G2"), (mod[:, 3, :], "mod"), lambda c: hT[:, c, :], hkeys, xkeys, stat_bank=0)
            ffn.emit(hT, hkeys, 512, xb, 0, xkeys, (mod[:, 5, :], "mod"))
        out_toks.append(P.dma("sp", out[:, :, qb * 512:(qb + 1) * 512], xb[:], reads=xkeys, writes=[("out", qb)]))

    for qb in range(4):
        do_block(qb)
    P.final_wait("sp", out_toks)
    return P.build()


def run_a2(a1res, x, mods_l, normg_l, lam_vec, subln_g, w_o, w_in, w_out, lambda_init, dbg=0):
    nc = build_a2(lambda_init, dbg)
    maps = []
    kv = []
    for b in range(2):
        kparts = [a1res[b * 4 + k]["kcT"] for k in range(4)] + [a1res[b * 4 + k]["kT"] for k in range(4)]
        kall = np.ascontiguousarray(np.concatenate(kparts, axis=2))
        vparts = [a1res[b * 4 + k]["vcT"] for k in range(4)] + [a1res[b * 4 + k]["vT"] for k in range(4)]
        vall = np.concatenate(vparts, axis=2)
        v5 = vall.reshape(8, 2, 128, NKT, 128)
        v5 = np.ascontiguousarray(v5.transpose(0, 4, 3, 1, 2).reshape(8, 128, NKT, 256))
        kv.append((kall, v5))
    for i in range(8):
        b, k = i // 4, i % 4
        maps.append({"qT": a1res[i]["qT"], "kT": kv[b][0], "vv": kv[b][1], "x_lat": to_fm(x[b, k * TLAT:(k + 1) * TLAT]),
                     "mod_lat": vec_fm(mods_l[b].reshape(6, 2048)), "normg": vec_fm(normg_l),
                     "lamv": np.ascontiguousarray(lam_vec.T), "sublng": np.ascontiguousarray(subln_g.reshape(2, 128).T),
                     "w_o": w_o, "w_in": w_in, "w_out": w_out})
    res = run_bass_kernel_spmd(nc, maps, core_ids=list(range(8)))
    xo = np.zeros_like(x)
    for i in range(8):
        b, k = i // 4, i % 4
        xo[b, k * TLAT:(k + 1) * TLAT] = from_fm(res.results[i]["out_lat"])
    return xo


LW = 96
NSET = 4
LG = 256
C0 = 0.6065306597126334
R1_TCTX = 128


def build_r1(segs=(("lat", TLAT), ("ctx", R1_TCTX)), NB=256):
    nc = bass.Bass("TRN2", target_bir_lowering=False)
    dr = {}
    for name, T in segs:
        dr[name] = dict(
            x=nc.dram_tensor("x_" + name, [128, 16, T + 2], F32, kind="ExternalInput").ap(),
            valid=nc.dram_tensor("valid_" + name, [T + 2], F32, kind="ExternalInput").ap(),
            mod=nc.dram_tensor("mod_" + name, [128, 6, 16], F32, kind="ExternalInput").ap(),
            ot=nc.dram_tensor("ot_" + name, [2, 4, 16, 128, T], BF16, kind="ExternalOutput").ap(),
            pc=nc.dram_tensor("pc_" + name, [128, 2, 16, T // 128], F32, kind="ExternalOutput").ap(),
            v=nc.dram_tensor("v_" + name, [16, 128, T], BF16, kind="ExternalOutput").ap(),
            bonus=nc.dram_tensor("bonus_" + name, [16, 128, T], F32, kind="ExternalOutput").ap(),
            g=nc.dram_tensor("g_" + name, [16, 128, T], F32, kind="ExternalOutput").ap(),
        )
    normg = nc.dram_tensor("normg", [128, 2, 16], F32, kind="ExternalInput").ap()
    mu_d = nc.dram_tensor("mu", [128, 6, 16], F32, kind="ExternalInput").ap()
    w_rkv = nc.dram_tensor("w_rkv", [3, 16, 128, 16, 128], BF16, kind="ExternalInput").ap()
    w_la = nc.dram_tensor("w_la", [2, D, LW], F32, kind="ExternalInput").ap()
    w_lb = nc.dram_tensor("w_lb", [2, LW, D], F32, kind="ExternalInput").ap()
    a_la = nc.dram_tensor("a_la", [2, D, LW], F32, kind="ExternalInput").ap()
    a_lb = nc.dram_tensor("a_lb", [2, LW, D], F32, kind="ExternalInput").ap()
    g_la = nc.dram_tensor("g_la", [D, LG], F32, kind="ExternalInput").ap()
    g_lb = nc.dram_tensor("g_lb", [LG, D], F32, kind="ExternalInput").ap()
    dirvec = nc.dram_tensor("dirvec", [128, 2, 4, 16], F32, kind="ExternalInput").ap()
    rk_d = nc.dram_tensor("r_k", [128, 16], F32, kind="ExternalInput").ap()
    rmask_d = nc.dram_tensor("rmask", [128, NB], F32, kind="ExternalInput").ap()

    P = Prog(nc)
    cm = Common(P, NB, halo=1)
    cm.setup_eps()
    W = NB + 2
    xb = P.sbuf("xb", [128, 16, W], F32)
    xx = P.sbuf("xx", [128, 16, NB], BF16)
    xm = [P.sbuf("xm%d" % i, [128, 16, NB], BF16) for i in range(3)]
    vmask = P.sbuf("vmask", [128, W], F32)
    ng = P.sbuf("ng", [128, 2, 16], F32)
    mu = P.sbuf("mu_sb", [128, 6, 16], F32)
    dv = P.sbuf("dv_sb", [128, 2, 4, 16], F32)
    rk = P.sbuf("rk_sb", [128, 16], F32)
    rmask = P.sbuf("rmask_sb", [128, NB], F32)
    bd = P.sbuf("bd_bf", [128, 128], BF16)
    wla = [P.sbuf("wla%d" % d, [128, 16, LW], BF16) for d in range(2)]
    ala = [P.sbuf("ala%d" % d, [128, 16, LW], BF16) for d in range(2)]
    gla = P.sbuf("gla", [128, 16, LG], BF16)
    wlb = [P.sbuf("wlb%d" % d, [LW, D], BF16) for d in range(2)]
    alb = [P.sbuf("alb%d" % d, [LW, D], BF16) for d in range(2)]
    glb = P.sbuf("glb", [128, 2, D], BF16)
    tw = [P.sbuf("tw%d" % d, [LW, NB], BF16) for d in range(2)]
    al = [P.sbuf("al%d" % d, [LW, NB], BF16) for d in range(2)]
    sgl = P.sbuf("sgl", [128, 2, NB], BF16)
    wt = [P.sbuf("wt%d" % i, [128, 16, 128], BF16) for i in range(6)]
    T2 = {}

    def tmp(name, dt=F32, n=NB):
        if name not in T2:
            T2[name] = P.sbuf("t_" + name, [128, n], dt)
        return T2[name]

    P.dma("sp", ng[:], normg, writes=["ng"])
    P.dma("sp", mu[:], mu_d, writes=["mu"])
    P.dma("sp", dv[:], dirvec, writes=["dv"])
    P.dma("sp", rk[:], rk_d, writes=["rk"])
    P.dma("sp", rmask[:], rmask_d, writes=["rmask"])
    P.op("pool", lambda e: e.memset(bd[:], 0.0), writes=["bd"])
    P.op("pool", lambda e: e.memset(bd[0:64, 0:64], 1.0), writes=["bd"])
    P.op("pool", lambda e: e.memset(bd[64:128, 64:128], 1.0), writes=["bd"])
    for d in range(2):
        P.dma("pool", wla[d][:], w_la[d].rearrange("(c p) n -> p c n", p=128), writes=[("wla", d)])
        P.dma("pool", ala[d][:], a_la[d].rearrange("(c p) n -> p c n", p=128), writes=[("ala", d)])
        P.dma("pool", wlb[d][:], w_lb[d], writes=[("wlb", d)])
        P.dma("pool", alb[d][:], a_lb[d], writes=[("alb", d)])
    P.dma("pool", gla[:], g_la.rearrange("(c p) n -> p c n", p=128), writes=["gla"])
    P.dma("pool", glb[:], g_lb.rearrange("(k p) n -> p k n", p=128), writes=["glb"])

    xkeys = [("xb", c) for c in range(16)]
    bank_rr = [0]

    def nb_():
        b_ = bank_rr[0] % 4 + 2
        bank_rr[0] += 1
        return cm.psb[b_], ("psb", b_)

    out_toks = []
    st = dict(wt=0, stg=0)

    def do_block(name, T, t0, n, mod, G1, pcs_all):
        d_ = dr[name]
        nw = n + 2
        nj = n // 128
        mk = ("mod", name)
        P.dma("sp", xb[:, :, :nw], d_["x"][:, :, t0:t0 + nw], writes=xkeys)
        P.dma("sp", vmask[:, :nw], d_["valid"][t0:t0 + nw].partition_broadcast(128), writes=["vmask"])
        cm.norm_mod(xb[:, :, :nw], nw, (G1, ("G1", name)), (mod[:, 0, :], mk),
                    lambda c: xb[:, c, :nw], xkeys, xkeys, stat_bank=0)
        for col in (0, nw - 1):
            P.op("dve", lambda e, col=col: e.tensor_tensor(
                out=xb[:, :, col:col + 1], in0=xb[:, :, col:col + 1],
                in1=vmask[:, col:col + 1].unsqueeze(1).broadcast_to([128, 16, 1]), op=ALU.mult),
                reads=xkeys + ["vmask"], writes=xkeys)
        for c in range(16):
            tq = tmp("xs%d" % (c % 2))
            P.op("pool", lambda e, c=c, tq=tq: e.tensor_tensor(out=tq[:, :n], in0=xb[:, c, 0:n], in1=xb[:, c, 2:n + 2], op=ALU.add),
                 reads=[xkeys[c]], writes=[("xs", c % 2)])
            P.op("dve", lambda e, c=c, tq=tq: e.scalar_tensor_tensor(out=xx[:, c, :n], in0=tq[:, :n], scalar=0.5, in1=xb[:, c, 1:n + 1],
                                                                     op0=ALU.mult, op1=ALU.subtract),
                 reads=[("xs", c % 2), xkeys[c]], writes=[("xx", c)])

        def mix(m, buf):
            for c in range(16):
                P.op("dve", lambda e, c=c: e.scalar_tensor_tensor(out=xm[buf][:, c, :n], in0=xx[:, c, :n], scalar=mu[:, m, c:c + 1],
                                                                  in1=xb[:, c, 1:n + 1], op0=ALU.mult, op1=ALU.add),
                     reads=[("xx", c), "mu", xkeys[c]], writes=[("xm", buf, c)])

        mix(1, 0)
        for d in range(2):
            bank, bk = nb_()
            for c in range(16):
                P.op("pe", lambda e, c=c, d=d, bank=bank: e.matmul(bank[0:LW, :n], lhsT=wla[d][:, c, :], rhs=xm[0][:, c, :n],
                                                                    start=(c == 0), stop=(c == 15)),
                     reads=[("wla", d), ("xm", 0, c)], writes=[bk])
            P.op("act", lambda e, d=d, bank=bank: e.activation(out=tw[d][:, :n], in_=bank[0:LW, :n], func=AF.Tanh),
                 reads=[bk], writes=[("tw", d)])
        mix(4, 1)
        for d in range(2):
            bank, bk = nb_()
            for c in range(16):
                P.op("pe", lambda e, c=c, d=d, bank=bank: e.matmul(bank[0:LW, :n], lhsT=ala[d][:, c, :], rhs=xm[1][:, c, :n],
                                                                    start=(c == 0), stop=(c == 15)),
                     reads=[("ala", d), ("xm", 1, c)], writes=[bk])
            P.op("act", lambda e, d=d, bank=bank: e.activation(out=al[d][:, :n], in_=bank[0:LW, :n], func=AF.Copy),
                 reads=[bk], writes=[("al", d)])
        mix(5, 2)
        for kc in range(2):
            bank, bk = nb_()
            for c in range(16):
                P.op("pe", lambda e, c=c, kc=kc, bank=bank: e.matmul(bank[:, :n], lhsT=gla[:, c, kc * 128:(kc + 1) * 128], rhs=xm[2][:, c, :n],
                                                                      start=(c == 0), stop=(c == 15)),
                     reads=["gla", ("xm", 2, c)], writes=[bk])
            P.op("act", lambda e, kc=kc, bank=bank: e.activation(out=sgl[:, kc, :n], in_=bank[:, :n], func=AF.Sigmoid),
                 reads=[bk], writes=[("sgl", kc)])
        mix(0, 0)
        mix(2, 1)
        mix(3, 2)
        def chain(c, d, rc_, kc_, vc_, cbank, cbk):
            sid = (2 * c + d) % NSET
            w0 = dv[:, d, 0, c:c + 1]
            a0 = dv[:, d, 1, c:c + 1]
            kk_ = dv[:, d, 2, c:c + 1]
            ka_ = dv[:, d, 3, c:c + 1]
            bank, bk = nb_()
            P.op("pe", lambda e, d=d, c=c, bank=bank: e.matmul(bank[:, :n], lhsT=wlb[d][:, c * 128:(c + 1) * 128], rhs=tw[d][:, :n], start=True, stop=True),
                 reads=[("wlb", d), ("tw", d)], writes=[bk])
            yield
            sg_ = tmp("sg_%d" % sid)
            P.op("act", lambda e, bank=bank, sg_=sg_, w0=w0: e.activation(out=sg_[:, :n], in_=bank[:, :n], func=AF.Sigmoid, bias=w0),
                 reads=[bk, "dv"], writes=[("sg", sid)])
            yield
            bank, bk = nb_()
            P.op("pe", lambda e, d=d, c=c, bank=bank: e.matmul(bank[:, :n], lhsT=alb[d][:, c * 128:(c + 1) * 128], rhs=al[d][:, :n], start=True, stop=True),
                 reads=[("alb", d), ("al", d)], writes=[bk])
            yield
            ag = tmp("ag_%d" % sid)
            P.op("act", lambda e, bank=bank, ag=ag, a0=a0: e.activation(out=ag[:, :n], in_=bank[:, :n], func=AF.Sigmoid, bias=a0),
                 reads=[bk, "dv"], writes=[("ag", sid)])
            yield
            sq = tmp("sqk_%d" % sid, BF16)
            P.op("act", lambda e, sq=sq, kc_=kc_, kk_=kk_: e.activation(out=sq[:, :n], in_=kc_[:, :n], func=AF.Square, scale=kk_),
                 reads=[("rkv", 1, c % 2), "dv"], writes=[("sqk", sid)])
            yield
            bank, bk = nb_()
            P.op("pe", lambda e, sq=sq, bank=bank: e.matmul(bank[:, :n], lhsT=bd[:], rhs=sq[:, :n], start=True, stop=True),
                 reads=["bd", ("sqk", sid)], writes=[bk])
            yield
            rn = tmp("rn_%d" % sid)
            P.op("act", lambda e, rn=rn, bank=bank: e.activation(out=rn[:, :n], in_=bank[:, :n], func=AF.Sqrt), reads=[bk], writes=[("rn", sid)])
            yield
            P.op("dve", lambda e, rn=rn: e.tensor_scalar(out=rn[:, :n], in0=rn[:, :n], scalar1=1e-12, scalar2=None, op0=ALU.max),
                 reads=[("rn", sid)], writes=[("rn", sid)])
            yield
            P.op("dve", lambda e, rn=rn: e.reciprocal(out=rn[:, :n], in_=rn[:, :n]), reads=[("rn", sid)], writes=[("rn", sid)])
            yield
            kkn = tmp("kkn_%d" % sid)
            P.op("dve", lambda e, kkn=kkn, kc_=kc_, rn=rn, kk_=kk_: e.scalar_tensor_tensor(out=kkn[:, :n], in0=kc_[:, :n], scalar=kk_, in1=rn[:, :n],
                                                                                   op0=ALU.mult, op1=ALU.mult),
                 reads=[("rkv", 1, c % 2), ("rn", sid), "dv"], writes=[("kkn", sid)])
            yield
            t1 = tmp("t1_%d" % sid)
            P.op("pool", lambda e, t1=t1, ag=ag, ka_=ka_: e.tensor_scalar(out=t1[:, :n], in0=ag[:, :n], scalar1=-1.0, scalar2=ka_, op0=ALU.add, op1=ALU.mult),
                 reads=[("ag", sid), "dv"], writes=[("t1", sid)])
            yield
            kd = tmp("kd_%d" % sid)
            P.op("dve", lambda e, kd=kd, t1=t1, kc_=kc_: e.scalar_tensor_tensor(out=kd[:, :n], in0=t1[:, :n], scalar=1.0, in1=kc_[:, :n],
                                                                            op0=ALU.add, op1=ALU.mult),
                 reads=[("t1", sid), ("rkv", 1, c % 2)], writes=[("kd", sid)])
            yield
            bs = tmp("bs_%d" % sid)
            P.op("pool", lambda e, bs=bs, kkn=kkn, ag=ag: e.tensor_tensor(out=bs[:, :n], in0=kkn[:, :n], in1=ag[:, :n], op=ALU.mult),
                 reads=[("kkn", sid), ("ag", sid)], writes=[("bs", sid)])
            yield
            Lf = tmp("Lf_%d" % sid)
            P.op("dve", lambda e, Lf=Lf, sg_=sg_: e.tensor_tensor_scan(out=Lf[:, :n], data0=rmask[:, :n], data1=sg_[:, :n], initial=0.0,
                                                                      op0=ALU.mult, op1=ALU.add),
                 reads=["rmask", ("sg", sid)], writes=[("Lf", sid)])
            yield
            Li = tmp("Li_%d" % sid)
            Lx = tmp("Lx_%d" % sid)
            Lf3 = Lf[:, :n].rearrange("p (j t) -> p j t", t=128)
            tot = Lf3[:, :, 127:128].broadcast_to([128, nj, 128])
            if d == 0:
                P.op("pool", lambda e, Lx=Lx, Lf=Lf, sg_=sg_: e.tensor_tensor(out=Lx[:, :n], in0=Lf[:, :n], in1=sg_[:, :n], op=ALU.subtract),
                     reads=[("Lf", sid), ("sg", sid)], writes=[("Lx", sid)])
                yield
                Li = Lf
                lik = ("Lf", sid)
            else:
                P.op("pool", lambda e, Lx=Lx, Lf3=Lf3, tot=tot: e.tensor_tensor(out=Lx[:, :n].rearrange("p (j t) -> p j t", t=128), in0=tot, in1=Lf3,
                                                                                op=ALU.subtract),
                     reads=[("Lf", sid)], writes=[("Lx", sid)])
                yield
                P.op("pool", lambda e, Li=Li, Lx=Lx, sg_=sg_: e.tensor_tensor(out=Li[:, :n], in0=Lx[:, :n], in1=sg_[:, :n], op=ALU.add),
                     reads=[("Lx", sid), ("sg", sid)], writes=[("Li", sid)])
                yield
                lik = ("Li", sid)
            ep = tmp("ep_%d" % sid)
            en = tmp("en_%d" % sid)
            ex = tmp("ex_%d" % sid)
            P.op("act", lambda e, ep=ep, Li=Li: e.activation(out=ep[:, :n], in_=Li[:, :n], func=AF.Exp, scale=-C0), reads=[lik], writes=[("ep", sid)])
            yield
            P.op("act", lambda e, en=en, Li=Li: e.activation(out=en[:, :n], in_=Li[:, :n], func=AF.Exp, scale=C0), reads=[lik], writes=[("en", sid)])
            yield
            P.op("act", lambda e, ex=ex, Lx=Lx: e.activation(out=ex[:, :n], in_=Lx[:, :n], func=AF.Exp, scale=-C0), reads=[("Lx", sid)], writes=[("ex", sid)])
            yield
            P.op("act", lambda e, d=d, c=c, Lf3=Lf3: e.activation(out=pcs_all[:, d, c, t0 // 128:t0 // 128 + nj], in_=Lf3[:, :, 127], func=AF.Exp, scale=-C0),
                 reads=[("Lf", sid)], writes=[("pcs", name)])
            yield
            stg = tmp("stg%d" % sid, BF16, 4 * NB)
            sk = ("stg", sid)
            P.op("dve", lambda e, stg=stg, kkn=kkn, ex=ex: e.scalar_tensor_tensor(out=stg[:, 0:n], in0=kkn[:, :n], scalar=-1.0, in1=ex[:, :n],
                                                                              op0=ALU.mult, op1=ALU.mult),
                 reads=[("kkn", sid), ("ex", sid)], writes=[sk])
            yield
            P.op("pool", lambda e, stg=stg, bs=bs, en=en: e.tensor_tensor(out=stg[:, NB:NB + n], in0=bs[:, :n], in1=en[:, :n], op=ALU.mult),
                 reads=[("bs", sid), ("en", sid)], writes=[sk])
            yield
            P.op("pool", lambda e, stg=stg, kd=kd, en=en: e.tensor_tensor(out=stg[:, 2 * NB:2 * NB + n], in0=kd[:, :n], in1=en[:, :n], op=ALU.mult),
                 reads=[("kd", sid), ("en", sid)], writes=[sk])
            yield
            P.op("pool", lambda e, stg=stg, rc_=rc_, ep=ep: e.tensor_tensor(out=stg[:, 3 * NB:3 * NB + n], in0=rc_[:, :n], in1=ep[:, :n], op=ALU.mult),
                 reads=[("rkv", 0, c % 2), ("ep", sid)], writes=[sk])
            yield
            out_toks.append(P.dma("pool", d_["ot"][d][:, c, :, t0:t0 + n].rearrange("o p n -> p o n"),
                                  stg[:].rearrange("p (o n) -> p o n", o=4)[:, :, :n], reads=[sk], writes=[("oo", name, d, c, t0)]))
            yield
            rkq = tmp("rkq_%d" % sid, BF16)
            P.op("dve", lambda e, rkq=rkq, rc_=rc_, kd=kd, c=c: e.scalar_tensor_tensor(out=rkq[:, :n], in0=rc_[:, :n], scalar=rk[:, c:c + 1], in1=kd[:, :n],
                                                                                   op0=ALU.mult, op1=ALU.mult),
                 reads=[("rkv", 0, c % 2), ("kd", sid), "rk"], writes=[("rkq", sid)])
            yield
            P.op("pe", lambda e, rkq=rkq, d=d, cbank=cbank: e.matmul(cbank[:, :n], lhsT=bd[:], rhs=rkq[:, :n], start=(d == 0), stop=(d == 1)),
                 reads=["bd", ("rkq", sid)], writes=[cbk])
            yield


        def prologue(c):
            outs = []
            for wi in range(3):
                s = st["wt"] % 6
                st["wt"] += 1
                P.dma("sp", wt[s][:], w_rkv[wi, c], writes=[("wt", s)])
                bank, bk = nb_()
                for kc in range(16):
                    P.op("pe", lambda e, kc=kc, s=s, wi=wi, bank=bank: e.matmul(bank[:, :n], lhsT=wt[s][:, kc, :], rhs=xm[wi][:, kc, :n],
                                                                                 start=(kc == 0), stop=(kc == 15)),
                         reads=[("wt", s), ("xm", wi, kc)], writes=[bk])
                dst = tmp(("rc", "kc", "vc")[wi] + str(c % 2))
                P.op("act", lambda e, dst=dst, bank=bank: e.activation(out=dst[:, :n], in_=bank[:, :n], func=AF.Copy),
                     reads=[bk], writes=[("rkv", wi, c % 2)])
                outs.append(dst)
            rc_, kc_, vc_ = outs
            vst = tmp("vst%d" % (c % 2), BF16)
            P.op("pool", lambda e, vst=vst, vc_=vc_: e.tensor_copy(out=vst[:, :n], in_=vc_[:, :n]), reads=[("rkv", 2, c % 2)], writes=[("vst", c % 2)])
            out_toks.append(P.dma("pool", d_["v"][c][:, t0:t0 + n], vst[:, :n], reads=[("vst", c % 2)], writes=[("ov", name, c, t0)]))
            bank, bk = nb_()
            for kc in range(2):
                P.op("pe", lambda e, kc=kc, c=c, bank=bank: e.matmul(bank[:, :n], lhsT=glb[:, kc, c * 128:(c + 1) * 128], rhs=sgl[:, kc, :n],
                                                                      start=(kc == 0), stop=(kc == 1)),
                     reads=["glb", ("sgl", 0), ("sgl", 1)], writes=[bk])
            gst = tmp("gst%d" % (c % 2))
            P.op("act", lambda e, gst=gst, bank=bank: e.activation(out=gst[:, :n], in_=bank[:, :n], func=AF.Copy), reads=[bk], writes=[("gst", c % 2)])
            out_toks.append(P.dma("act", d_["g"][c][:, t0:t0 + n], gst[:, :n], reads=[("gst", c % 2)], writes=[("og", name, c, t0)]))
            cb_ = 6 + (c % 2)
            cbank, cbk = cm.psb[cb_], ("psb", cb_)
            return rc_, kc_, vc_, cbank, cbk

        def epilogue(c, rc_, kc_, vc_, cbank, cbk):
            bst = tmp("bst%d" % (c % 2))
            P.op("dve", lambda e, bst=bst, vc_=vc_, cbank=cbank: e.tensor_tensor(out=bst[:, :n], in0=cbank[:, :n], in1=vc_[:, :n], op=ALU.mult),
                 reads=[cbk, ("rkv", 2, c % 2)], writes=[("bst", c % 2)])
            out_toks.append(P.dma("act", d_["bonus"][c][:, t0:t0 + n], bst[:, :n], reads=[("bst", c % 2)], writes=[("ob", name, c, t0)]))

        for cp in range(0, 16, 2):
            units = []
            for c in (cp, cp + 1):
                units.append((c,) + prologue(c))
            gens = [chain(u[0], d, *u[1:]) for u in units for d in range(2)]
            while gens:
                alive = []
                for g_ in gens:
                    try:
                        next(g_)
                        alive.append(g_)
                    except StopIteration:
                        pass
                gens = alive
            for u in units:
                epilogue(*u)

    for name, T in segs:
        mod = P.sbuf("modsb_" + name, [128, 6, 16], F32)
        G1 = P.sbuf("G1_" + name, [128, 16], F32)
        mk = ("mod", name)
        P.dma("sp", mod[:], dr[name]["mod"], writes=[mk])
        P.op("dve", lambda e, G1=G1, mod=mod: e.scalar_tensor_tensor(
            out=G1[:], in0=mod[:, 1, :], scalar=1.0, in1=ng[:, 0, :], op0=ALU.add, op1=ALU.mult),
            reads=[mk, "ng"], writes=[("G1", name)])
        pcs_all = P.sbuf("pcs_" + name, [128, 2, 16, T // 128], F32)
        for t0 in range(0, T, NB):
            do_block(name, T, t0, min(NB, T - t0), mod, G1, pcs_all)
        out_toks.append(P.dma("sp", dr[name]["pc"], pcs_all[:], reads=[("pcs", name)], writes=[("opc", name)]))
    P.final_wait("sp", out_toks)
    return P.build()


NCH = 66
NFC = 8
OPA, OPB, OPK, OPR = 0, 1, 2, 3


def build_r2(nch=NCH):
    nc = bass.Bass("TRN2", target_bir_lowering=False)
    fm = nc.dram_tensor("fm", [4, nch, 128, NFC, 128], BF16, kind="ExternalInput").ap()
    tmj = nc.dram_tensor("tm", [3, nch, 128, NFC, 128], BF16, kind="ExternalInput").ap()
    pcd = nc.dram_tensor("pc", [nch, 128, NFC], F32, kind="ExternalInput").ap()
    m4d = nc.dram_tensor("m4", [128, 2, 512], F32, kind="ExternalInput").ap()
    mld = nc.dram_tensor("ml", [128, 512], F32, kind="ExternalInput").ap()
    idd = nc.dram_tensor("idm", [128, 2, 128], F32, kind="ExternalInput").ap()
    mbd = nc.dram_tensor("mbd", [128, 512], F32, kind="ExternalInput").ap()
    yout = nc.dram_tensor("y", [nch, 128, NFC, 128], F32, kind="ExternalOutput").ap()

    P = Prog(nc)
    psb = [P.psum("psb%d" % i, [128, 512]) for i in range(8)]
    m4 = P.sbuf("m4s", [128, 2, 512], F32)
    ml = P.sbuf("mls", [128, 512], F32)
    idm = P.sbuf("ids", [128, 2, 128], F32)
    mbd4 = P.sbuf("mbds", [128, 512], F32)
    P.dma("sp", m4[:], m4d, writes=["m4"])
    P.dma("sp", ml[:], mld, writes=["ml"])
    P.dma("sp", idm[:], idd, writes=["idm"])
    P.dma("sp", mbd4[:], mbd, writes=["mbd4"])
    slots = []
    for s in range(2):
        d = dict(
            fa=P.sbuf("fa%d" % s, [128, NFC, 128], BF16), fb=P.sbuf("fb%d" % s, [128, NFC, 128], BF16),
            fr=P.sbuf("fr%d" % s, [128, NFC, 128], BF16),
            pa=[P.sbuf("pa%d_%d" % (s, h), [128, NFC, 128], BF16) for h in range(2)],
            pb=[P.sbuf("pb%d_%d" % (s, h), [128, NFC, 128], BF16) for h in range(2)],
            pk=[P.sbuf("pk%d_%d" % (s, h), [128, NFC, 128], BF16) for h in range(2)],
            tB=P.sbuf("tB%d" % s, [128, NFC, 128], BF16), tK=P.sbuf("tK%d" % s, [128, NFC, 128], BF16),
            tV=P.sbuf("tV%d" % s, [128, NFC, 128], BF16), pc=P.sbuf("pcs%d" % s, [128, NFC], F32),
        )
        for nm in ("pa", "pb", "pk"):
            for h in range(2):
                P.op("pool", lambda e, t=d[nm][h]: e.memset(t[:], 0.0), writes=[(nm, s, h)])
        slots.append(d)
    Hf = P.sbuf("Hf", [128, NFC, 128], F32)
    Hb = P.sbuf("Hb", [128, NFC, 128], BF16)
    P.op("dve", lambda e: e.memset(Hf[:], 0.0), writes=[("Hf", f) for f in range(NFC // 4)])
    P.op("dve", lambda e: e.memset(Hb[:], 0.0), writes=[("Hb", f) for f in range(NFC // 4)])
    A4p = [[P.sbuf("A4_%d_%d" % (f, par), [128, 2, 512], BF16) for f in range(NFC)] for par in range(2)]
    NN = [P.sbuf("NN_%d" % f, [128, 4, 128], F32) for f in range(NFC)]
    X = [[P.sbuf("X%d_%d" % (f, i), [128, 512], F32) for i in range(2)] for f in range(NFC // 2)]
    XT = [[P.sbuf("XT%d_%d" % (f, i), [128, 512], F32) for i in range(2)] for f in range(NFC // 2)]
    TTf = [[P.sbuf("TTf%d_%d" % (f, i), [128, 2, 128], F32) for i in range(2)] for f in range(NFC)]
    TTb = [[P.sbuf("TTb%d_%d" % (f, par), [128, 2, 128], BF16) for f in range(NFC)] for par in range(2)]
    Wsb = [P.sbuf("W%d" % f, [128, 512], BF16) for f in range(NFC // 4)]
    Usb = [P.sbuf("U%d" % f, [128, 512], BF16) for f in range(NFC // 4)]
    Ht = [P.sbuf("Ht%d" % i, [128, 512], F32) for i in range(NFC // 4)]
    yst = [P.sbuf("yst%d" % i, [128, NFC, 128], F32) for i in range(2)]
    out_toks = []

    def load(t):
        s = t % 2
        d = slots[s]
        P.dma("sp", d["fa"][:], fm[OPA, t], writes=[("fa", s)])
        P.dma("sp", d["fb"][:], fm[OPB, t], writes=[("fb", s)])
        P.dma("sp", d["fr"][:], fm[OPR, t], writes=[("fr", s)])
        for h in range(2):
            hp = slice(64 * h, 64 * h + 64)
            P.dma("sp", d["pa"][h][hp, :, :], fm[OPA, t, hp], writes=[("pa", s, h)])
            P.dma("sp", d["pb"][h][hp, :, :], fm[OPB, t, hp], writes=[("pb", s, h)])
            P.dma("sp", d["pk"][h][hp, :, :], fm[OPK, t, hp], writes=[("pk", s, h)])
        P.dma("sp", d["tB"][:], tmj[0, t], writes=[("tB", s)])
        P.dma("sp", d["tK"][:], tmj[1, t], writes=[("tK", s)])
        P.dma("sp", d["tV"][:], tmj[2, t], writes=[("tV", s)])
        P.dma("sp", d["pc"][:], pcd[t], writes=[("pc", s)])

    bank_rr = [0]

    def nb():
        b_ = bank_rr[0] % 8
        bank_rr[0] += 1
        return psb[b_], ("psb", b_)

    def stage_A(t):
        par = t % 2
        A4 = A4p[par]
        s = t % 2
        d = slots[s]
        for f in range(NFC):
            for h in range(2):
                bank, bk = nb()
                specs = [(d["pb"][h], ("pb", s, h), d["fa"], ("fa", s)),
                         (d["pk"][h], ("pk", s, h), d["fa"], ("fa", s)),
                         (d["pb"][h], ("pb", s, h), d["fr"], ("fr", s)),
                         (d["pk"][h], ("pk", s, h), d["fr"], ("fr", s))]
                for q, (lt, lk, rt, rk) in enumerate(specs):
                    P.op("pe", lambda e, bank=bank, q=q, lt=lt, rt=rt, f=f: e.matmul(
                        bank[:, q * 128:(q + 1) * 128], lhsT=lt[:, f, :], rhs=rt[:, f, :], start=True, stop=True),
                        reads=[lk, rk], writes=[bk])
                P.op("dve", lambda e, bank=bank, f=f, h=h: e.tensor_tensor(out=A4[f][:, h, :], in0=bank[:, :], in1=m4[:, h, :], op=ALU.mult),
                     reads=[bk, "m4"], writes=[("A4", par, f, h)])
            bank, bk = nb()
            for h in range(2):
                P.op("pe", lambda e, h=h, f=f, bank=bank: e.matmul(
                    bank[:, h * 128:(h + 1) * 128], lhsT=d["pb"][h][:, f, :], rhs=d["fa"][:, f, :], start=True, stop=True),
                    reads=[("pb", s, h), ("fa", s)], writes=[bk])
            for h in range(2):
                P.op("pe", lambda e, h=h, f=f, bank=bank: e.matmul(
                    bank[:, 256 + h * 128:256 + (h + 1) * 128], lhsT=d["pa"][h][:, f, :], rhs=d["fb"][:, f, :], start=True, stop=True),
                    reads=[("pa", s, h), ("fb", s)], writes=[bk])
            P.op("dve", lambda e, f=f, bank=bank: e.tensor_tensor(out=NN[f][:].rearrange("p q c -> p (q c)"), in0=bank[:, :], in1=ml[:], op=ALU.mult),
                 reads=[bk, "ml"], writes=[("NN", f)])
            P.op("pool", lambda e, f=f: e.tensor_tensor(out=TTf[f][0][:], in0=NN[f][:, 0:2, :], in1=idm[:], op=ALU.add),
                 reads=[("NN", f), "idm"], writes=[("TTf", f, 0)])

    def stage_D(t):
        par = t % 2
        for lvl in range(1, 7):
            pi, po = (lvl - 1) % 2, lvl % 2
            for fp in range(NFC // 2):
                def xin(ff, h, fp=fp, pi=pi, lvl=lvl):
                    if lvl == 1:
                        return NN[2 * fp + ff][:, 2 + h, :]
                    return X[fp][pi][:, ff * 256 + h * 128: ff * 256 + (h + 1) * 128]

                def xtin(ff, h, fp=fp, pi=pi, lvl=lvl):
                    if lvl == 1:
                        return NN[2 * fp + ff][:, h, :]
                    return XT[fp][pi][:, ff * 256 + h * 128: ff * 256 + (h + 1) * 128]
                if lvl == 1:
                    xk = [("NN", 2 * fp), ("NN", 2 * fp + 1)]
                    xtk = []
                else:
                    xk = [("X", fp, pi)]
                    xtk = [("XT", fp, pi)]
                bank, bk = nb()
                for ff in range(2):
                    for h in range(2):
                        P.op("pe", lambda e, h=h, ff=ff, xin=xin, xtin=xtin, bank=bank: e.matmul(
                            bank[:, ff * 256 + h * 128: ff * 256 + (h + 1) * 128], lhsT=xtin(ff, h), rhs=xin(ff, h), start=True, stop=True),
                            reads=xk + xtk, writes=[bk])
                P.op("act", lambda e, fp=fp, po=po, bank=bank: e.activation(out=X[fp][po][:], in_=bank[:, :], func=AF.Copy),
                     reads=[bk], writes=[("X", fp, po)])
                if lvl < 6:
                    bank2, bk2 = nb()
                    for ff in range(2):
                        for h in range(2):
                            P.op("pe", lambda e, h=h, ff=ff, xin=xin, xtin=xtin, bank2=bank2: e.matmul(
                                bank2[:, ff * 256 + h * 128: ff * 256 + (h + 1) * 128], lhsT=xin(ff, h), rhs=xtin(ff, h), start=True, stop=True),
                                reads=xk + xtk, writes=[bk2])
                    P.op("act", lambda e, fp=fp, po=po, bank2=bank2: e.activation(out=XT[fp][po][:], in_=bank2[:, :], func=AF.Copy),
                         reads=[bk2], writes=[("XT", fp, po)])
            for fp in range(NFC // 2):
                bank, bk = nb()
                for ff in range(2):
                    f = 2 * fp + ff
                    for h in range(2):
                        P.op("pe", lambda e, h=h, f=f, ff=ff, fp=fp, po=po, pi=pi, bank=bank: e.matmul(
                            bank[:, ff * 256 + h * 128: ff * 256 + (h + 1) * 128],
                            lhsT=X[fp][po][:, ff * 256 + h * 128: ff * 256 + (h + 1) * 128], rhs=TTf[f][pi][:, h, :], start=True, stop=True),
                            reads=[("X", fp, po), ("TTf", f, pi)], writes=[bk])
                for ff in range(2):
                    f = 2 * fp + ff
                    if lvl < 6:
                        P.op("dve", lambda e, f=f, ff=ff, po=po, pi=pi, bank=bank: e.tensor_tensor(
                            out=TTf[f][po][:].rearrange("p h c -> p (h c)"), in0=bank[:, ff * 256:(ff + 1) * 256],
                            in1=TTf[f][pi][:].rearrange("p h c -> p (h c)"), op=ALU.add),
                            reads=[bk, ("TTf", f, pi)], writes=[("TTf", f, po)])
                    else:
                        P.op("dve", lambda e, f=f, ff=ff, po=po, pi=pi, bank=bank: e.tensor_tensor(
                            out=TTb[par][f][:].rearrange("p h c -> p (h c)"), in0=bank[:, ff * 256:(ff + 1) * 256],
                            in1=TTf[f][pi][:].rearrange("p h c -> p (h c)"), op=ALU.add),
                            reads=[bk, ("TTf", f, pi)], writes=[("TTb", par, f)])
            yield

    def stage_S(t):
        par = t % 2
        A4 = A4p[par]
        s = t % 2
        d = slots[s]
        ys = yst[t % 2]
        NG = NFC // 4
        for g in range(NG):
            bank, bk = nb()
            for fi in range(4):
                f = 4 * g + fi
                off = fi * 128
                P.op("pe", lambda e, f=f, off=off, bank=bank: e.matmul(bank[:, off:off + 128], lhsT=d["fa"][:, f, :], rhs=Hb[:, f, :], start=True, stop=False),
                     reads=[("fa", s), ("Hb", g)], writes=[bk])
                for h in range(2):
                    P.op("pe", lambda e, f=f, h=h, off=off, bank=bank: e.matmul(bank[:, off + 64 * h: off + 64 * h + 64], lhsT=A4[f][:, h, 128:256],
                                                                                 rhs=d["tV"][:, f, 64 * h:64 * h + 64], start=False, stop=(h == 1)),
                         reads=[("A4", par, f, h), ("tV", s)], writes=[bk])
            P.op("act", lambda e, g=g, bank=bank: e.activation(out=Wsb[g][:], in_=bank[:, :], func=AF.Copy),
                 reads=[bk], writes=[("W", g)])
        yield
        for g in range(NG):
            bank, bk = nb()
            for fi in range(4):
                f = 4 * g + fi
                off = fi * 128
                for h in range(2):
                    P.op("pe", lambda e, f=f, g=g, h=h, off=off, bank=bank: e.matmul(bank[:, off + 64 * h: off + 64 * h + 64], lhsT=TTb[par][f][:, h, :],
                                                                                      rhs=Wsb[g][:, off + 64 * h: off + 64 * h + 64], start=True, stop=True),
                         reads=[("TTb", par, f), ("W", g)], writes=[bk])
            P.op("act", lambda e, g=g, bank=bank: e.activation(out=Usb[g][:], in_=bank[:, :], func=AF.Copy),
                 reads=[bk], writes=[("U", g)])
        yield
        for g in range(NG):
            bank, bk = nb()
            for fi in range(4):
                f = 4 * g + fi
                off = fi * 128
                P.op("pe", lambda e, f=f, off=off, bank=bank: e.matmul(bank[:, off:off + 128], lhsT=d["fr"][:, f, :], rhs=Hb[:, f, :], start=True, stop=False),
                     reads=[("fr", s), ("Hb", g)], writes=[bk])
                for h in range(2):
                    P.op("pe", lambda e, f=f, g=g, h=h, off=off, bank=bank: e.matmul(bank[:, off + 64 * h: off + 64 * h + 64], lhsT=A4[f][:, h, 256:384],
                                                                                      rhs=Usb[g][:, off + 64 * h: off + 64 * h + 64], start=False, stop=False),
                         reads=[("A4", par, f, h), ("U", g)], writes=[bk])
                    P.op("pe", lambda e, f=f, h=h, off=off, bank=bank: e.matmul(bank[:, off + 64 * h: off + 64 * h + 64], lhsT=A4[f][:, h, 384:512],
                                                                                 rhs=d["tV"][:, f, 64 * h:64 * h + 64], start=False, stop=(h == 1)),
                         reads=[("A4", par, f, h), ("tV", s)], writes=[bk])
            P.op("act", lambda e, g=g, bank=bank, ys=ys: e.activation(out=ys[:, 4 * g:4 * g + 4, :].rearrange("p f c -> p (f c)"), in_=bank[:, :], func=AF.Copy),
                 reads=[bk], writes=[("yst", t % 2, g)])
        yield
        for g in range(NG):
            bank, bk = nb()
            for fi in range(4):
                f = 4 * g + fi
                off = fi * 128
                P.op("pe", lambda e, f=f, g=g, off=off, bank=bank: e.matmul(bank[:, off:off + 128], lhsT=d["tB"][:, f, :], rhs=Usb[g][:, off:off + 128], start=True, stop=False),
                     reads=[("tB", s), ("U", g)], writes=[bk])
                P.op("pe", lambda e, f=f, off=off, bank=bank: e.matmul(bank[:, off:off + 128], lhsT=d["tK"][:, f, :], rhs=d["tV"][:, f, :], start=False, stop=True),
                     reads=[("tK", s), ("tV", s)], writes=[bk])
            hf_g = Hf[:, 4 * g:4 * g + 4, :]
            P.op("dve", lambda e, g=g, bank=bank: e.tensor_tensor(out=Ht[g][:], in0=bank[:, :], in1=mbd4[:], op=ALU.mult),
                 reads=[bk, "mbd4"], writes=[("Ht", g)])
            P.op("pool", lambda e, g=g, hf_g=hf_g: e.tensor_tensor(out=hf_g, in0=Ht[g][:].rearrange("p (f c) -> p f c", f=4), in1=hf_g, op=ALU.add),
                 reads=[("Ht", g), ("Hf", g)], writes=[("Hf", g)])
            P.op("pool", lambda e, g=g, hf_g=hf_g: e.tensor_tensor(out=hf_g, in0=hf_g, in1=d["pc"][:, 4 * g:4 * g + 4].unsqueeze(2).broadcast_to([128, 4, 128]), op=ALU.mult),
                 reads=[("Hf", g), ("pc", s)], writes=[("Hf", g)])
            P.op("act", lambda e, g=g, hf_g=hf_g: e.activation(out=Hb[:, 4 * g:4 * g + 4, :], in_=hf_g, func=AF.Copy),
                 reads=[("Hf", g)], writes=[("Hb", g)])
        out_toks.append(P.dma("sp", yout[t], ys[:], reads=[("yst", t % 2, g) for g in range(NG)], writes=[("yo", t)]))

    def drain(g_):
        for _ in g_:
            pass

    load(0)
    stage_A(0)
    drain(stage_D(0))
    for t in range(nch):
        gs = stage_S(t)
        if t + 1 < nch:
            load(t + 1)
            stage_A(t + 1)
            gd = stage_D(t + 1)
            for lvl_i in range(6):
                if lvl_i in (0, 1, 3, 4):
                    next(gs, None)
                next(gd, None)
            drain(gd)
        drain(gs)
    P.final_wait("sp", out_toks)
    return P.build()


def r2_consts():
    i = np.arange(128)[:, None]
    c = np.arange(128)[None, :]
    su = (c > i).astype(np.float32)
    ue = (c >= i).astype(np.float32)
    m4h = np.concatenate([su, su, ue, ue], 1)
    m4 = np.stack([m4h, m4h], 1)
    sl = (c < i).astype(np.float32)
    ml = np.concatenate([su, su, sl, sl], 1)
    idm = np.stack([np.eye(128, dtype=np.float32)] * 2, 1)
    bd = np.zeros((128, 128), np.float32)
    bd[:64, :64] = 1
    bd[64:, 64:] = 1
    bd = np.concatenate([bd] * 4, 1)
    return dict(m4=np.ascontiguousarray(m4), ml=np.ascontiguousarray(ml), idm=np.ascontiguousarray(idm), mbd=bd)


LN_X_EPS = 64e-5


def build_r3(segs=(("lat", 2048), ("ctx", 64)), TB=512):
    nc = bass.Bass("TRN2", target_bir_lowering=False)
    dr = {}
    for name, T in segs:
        dr[name] = dict(
            x=nc.dram_tensor("x_" + name, [128, 16, T], F32, kind="ExternalInput").ap(),
            y=nc.dram_tensor("y_" + name, [2, 16, 128, T], F32, kind="ExternalInput").ap(),
            bonus=nc.dram_tensor("bonus_" + name, [16, 128, T], F32, kind="ExternalInput").ap(),
            g=nc.dram_tensor("g_" + name, [16, 128, T], F32, kind="ExternalInput").ap(),
            mod=nc.dram_tensor("mod_" + name, [128, 6, 16], F32, kind="ExternalInput").ap(),
            out=nc.dram_tensor("out_" + name, [128, 16, T], F32, kind="ExternalOutput").ap(),
        )
    normg = nc.dram_tensor("normg", [128, 2, 16], F32, kind="ExternalInput").ap()
    lnx_d = nc.dram_tensor("lnx", [128, 2, 16], F32, kind="ExternalInput").ap()
    w_o = nc.dram_tensor("w_o", [16, 128, 16, 128], BF16, kind="ExternalInput").ap()
    w_in = nc.dram_tensor("w_in", [NJ, 2, 128, 16, 128], BF16, kind="ExternalInput").ap()
    w_out = nc.dram_tensor("w_out", [16, 128, NJ, 128], BF16, kind="ExternalInput").ap()

    P = Prog(nc)
    cm = Common(P, TB, halo=0)
    cm.setup_eps()
    ffn = FFN(P, cm, w_in, w_out, TB, nsplit=2, WC=128)
    xb = P.sbuf("xb", [128, 16, TB], F32)
    hT = P.sbuf("hT", [128, 16, TB], BF16)
    ng = P.sbuf("ng", [128, 2, 16], F32)
    lnx = P.sbuf("lnx_sb", [128, 2, 16], F32)
    bd64 = P.sbuf("bd64", [128, 128], F32)
    epsl = P.sbuf("epsl", [128, 1], F32)
    inb = [dict(y0=P.sbuf("y0_%d" % i, [128, TB], F32), y1=P.sbuf("y1_%d" % i, [128, TB], F32),
                bo=P.sbuf("bo_%d" % i, [128, TB], F32), g=P.sbuf("g_%d" % i, [128, TB], F32)) for i in range(2)]
    ysum = P.sbuf("ysum", [128, TB], F32)
    cen = P.sbuf("cen", [128, TB], F32)
    sq = P.sbuf("sq", [128, TB], F32)
    rstd = P.sbuf("rstd", [128, TB], F32)
    yn = P.sbuf("yn", [128, TB], F32)
    oo = P.sbuf("oo", [128, TB], F32)
    P.dma("sp", ng[:], normg, writes=["ng"])
    P.dma("sp", lnx[:], lnx_d, writes=["lnx"])
    P.op("pool", lambda e: e.memset(bd64[:], 0.0), writes=["bd64"])
    P.op("pool", lambda e: e.memset(bd64[0:64, 0:64], 1.0 / 64), writes=["bd64"])
    P.op("pool", lambda e: e.memset(bd64[64:128, 64:128], 1.0 / 64), writes=["bd64"])
    P.op("pool", lambda e: e.memset(epsl[:], LN_X_EPS), writes=["epsl"])
    xkeys = [("xb", c) for c in range(16)]
    hkeys = [("hT", c) for c in range(16)]
    out_toks = []
    st = dict(i=0)

    def do_block(name, t0, n, mod, G2):
        d_ = dr[name]
        mk = ("mod", name)
        P.dma("sp", xb[:, :, :n], d_["x"][:, :, t0:t0 + n], writes=xkeys)
        for c in range(16):
            s = st["i"] % 2
            st["i"] += 1
            ib = inb[s]
            P.dma("sp", ib["y0"][:, :n], d_["y"][0, c][:, t0:t0 + n], writes=[("y0", s)])
            P.dma("sp", ib["y1"][:, :n], d_["y"][1, c][:, t0:t0 + n], writes=[("y1", s)])
            P.dma("sp", ib["bo"][:, :n], d_["bonus"][c][:, t0:t0 + n], writes=[("bo", s)])
            P.dma("sp", ib["g"][:, :n], d_["g"][c][:, t0:t0 + n], writes=[("g", s)])
            P.op("pool", lambda e, ib=ib: e.tensor_tensor(out=ysum[:, :n], in0=ib["y0"][:, :n], in1=ib["y1"][:, :n], op=ALU.add),
                 reads=[("y0", s), ("y1", s)], writes=["ysum"])
            P.op("pe", lambda e: e.matmul(cm.psb[2][:, :n], lhsT=bd64[:], rhs=ysum[:, :n], start=True, stop=True),
                 reads=["bd64", "ysum"], writes=[("psb", 2)])
            P.op("dve", lambda e: e.tensor_tensor(out=cen[:, :n], in0=ysum[:, :n], in1=cm.psb[2][:, :n], op=ALU.subtract),
                 reads=["ysum", ("psb", 2)], writes=["cen"])
            P.op("act", lambda e: e.activation(out=sq[:, :n], in_=cen[:, :n], func=AF.Square), reads=["cen"], writes=["sq"])
            P.op("pe", lambda e: e.matmul(cm.psb[3][:, :n], lhsT=bd64[:], rhs=sq[:, :n], start=True, stop=True),
                 reads=["bd64", "sq"], writes=[("psb", 3)])
            P.op("act", lambda e: e.activation(out=rstd[:, :n], in_=cm.psb[3][:, :n], func=AF.Sqrt, bias=epsl[:, 0:1]),
                 reads=[("psb", 3), "epsl"], writes=["rstd"])
            P.op("dve", lambda e: e.reciprocal(out=rstd[:, :n], in_=rstd[:, :n]), reads=["rstd"], writes=["rstd"])
            P.op("dve", lambda e: e.tensor_tensor(out=yn[:, :n], in0=cen[:, :n], in1=rstd[:, :n], op=ALU.mult),
                 reads=["cen", "rstd"], writes=["yn"])
            P.op("act", lambda e, c=c: e.activation(out=oo[:, :n], in_=yn[:, :n], func=AF.Identity, scale=lnx[:, 0, c:c + 1], bias=lnx[:, 1, c:c + 1]),
                 reads=["yn", "lnx"], writes=["oo"])
            P.op("pool", lambda e, ib=ib: e.tensor_tensor(out=oo[:, :n], in0=oo[:, :n], in1=ib["bo"][:, :n], op=ALU.add),
                 reads=["oo", ("bo", s)], writes=["oo"])
            P.op("pool", lambda e, ib=ib, c=c: e.tensor_tensor(out=hT[:, c, :n], in0=oo[:, :n], in1=ib["g"][:, :n], op=ALU.mult),
                 reads=["oo", ("g", s)], writes=[hkeys[c]])
        for m in range(16):
            s = ffn.kin % 2
            ffn.kin += 1
            wt = ffn.wg[s]
            P.dma("sp", wt[:], w_o[m], writes=[("wg", s)])
            q = ffn.km % 2
            ffn.km += 1
            py = cm.psb[6 + q]
            for c in range(16):
                P.op("pe", lambda e, wt=wt, py=py, c=c: e.matmul(py[:, :n], lhsT=wt[:, c, :], rhs=hT[:, c, :n],
                                                                  start=(c == 0), stop=(c == 15)),
                     reads=[("wg", s), hkeys[c]], writes=[("psb", 6 + q)])
            P.op("dve", lambda e, py=py, m=m: e.scalar_tensor_tensor(
                out=xb[:, m, :n], in0=py[:, :n], scalar=mod[:, 2, m:m + 1], in1=xb[:, m, :n], op0=ALU.mult, op1=ALU.add),
                reads=[("psb", 6 + q), mk, xkeys[m]], writes=[xkeys[m]])
        cm.norm_mod(xb[:, :, :n], n, (G2, ("G2", name)), (mod[:, 3, :], mk), lambda c: hT[:, c, :n], hkeys, xkeys, stat_bank=0)
        ffn.emit(hT, hkeys, n, xb, 0, xkeys, (mod[:, 5, :], mk))
        out_toks.append(P.dma("sp", d_["out"][:, :, t0:t0 + n], xb[:, :, :n], reads=xkeys, writes=[("out", name, t0)]))

    for name, T in segs:
        mod = P.sbuf("modsb_" + name, [128, 6, 16], F32)
        G2 = P.sbuf("G2_" + name, [128, 16], F32)
        mk = ("mod", name)
        P.dma("sp", mod[:], dr[name]["mod"], writes=[mk])
        P.op("dve", lambda e, G2=G2, mod=mod: e.scalar_tensor_tensor(
            out=G2[:], in0=mod[:, 4, :], scalar=1.0, in1=ng[:, 1, :], op0=ALU.add, op1=ALU.mult),
            reads=[mk, "ng"], writes=[("G2", name)])
        for t0 in range(0, T, TB):
            do_block(name, t0, min(TB, T - t0), mod, G2)
    P.final_wait("sp", out_toks)
    return P.build()


def r2_maps_from_r1(r1, cons):
    maps = []
    for b in range(2):
        cores = [r1[b * 4 + k] for k in range(4)]
        ot_lat = np.concatenate([c["ot_lat"] for c in cores], axis=4)
        ot_ctx = np.concatenate([cores[0]["ot_ctx"], cores[1]["ot_ctx"]], axis=4)
        v_lat = np.concatenate([c["v_lat"] for c in cores], axis=2)
        v_ctx = np.concatenate([cores[0]["v_ctx"], cores[1]["v_ctx"]], axis=2)
        pc_lat = np.concatenate([c["pc_lat"] for c in cores], axis=3)
        pc_ctx = np.concatenate([cores[0]["pc_ctx"], cores[1]["pc_ctx"]], axis=3)
        for d in range(2):
            if d == 0:
                ot = np.concatenate([ot_ctx[d], ot_lat[d]], axis=3)
                vv = np.concatenate([v_ctx, v_lat], axis=2)
                pc = np.concatenate([pc_ctx[:, d], pc_lat[:, d]], axis=2)
            else:
                ot = np.concatenate([ot_ctx[d][..., ::-1], ot_lat[d][..., ::-1]], axis=3)
                vv = np.concatenate([v_ctx[..., ::-1], v_lat[..., ::-1]], axis=2)
                pc = np.concatenate([pc_ctx[:, d][..., ::-1], pc_lat[:, d][..., ::-1]], axis=2)
            for hh in range(2):
                cs = slice(8 * hh, 8 * hh + 8)
                o = ot[:, cs].reshape(4, 8, 128, 66, 128)
                fm = np.ascontiguousarray(o.transpose(0, 3, 2, 1, 4))
                tmB = o[1].transpose(2, 3, 0, 1)
                tmK = o[2].transpose(2, 3, 0, 1)
                tmV = vv[cs].reshape(8, 128, 66, 128).transpose(2, 3, 0, 1)
                tm = np.ascontiguousarray(np.stack([tmB, tmK, tmV]))
                pcl = np.ascontiguousarray(pc[:, cs].transpose(2, 0, 1))
                m = dict(fm=fm, tm=tm, pc=pcl)
                m.update(cons)
                maps.append(((b, d, hh), m))
    maps.sort(key=lambda t: t[0][0] * 4 + t[0][1] * 2 + t[0][2])
    return [m for _, m in maps]


def r3_y_from_r2(r2res):
    yl = [[None, None], [None, None]]
    yc = [[None, None], [None, None]]
    for b in range(2):
        for d in range(2):
            halves = []
            for hh in range(2):
                y = np.asarray(r2res[b * 4 + d * 2 + hh]["y"])
                halves.append(y.reshape(66 * 128, 8 * 128))
            seq = np.concatenate(halves, axis=1)
            c, l = seq[:256], seq[256:]
            if d == 1:
                c, l = c[::-1], l[::-1]
            yc[b][d], yl[b][d] = c, l
    return yl, yc


def _r1_maps(x, ctx, mods_l, inp, wb):
    maps = []
    rmask = np.ones((128, 256), np.float32)
    rmask[:, ::128] = 0.0
    for i in range(8):
        b, k = i // 4, i % 4
        xl = np.zeros((TLAT + 2, 2048), np.float32)
        vl = np.zeros(TLAT + 2, np.float32)
        lo, hi = k * TLAT - 1, (k + 1) * TLAT + 1
        s0, s1 = max(lo, 0), min(hi, 8192)
        xl[s0 - lo:s1 - lo] = x[b, s0:s1]
        vl[s0 - lo:s1 - lo] = 1
        ck = k % 2
        xc = np.zeros((R1_TCTX + 2, 2048), np.float32)
        vc = np.zeros(R1_TCTX + 2, np.float32)
        lo, hi = ck * R1_TCTX - 1, (ck + 1) * R1_TCTX + 1
        s0, s1 = max(lo, 0), min(hi, 256)
        xc[s0 - lo:s1 - lo] = ctx[b, s0:s1]
        vc[s0 - lo:s1 - lo] = 1
        maps.append({"x_lat": to_fm(xl), "valid_lat": vl, "mod_lat": vec_fm(mods_l[b].reshape(6, 2048)),
                     "x_ctx": to_fm(xc), "valid_ctx": vc, "mod_ctx": vec_fm(mods_l[2].reshape(6, 2048)),
                     "normg": vec_fm(inp["norm_g"][1]), "mu": vec_fm(inp["rwkv_mu"][0]), "w_rkv": wb["rwkv_rkv"],
                     "w_la": inp["rwkv_w_lora_a"][0], "w_lb": inp["rwkv_w_lora_b"][0], "a_la": inp["rwkv_a_lora_a"][0],
                     "a_lb": inp["rwkv_a_lora_b"][0], "g_la": inp["rwkv_g_lora_a"][0], "g_lb": inp["rwkv_g_lora_b"][0],
                     "dirvec": vec_fm(inp["rwkv_dir_vec"][0]), "r_k": vec_fm(inp["rwkv_r_k"][0].reshape(2048)), "rmask": rmask})
    return maps


def _r3_maps(x, ctx, yl, yc, r1, mods_l, inp, li, wb):
    maps = []
    fmc = lambda a: np.ascontiguousarray(a.reshape(a.shape[0], 16, 128).transpose(1, 2, 0))
    for i in range(8):
        b, k = i // 4, i % 4
        ls = slice(k * 2048, (k + 1) * 2048)
        cs = slice(k * 64, (k + 1) * 64)
        cc = r1[b * 4 + (k // 2)]
        co = (k % 2) * 64
        maps.append({"x_lat": to_fm(x[b, ls]), "x_ctx": to_fm(ctx[b, cs]),
                     "y_lat": np.stack([fmc(yl[b][0][ls]), fmc(yl[b][1][ls])]),
                     "y_ctx": np.stack([fmc(yc[b][0][cs]), fmc(yc[b][1][cs])]),
                     "bonus_lat": r1[i]["bonus_lat"], "g_lat": r1[i]["g_lat"],
                     "bonus_ctx": np.ascontiguousarray(cc["bonus_ctx"][:, :, co:co + 64]),
                     "g_ctx": np.ascontiguousarray(cc["g_ctx"][:, :, co:co + 64]),
                     "mod_lat": vec_fm(mods_l[b].reshape(6, 2048)), "mod_ctx": vec_fm(mods_l[2].reshape(6, 2048)),
                     "normg": vec_fm(inp["norm_g"][li]), "lnx": vec_fm(inp["rwkv_ln_x"][0]), "w_o": wb["rwkv_wo"],
                     "w_in": wb["ffn_in%d" % li], "w_out": wb["ffn_out%d" % li]})
    return maps


def _run_rwkv_layer(x, ctx, mods_l, inp, li, wb):
    nc1 = build_r1()
    res1 = run_bass_kernel_spmd(nc1, _r1_maps(x, ctx, mods_l, inp, wb), core_ids=list(range(8)))
    r1 = [{k: np.asarray(v) for k, v in r.items()} for r in res1.results]
    nc2 = build_r2(NCH)
    res2 = run_bass_kernel_spmd(nc2, r2_maps_from_r1(r1, r2_consts()), core_ids=list(range(8)))
    r2res = [{"y": np.asarray(r["y"])} for r in res2.results]
    yl, yc = r3_y_from_r2(r2res)
    nc3 = build_r3()
    res3 = run_bass_kernel_spmd(nc3, _r3_maps(x, ctx, yl, yc, r1, mods_l, inp, li, wb), core_ids=list(range(8)))
    xo = np.zeros_like(x)
    co = np.zeros_like(ctx)
    for i in range(8):
        b, k = i // 4, i % 4
        xo[b, k * 2048:(k + 1) * 2048] = from_fm(res3.results[i]["out_lat"])
        co[b, k * 64:(k + 1) * 64] = from_fm(res3.results[i]["out_ctx"])
    return xo, co


def kernel(**inputs):
    inp = {k: np.ascontiguousarray(np.asarray(v, dtype=np.float32)) for k, v in inputs.items()}
    x, ctx = inp["x"], inp["ctx"]
    mods, wb = run_l0(inp)
    x, ctx = run_pool_layer(x, ctx, mods[:, 0], inp["norm_g"][0], inp["pool_scale"][0], inp["pool_w"][0],
                            wb["ffn_in0"], wb["ffn_out0"], True)
    x, ctx = _run_rwkv_layer(x, ctx, mods[:, 1], inp, 1, wb)
    a1 = run_a1(x, ctx, mods[:, 2], inp["norm_g"][2], wb["diff_qkv"], inp["diff_qk_g"][0])
    a1 = [{k: np.asarray(v) for k, v in r.items()} for r in a1]
    lambda_init = 0.8 - 0.6 * math.exp(-0.3 * 2)
    x = run_a2(a1, x, mods[:, 2], inp["norm_g"][2], inp["diff_lambda"][0], inp["diff_subln_g"][0], wb["diff_wo"],
               wb["ffn_in2"], wb["ffn_out2"], lambda_init)
    x, _ = run_pool_layer(x, None, mods[:, 3], inp["norm_g"][3], inp["pool_scale"][1], inp["pool_w"][1],
                          wb["ffn_in3"], wb["ffn_out3"], False)
    return x.astype(np.float32)
```

```python
import math


import numpy as np
import concourse.bass as bass
import concourse.mybir as mybir
from concourse.bass_utils import run_bass_kernel_spmd

F32 = mybir.dt.float32
BF16 = mybir.dt.bfloat16
ALU = mybir.AluOpType
AF = mybir.ActivationFunctionType
AX = mybir.AxisListType

N_DMA_SEMS = 24


class Prog:
    ENGS = ("pe", "act", "dve", "pool", "sp")

    def __init__(self, nc):
        self.nc = nc
        self.q = {e: [] for e in self.ENGS}
        self.n = {e: 0 for e in self.ENGS}
        self.waited = {e: {} for e in self.ENGS}
        self.lastw = {}
        self.readers = {}
        self.dma_rr = 0
        self.dma_cnt = [0] * N_DMA_SEMS
        self.dma_last = [None] * N_DMA_SEMS
        self.ctx = []
        self.sems = {}

    def enter(self, cm):
        v = cm.__enter__()
        self.ctx.append(cm)
        return v

    def sbuf(self, name, shape, dt):
        return self.enter(self.nc.sbuf_tensor(name, list(shape), dt))

    def psum(self, name, shape, dt=F32):
        return self.enter(self.nc.psum_tensor(name, list(shape), dt))

    def _deps(self, reads, writes):
        toks = []
        for r in reads:
            t = self.lastw.get(r)
            if t is not None:
                toks.append(t)
        for w in writes:
            t = self.lastw.get(w)
            if t is not None:
                toks.append(t)
            toks.extend(self.readers.get(w, ()))
        return toks

    def _commit(self, tok, reads, writes):
        for r in reads:
            self.readers.setdefault(r, []).append(tok)
        for w in writes:
            self.lastw[w] = tok
            self.readers[w] = []

    def _waits(self, eng, toks):
        need = {}
        for (k, v) in toks:
            if v > need.get(k, 0):
                need[k] = v
        out = []
        wd = self.waited[eng]
        for k, v in need.items():
            if wd.get(k, 0) >= v:
                continue
            wd[k] = v
            out.append((k, v))
        return out

    def op(self, eng, fn, reads=(), writes=()):
        toks = self._deps(reads, writes)
        if eng == "pe":
            toks = [t for t in toks if t[0] != "pe"]
        waits = self._waits(eng, toks)
        self.n[eng] += 1
        tok = (eng, self.n[eng])
        self.q[eng].append((fn, waits, ("self", eng, 1)))
        self._commit(tok, reads, writes)
        return tok

    def dma(self, eng, out, in_, reads=(), writes=(), **kw):
        toks = self._deps(reads, writes)
        s = self.dma_rr
        self.dma_rr = (self.dma_rr + 1) % N_DMA_SEMS
        if self.dma_last[s] is not None:
            toks.append(self.dma_last[s])
        waits = self._waits(eng, toks)
        self.dma_cnt[s] += 1
        tok = (("dma", s), 16 * self.dma_cnt[s])
        self.dma_last[s] = tok
        self.q[eng].append((lambda e: e.dma_start(out=out, in_=in_, **kw), waits, ("dma", s, 16)))
        self._commit(tok, reads, writes)
        return tok

    def final_wait(self, eng, toks):
        waits = self._waits(eng, toks)
        self.q[eng].append((None, waits, None))

    def build(self):
        nc = self.nc
        semobjs = {}
        for e in self.ENGS:
            if e != "sp":
                semobjs[e] = self.enter(nc.semaphore("prog_" + e))
        for s in range(N_DMA_SEMS):
            semobjs[("dma", s)] = self.enter(nc.semaphore("dma%d" % s))
        q = self.q

        def emit(engname, e):
            for fn, waits, inc in q[engname]:
                for k, v in waits:
                    e.wait_ge(semobjs[k], v)
                if fn is None:
                    continue
                ins = fn(e)
                if inc[0] == "self":
                    ins.then_inc(semobjs[inc[1]], 1)
                else:
                    ins.then_inc(semobjs[("dma", inc[1])], 16)

        with nc.Block() as block:
            @block.tensor
            def _(e):
                emit("pe", e)

            @block.scalar
            def _(e):
                emit("act", e)

            @block.vector
            def _(e):
                emit("dve", e)

            @block.gpsimd
            def _(e):
                emit("pool", e)

            @block.sync
            def _(e):
                emit("sp", e)
        for cm in reversed(self.ctx):
            cm.__exit__(None, None, None)
        self.ctx = []
        return nc


D = 2048
NL = 4
NM = 6 * D
COLS_PER_CORE = NM // 8
L0_NB = COLS_PER_CORE // 512

CAST_CH = 8192


def build_l0(ncast=0):
    nc = bass.Bass("TRN2", target_bir_lowering=False)
    cT = nc.dram_tensor("cT", [128, 16, 3], F32, kind="ExternalInput").ap()
    w = nc.dram_tensor("w", [NL, D, COLS_PER_CORE], F32, kind="ExternalInput").ap()
    b = nc.dram_tensor("b", [NL, COLS_PER_CORE], F32, kind="ExternalInput").ap()
    out = nc.dram_tensor("out", [3, NL * COLS_PER_CORE], F32, kind="ExternalOutput").ap()
    if ncast:
        cin = nc.dram_tensor("cin", [128, ncast * CAST_CH], F32, kind="ExternalInput").ap()
        cout = nc.dram_tensor("cout", [128, ncast * CAST_CH], BF16, kind="ExternalOutput").ap()
    P = Prog(nc)
    c_sb = P.sbuf("c_sb", [128, 16, 3], F32)
    s_sb = P.sbuf("s_sb", [128, 16, 3], F32)
    b_sb = P.sbuf("b_sb", [3, NL * COLS_PER_CORE], F32)
    o_sb = P.sbuf("o_sb", [3, NL * COLS_PER_CORE], F32)
    wt = [P.sbuf("wt%d" % i, [128, 16, 512], F32) for i in range(2)]
    ps = [P.psum("ps%d" % i, [128, 512]) for i in range(2)]
    P.dma("sp", c_sb[:], cT, writes=["c"])
    for l in range(NL):
        P.dma("sp", b_sb[:, l * COLS_PER_CORE:(l + 1) * COLS_PER_CORE],
              b[l, :].partition_broadcast(3), writes=[("b", l)])
    P.op("act", lambda e: e.activation(out=s_sb[:], in_=c_sb[:], func=AF.Silu), reads=["c"], writes=["s"])
    k = 0
    for l in range(NL):
        for nb in range(L0_NB):
            wb = wt[k % 2]
            pb = ps[k % 2]
            src = w[l, :, nb * 512:(nb + 1) * 512].rearrange("(c p) n -> p c n", p=128)
            P.dma("sp" if k % 2 == 0 else "pool", wb[:], src, writes=[("wt", k % 2)])
            for c in range(16):
                P.op("pe", lambda e, wb=wb, pb=pb, c=c: e.matmul(pb[0:3, :], lhsT=s_sb[:, c, :], rhs=wb[:, c, :],
                                                                   start=(c == 0), stop=(c == 15)),
                     reads=["s", ("wt", k % 2)], writes=[("ps", k % 2)])
            col = l * COLS_PER_CORE + nb * 512
            P.op("dve", lambda e, pb=pb, col=col: e.tensor_tensor(out=o_sb[:, col:col + 512], in0=pb[0:3, :],
                                                                   in1=b_sb[:, col:col + 512], op=ALU.add),
                 reads=[("ps", k % 2), ("b", l)], writes=[("o", k)])
            k += 1
    t = P.dma("sp", out, o_sb[:], reads=[("o", i) for i in range(k)], writes=["out"])
    toks = [t]
    if ncast:
        cb = [P.sbuf("cb%d" % i, [128, CAST_CH], BF16) for i in range(3)]
        for i in range(ncast):
            sl = slice(i * CAST_CH, (i + 1) * CAST_CH)
            P.dma("pool", cb[i % 3][:], cin[:, sl], writes=[("cb", i % 3)])
            toks.append(P.dma("act", cout[:, sl], cb[i % 3][:], reads=[("cb", i % 3)], writes=[("cout", i)]))
    P.final_wait("sp", toks)
    return P.build()

def blocked_weights(inputs):
    perm = np.concatenate([np.arange(0, 128, 2), np.arange(1, 128, 2)])
    out = {}
    for l in range(4):
        wi = inputs["ffn_w_in"][l].reshape(16, 128, 2, 44, 128)
        out["ffn_in%d" % l] = np.ascontiguousarray(wi.transpose(3, 2, 1, 0, 4))
        wo = inputs["ffn_w_out"][l].reshape(44, 128, 16, 128)
        out["ffn_out%d" % l] = np.ascontiguousarray(wo.transpose(2, 1, 0, 3))
    sq = lambda w: np.ascontiguousarray(w.reshape(16, 128, -1, 128).transpose(2, 1, 0, 3))
    out["rwkv_wo"] = sq(inputs["rwkv_w_o"][0])
    out["diff_wo"] = sq(inputs["diff_w_o"][0])
    out["rwkv_rkv"] = np.stack([sq(inputs["rwkv_w_rkv"][0][i]) for i in range(3)])
    wq = inputs["diff_w_qkv"][0]
    cols = (np.arange(32)[:, None] * 128 + perm[None, :]).reshape(-1)
    wq = np.concatenate([wq[:, cols], wq[:, 4096:]], axis=1)
    out["diff_qkv"] = sq(wq)
    return out


def run_l0(inputs, cast=True):
    c = np.concatenate([inputs["c"], inputs["c_ctx"][None]], 0)
    cT = np.ascontiguousarray(c.reshape(3, 16, 128).transpose(2, 1, 0))
    blk = blocked_weights(inputs) if cast else {}
    names = list(blk)
    total = sum(blk[n].size for n in names)
    per = 8 * 128 * CAST_CH
    ncast = (total + per - 1) // per
    nc = build_l0(ncast)
    if ncast:
        flat = np.zeros(ncast * per, np.float32)
        o = 0
        for n in names:
            flat[o:o + blk[n].size] = blk[n].reshape(-1)
            o += blk[n].size
        flat = flat.reshape(8, 128, ncast * CAST_CH)
    maps = []
    for i in range(8):
        sl = slice(i * COLS_PER_CORE, (i + 1) * COLS_PER_CORE)
        m = {"cT": cT, "w": np.ascontiguousarray(inputs["ada_w"][:, :, sl]),
             "b": np.ascontiguousarray(inputs["ada_b"][:, sl])}
        if ncast:
            m["cin"] = flat[i]
        maps.append(m)
    res = run_bass_kernel_spmd(nc, maps, core_ids=list(range(8)))
    outs = [r["out"].reshape(3, NL, COLS_PER_CORE) for r in res.results]
    mods = np.concatenate(outs, axis=2)
    wb = {}
    if ncast:
        cf = np.concatenate([np.asarray(r["cout"]).reshape(-1) for r in res.results])
        o = 0
        for n in names:
            wb[n] = cf[o:o + blk[n].size].reshape(blk[n].shape)
            o += blk[n].size
    return mods, wb


D = 2048
F = 5632
NC16 = 16
NJ = F // 128
EPS = 1e-6
HALO = 8
TBG = 512
WINS = (2, 4, 8, 16)


class Common:
    def __init__(self, P, TBMAX=512, halo=HALO):
        self.P = P
        W = TBMAX + 2 * halo
        self.W = W
        self.ones = P.sbuf("ones_bf", [128, 128], BF16)
        self.rs = P.sbuf("rs", [128, W], F32)
        self.sqc = [P.sbuf("sqc%d" % i, [128, W], BF16) for i in range(2)]
        self.tmp = [P.sbuf("ntmp%d" % i, [128, W], F32) for i in range(2)]
        self.psb = [P.psum("psb%d" % i, [128, 512]) for i in range(8)]
        P.op("dve", lambda e: e.memset(self.ones[:], 1.0), writes=["ones"])
        self.k = 0

    def norm_mod(self, xb, ncols, G, SH, dest, dest_keys, xkeys, stat_bank=0, post=None):
        P = self
        P = self.P
        pst = self.psb[stat_bank]
        pkey = ("psb", stat_bank)
        n2 = ncols
        halves = [(0, min(512, n2))]
        if n2 > 512:
            halves.append((512, n2))
        for c in range(16):
            sq = self.sqc[c % 2]
            P.op("act", lambda e, sq=sq, c=c: e.activation(out=sq[:, :n2], in_=xb[:, c, :], func=AF.Square),
                 reads=[xkeys[c]], writes=[("sqc", c % 2)])
            for hi, (a, b) in enumerate(halves):
                bank = self.psb[stat_bank + hi]
                P.op("pe", lambda e, sq=sq, c=c, a=a, b=b, bank=bank: e.matmul(
                    bank[:, 0:b - a], lhsT=self.ones[:], rhs=sq[:, a:b], start=(c == 0), stop=(c == 15)),
                    reads=[("sqc", c % 2), "ones"], writes=[("psb", stat_bank + hi)])
        for hi, (a, b) in enumerate(halves):
            bank = self.psb[stat_bank + hi]
            P.op("act", lambda e, a=a, b=b, bank=bank: e.activation(
                out=self.rs[:, a:b], in_=bank[:, 0:b - a], func=AF.Sqrt, scale=1.0 / D, bias=self.epsb[:, 0:1]),
                reads=[("psb", stat_bank + hi), "epsb"], writes=[("rs", hi)])
            P.op("dve", lambda e, a=a, b=b: e.reciprocal(out=self.rs[:, a:b], in_=self.rs[:, a:b]),
                 reads=[("rs", hi)], writes=[("rs", hi)])
        Gt, Gk = G
        St, Sk = SH
        for c in range(16):
            tmp = self.tmp[c % 2]
            P.op("dve", lambda e, tmp=tmp, c=c: e.tensor_tensor(out=tmp[:, :n2], in0=xb[:, c, :], in1=self.rs[:, :n2],
                                                                 op=ALU.mult),
                 reads=[xkeys[c], ("rs", 0), ("rs", 1)], writes=[("ntmp", c % 2)])
            P.op("act", lambda e, tmp=tmp, c=c: e.activation(out=dest(c), in_=tmp[:, :n2], func=AF.Identity,
                                                             scale=Gt[:, c:c + 1], bias=St[:, c:c + 1]),
                 reads=[("ntmp", c % 2), Gk, Sk], writes=[dest_keys[c]])
            if post is not None:
                post(c)

    def setup_eps(self):
        P = self.P
        self.epsb = P.sbuf("epsb", [128, 1], F32)
        P.op("dve", lambda e: e.memset(self.epsb[:], EPS), writes=["epsb"])


class FFN:
    def __init__(self, P, cm, w_in, w_out, TB=512, nsplit=1, WC=256):
        self.P, self.cm = P, cm
        self.w_in, self.w_out = w_in, w_out
        WC = 128
        self.nsplit, self.WC = nsplit, WC
        self.NJS = NJ // nsplit
        self.actT = P.sbuf("actT", [128, self.NJS, TB], BF16)
        self.wg = [P.sbuf("wg%d" % i, [128, 16, WC], BF16) for i in range(2)]
        self.wu = [P.sbuf("wu%d" % i, [128, 16, WC], BF16) for i in range(2)]
        self.wo = [P.sbuf("wo%d" % i, [128, self.NJS, 128], BF16) for i in range(2)]
        self.silu = [P.sbuf("silu%d" % i, [128, TB], F32) for i in range(2)]
        self.kin = 0
        self.kout = 0
        self.kj = 0
        self.km = 0

    def emit(self, hT, hkeys, n, xb, xoff, xkeys, g2):
        P, cm = self.P, self.cm
        g2t, g2k = g2
        WC, NJS = self.WC, self.NJS
        per = WC // 128
        for sp in range(self.nsplit):
            j0 = sp * NJS
            for jb in range(NJS // per):
                s = self.kin % 2
                self.kin += 1
                wg, wu = self.wg[s], self.wu[s]
                jg = j0 + jb
                P.dma("sp", wg[:], self.w_in[jg, 0], writes=[("wg", s)])
                P.dma("sp", wu[:], self.w_in[jg, 1], writes=[("wu", s)])
                for jj in range(per):
                    jl = jb * per + jj
                    q = self.kj % 2
                    self.kj += 1
                    pg, pu = cm.psb[2 + q], cm.psb[4 + q]
                    for c in range(16):
                        P.op("pe", lambda e, wg=wg, pg=pg, c=c, jj=jj: e.matmul(
                            pg[:, :n], lhsT=wg[:, c, jj * 128:(jj + 1) * 128], rhs=hT[:, c, :n],
                            start=(c == 0), stop=(c == 15)),
                            reads=[("wg", s), hkeys[c]], writes=[("psb", 2 + q)])
                    for c in range(16):
                        P.op("pe", lambda e, wu=wu, pu=pu, c=c, jj=jj: e.matmul(
                            pu[:, :n], lhsT=wu[:, c, jj * 128:(jj + 1) * 128], rhs=hT[:, c, :n],
                            start=(c == 0), stop=(c == 15)),
                            reads=[("wu", s), hkeys[c]], writes=[("psb", 4 + q)])
                    sl = self.silu[q]
                    P.op("act", lambda e, sl=sl, pg=pg: e.activation(out=sl[:, :n], in_=pg[:, :n], func=AF.Silu),
                         reads=[("psb", 2 + q)], writes=[("silu", q)])
                    P.op("dve", lambda e, sl=sl, pu=pu, jl=jl: e.tensor_tensor(
                        out=self.actT[:, jl, :n], in0=sl[:, :n], in1=pu[:, :n], op=ALU.mult),
                        reads=[("silu", q), ("psb", 4 + q)], writes=[("actT", jl)])
            for m in range(16):
                s = self.kout % 2
                self.kout += 1
                wo = self.wo[s]
                P.dma("sp", wo[:], self.w_out[m][:, j0:j0 + NJS, :], writes=[("wo", s)])
                q = self.km % 2
                self.km += 1
                py = cm.psb[6 + q]
                for jl in range(NJS):
                    P.op("pe", lambda e, wo=wo, py=py, jl=jl: e.matmul(
                        py[:, :n], lhsT=wo[:, jl, :], rhs=self.actT[:, jl, :n], start=(jl == 0), stop=(jl == NJS - 1)),
                        reads=[("wo", s), ("actT", jl)], writes=[("psb", 6 + q)])
                P.op("dve", lambda e, py=py, m=m: e.scalar_tensor_tensor(
                    out=xb[:, m, xoff:xoff + n], in0=py[:, :n], scalar=g2t[:, m:m + 1], in1=xb[:, m, xoff:xoff + n],
                    op0=ALU.mult, op1=ALU.add),
                    reads=[("psb", 6 + q), g2k, xkeys[m]], writes=[xkeys[m]])


def build_pool_layer(segs, TB=512, dbg=0):
    nc = bass.Bass("TRN2", target_bir_lowering=False)
    dram = {}
    for name, T in segs:
        dram[name] = dict(
            x=nc.dram_tensor("x_" + name, [128, 16, T + 2 * HALO], F32, kind="ExternalInput").ap(),
            valid=nc.dram_tensor("valid_" + name, [T + 2 * HALO], F32, kind="ExternalInput").ap(),
            invc=nc.dram_tensor("invc_" + name, [4, T], F32, kind="ExternalInput").ap(),
            mod=nc.dram_tensor("mod_" + name, [128, 6, 16], F32, kind="ExternalInput").ap(),
            out=nc.dram_tensor("out_" + name, [128, 16, T], F32, kind="ExternalOutput").ap(),
        )
    normg = nc.dram_tensor("normg", [128, 2, 16], F32, kind="ExternalInput").ap()
    pscale = nc.dram_tensor("pscale", [128, 16], F32, kind="ExternalInput").ap()
    poolw = nc.dram_tensor("poolw", [4, 128, 4, 512], F32, kind="ExternalInput").ap()
    w_in = nc.dram_tensor("w_in", [NJ, 2, 128, 16, 128], BF16, kind="ExternalInput").ap()
    w_out = nc.dram_tensor("w_out", [16, 128, NJ, 128], BF16, kind="ExternalInput").ap()

    P = Prog(nc)
    cm = Common(P, TB)
    cm.setup_eps()
    W = TB + 2 * HALO
    ffn = FFN(P, cm, w_in, w_out, TB)
    xb = P.sbuf("xb", [128, 16, W], F32)
    hT = P.sbuf("hT", [128, 16, W], BF16)
    hc = [P.sbuf("hc%d" % i, [128, W], F32) for i in range(2)]
    pa = [P.sbuf("pa%d" % i, [128, W], F32) for i in range(2)]
    pm = P.sbuf("pm", [128, TB], F32)
    pw = [P.sbuf("pw%d" % i, [128, 4, 512], BF16) for i in range(2)]
    invc = P.sbuf("invc", [128, 4, TB], F32)
    vmask = P.sbuf("vmask", [128, W], F32)
    ng = P.sbuf("ng", [128, 2, 16], F32)
    psc = P.sbuf("psc", [128, 16], F32)
    P.dma("sp", ng[:], normg, writes=["ng"])
    P.dma("sp", psc[:], pscale, writes=["psc"])
    xkeys = [("xb", c) for c in range(16)]
    hkeys = [("hT", c) for c in range(16)]
    out_toks = []
    st = dict(kpw=0, kpy=0)
    def do_block(dr, mod, G1, G2, GL, mk, name, bi, t0, n):
        nw = n + 2 * HALO
        P.dma("sp", xb[:, :, :nw], dr["x"][:, :, t0:t0 + nw], writes=xkeys)
        P.dma("sp", vmask[:, :nw], dr["valid"][t0:t0 + nw].partition_broadcast(128), writes=["vmask"])
        P.dma("sp", invc[:, :, :n], dr["invc"][:, t0:t0 + n].partition_broadcast(128), writes=["invc"])

        def pool_chunk(c, n=n, nw=nw):
            g = c // 4
            w = WINS[g]
            h = hc[c % 2]
            hk = ("hc", c % 2)
            P.op("dve", lambda e: e.tensor_tensor(out=h[:, 0:HALO], in0=h[:, 0:HALO], in1=vmask[:, 0:HALO], op=ALU.mult),
                 reads=[hk, "vmask"], writes=[hk])
            P.op("dve", lambda e: e.tensor_tensor(out=h[:, nw - HALO:nw], in0=h[:, nw - HALO:nw],
                                                  in1=vmask[:, nw - HALO:nw], op=ALU.mult),
                 reads=[hk, "vmask"], writes=[hk])
            cur, curk, ln = h, hk, nw
            s = 1
            i = 0
            while s < w:
                dst = pa[i % 2]
                P.op("dve", lambda e, cur=cur, dst=dst, s=s, ln=ln: e.tensor_tensor(
                    out=dst[:, 0:ln - s], in0=cur[:, 0:ln - s], in1=cur[:, s:ln], op=ALU.add),
                    reads=[curk], writes=[("pa", i % 2)])
                cur, curk, ln = dst, ("pa", i % 2), ln - s
                s *= 2
                i += 1
            o = HALO - w // 2
            P.op("dve", lambda e, cur=cur, o=o, g=g: e.tensor_tensor(
                out=pm[:, :n], in0=cur[:, o:o + n], in1=invc[:, g, :n], op=ALU.mult),
                reads=[curk, "invc"], writes=["pm"])
            P.op("dve", lambda e, c=c: e.tensor_tensor(
                out=hT[:, c, :n], in0=pm[:, :n], in1=h[:, HALO:HALO + n], op=ALU.subtract),
                reads=["pm", hk], writes=[hkeys[c]])

        hck = [("hc", c % 2) for c in range(16)]
        cm.norm_mod(xb[:, :, :nw], nw, (G1, ("G1", name)), (mod[:, 0, :], mk),
                    lambda c, nw=nw: hc[c % 2][:, :nw], hck, xkeys, stat_bank=0, post=pool_chunk)
        for g in range(4):
            s = st["kpw"] % 2
            st["kpw"] += 1
            P.dma("pool", pw[s][:], poolw[g], writes=[("pw", s)])
            for mm in range(4):
                m = 4 * g + mm
                q = st["kpy"] % 2
                st["kpy"] += 1
                py = cm.psb[6 + q]
                for cc in range(4):
                    P.op("pe", lambda e, s=s, py=py, cc=cc, mm=mm, g=g: e.matmul(
                        py[:, :n], lhsT=pw[s][:, cc, mm * 128:(mm + 1) * 128], rhs=hT[:, 4 * g + cc, :n],
                        start=(cc == 0), stop=(cc == 3)),
                        reads=[("pw", s), hkeys[4 * g + cc]], writes=[("psb", 6 + q)])
                P.op("dve", lambda e, py=py, m=m: e.scalar_tensor_tensor(
                    out=xb[:, m, HALO:HALO + n], in0=py[:, :n], scalar=GL[:, m:m + 1], in1=xb[:, m, HALO:HALO + n],
                    op0=ALU.mult, op1=ALU.add),
                    reads=[("psb", 6 + q), ("GL", name), xkeys[m]], writes=[xkeys[m]])
        if dbg == 0:
            cm.norm_mod(xb[:, :, HALO:HALO + n], n, (G2, ("G2", name)), (mod[:, 3, :], mk),
                        lambda c, n=n: hT[:, c, :n], hkeys, xkeys, stat_bank=0)
            ffn.emit(hT, hkeys, n, xb, HALO, xkeys, (mod[:, 5, :], mk))
        t = P.dma("sp", dr["out"][:, :, t0:t0 + n], xb[:, :, HALO:HALO + n], reads=xkeys, writes=[("out", name, bi)])
        out_toks.append(t)

    for si, (name, T) in enumerate(segs):
        dr = dram[name]
        mod = P.sbuf("modsb_" + name, [128, 6, 16], F32)
        G1 = P.sbuf("G1_" + name, [128, 16], F32)
        G2 = P.sbuf("G2_" + name, [128, 16], F32)
        GL = P.sbuf("GL_" + name, [128, 16], F32)
        mk = ("mod", name)
        P.dma("sp", mod[:], dr["mod"], writes=[mk])
        P.op("dve", lambda e, G1=G1, mod=mod: e.scalar_tensor_tensor(
            out=G1[:], in0=mod[:, 1, :], scalar=1.0, in1=ng[:, 0, :], op0=ALU.add, op1=ALU.mult),
            reads=[mk, "ng"], writes=[("G1", name)])
        P.op("dve", lambda e, G2=G2, mod=mod: e.scalar_tensor_tensor(
            out=G2[:], in0=mod[:, 4, :], scalar=1.0, in1=ng[:, 1, :], op0=ALU.add, op1=ALU.mult),
            reads=[mk, "ng"], writes=[("G2", name)])
        P.op("dve", lambda e, GL=GL, mod=mod: e.tensor_tensor(out=GL[:], in0=mod[:, 2, :], in1=psc[:], op=ALU.mult),
             reads=[mk, "psc"], writes=[("GL", name)])
        nblk = (T + TB - 1) // TB
        for bi in range(nblk):
            do_block(dr, mod, G1, G2, GL, mk, name, bi, bi * TB, min(TB, T - bi * TB))
    P.final_wait("sp", out_toks)
    return P.build()


def to_fm(a):
    T = a.shape[0]
    return np.ascontiguousarray(a.reshape(T, 16, 128).transpose(2, 1, 0))


def from_fm(a):
    T = a.shape[2]
    return np.ascontiguousarray(a.transpose(2, 1, 0).reshape(T, 2048))


def vec_fm(v):
    lead = v.shape[:-1]
    r = v.reshape(lead + (16, 128))
    return np.ascontiguousarray(np.moveaxis(r, -1, 0))


def seg_shards(seq, T):
    S = seq.shape[0]
    pad = np.zeros((S + 2 * HALO, 2048), np.float32)
    pad[HALO:HALO + S] = seq
    t = np.arange(S)
    inv = np.zeros((4, S), np.float32)
    for g, w in enumerate(WINS):
        lo = np.clip(t - w // 2, 0, S)
        hi = np.clip(t + w - w // 2, 0, S)
        inv[g] = 1.0 / (hi - lo)
    valid = np.zeros(S + 2 * HALO, np.float32)
    valid[HALO:HALO + S] = 1.0
    out = []
    for s0 in range(0, S, T):
        out.append(dict(x=to_fm(pad[s0:s0 + T + 2 * HALO]), valid=np.ascontiguousarray(valid[s0:s0 + T + 2 * HALO]),
                        invc=np.ascontiguousarray(inv[:, s0:s0 + T])))
    return out


def run_pool_layer(x, ctx, mods_l, normg_l, pscale, poolw, w_in, w_out, with_ctx, dbg=0):
    segs = [("lat", 2048)] + ([("ctx", 64)] if with_ctx else [])
    nc = build_pool_layer(segs, TB=TBG, dbg=dbg)
    maps = []
    pw_l = np.ascontiguousarray(poolw.reshape(4, 4, 128, 512).transpose(0, 2, 1, 3))
    lat = [seg_shards(x[b], 2048) for b in range(2)]
    cs = [seg_shards(ctx[b], 64) for b in range(2)] if with_ctx else None
    for i in range(8):
        b, k = i // 4, i % 4
        m = {"normg": vec_fm(normg_l), "pscale": vec_fm(pscale), "poolw": pw_l, "w_in": w_in, "w_out": w_out}
        sh = lat[b][k]
        m.update({"x_lat": sh["x"], "valid_lat": sh["valid"], "invc_lat": sh["invc"],
                  "mod_lat": vec_fm(mods_l[b].reshape(6, 2048))})
        if with_ctx:
            sh = cs[b][k]
            m.update({"x_ctx": sh["x"], "valid_ctx": sh["valid"], "invc_ctx": sh["invc"],
                      "mod_ctx": vec_fm(mods_l[2].reshape(6, 2048))})
        maps.append(m)
    res = run_bass_kernel_spmd(nc, maps, core_ids=list(range(8)))
    xo = np.zeros_like(x)
    co = np.zeros_like(ctx) if with_ctx else None
    for i in range(8):
        b, k = i // 4, i % 4
        xo[b, k * 2048:(k + 1) * 2048] = from_fm(res.results[i]["out_lat"])
        if with_ctx:
            co[b, k * 64:(k + 1) * 64] = from_fm(res.results[i]["out_ctx"])
    return xo, co


DH = 128
NHEAD = 8
GRID_W = 64
CTX = 256
TLAT = 2048
TCTX = 64


def build_a1():
    nc = bass.Bass("TRN2", target_bir_lowering=False)
    x_lat = nc.dram_tensor("x_lat", [128, 16, TLAT], F32, kind="ExternalInput").ap()
    x_ctx = nc.dram_tensor("x_ctx", [128, 16, TCTX], F32, kind="ExternalInput").ap()
    mod_lat = nc.dram_tensor("mod_lat", [128, 6, 16], F32, kind="ExternalInput").ap()
    mod_ctx = nc.dram_tensor("mod_ctx", [128, 6, 16], F32, kind="ExternalInput").ap()
    normg = nc.dram_tensor("normg", [128, 2, 16], F32, kind="ExternalInput").ap()
    wqkv = nc.dram_tensor("wqkv", [48, 128, 16, 128], BF16, kind="ExternalInput").ap()
    qkg = nc.dram_tensor("qkg", [128, 2], F32, kind="ExternalInput").ap()
    cs_d = nc.dram_tensor("cs", [128, TLAT], F32, kind="ExternalInput").ap()
    sn_d = nc.dram_tensor("sn", [128, TLAT], F32, kind="ExternalInput").ap()
    qT = nc.dram_tensor("qT", [16, 128, TLAT], BF16, kind="ExternalOutput").ap()
    kT = nc.dram_tensor("kT", [16, 128, TLAT], BF16, kind="ExternalOutput").ap()
    vT = nc.dram_tensor("vT", [16, 128, TLAT], BF16, kind="ExternalOutput").ap()
    kcT = nc.dram_tensor("kcT", [16, 128, TCTX], BF16, kind="ExternalOutput").ap()
    vcT = nc.dram_tensor("vcT", [16, 128, TCTX], BF16, kind="ExternalOutput").ap()

    P = Prog(nc)
    cm = Common(P, 512, halo=0)
    cm.setup_eps()
    TALL = TLAT + TCTX
    xb = P.sbuf("xb", [128, 16, 512], F32)
    hT = P.sbuf("hT", [128, 16, TALL], BF16)
    wt = [P.sbuf("wt%d" % i, [128, 16, 128], BF16) for i in range(3)]
    cs = P.sbuf("cs_sb", [128, TLAT], F32)
    sn = P.sbuf("sn_sb", [128, TLAT], F32)
    ng = P.sbuf("ng", [128, 2, 16], F32)
    g_sb = P.sbuf("qkg_sb", [128, 2], F32)
    qn = [P.sbuf("qn%d" % i, [128, 512], F32) for i in range(2)]
    sw = [P.sbuf("sw%d" % i, [128, 512], F32) for i in range(2)]
    tm = [P.sbuf("tm%d" % i, [128, 512], F32) for i in range(2)]
    oo = [P.sbuf("oo%d" % i, [128, 512], F32) for i in range(2)]
    sq = [P.sbuf("sq%d" % i, [128, 512], BF16) for i in range(2)]
    rr = [P.sbuf("rr%d" % i, [128, 512], F32) for i in range(2)]
    stg = [P.sbuf("stg%d" % i, [128, TLAT], BF16) for i in range(2)]
    stgc = [P.sbuf("stgc%d" % i, [128, TCTX], BF16) for i in range(2)]
    P.dma("sp", ng[:], normg, writes=["ng"])
    P.dma("sp", g_sb[:], qkg, writes=["qkg"])
    P.dma("sp", cs[:], cs_d, writes=["cs"])
    P.dma("sp", sn[:], sn_d, writes=["sn"])
    xkeys = [("xb", c) for c in range(16)]
    segs = [("lat", x_lat, mod_lat, TLAT, 0), ("ctx", x_ctx, mod_ctx, TCTX, TLAT)]
    for name, xd, md, T, off in segs:
        mod = P.sbuf("modsb_" + name, [128, 6, 16], F32)
        G1 = P.sbuf("G1_" + name, [128, 16], F32)
        mk = ("mod", name)
        P.dma("sp", mod[:], md, writes=[mk])
        P.op("dve", lambda e, G1=G1, mod=mod: e.scalar_tensor_tensor(
            out=G1[:], in0=mod[:, 1, :], scalar=1.0, in1=ng[:, 0, :], op0=ALU.add, op1=ALU.mult),
            reads=[mk, "ng"], writes=[("G1", name)])
        for t0 in range(0, T, 512):
            n = min(512, T - t0)
            P.dma("sp", xb[:, :, :n], xd[:, :, t0:t0 + n], writes=xkeys)
            hk = [("hT", c, (off + t0) // 512) for c in range(16)]
            cm.norm_mod(xb[:, :, :n], n, (G1, ("G1", name)), (mod[:, 0, :], mk),
                        lambda c, o=off + t0, n=n: hT[:, c, o:o + n], hk, xkeys, stat_bank=0)
    blocks = [(i * 512, 512, i) for i in range(4)] + [(TLAT, TCTX, 4)]
    st = dict(k=0, ps=0, ss=0, t=0)
    out_toks = []

    def proj(m, wtile, wkey, jj, t0, n, bi, kind):
        pb = 2 + st["ps"] % 2
        st["ps"] += 1
        ps = cm.psb[pb]
        for c in range(16):
            P.op("pe", lambda e, c=c: e.matmul(ps[:, :n], lhsT=wtile[:, c, jj * 128:(jj + 1) * 128], rhs=hT[:, c, t0:t0 + n],
                                                start=(c == 0), stop=(c == 15)),
                 reads=[wkey, ("hT", c, bi)], writes=[("psb", pb)])
        return ps, ("psb", pb)

    for mb in range(48):
        s = st["k"] % 3
        st["k"] += 1
        P.dma("sp", wt[s][:], wqkv[mb], writes=[("wt", s)])
        for jj in range(1):
            m = mb
            kind = "q" if m < 16 else ("k" if m < 32 else "v")
            mi = m % 16
            sg = m % 2
            gcol = 0 if kind == "q" else 1
            for (t0, n, bi) in blocks:
                is_ctx = bi == 4
                if is_ctx and kind == "q":
                    continue
                ps, pk = proj(m, wt[s], ("wt", s), jj, t0, n, bi, kind)
                dst = stgc[sg][:, :n] if is_ctx else stg[sg][:, t0:t0 + n]
                dkey = ("stgc", sg) if is_ctx else ("stg", sg, bi)
                if kind == "v":
                    P.op("act", lambda e, ps=ps, dst=dst, n=n: e.activation(out=dst, in_=ps[:, :n], func=AF.Copy),
                         reads=[pk], writes=[dkey])
                    continue
                u = st["t"] % 2
                st["t"] += 1
                sb = 4 + st["ss"] % 2
                st["ss"] += 1
                pss = cm.psb[sb]
                P.op("act", lambda e, ps=ps, u=u, n=n: e.activation(out=sq[u][:, :n], in_=ps[:, :n], func=AF.Square),
                     reads=[pk], writes=[("sq", u)])
                P.op("pe", lambda e, pss=pss, u=u, n=n: e.matmul(pss[:, :n], lhsT=cm.ones[:], rhs=sq[u][:, :n], start=True, stop=True),
                     reads=[("sq", u), "ones"], writes=[("psb", sb)])
                P.op("act", lambda e, pss=pss, u=u, n=n: e.activation(out=rr[u][:, :n], in_=pss[:, :n], func=AF.Sqrt,
                                                                    scale=1.0 / DH, bias=cm.epsb[:, 0:1]),
                     reads=[("psb", sb), "epsb"], writes=[("rr", u)])
                P.op("dve", lambda e, u=u, n=n: e.reciprocal(out=rr[u][:, :n], in_=rr[u][:, :n]),
                     reads=[("rr", u)], writes=[("rr", u)])
                if is_ctx:
                    P.op("dve", lambda e, ps=ps, u=u, n=n, dst=dst, gcol=gcol: e.scalar_tensor_tensor(
                        out=dst, in0=ps[:, :n], scalar=g_sb[:, gcol:gcol + 1], in1=rr[u][:, :n], op0=ALU.mult, op1=ALU.mult),
                        reads=[pk, "qkg", ("rr", u)], writes=[dkey])
                    continue
                P.op("dve", lambda e, ps=ps, u=u, n=n, gcol=gcol: e.scalar_tensor_tensor(
                    out=qn[u][:, :n], in0=ps[:, :n], scalar=g_sb[:, gcol:gcol + 1], in1=rr[u][:, :n], op0=ALU.mult, op1=ALU.mult),
                    reads=[pk, "qkg", ("rr", u)], writes=[("qn", u)])
                P.op("act", lambda e, u=u, n=n: e.activation(out=sw[u][0:64, :n], in_=qn[u][64:128, :n], func=AF.Copy),
                     reads=[("qn", u)], writes=[("sw", u, 0)])
                P.op("act", lambda e, u=u, n=n: e.activation(out=sw[u][64:128, :n], in_=qn[u][0:64, :n], func=AF.Copy),
                     reads=[("qn", u)], writes=[("sw", u, 1)])
                P.op("pool", lambda e, u=u, n=n, t0=t0: e.tensor_tensor(out=tm[u][:, :n], in0=sw[u][:, :n], in1=sn[:, t0:t0 + n], op=ALU.mult),
                     reads=[("sw", u, 0), ("sw", u, 1), "sn"], writes=[("tm", u)])
                P.op("dve", lambda e, u=u, n=n, t0=t0: e.tensor_tensor(out=oo[u][:, :n], in0=qn[u][:, :n], in1=cs[:, t0:t0 + n], op=ALU.mult),
                     reads=[("qn", u), "cs"], writes=[("oo", u)])
                P.op("dve", lambda e, u=u, n=n, dst=dst: e.tensor_tensor(out=dst, in0=oo[u][:, :n], in1=tm[u][:, :n], op=ALU.add),
                     reads=[("oo", u), ("tm", u)], writes=[dkey])
            od = {"q": qT, "k": kT, "v": vT}[kind]
            out_toks.append(P.dma("pool", od[mi], stg[sg][:], reads=[("stg", sg, bi) for bi in range(4)], writes=[("o", kind, mi)]))
            if kind != "q":
                oc = {"k": kcT, "v": vcT}[kind]
                out_toks.append(P.dma("pool", oc[mi], stgc[sg][:], reads=[("stgc", sg)], writes=[("oc", kind, mi)]))
    P.final_wait("sp", out_toks)
    return P.build()


def rope_perm():
    return np.concatenate([np.arange(0, 128, 2), np.arange(1, 128, 2)])


def rope_tables(tok0, n):
    t = np.arange(tok0, tok0 + n)
    row = (t // GRID_W).astype(np.float32)
    col = (t % GRID_W).astype(np.float32)
    nf = DH // 4
    inv = (10000.0 ** (-np.arange(nf, dtype=np.float32) / nf)).astype(np.float32)
    ang = np.concatenate([row[:, None] * inv, col[:, None] * inv], -1)
    c = np.cos(ang).astype(np.float32).T
    s = np.sin(ang).astype(np.float32).T
    cs = np.concatenate([c, c], 0)
    sn = np.concatenate([-s, s], 0)
    return np.ascontiguousarray(cs), np.ascontiguousarray(sn)


def run_a1(x, ctx, mods_l, normg_l, w_qkv, qk_g):
    nc = build_a1()
    perm = rope_perm()
    w = w_qkv
    qkg = np.ascontiguousarray(qk_g[:, perm].T)
    maps = []
    for i in range(8):
        b, k = i // 4, i % 4
        cs, sn = rope_tables(k * TLAT, TLAT)
        maps.append({"x_lat": to_fm(x[b, k * TLAT:(k + 1) * TLAT]), "x_ctx": to_fm(ctx[b, k * TCTX:(k + 1) * TCTX]),
                     "mod_lat": vec_fm(mods_l[b].reshape(6, 2048)), "mod_ctx": vec_fm(mods_l[2].reshape(6, 2048)),
                     "normg": vec_fm(normg_l), "wqkv": w, "qkg": qkg, "cs": cs, "sn": sn})
    res = run_bass_kernel_spmd(nc, maps, core_ids=list(range(8)))
    return res.results


NKT = (CTX + 8192) // 128
KH = NKT // 2


def build_a2(lambda_init, dbg=0):
    nc = bass.Bass("TRN2", target_bir_lowering=False)
    qT = nc.dram_tensor("qT", [16, 128, TLAT], BF16, kind="ExternalInput").ap()
    kT = nc.dram_tensor("kT", [16, 128, NKT * 128], BF16, kind="ExternalInput").ap()
    vv = nc.dram_tensor("vv", [8, 128, NKT, 256], BF16, kind="ExternalInput").ap()
    x_lat = nc.dram_tensor("x_lat", [128, 16, TLAT], F32, kind="ExternalInput").ap()
    mod_lat = nc.dram_tensor("mod_lat", [128, 6, 16], F32, kind="ExternalInput").ap()
    normg = nc.dram_tensor("normg", [128, 2, 16], F32, kind="ExternalInput").ap()
    lamv = nc.dram_tensor("lamv", [128, 4], F32, kind="ExternalInput").ap()
    sublng = nc.dram_tensor("sublng", [128, 2], F32, kind="ExternalInput").ap()
    w_o = nc.dram_tensor("w_o", [16, 128, 16, 128], BF16, kind="ExternalInput").ap()
    w_in = nc.dram_tensor("w_in", [NJ, 2, 128, 16, 128], BF16, kind="ExternalInput").ap()
    w_out = nc.dram_tensor("w_out", [16, 128, NJ, 128], BF16, kind="ExternalInput").ap()
    out = nc.dram_tensor("out_lat", [128, 16, TLAT], F32, kind="ExternalOutput").ap()

    P = Prog(nc)
    cm = Common(P, 512, halo=0)
    cm.setup_eps()
    ffn = FFN(P, cm, w_in, w_out, 512, nsplit=4, WC=128)
    xb = P.sbuf("xb", [128, 16, 512], F32)
    hT = P.sbuf("hT", [128, 16, 512], BF16)
    ring = [dict(k=P.sbuf("rk%d" % s, [128, 2, KH * 128], BF16), v=P.sbuf("rv%d" % s, [128, KH, 256], BF16)) for s in range(2)]
    qsb = [P.sbuf("qsb%d" % s, [128, 2, 512], BF16) for s in range(2)]
    pT = [P.sbuf("pT%d" % s, [128, 512], BF16) for s in range(3)]
    osb = [P.sbuf("osb%d" % i, [128, 2, 512], F32) for i in range(2)]
    rden = [P.sbuf("rden%d" % i, [128, 512], F32) for i in range(2)]
    dacc = [P.sbuf("dacc%d" % i, [128, 512], F32) for i in range(2)]
    dif = P.sbuf("dif", [128, 2, 512], F32)
    sqd = P.sbuf("sqd", [128, 2, 512], BF16)
    rst = P.sbuf("rst", [128, 512], F32)
    ng = P.sbuf("ng", [128, 2, 16], F32)
    mod = P.sbuf("modsb", [128, 6, 16], F32)
    G2 = P.sbuf("G2", [128, 16], F32)
    lv = P.sbuf("lv", [128, 4], F32)
    lpr = P.sbuf("lpr", [128, 2], F32)
    lex = P.sbuf("lex", [128, 2], F32)
    nlam = P.sbuf("nlam", [128, 1], F32)
    sg = P.sbuf("sg", [128, 2], F32)
    ones32 = P.sbuf("ones32", [128, 128], F32)
    eps256 = cm.epsb
    P.dma("sp", ng[:], normg, writes=["ng"])
    P.dma("sp", mod[:], mod_lat, writes=["mod"])
    P.dma("sp", lv[:], lamv, writes=["lv"])
    P.dma("sp", sg[:], sublng, writes=["sg"])
    P.op("dve", lambda e: e.memset(ones32[:], 1.0), writes=["ones32"])
    P.op("dve", lambda e: e.scalar_tensor_tensor(out=G2[:], in0=mod[:, 4, :], scalar=1.0, in1=ng[:, 1, :],
                                                 op0=ALU.add, op1=ALU.mult), reads=["mod", "ng"], writes=["G2"])
    P.op("dve", lambda e: e.tensor_tensor(out=lpr[:, 0:1], in0=lv[:, 0:1], in1=lv[:, 1:2], op=ALU.mult), reads=["lv"], writes=["lpr0"])
    P.op("dve", lambda e: e.tensor_tensor(out=lpr[:, 1:2], in0=lv[:, 2:3], in1=lv[:, 3:4], op=ALU.mult), reads=["lv"], writes=["lpr1"])
    P.op("pe", lambda e: e.matmul(cm.psb[0][:, 0:2], lhsT=ones32[:], rhs=lpr[:], start=True, stop=True),
         reads=["ones32", "lpr0", "lpr1"], writes=[("psb", 0)])
    P.op("act", lambda e: e.activation(out=lex[:], in_=cm.psb[0][:, 0:2], func=AF.Exp), reads=[("psb", 0)], writes=["lex"])
    P.op("dve", lambda e: e.tensor_tensor(out=nlam[:], in0=lex[:, 1:2], in1=lex[:, 0:1], op=ALU.subtract), reads=["lex"], writes=["nlam"])
    P.op("dve", lambda e: e.tensor_scalar(out=nlam[:], in0=nlam[:], scalar1=-float(lambda_init), scalar2=None, op0=ALU.add),
         reads=["nlam"], writes=["nlam"])
    P.op("dve", lambda e: e.tensor_scalar(out=sg[:], in0=sg[:], scalar1=float(1.0 - lambda_init), scalar2=None, op0=ALU.mult),
         reads=["sg"], writes=["sg"])
    xkeys = [("xb", c) for c in range(16)]
    hkeys = [("hT", c) for c in range(16)]
    out_toks = []
    st = dict(ring=0, q=0, pt=0, sT=0, wo=0, py=0)
    SCALE = float(DH) ** -0.5

    def attn_head(qb, h):
        qs = st["q"] % 2
        st["q"] += 1
        for i in range(2):
            P.dma("sp", qsb[qs][:, i, :], qT[2 * h + i][:, qb * 512:(qb + 1) * 512], writes=[("qsb", qs, i)])
        for half in range(2):
            rs_ = st["ring"] % 2
            st["ring"] += 1
            rg = ring[rs_]
            for i in range(2):
                P.dma("sp", rg["k"][:, i, :], kT[2 * h + i][:, half * KH * 128:(half + 1) * KH * 128], writes=[("rk", rs_, i)])
            P.dma("sp", rg["v"][:], vv[h][:, half * KH:(half + 1) * KH, :], writes=[("rv", rs_)])
            steps = [(i, kt) for i in range(2) for kt in range(KH)]

            def emit_s(i, kt, rg=rg, rs_=rs_):
                sb = st["sT"] % 2
                st["sT"] += 1
                pss = cm.psb[sb]
                P.op("pe", lambda e, pss=pss, rg=rg, i=i, kt=kt: e.matmul(
                    pss[:, :], lhsT=rg["k"][:, i, kt * 128:(kt + 1) * 128], rhs=qsb[qs][:, i, :], start=True, stop=True),
                    reads=[("rk", rs_, i), ("qsb", qs, i)], writes=[("psb", sb)])
                pi = st["pt"] % 3
                st["pt"] += 1
                P.op("act", lambda e, pss=pss, pi=pi: e.activation(out=pT[pi][:], in_=pss[:, :], func=AF.Exp, scale=SCALE),
                     reads=[("psb", sb)], writes=[("pT", pi)])
                return pi

            def emit_pv(i, kt, pi, rg=rg, rs_=rs_, half=half):
                first = (half == 0 and kt == 0)
                last = (half == 1 and kt == KH - 1)
                for ec in range(2):
                    P.op("pe", lambda e, rg=rg, kt=kt, ec=ec, pi=pi, i=i, first=first, last=last: e.matmul(
                        cm.psb[2 + 2 * i + ec][:, :], lhsT=rg["v"][:, kt, ec * 128:(ec + 1) * 128], rhs=pT[pi][:],
                        start=first, stop=last),
                        reads=[("rv", rs_), ("pT", pi)], writes=[("psb", 2 + 2 * i + ec)])
                if first:
                    P.op("dve", lambda e, pi=pi, i=i: e.tensor_copy(out=dacc[i][:], in_=pT[pi][:]),
                         reads=[("pT", pi)], writes=[("dacc", i)])
                else:
                    P.op("dve", lambda e, pi=pi, i=i: e.tensor_tensor(out=dacc[i][:], in0=dacc[i][:], in1=pT[pi][:], op=ALU.add),
                         reads=[("pT", pi), ("dacc", i)], writes=[("dacc", i)])

            pend = emit_s(*steps[0])
            for j in range(len(steps)):
                nxt = emit_s(*steps[j + 1]) if j + 1 < len(steps) else None
                emit_pv(steps[j][0], steps[j][1], pend)
                pend = nxt
        for i in range(2):
            P.op("pe", lambda e, i=i: e.matmul(cm.psb[6 + i][:, :], lhsT=ones32[:], rhs=dacc[i][:], start=True, stop=True),
                 reads=["ones32", ("dacc", i)], writes=[("psb", 6 + i)])
            P.op("dve", lambda e, i=i: e.reciprocal(out=rden[i][:], in_=cm.psb[6 + i][:, :]),
                 reads=[("psb", 6 + i)], writes=[("rden", i)])
            for ec in range(2):
                P.op("dve", lambda e, i=i, ec=ec: e.tensor_tensor(out=osb[i][:, ec, :], in0=cm.psb[2 + 2 * i + ec][:, :],
                                                                  in1=rden[i][:], op=ALU.mult),
                     reads=[("psb", 2 + 2 * i + ec), ("rden", i)], writes=[("osb", i, ec)])
        for ec in range(2):
            P.op("dve", lambda e, ec=ec: e.scalar_tensor_tensor(out=dif[:, ec, :], in0=osb[1][:, ec, :], scalar=nlam[:, 0:1],
                                                                in1=osb[0][:, ec, :], op0=ALU.mult, op1=ALU.add),
                 reads=[("osb", 1, ec), ("osb", 0, ec), "nlam"], writes=[("dif", ec)])
            P.op("act", lambda e, ec=ec: e.activation(out=sqd[:, ec, :], in_=dif[:, ec, :], func=AF.Square),
                 reads=[("dif", ec)], writes=[("sqd", ec)])
        for ec in range(2):
            P.op("pe", lambda e, ec=ec: e.matmul(cm.psb[0][:, :], lhsT=cm.ones[:], rhs=sqd[:, ec, :], start=(ec == 0), stop=(ec == 1)),
                 reads=["ones", ("sqd", ec)], writes=[("psb", 0)])
        P.op("act", lambda e: e.activation(out=rst[:], in_=cm.psb[0][:, :], func=AF.Sqrt, scale=1.0 / 256.0, bias=cm.epsb[:, 0:1]),
             reads=[("psb", 0), "epsb"], writes=["rst"])
        P.op("dve", lambda e: e.reciprocal(out=rst[:], in_=rst[:]), reads=["rst"], writes=["rst"])
        for ec in range(2):
            P.op("dve", lambda e, ec=ec: e.scalar_tensor_tensor(out=hT[:, 2 * h + ec, :], in0=dif[:, ec, :], scalar=sg[:, ec:ec + 1],
                                                                in1=rst[:], op0=ALU.mult, op1=ALU.mult),
                 reads=[("dif", ec), "sg", "rst"], writes=[hkeys[2 * h + ec]])

    def do_block(qb):
        for h in range(NHEAD):
            attn_head(qb, h)
        P.dma("sp", xb[:], x_lat[:, :, qb * 512:(qb + 1) * 512], writes=xkeys)
        for m in range(16):
            s = ffn.kin % 2
            ffn.kin += 1
            wt = ffn.wg[s]
            P.dma("sp", wt[:], w_o[m], writes=[("wg", s)])
            q = ffn.km % 2
            ffn.km += 1
            py = cm.psb[6 + q]
            for c in range(16):
                P.op("pe", lambda e, wt=wt, py=py, c=c: e.matmul(py[:, :], lhsT=wt[:, c, :], rhs=hT[:, c, :],
                                                                  start=(c == 0), stop=(c == 15)),
                     reads=[("wg", s), hkeys[c]], writes=[("psb", 6 + q)])
            P.op("dve", lambda e, py=py, m=m: e.scalar_tensor_tensor(
                out=xb[:, m, :], in0=py[:, :], scalar=mod[:, 2, m:m + 1], in1=xb[:, m, :], op0=ALU.mult, op1=ALU.add),
                reads=[("psb", 6 + q), "mod", xkeys[m]], writes=[xkeys[m]])
        if dbg == 0:
            cm.norm_mod(xb[:, :, :], 512, (G2, "G2"), (mod[:, 3, :], "mod"), lambda c: hT[:, c, :], hkeys, xkeys, stat_bank=0)
            ffn.emit(hT, hkeys, 512, xb, 0, xkeys, (mod[:, 5, :], "mod"))
        out_toks.append(P.dma("sp", out[:, :, qb * 512:(qb + 1) * 512], xb[:], reads=xkeys, writes=[("out", qb)]))

    for qb in range(4):
        do_block(qb)
    P.final_wait("sp", out_toks)
    return P.build()


def run_a2(a1res, x, mods_l, normg_l, lam_vec, subln_g, w_o, w_in, w_out, lambda_init, dbg=0):
    nc = build_a2(lambda_init, dbg)
    maps = []
    kv = []
    for b in range(2):
        kparts = [a1res[b * 4 + k]["kcT"] for k in range(4)] + [a1res[b * 4 + k]["kT"] for k in range(4)]
        kall = np.ascontiguousarray(np.concatenate(kparts, axis=2))
        vparts = [a1res[b * 4 + k]["vcT"] for k in range(4)] + [a1res[b * 4 + k]["vT"] for k in range(4)]
        vall = np.concatenate(vparts, axis=2)
        v5 = vall.reshape(8, 2, 128, NKT, 128)
        v5 = np.ascontiguousarray(v5.transpose(0, 4, 3, 1, 2).reshape(8, 128, NKT, 256))
        kv.append((kall, v5))
    for i in range(8):
        b, k = i // 4, i % 4
        maps.append({"qT": a1res[i]["qT"], "kT": kv[b][0], "vv": kv[b][1], "x_lat": to_fm(x[b, k * TLAT:(k + 1) * TLAT]),
                     "mod_lat": vec_fm(mods_l[b].reshape(6, 2048)), "normg": vec_fm(normg_l),
                     "lamv": np.ascontiguousarray(lam_vec.T), "sublng": np.ascontiguousarray(subln_g.reshape(2, 128).T),
                     "w_o": w_o, "w_in": w_in, "w_out": w_out})
    res = run_bass_kernel_spmd(nc, maps, core_ids=list(range(8)))
    xo = np.zeros_like(x)
    for i in range(8):
        b, k = i // 4, i % 4
        xo[b, k * TLAT:(k + 1) * TLAT] = from_fm(res.results[i]["out_lat"])
    return xo


LW = 96
NSET = 4
LG = 256
C0 = 0.6065306597126334
R1_TCTX = 128


def build_r1(segs=(("lat", TLAT), ("ctx", R1_TCTX)), NB=256):
    nc = bass.Bass("TRN2", target_bir_lowering=False)
    dr = {}
    for name, T in segs:
        dr[name] = dict(
            x=nc.dram_tensor("x_" + name, [128, 16, T + 2], F32, kind="ExternalInput").ap(),
            valid=nc.dram_tensor("valid_" + name, [T + 2], F32, kind="ExternalInput").ap(),
            mod=nc.dram_tensor("mod_" + name, [128, 6, 16], F32, kind="ExternalInput").ap(),
            ot=nc.dram_tensor("ot_" + name, [2, 4, 16, 128, T], BF16, kind="ExternalOutput").ap(),
            pc=nc.dram_tensor("pc_" + name, [128, 2, 16, T // 128], F32, kind="ExternalOutput").ap(),
            v=nc.dram_tensor("v_" + name, [16, 128, T], BF16, kind="ExternalOutput").ap(),
            bonus=nc.dram_tensor("bonus_" + name, [16, 128, T], F32, kind="ExternalOutput").ap(),
            g=nc.dram_tensor("g_" + name, [16, 128, T], F32, kind="ExternalOutput").ap(),
        )
    normg = nc.dram_tensor("normg", [128, 2, 16], F32, kind="ExternalInput").ap()
    mu_d = nc.dram_tensor("mu", [128, 6, 16], F32, kind="ExternalInput").ap()
    w_rkv = nc.dram_tensor("w_rkv", [3, 16, 128, 16, 128], BF16, kind="ExternalInput").ap()
    w_la = nc.dram_tensor("w_la", [2, D, LW], F32, kind="ExternalInput").ap()
    w_lb = nc.dram_tensor("w_lb", [2, LW, D], F32, kind="ExternalInput").ap()
    a_la = nc.dram_tensor("a_la", [2, D, LW], F32, kind="ExternalInput").ap()
    a_lb = nc.dram_tensor("a_lb", [2, LW, D], F32, kind="ExternalInput").ap()
    g_la = nc.dram_tensor("g_la", [D, LG], F32, kind="ExternalInput").ap()
    g_lb = nc.dram_tensor("g_lb", [LG, D], F32, kind="ExternalInput").ap()
    dirvec = nc.dram_tensor("dirvec", [128, 2, 4, 16], F32, kind="ExternalInput").ap()
    rk_d = nc.dram_tensor("r_k", [128, 16], F32, kind="ExternalInput").ap()
    rmask_d = nc.dram_tensor("rmask", [128, NB], F32, kind="ExternalInput").ap()

    P = Prog(nc)
    cm = Common(P, NB, halo=1)
    cm.setup_eps()
    W = NB + 2
    xb = P.sbuf("xb", [128, 16, W], F32)
    xx = P.sbuf("xx", [128, 16, NB], BF16)
    xm = [P.sbuf("xm%d" % i, [128, 16, NB], BF16) for i in range(3)]
    vmask = P.sbuf("vmask", [128, W], F32)
    ng = P.sbuf("ng", [128, 2, 16], F32)
    mu = P.sbuf("mu_sb", [128, 6, 16], F32)
    dv = P.sbuf("dv_sb", [128, 2, 4, 16], F32)
    rk = P.sbuf("rk_sb", [128, 16], F32)
    rmask = P.sbuf("rmask_sb", [128, NB], F32)
    bd = P.sbuf("bd_bf", [128, 128], BF16)
    wla = [P.sbuf("wla%d" % d, [128, 16, LW], BF16) for d in range(2)]
    ala = [P.sbuf("ala%d" % d, [128, 16, LW], BF16) for d in range(2)]
    gla = P.sbuf("gla", [128, 16, LG], BF16)
    wlb = [P.sbuf("wlb%d" % d, [LW, D], BF16) for d in range(2)]
    alb = [P.sbuf("alb%d" % d, [LW, D], BF16) for d in range(2)]
    glb = P.sbuf("glb", [128, 2, D], BF16)
    tw = [P.sbuf("tw%d" % d, [LW, NB], BF16) for d in range(2)]
    al = [P.sbuf("al%d" % d, [LW, NB], BF16) for d in range(2)]
    sgl = P.sbuf("sgl", [128, 2, NB], BF16)
    wt = [P.sbuf("wt%d" % i, [128, 16, 128], BF16) for i in range(6)]
    T2 = {}

    def tmp(name, dt=F32, n=NB):
        if name not in T2:
            T2[name] = P.sbuf("t_" + name, [128, n], dt)
        return T2[name]

    P.dma("sp", ng[:], normg, writes=["ng"])
    P.dma("sp", mu[:], mu_d, writes=["mu"])
    P.dma("sp", dv[:], dirvec, writes=["dv"])
    P.dma("sp", rk[:], rk_d, writes=["rk"])
    P.dma("sp", rmask[:], rmask_d, writes=["rmask"])
    P.op("pool", lambda e: e.memset(bd[:], 0.0), writes=["bd"])
    P.op("pool", lambda e: e.memset(bd[0:64, 0:64], 1.0), writes=["bd"])
    P.op("pool", lambda e: e.memset(bd[64:128, 64:128], 1.0), writes=["bd"])
    for d in range(2):
        P.dma("pool", wla[d][:], w_la[d].rearrange("(c p) n -> p c n", p=128), writes=[("wla", d)])
        P.dma("pool", ala[d][:], a_la[d].rearrange("(c p) n -> p c n", p=128), writes=[("ala", d)])
        P.dma("pool", wlb[d][:], w_lb[d], writes=[("wlb", d)])
        P.dma("pool", alb[d][:], a_lb[d], writes=[("alb", d)])
    P.dma("pool", gla[:], g_la.rearrange("(c p) n -> p c n", p=128), writes=["gla"])
    P.dma("pool", glb[:], g_lb.rearrange("(k p) n -> p k n", p=128), writes=["glb"])

    xkeys = [("xb", c) for c in range(16)]
    bank_rr = [0]

    def nb_():
        b_ = bank_rr[0] % 4 + 2
        bank_rr[0] += 1
        return cm.psb[b_], ("psb", b_)

    out_toks = []
    st = dict(wt=0, stg=0)

    def do_block(name, T, t0, n, mod, G1, pcs_all):
        d_ = dr[name]
        nw = n + 2
        nj = n // 128
        mk = ("mod", name)
        P.dma("sp", xb[:, :, :nw], d_["x"][:, :, t0:t0 + nw], writes=xkeys)
        P.dma("sp", vmask[:, :nw], d_["valid"][t0:t0 + nw].partition_broadcast(128), writes=["vmask"])
        cm.norm_mod(xb[:, :, :nw], nw, (G1, ("G1", name)), (mod[:, 0, :], mk),
                    lambda c: xb[:, c, :nw], xkeys, xkeys, stat_bank=0)
        for col in (0, nw - 1):
            P.op("dve", lambda e, col=col: e.tensor_tensor(
                out=xb[:, :, col:col + 1], in0=xb[:, :, col:col + 1],
                in1=vmask[:, col:col + 1].unsqueeze(1).broadcast_to([128, 16, 1]), op=ALU.mult),
                reads=xkeys + ["vmask"], writes=xkeys)
        for c in range(16):
            tq = tmp("xs%d" % (c % 2))
            P.op("pool", lambda e, c=c, tq=tq: e.tensor_tensor(out=tq[:, :n], in0=xb[:, c, 0:n], in1=xb[:, c, 2:n + 2], op=ALU.add),
                 reads=[xkeys[c]], writes=[("xs", c % 2)])
            P.op("dve", lambda e, c=c, tq=tq: e.scalar_tensor_tensor(out=xx[:, c, :n], in0=tq[:, :n], scalar=0.5, in1=xb[:, c, 1:n + 1],
                                                                     op0=ALU.mult, op1=ALU.subtract),
                 reads=[("xs", c % 2), xkeys[c]], writes=[("xx", c)])

        def mix(m, buf):
            for c in range(16):
                P.op("dve", lambda e, c=c: e.scalar_tensor_tensor(out=xm[buf][:, c, :n], in0=xx[:, c, :n], scalar=mu[:, m, c:c + 1],
                                                                  in1=xb[:, c, 1:n + 1], op0=ALU.mult, op1=ALU.add),
                     reads=[("xx", c), "mu", xkeys[c]], writes=[("xm", buf, c)])

        mix(1, 0)
        for d in range(2):
            bank, bk = nb_()
            for c in range(16):
                P.op("pe", lambda e, c=c, d=d, bank=bank: e.matmul(bank[0:LW, :n], lhsT=wla[d][:, c, :], rhs=xm[0][:, c, :n],
                                                                    start=(c == 0), stop=(c == 15)),
                     reads=[("wla", d), ("xm", 0, c)], writes=[bk])
            P.op("act", lambda e, d=d, bank=bank: e.activation(out=tw[d][:, :n], in_=bank[0:LW, :n], func=AF.Tanh),
                 reads=[bk], writes=[("tw", d)])
        mix(4, 1)
        for d in range(2):
            bank, bk = nb_()
            for c in range(16):
                P.op("pe", lambda e, c=c, d=d, bank=bank: e.matmul(bank[0:LW, :n], lhsT=ala[d][:, c, :], rhs=xm[1][:, c, :n],
                                                                    start=(c == 0), stop=(c == 15)),
                     reads=[("ala", d), ("xm", 1, c)], writes=[bk])
            P.op("act", lambda e, d=d, bank=bank: e.activation(out=al[d][:, :n], in_=bank[0:LW, :n], func=AF.Copy),
                 reads=[bk], writes=[("al", d)])
        mix(5, 2)
        for kc in range(2):
            bank, bk = nb_()
            for c in range(16):
                P.op("pe", lambda e, c=c, kc=kc, bank=bank: e.matmul(bank[:, :n], lhsT=gla[:, c, kc * 128:(kc + 1) * 128], rhs=xm[2][:, c, :n],
                                                                      start=(c == 0), stop=(c == 15)),
                     reads=["gla", ("xm", 2, c)], writes=[bk])
            P.op("act", lambda e, kc=kc, bank=bank: e.activation(out=sgl[:, kc, :n], in_=bank[:, :n], func=AF.Sigmoid),
                 reads=[bk], writes=[("sgl", kc)])
        mix(0, 0)
        mix(2, 1)
        mix(3, 2)
        def chain(c, d, rc_, kc_, vc_, cbank, cbk):
            sid = (2 * c + d) % NSET
            w0 = dv[:, d, 0, c:c + 1]
            a0 = dv[:, d, 1, c:c + 1]
            kk_ = dv[:, d, 2, c:c + 1]
            ka_ = dv[:, d, 3, c:c + 1]
            bank, bk = nb_()
            P.op("pe", lambda e, d=d, c=c, bank=bank: e.matmul(bank[:, :n], lhsT=wlb[d][:, c * 128:(c + 1) * 128], rhs=tw[d][:, :n], start=True, stop=True),
                 reads=[("wlb", d), ("tw", d)], writes=[bk])
            yield
            sg_ = tmp("sg_%d" % sid)
            P.op("act", lambda e, bank=bank, sg_=sg_, w0=w0: e.activation(out=sg_[:, :n], in_=bank[:, :n], func=AF.Sigmoid, bias=w0),
                 reads=[bk, "dv"], writes=[("sg", sid)])
            yield
            bank, bk = nb_()
            P.op("pe", lambda e, d=d, c=c, bank=bank: e.matmul(bank[:, :n], lhsT=alb[d][:, c * 128:(c + 1) * 128], rhs=al[d][:, :n], start=True, stop=True),
                 reads=[("alb", d), ("al", d)], writes=[bk])
            yield
            ag = tmp("ag_%d" % sid)
            P.op("act", lambda e, bank=bank, ag=ag, a0=a0: e.activation(out=ag[:, :n], in_=bank[:, :n], func=AF.Sigmoid, bias=a0),
                 reads=[bk, "dv"], writes=[("ag", sid)])
            yield
            sq = tmp("sqk_%d" % sid, BF16)
            P.op("act", lambda e, sq=sq, kc_=kc_, kk_=kk_: e.activation(out=sq[:, :n], in_=kc_[:, :n], func=AF.Square, scale=kk_),
                 reads=[("rkv", 1, c % 2), "dv"], writes=[("sqk", sid)])
            yield
            bank, bk = nb_()
            P.op("pe", lambda e, sq=sq, bank=bank: e.matmul(bank[:, :n], lhsT=bd[:], rhs=sq[:, :n], start=True, stop=True),
                 reads=["bd", ("sqk", sid)], writes=[bk])
            yield
            rn = tmp("rn_%d" % sid)
            P.op("act", lambda e, rn=rn, bank=bank: e.activation(out=rn[:, :n], in_=bank[:, :n], func=AF.Sqrt), reads=[bk], writes=[("rn", sid)])
            yield
            P.op("dve", lambda e, rn=rn: e.tensor_scalar(out=rn[:, :n], in0=rn[:, :n], scalar1=1e-12, scalar2=None, op0=ALU.max),
                 reads=[("rn", sid)], writes=[("rn", sid)])
            yield
            P.op("dve", lambda e, rn=rn: e.reciprocal(out=rn[:, :n], in_=rn[:, :n]), reads=[("rn", sid)], writes=[("rn", sid)])
            yield
            kkn = tmp("kkn_%d" % sid)
            P.op("dve", lambda e, kkn=kkn, kc_=kc_, rn=rn, kk_=kk_: e.scalar_tensor_tensor(out=kkn[:, :n], in0=kc_[:, :n], scalar=kk_, in1=rn[:, :n],
                                                                                   op0=ALU.mult, op1=ALU.mult),
                 reads=[("rkv", 1, c % 2), ("rn", sid), "dv"], writes=[("kkn", sid)])
            yield
            t1 = tmp("t1_%d" % sid)
            P.op("pool", lambda e, t1=t1, ag=ag, ka_=ka_: e.tensor_scalar(out=t1[:, :n], in0=ag[:, :n], scalar1=-1.0, scalar2=ka_, op0=ALU.add, op1=ALU.mult),
                 reads=[("ag", sid), "dv"], writes=[("t1", sid)])
            yield
            kd = tmp("kd_%d" % sid)
            P.op("dve", lambda e, kd=kd, t1=t1, kc_=kc_: e.scalar_tensor_tensor(out=kd[:, :n], in0=t1[:, :n], scalar=1.0, in1=kc_[:, :n],
                                                                            op0=ALU.add, op1=ALU.mult),
                 reads=[("t1", sid), ("rkv", 1, c % 2)], writes=[("kd", sid)])
            yield
            bs = tmp("bs_%d" % sid)
            P.op("pool", lambda e, bs=bs, kkn=kkn, ag=ag: e.tensor_tensor(out=bs[:, :n], in0=kkn[:, :n], in1=ag[:, :n], op=ALU.mult),
                 reads=[("kkn", sid), ("ag", sid)], writes=[("bs", sid)])
            yield
            Lf = tmp("Lf_%d" % sid)
            P.op("dve", lambda e, Lf=Lf, sg_=sg_: e.tensor_tensor_scan(out=Lf[:, :n], data0=rmask[:, :n], data1=sg_[:, :n], initial=0.0,
                                                                      op0=ALU.mult, op1=ALU.add),
                 reads=["rmask", ("sg", sid)], writes=[("Lf", sid)])
            yield
            Li = tmp("Li_%d" % sid)
            Lx = tmp("Lx_%d" % sid)
            Lf3 = Lf[:, :n].rearrange("p (j t) -> p j t", t=128)
            tot = Lf3[:, :, 127:128].broadcast_to([128, nj, 128])
            if d == 0:
                P.op("pool", lambda e, Lx=Lx, Lf=Lf, sg_=sg_: e.tensor_tensor(out=Lx[:, :n], in0=Lf[:, :n], in1=sg_[:, :n], op=ALU.subtract),
                     reads=[("Lf", sid), ("sg", sid)], writes=[("Lx", sid)])
                yield
                Li = Lf
                lik = ("Lf", sid)
            else:
                P.op("pool", lambda e, Lx=Lx, Lf3=Lf3, tot=tot: e.tensor_tensor(out=Lx[:, :n].rearrange("p (j t) -> p j t", t=128), in0=tot, in1=Lf3,
                                                                                op=ALU.subtract),
                     reads=[("Lf", sid)], writes=[("Lx", sid)])
                yield
                P.op("pool", lambda e, Li=Li, Lx=Lx, sg_=sg_: e.tensor_tensor(out=Li[:, :n], in0=Lx[:, :n], in1=sg_[:, :n], op=ALU.add),
                     reads=[("Lx", sid), ("sg", sid)], writes=[("Li", sid)])
                yield
                lik = ("Li", sid)
            ep = tmp("ep_%d" % sid)
            en = tmp("en_%d" % sid)
            ex = tmp("ex_%d" % sid)
            P.op("act", lambda e, ep=ep, Li=Li: e.activation(out=ep[:, :n], in_=Li[:, :n], func=AF.Exp, scale=-C0), reads=[lik], writes=[("ep", sid)])
            yield
            P.op("act", lambda e, en=en, Li=Li: e.activation(out=en[:, :n], in_=Li[:, :n], func=AF.Exp, scale=C0), reads=[lik], writes=[("en", sid)])
            yield
            P.op("act", lambda e, ex=ex, Lx=Lx: e.activation(out=ex[:, :n], in_=Lx[:, :n], func=AF.Exp, scale=-C0), reads=[("Lx", sid)], writes=[("ex", sid)])
            yield
            P.op("act", lambda e, d=d, c=c, Lf3=Lf3: e.activation(out=pcs_all[:, d, c, t0 // 128:t0 // 128 + nj], in_=Lf3[:, :, 127], func=AF.Exp, scale=-C0),
                 reads=[("Lf", sid)], writes=[("pcs", name)])
            yield
            stg = tmp("stg%d" % sid, BF16, 4 * NB)
            sk = ("stg", sid)
            P.op("dve", lambda e, stg=stg, kkn=kkn, ex=ex: e.scalar_tensor_tensor(out=stg[:, 0:n], in0=kkn[:, :n], scalar=-1.0, in1=ex[:, :n],
                                                                              op0=ALU.mult, op1=ALU.mult),
                 reads=[("kkn", sid), ("ex", sid)], writes=[sk])
            yield
            P.op("pool", lambda e, stg=stg, bs=bs, en=en: e.tensor_tensor(out=stg[:, NB:NB + n], in0=bs[:, :n], in1=en[:, :n], op=ALU.mult),
                 reads=[("bs", sid), ("en", sid)], writes=[sk])
            yield
            P.op("pool", lambda e, stg=stg, kd=kd, en=en: e.tensor_tensor(out=stg[:, 2 * NB:2 * NB + n], in0=kd[:, :n], in1=en[:, :n], op=ALU.mult),
                 reads=[("kd", sid), ("en", sid)], writes=[sk])
            yield
            P.op("pool", lambda e, stg=stg, rc_=rc_, ep=ep: e.tensor_tensor(out=stg[:, 3 * NB:3 * NB + n], in0=rc_[:, :n], in1=ep[:, :n], op=ALU.mult),
                 reads=[("rkv", 0, c % 2), ("ep", sid)], writes=[sk])
            yield
            out_toks.append(P.dma("pool", d_["ot"][d][:, c, :, t0:t0 + n].rearrange("o p n -> p o n"),
                                  stg[:].rearrange("p (o n) -> p o n", o=4)[:, :, :n], reads=[sk], writes=[("oo", name, d, c, t0)]))
            yield
            rkq = tmp("rkq_%d" % sid, BF16)
            P.op("dve", lambda e, rkq=rkq, rc_=rc_, kd=kd, c=c: e.scalar_tensor_tensor(out=rkq[:, :n], in0=rc_[:, :n], scalar=rk[:, c:c + 1], in1=kd[:, :n],
                                                                                   op0=ALU.mult, op1=ALU.mult),
                 reads=[("rkv", 0, c % 2), ("kd", sid), "rk"], writes=[("rkq", sid)])
            yield
            P.op("pe", lambda e, rkq=rkq, d=d, cbank=cbank: e.matmul(cbank[:, :n], lhsT=bd[:], rhs=rkq[:, :n], start=(d == 0), stop=(d == 1)),
                 reads=["bd", ("rkq", sid)], writes=[cbk])
            yield


        def prologue(c):
            outs = []
            for wi in range(3):
                s = st["wt"] % 6
                st["wt"] += 1
                P.dma("sp", wt[s][:], w_rkv[wi, c], writes=[("wt", s)])
                bank, bk = nb_()
                for kc in range(16):
                    P.op("pe", lambda e, kc=kc, s=s, wi=wi, bank=bank: e.matmul(bank[:, :n], lhsT=wt[s][:, kc, :], rhs=xm[wi][:, kc, :n],
                                                                                 start=(kc == 0), stop=(kc == 15)),
                         reads=[("wt", s), ("xm", wi, kc)], writes=[bk])
                dst = tmp(("rc", "kc", "vc")[wi] + str(c % 2))
                P.op("act", lambda e, dst=dst, bank=bank: e.activation(out=dst[:, :n], in_=bank[:, :n], func=AF.Copy),
                     reads=[bk], writes=[("rkv", wi, c % 2)])
                outs.append(dst)
            rc_, kc_, vc_ = outs
            vst = tmp("vst%d" % (c % 2), BF16)
            P.op("pool", lambda e, vst=vst, vc_=vc_: e.tensor_copy(out=vst[:, :n], in_=vc_[:, :n]), reads=[("rkv", 2, c % 2)], writes=[("vst", c % 2)])
            out_toks.append(P.dma("pool", d_["v"][c][:, t0:t0 + n], vst[:, :n], reads=[("vst", c % 2)], writes=[("ov", name, c, t0)]))
            bank, bk = nb_()
            for kc in range(2):
                P.op("pe", lambda e, kc=kc, c=c, bank=bank: e.matmul(bank[:, :n], lhsT=glb[:, kc, c * 128:(c + 1) * 128], rhs=sgl[:, kc, :n],
                                                                      start=(kc == 0), stop=(kc == 1)),
                     reads=["glb", ("sgl", 0), ("sgl", 1)], writes=[bk])
            gst = tmp("gst%d" % (c % 2))
            P.op("act", lambda e, gst=gst, bank=bank: e.activation(out=gst[:, :n], in_=bank[:, :n], func=AF.Copy), reads=[bk], writes=[("gst", c % 2)])
            out_toks.append(P.dma("act", d_["g"][c][:, t0:t0 + n], gst[:, :n], reads=[("gst", c % 2)], writes=[("og", name, c, t0)]))
            cb_ = 6 + (c % 2)
            cbank, cbk = cm.psb[cb_], ("psb", cb_)
            return rc_, kc_, vc_, cbank, cbk

        def epilogue(c, rc_, kc_, vc_, cbank, cbk):
            bst = tmp("bst%d" % (c % 2))
            P.op("dve", lambda e, bst=bst, vc_=vc_, cbank=cbank: e.tensor_tensor(out=bst[:, :n], in0=cbank[:, :n], in1=vc_[:, :n], op=ALU.mult),
                 reads=[cbk, ("rkv", 2, c % 2)], writes=[("bst", c % 2)])
            out_toks.append(P.dma("act", d_["bonus"][c][:, t0:t0 + n], bst[:, :n], reads=[("bst", c % 2)], writes=[("ob", name, c, t0)]))

        for cp in range(0, 16, 2):
            units = []
            for c in (cp, cp + 1):
                units.append((c,) + prologue(c))
            gens = [chain(u[0], d, *u[1:]) for u in units for d in range(2)]
            while gens:
                alive = []
                for g_ in gens:
                    try:
                        next(g_)
                        alive.append(g_)
                    except StopIteration:
                        pass
                gens = alive
            for u in units:
                epilogue(*u)

    for name, T in segs:
        mod = P.sbuf("modsb_" + name, [128, 6, 16], F32)
        G1 = P.sbuf("G1_" + name, [128, 16], F32)
        mk = ("mod", name)
        P.dma("sp", mod[:], dr[name]["mod"], writes=[mk])
        P.op("dve", lambda e, G1=G1, mod=mod: e.scalar_tensor_tensor(
            out=G1[:], in0=mod[:, 1, :], scalar=1.0, in1=ng[:, 0, :], op0=ALU.add, op1=ALU.mult),
            reads=[mk, "ng"], writes=[("G1", name)])
        pcs_all = P.sbuf("pcs_" + name, [128, 2, 16, T // 128], F32)
        for t0 in range(0, T, NB):
            do_block(name, T, t0, min(NB, T - t0), mod, G1, pcs_all)
        out_toks.append(P.dma("sp", dr[name]["pc"], pcs_all[:], reads=[("pcs", name)], writes=[("opc", name)]))
    P.final_wait("sp", out_toks)
    return P.build()


NCH = 66
NFC = 8
OPA, OPB, OPK, OPR = 0, 1, 2, 3


def build_r2(nch=NCH):
    nc = bass.Bass("TRN2", target_bir_lowering=False)
    fm = nc.dram_tensor("fm", [4, nch, 128, NFC, 128], BF16, kind="ExternalInput").ap()
    tmj = nc.dram_tensor("tm", [3, nch, 128, NFC, 128], BF16, kind="ExternalInput").ap()
    pcd = nc.dram_tensor("pc", [nch, 128, NFC], F32, kind="ExternalInput").ap()
    m4d = nc.dram_tensor("m4", [128, 2, 512], F32, kind="ExternalInput").ap()
    mld = nc.dram_tensor("ml", [128, 512], F32, kind="ExternalInput").ap()
    idd = nc.dram_tensor("idm", [128, 2, 128], F32, kind="ExternalInput").ap()
    mbd = nc.dram_tensor("mbd", [128, 512], F32, kind="ExternalInput").ap()
    yout = nc.dram_tensor("y", [nch, 128, NFC, 128], F32, kind="ExternalOutput").ap()

    P = Prog(nc)
    psb = [P.psum("psb%d" % i, [128, 512]) for i in range(8)]
    m4 = P.sbuf("m4s", [128, 2, 512], F32)
    ml = P.sbuf("mls", [128, 512], F32)
    idm = P.sbuf("ids", [128, 2, 128], F32)
    mbd4 = P.sbuf("mbds", [128, 512], F32)
    P.dma("sp", m4[:], m4d, writes=["m4"])
    P.dma("sp", ml[:], mld, writes=["ml"])
    P.dma("sp", idm[:], idd, writes=["idm"])
    P.dma("sp", mbd4[:], mbd, writes=["mbd4"])
    slots = []
    for s in range(2):
        d = dict(
            fa=P.sbuf("fa%d" % s, [128, NFC, 128], BF16), fb=P.sbuf("fb%d" % s, [128, NFC, 128], BF16),
            fr=P.sbuf("fr%d" % s, [128, NFC, 128], BF16),
            pa=[P.sbuf("pa%d_%d" % (s, h), [128, NFC, 128], BF16) for h in range(2)],
            pb=[P.sbuf("pb%d_%d" % (s, h), [128, NFC, 128], BF16) for h in range(2)],
            pk=[P.sbuf("pk%d_%d" % (s, h), [128, NFC, 128], BF16) for h in range(2)],
            tB=P.sbuf("tB%d" % s, [128, NFC, 128], BF16), tK=P.sbuf("tK%d" % s, [128, NFC, 128], BF16),
            tV=P.sbuf("tV%d" % s, [128, NFC, 128], BF16), pc=P.sbuf("pcs%d" % s, [128, NFC], F32),
        )
        for nm in ("pa", "pb", "pk"):
            for h in range(2):
                P.op("pool", lambda e, t=d[nm][h]: e.memset(t[:], 0.0), writes=[(nm, s, h)])
        slots.append(d)
    Hf = P.sbuf("Hf", [128, NFC, 128], F32)
    Hb = P.sbuf("Hb", [128, NFC, 128], BF16)
    P.op("dve", lambda e: e.memset(Hf[:], 0.0), writes=[("Hf", f) for f in range(NFC // 4)])
    P.op("dve", lambda e: e.memset(Hb[:], 0.0), writes=[("Hb", f) for f in range(NFC // 4)])
    A4p = [[P.sbuf("A4_%d_%d" % (f, par), [128, 2, 512], BF16) for f in range(NFC)] for par in range(2)]
    NN = [P.sbuf("NN_%d" % f, [128, 4, 128], F32) for f in range(NFC)]
    X = [[P.sbuf("X%d_%d" % (f, i), [128, 512], F32) for i in range(2)] for f in range(NFC // 2)]
    XT = [[P.sbuf("XT%d_%d" % (f, i), [128, 512], F32) for i in range(2)] for f in range(NFC // 2)]
    TTf = [[P.sbuf("TTf%d_%d" % (f, i), [128, 2, 128], F32) for i in range(2)] for f in range(NFC)]
    TTb = [[P.sbuf("TTb%d_%d" % (f, par), [128, 2, 128], BF16) for f in range(NFC)] for par in range(2)]
    Wsb = [P.sbuf("W%d" % f, [128, 512], BF16) for f in range(NFC // 4)]
    Usb = [P.sbuf("U%d" % f, [128, 512], BF16) for f in range(NFC // 4)]
    Ht = [P.sbuf("Ht%d" % i, [128, 512], F32) for i in range(NFC // 4)]
    yst = [P.sbuf("yst%d" % i, [128, NFC, 128], F32) for i in range(2)]
    out_toks = []

    def load(t):
        s = t % 2
        d = slots[s]
        P.dma("sp", d["fa"][:], fm[OPA, t], writes=[("fa", s)])
        P.dma("sp", d["fb"][:], fm[OPB, t], writes=[("fb", s)])
        P.dma("sp", d["fr"][:], fm[OPR, t], writes=[("fr", s)])
        for h in range(2):
            hp = slice(64 * h, 64 * h + 64)
            P.dma("sp", d["pa"][h][hp, :, :], fm[OPA, t, hp], writes=[("pa", s, h)])
            P.dma("sp", d["pb"][h][hp, :, :], fm[OPB, t, hp], writes=[("pb", s, h)])
            P.dma("sp", d["pk"][h][hp, :, :], fm[OPK, t, hp], writes=[("pk", s, h)])
        P.dma("sp", d["tB"][:], tmj[0, t], writes=[("tB", s)])
        P.dma("sp", d["tK"][:], tmj[1, t], writes=[("tK", s)])
        P.dma("sp", d["tV"][:], tmj[2, t], writes=[("tV", s)])
        P.dma("sp", d["pc"][:], pcd[t], writes=[("pc", s)])

    bank_rr = [0]

    def nb():
        b_ = bank_rr[0] % 8
        bank_rr[0] += 1
        return psb[b_], ("psb", b_)

    def stage_A(t):
        par = t % 2
        A4 = A4p[par]
        s = t % 2
        d = slots[s]
        for f in range(NFC):
            for h in range(2):
                bank, bk = nb()
                specs = [(d["pb"][h], ("pb", s, h), d["fa"], ("fa", s)),
                         (d["pk"][h], ("pk", s, h), d["fa"], ("fa", s)),
                         (d["pb"][h], ("pb", s, h), d["fr"], ("fr", s)),
                         (d["pk"][h], ("pk", s, h), d["fr"], ("fr", s))]
                for q, (lt, lk, rt, rk) in enumerate(specs):
                    P.op("pe", lambda e, bank=bank, q=q, lt=lt, rt=rt, f=f: e.matmul(
                        bank[:, q * 128:(q + 1) * 128], lhsT=lt[:, f, :], rhs=rt[:, f, :], start=True, stop=True),
                        reads=[lk, rk], writes=[bk])
                P.op("dve", lambda e, bank=bank, f=f, h=h: e.tensor_tensor(out=A4[f][:, h, :], in0=bank[:, :], in1=m4[:, h, :], op=ALU.mult),
                     reads=[bk, "m4"], writes=[("A4", par, f, h)])
            bank, bk = nb()
            for h in range(2):
                P.op("pe", lambda e, h=h, f=f, bank=bank: e.matmul(
                    bank[:, h * 128:(h + 1) * 128], lhsT=d["pb"][h][:, f, :], rhs=d["fa"][:, f, :], start=True, stop=True),
                    reads=[("pb", s, h), ("fa", s)], writes=[bk])
            for h in range(2):
                P.op("pe", lambda e, h=h, f=f, bank=bank: e.matmul(
                    bank[:, 256 + h * 128:256 + (h + 1) * 128], lhsT=d["pa"][h][:, f, :], rhs=d["fb"][:, f, :], start=True, stop=True),
                    reads=[("pa", s, h), ("fb", s)], writes=[bk])
            P.op("dve", lambda e, f=f, bank=bank: e.tensor_tensor(out=NN[f][:].rearrange("p q c -> p (q c)"), in0=bank[:, :], in1=ml[:], op=ALU.mult),
                 reads=[bk, "ml"], writes=[("NN", f)])
            P.op("pool", lambda e, f=f: e.tensor_tensor(out=TTf[f][0][:], in0=NN[f][:, 0:2, :], in1=idm[:], op=ALU.add),
                 reads=[("NN", f), "idm"], writes=[("TTf", f, 0)])

    def stage_D(t):
        par = t % 2
        for lvl in range(1, 7):
            pi, po = (lvl - 1) % 2, lvl % 2
            for fp in range(NFC // 2):
                def xin(ff, h, fp=fp, pi=pi, lvl=lvl):
                    if lvl == 1:
                        return NN[2 * fp + ff][:, 2 + h, :]
                    return X[fp][pi][:, ff * 256 + h * 128: ff * 256 + (h + 1) * 128]

                def xtin(ff, h, fp=fp, pi=pi, lvl=lvl):
                    if lvl == 1:
                        return NN[2 * fp + ff][:, h, :]
                    return XT[fp][pi][:, ff * 256 + h * 128: ff * 256 + (h + 1) * 128]
                if lvl == 1:
                    xk = [("NN", 2 * fp), ("NN", 2 * fp + 1)]
                    xtk = []
                else:
                    xk = [("X", fp, pi)]
                    xtk = [("XT", fp, pi)]
                bank, bk = nb()
                for ff in range(2):
                    for h in range(2):
                        P.op("pe", lambda e, h=h, ff=ff, xin=xin, xtin=xtin, bank=bank: e.matmul(
                            bank[:, ff * 256 + h * 128: ff * 256 + (h + 1) * 128], lhsT=xtin(ff, h), rhs=xin(ff, h), start=True, stop=True),
                            reads=xk + xtk, writes=[bk])
                P.op("act", lambda e, fp=fp, po=po, bank=bank: e.activation(out=X[fp][po][:], in_=bank[:, :], func=AF.Copy),
                     reads=[bk], writes=[("X", fp, po)])
                if lvl < 6:
                    bank2, bk2 = nb()
                    for ff in range(2):
                        for h in range(2):
                            P.op("pe", lambda e, h=h, ff=ff, xin=xin, xtin=xtin, bank2=bank2: e.matmul(
                                bank2[:, ff * 256 + h * 128: ff * 256 + (h + 1) * 128], lhsT=xin(ff, h), rhs=xtin(ff, h), start=True, stop=True),
                                reads=xk + xtk, writes=[bk2])
                    P.op("act", lambda e, fp=fp, po=po, bank2=bank2: e.activation(out=XT[fp][po][:], in_=bank2[:, :], func=AF.Copy),
                         reads=[bk2], writes=[("XT", fp, po)])
            for fp in range(NFC // 2):
                bank, bk = nb()
                for ff in range(2):
                    f = 2 * fp + ff
                    for h in range(2):
                        P.op("pe", lambda e, h=h, f=f, ff=ff, fp=fp, po=po, pi=pi, bank=bank: e.matmul(
                            bank[:, ff * 256 + h * 128: ff * 256 + (h + 1) * 128],
                            lhsT=X[fp][po][:, ff * 256 + h * 128: ff * 256 + (h + 1) * 128], rhs=TTf[f][pi][:, h, :], start=True, stop=True),
                            reads=[("X", fp, po), ("TTf", f, pi)], writes=[bk])
                for ff in range(2):
                    f = 2 * fp + ff
                    if lvl < 6:
                        P.op("dve", lambda e, f=f, ff=ff, po=po, pi=pi, bank=bank: e.tensor_tensor(
                            out=TTf[f][po][:].rearrange("p h c -> p (h c)"), in0=bank[:, ff * 256:(ff + 1) * 256],
                            in1=TTf[f][pi][:].rearrange("p h c -> p (h c)"), op=ALU.add),
                            reads=[bk, ("TTf", f, pi)], writes=[("TTf", f, po)])
                    else:
                        P.op("dve", lambda e, f=f, ff=ff, po=po, pi=pi, bank=bank: e.tensor_tensor(
                            out=TTb[par][f][:].rearrange("p h c -> p (h c)"), in0=bank[:, ff * 256:(ff + 1) * 256],
                            in1=TTf[f][pi][:].rearrange("p h c -> p (h c)"), op=ALU.add),
                            reads=[bk, ("TTf", f, pi)], writes=[("TTb", par, f)])
            yield

    def stage_S(t):
        par = t % 2
        A4 = A4p[par]
        s = t % 2
        d = slots[s]
        ys = yst[t % 2]
        NG = NFC // 4
        for g in range(NG):
            bank, bk = nb()
            for fi in range(4):
                f = 4 * g + fi
                off = fi * 128
                P.op("pe", lambda e, f=f, off=off, bank=bank: e.matmul(bank[:, off:off + 128], lhsT=d["fa"][:, f, :], rhs=Hb[:, f, :], start=True, stop=False),
                     reads=[("fa", s), ("Hb", g)], writes=[bk])
                for h in range(2):
                    P.op("pe", lambda e, f=f, h=h, off=off, bank=bank: e.matmul(bank[:, off + 64 * h: off + 64 * h + 64], lhsT=A4[f][:, h, 128:256],
                                                                                 rhs=d["tV"][:, f, 64 * h:64 * h + 64], start=False, stop=(h == 1)),
                         reads=[("A4", par, f, h), ("tV", s)], writes=[bk])
            P.op("act", lambda e, g=g, bank=bank: e.activation(out=Wsb[g][:], in_=bank[:, :], func=AF.Copy),
                 reads=[bk], writes=[("W", g)])
        yield
        for g in range(NG):
            bank, bk = nb()
            for fi in range(4):
                f = 4 * g + fi
                off = fi * 128
                for h in range(2):
                    P.op("pe", lambda e, f=f, g=g, h=h, off=off, bank=bank: e.matmul(bank[:, off + 64 * h: off + 64 * h + 64], lhsT=TTb[par][f][:, h, :],
                                                                                      rhs=Wsb[g][:, off + 64 * h: off + 64 * h + 64], start=True, stop=True),
                         reads=[("TTb", par, f), ("W", g)], writes=[bk])
            P.op("act", lambda e, g=g, bank=bank: e.activation(out=Usb[g][:], in_=bank[:, :], func=AF.Copy),
                 reads=[bk], writes=[("U", g)])
        yield
        for g in range(NG):
            bank, bk = nb()
            for fi in range(4):
                f = 4 * g + fi
                off = fi * 128
                P.op("pe", lambda e, f=f, off=off, bank=bank: e.matmul(bank[:, off:off + 128], lhsT=d["fr"][:, f, :], rhs=Hb[:, f, :], start=True, stop=False),
                     reads=[("fr", s), ("Hb", g)], writes=[bk])
                for h in range(2):
                    P.op("pe", lambda e, f=f, g=g, h=h, off=off, bank=bank: e.matmul(bank[:, off + 64 * h: off + 64 * h + 64], lhsT=A4[f][:, h, 256:384],
                                                                                      rhs=Usb[g][:, off + 64 * h: off + 64 * h + 64], start=False, stop=False),
                         reads=[("A4", par, f, h), ("U", g)], writes=[bk])
                    P.op("pe", lambda e, f=f, h=h, off=off, bank=bank: e.matmul(bank[:, off + 64 * h: off + 64 * h + 64], lhsT=A4[f][:, h, 384:512],
                                                                                 rhs=d["tV"][:, f, 64 * h:64 * h + 64], start=False, stop=(h == 1)),
                         reads=[("A4", par, f, h), ("tV", s)], writes=[bk])
            P.op("act", lambda e, g=g, bank=bank, ys=ys: e.activation(out=ys[:, 4 * g:4 * g + 4, :].rearrange("p f c -> p (f c)"), in_=bank[:, :], func=AF.Copy),
                 reads=[bk], writes=[("yst", t % 2, g)])
        yield
        for g in range(NG):
            bank, bk = nb()
            for fi in range(4):
                f = 4 * g + fi
                off = fi * 128
                P.op("pe", lambda e, f=f, g=g, off=off, bank=bank: e.matmul(bank[:, off:off + 128], lhsT=d["tB"][:, f, :], rhs=Usb[g][:, off:off + 128], start=True, stop=False),
                     reads=[("tB", s), ("U", g)], writes=[bk])
                P.op("pe", lambda e, f=f, off=off, bank=bank: e.matmul(bank[:, off:off + 128], lhsT=d["tK"][:, f, :], rhs=d["tV"][:, f, :], start=False, stop=True),
                     reads=[("tK", s), ("tV", s)], writes=[bk])
            hf_g = Hf[:, 4 * g:4 * g + 4, :]
            P.op("dve", lambda e, g=g, bank=bank: e.tensor_tensor(out=Ht[g][:], in0=bank[:, :], in1=mbd4[:], op=ALU.mult),
                 reads=[bk, "mbd4"], writes=[("Ht", g)])
            P.op("pool", lambda e, g=g, hf_g=hf_g: e.tensor_tensor(out=hf_g, in0=Ht[g][:].rearrange("p (f c) -> p f c", f=4), in1=hf_g, op=ALU.add),
                 reads=[("Ht", g), ("Hf", g)], writes=[("Hf", g)])
            P.op("pool", lambda e, g=g, hf_g=hf_g: e.tensor_tensor(out=hf_g, in0=hf_g, in1=d["pc"][:, 4 * g:4 * g + 4].unsqueeze(2).broadcast_to([128, 4, 128]), op=ALU.mult),
                 reads=[("Hf", g), ("pc", s)], writes=[("Hf", g)])
            P.op("act", lambda e, g=g, hf_g=hf_g: e.activation(out=Hb[:, 4 * g:4 * g + 4, :], in_=hf_g, func=AF.Copy),
                 reads=[("Hf", g)], writes=[("Hb", g)])
        out_toks.append(P.dma("sp", yout[t], ys[:], reads=[("yst", t % 2, g) for g in range(NG)], writes=[("yo", t)]))

    def drain(g_):
        for _ in g_:
            pass

    load(0)
    stage_A(0)
    drain(stage_D(0))
    for t in range(nch):
        gs = stage_S(t)
        if t + 1 < nch:
            load(t + 1)
            stage_A(t + 1)
            gd = stage_D(t + 1)
            for lvl_i in range(6):
                if lvl_i in (0, 1, 3, 4):
                    next(gs, None)
                next(gd, None)
            drain(gd)
        drain(gs)
    P.final_wait("sp", out_toks)
    return P.build()


def r2_consts():
    i = np.arange(128)[:, None]
    c = np.arange(128)[None, :]
    su = (c > i).astype(np.float32)
    ue = (c >= i).astype(np.float32)
    m4h = np.concatenate([su, su, ue, ue], 1)
    m4 = np.stack([m4h, m4h], 1)
    sl = (c < i).astype(np.float32)
    ml = np.concatenate([su, su, sl, sl], 1)
    idm = np.stack([np.eye(128, dtype=np.float32)] * 2, 1)
    bd = np.zeros((128, 128), np.float32)
    bd[:64, :64] = 1
    bd[64:, 64:] = 1
    bd = np.concatenate([bd] * 4, 1)
    return dict(m4=np.ascontiguousarray(m4), ml=np.ascontiguousarray(ml), idm=np.ascontiguousarray(idm), mbd=bd)


LN_X_EPS = 64e-5


def build_r3(segs=(("lat", 2048), ("ctx", 64)), TB=512):
    nc = bass.Bass("TRN2", target_bir_lowering=False)
    dr = {}
    for name, T in segs:
        dr[name] = dict(
            x=nc.dram_tensor("x_" + name, [128, 16, T], F32, kind="ExternalInput").ap(),
            y=nc.dram_tensor("y_" + name, [2, 16, 128, T], F32, kind="ExternalInput").ap(),
            bonus=nc.dram_tensor("bonus_" + name, [16, 128, T], F32, kind="ExternalInput").ap(),
            g=nc.dram_tensor("g_" + name, [16, 128, T], F32, kind="ExternalInput").ap(),
            mod=nc.dram_tensor("mod_" + name, [128, 6, 16], F32, kind="ExternalInput").ap(),
            out=nc.dram_tensor("out_" + name, [128, 16, T], F32, kind="ExternalOutput").ap(),
        )
    normg = nc.dram_tensor("normg", [128, 2, 16], F32, kind="ExternalInput").ap()
    lnx_d = nc.dram_tensor("lnx", [128, 2, 16], F32, kind="ExternalInput").ap()
    w_o = nc.dram_tensor("w_o", [16, 128, 16, 128], BF16, kind="ExternalInput").ap()
    w_in = nc.dram_tensor("w_in", [NJ, 2, 128, 16, 128], BF16, kind="ExternalInput").ap()
    w_out = nc.dram_tensor("w_out", [16, 128, NJ, 128], BF16, kind="ExternalInput").ap()

    P = Prog(nc)
    cm = Common(P, TB, halo=0)
    cm.setup_eps()
    ffn = FFN(P, cm, w_in, w_out, TB, nsplit=2, WC=128)
    xb = P.sbuf("xb", [128, 16, TB], F32)
    hT = P.sbuf("hT", [128, 16, TB], BF16)
    ng = P.sbuf("ng", [128, 2, 16], F32)
    lnx = P.sbuf("lnx_sb", [128, 2, 16], F32)
    bd64 = P.sbuf("bd64", [128, 128], F32)
    epsl = P.sbuf("epsl", [128, 1], F32)
    inb = [dict(y0=P.sbuf("y0_%d" % i, [128, TB], F32), y1=P.sbuf("y1_%d" % i, [128, TB], F32),
                bo=P.sbuf("bo_%d" % i, [128, TB], F32), g=P.sbuf("g_%d" % i, [128, TB], F32)) for i in range(2)]
    ysum = P.sbuf("ysum", [128, TB], F32)
    cen = P.sbuf("cen", [128, TB], F32)
    sq = P.sbuf("sq", [128, TB], F32)
    rstd = P.sbuf("rstd", [128, TB], F32)
    yn = P.sbuf("yn", [128, TB], F32)
    oo = P.sbuf("oo", [128, TB], F32)
    P.dma("sp", ng[:], normg, writes=["ng"])
    P.dma("sp", lnx[:], lnx_d, writes=["lnx"])
    P.op("pool", lambda e: e.memset(bd64[:], 0.0), writes=["bd64"])
    P.op("pool", lambda e: e.memset(bd64[0:64, 0:64], 1.0 / 64), writes=["bd64"])
    P.op("pool", lambda e: e.memset(bd64[64:128, 64:128], 1.0 / 64), writes=["bd64"])
    P.op("pool", lambda e: e.memset(epsl[:], LN_X_EPS), writes=["epsl"])
    xkeys = [("xb", c) for c in range(16)]
    hkeys = [("hT", c) for c in range(16)]
    out_toks = []
    st = dict(i=0)

    def do_block(name, t0, n, mod, G2):
        d_ = dr[name]
        mk = ("mod", name)
        P.dma("sp", xb[:, :, :n], d_["x"][:, :, t0:t0 + n], writes=xkeys)
        for c in range(16):
            s = st["i"] % 2
            st["i"] += 1
            ib = inb[s]
            P.dma("sp", ib["y0"][:, :n], d_["y"][0, c][:, t0:t0 + n], writes=[("y0", s)])
            P.dma("sp", ib["y1"][:, :n], d_["y"][1, c][:, t0:t0 + n], writes=[("y1", s)])
            P.dma("sp", ib["bo"][:, :n], d_["bonus"][c][:, t0:t0 + n], writes=[("bo", s)])
            P.dma("sp", ib["g"][:, :n], d_["g"][c][:, t0:t0 + n], writes=[("g", s)])
            P.op("pool", lambda e, ib=ib: e.tensor_tensor(out=ysum[:, :n], in0=ib["y0"][:, :n], in1=ib["y1"][:, :n], op=ALU.add),
                 reads=[("y0", s), ("y1", s)], writes=["ysum"])
            P.op("pe", lambda e: e.matmul(cm.psb[2][:, :n], lhsT=bd64[:], rhs=ysum[:, :n], start=True, stop=True),
                 reads=["bd64", "ysum"], writes=[("psb", 2)])
            P.op("dve", lambda e: e.tensor_tensor(out=cen[:, :n], in0=ysum[:, :n], in1=cm.psb[2][:, :n], op=ALU.subtract),
                 reads=["ysum", ("psb", 2)], writes=["cen"])
            P.op("act", lambda e: e.activation(out=sq[:, :n], in_=cen[:, :n], func=AF.Square), reads=["cen"], writes=["sq"])
            P.op("pe", lambda e: e.matmul(cm.psb[3][:, :n], lhsT=bd64[:], rhs=sq[:, :n], start=True, stop=True),
                 reads=["bd64", "sq"], writes=[("psb", 3)])
            P.op("act", lambda e: e.activation(out=rstd[:, :n], in_=cm.psb[3][:, :n], func=AF.Sqrt, bias=epsl[:, 0:1]),
                 reads=[("psb", 3), "epsl"], writes=["rstd"])
            P.op("dve", lambda e: e.reciprocal(out=rstd[:, :n], in_=rstd[:, :n]), reads=["rstd"], writes=["rstd"])
            P.op("dve", lambda e: e.tensor_tensor(out=yn[:, :n], in0=cen[:, :n], in1=rstd[:, :n], op=ALU.mult),
                 reads=["cen", "rstd"], writes=["yn"])
            P.op("act", lambda e, c=c: e.activation(out=oo[:, :n], in_=yn[:, :n], func=AF.Identity, scale=lnx[:, 0, c:c + 1], bias=lnx[:, 1, c:c + 1]),
                 reads=["yn", "lnx"], writes=["oo"])
            P.op("pool", lambda e, ib=ib: e.tensor_tensor(out=oo[:, :n], in0=oo[:, :n], in1=ib["bo"][:, :n], op=ALU.add),
                 reads=["oo", ("bo", s)], writes=["oo"])
            P.op("pool", lambda e, ib=ib, c=c: e.tensor_tensor(out=hT[:, c, :n], in0=oo[:, :n], in1=ib["g"][:, :n], op=ALU.mult),
                 reads=["oo", ("g", s)], writes=[hkeys[c]])
        for m in range(16):
            s = ffn.kin % 2
            ffn.kin += 1
            wt = ffn.wg[s]
            P.dma("sp", wt[:], w_o[m], writes=[("wg", s)])
            q = ffn.km % 2
            ffn.km += 1
            py = cm.psb[6 + q]
            for c in range(16):
                P.op("pe", lambda e, wt=wt, py=py, c=c: e.matmul(py[:, :n], lhsT=wt[:, c, :], rhs=hT[:, c, :n],
                                                                  start=(c == 0), stop=(c == 15)),
                     reads=[("wg", s), hkeys[c]], writes=[("psb", 6 + q)])
            P.op("dve", lambda e, py=py, m=m: e.scalar_tensor_tensor(
                out=xb[:, m, :n], in0=py[:, :n], scalar=mod[:, 2, m:m + 1], in1=xb[:, m, :n], op0=ALU.mult, op1=ALU.add),
                reads=[("psb", 6 + q), mk, xkeys[m]], writes=[xkeys[m]])
        cm.norm_mod(xb[:, :, :n], n, (G2, ("G2", name)), (mod[:, 3, :], mk), lambda c: hT[:, c, :n], hkeys, xkeys, stat_bank=0)
        ffn.emit(hT, hkeys, n, xb, 0, xkeys, (mod[:, 5, :], mk))
        out_toks.append(P.dma("sp", d_["out"][:, :, t0:t0 + n], xb[:, :, :n], reads=xkeys, writes=[("out", name, t0)]))

    for name, T in segs:
        mod = P.sbuf("modsb_" + name, [128, 6, 16], F32)
        G2 = P.sbuf("G2_" + name, [128, 16], F32)
        mk = ("mod", name)
        P.dma("sp", mod[:], dr[name]["mod"], writes=[mk])
        P.op("dve", lambda e, G2=G2, mod=mod: e.scalar_tensor_tensor(
            out=G2[:], in0=mod[:, 4, :], scalar=1.0, in1=ng[:, 1, :], op0=ALU.add, op1=ALU.mult),
            reads=[mk, "ng"], writes=[("G2", name)])
        for t0 in range(0, T, TB):
            do_block(name, t0, min(TB, T - t0), mod, G2)
    P.final_wait("sp", out_toks)
    return P.build()


def r2_maps_from_r1(r1, cons):
    maps = []
    for b in range(2):
        cores = [r1[b * 4 + k] for k in range(4)]
        ot_lat = np.concatenate([c["ot_lat"] for c in cores], axis=4)
        ot_ctx = np.concatenate([cores[0]["ot_ctx"], cores[1]["ot_ctx"]], axis=4)
        v_lat = np.concatenate([c["v_lat"] for c in cores], axis=2)
        v_ctx = np.concatenate([cores[0]["v_ctx"], cores[1]["v_ctx"]], axis=2)
        pc_lat = np.concatenate([c["pc_lat"] for c in cores], axis=3)
        pc_ctx = np.concatenate([cores[0]["pc_ctx"], cores[1]["pc_ctx"]], axis=3)
        for d in range(2):
            if d == 0:
                ot = np.concatenate([ot_ctx[d], ot_lat[d]], axis=3)
                vv = np.concatenate([v_ctx, v_lat], axis=2)
                pc = np.concatenate([pc_ctx[:, d], pc_lat[:, d]], axis=2)
            else:
                ot = np.concatenate([ot_ctx[d][..., ::-1], ot_lat[d][..., ::-1]], axis=3)
                vv = np.concatenate([v_ctx[..., ::-1], v_lat[..., ::-1]], axis=2)
                pc = np.concatenate([pc_ctx[:, d][..., ::-1], pc_lat[:, d][..., ::-1]], axis=2)
            for hh in range(2):
                cs = slice(8 * hh, 8 * hh + 8)
                o = ot[:, cs].reshape(4, 8, 128, 66, 128)
                fm = np.ascontiguousarray(o.transpose(0, 3, 2, 1, 4))
                tmB = o[1].transpose(2, 3, 0, 1)
                tmK = o[2].transpose(2, 3, 0, 1)
                tmV = vv[cs].reshape(8, 128, 66, 128).transpose(2, 3, 0, 1)
                tm = np.ascontiguousarray(np.stack([tmB, tmK, tmV]))
                pcl = np.ascontiguousarray(pc[:, cs].transpose(2, 0, 1))
                m = dict(fm=fm, tm=tm, pc=pcl)
                m.update(cons)
                maps.append(((b, d, hh), m))
    maps.sort(key=lambda t: t[0][0] * 4 + t[0][1] * 2 + t[0][2])
    return [m for _, m in maps]


def r3_y_from_r2(r2res):
    yl = [[None, None], [None, None]]
    yc = [[None, None], [None, None]]
    for b in range(2):
        for d in range(2):
            halves = []
            for hh in range(2):
                y = np.asarray(r2res[b * 4 + d * 2 + hh]["y"])
                halves.append(y.reshape(66 * 128, 8 * 128))
            seq = np.concatenate(halves, axis=1)
            c, l = seq[:256], seq[256:]
            if d == 1:
                c, l = c[::-1], l[::-1]
            yc[b][d], yl[b][d] = c, l
    return yl, yc


def _r1_maps(x, ctx, mods_l, inp, wb):
    maps = []
    rmask = np.ones((128, 256), np.float32)
    rmask[:, ::128] = 0.0
    for i in range(8):
        b, k = i // 4, i % 4
        xl = np.zeros((TLAT + 2, 2048), np.float32)
        vl = np.zeros(TLAT + 2, np.float32)
        lo, hi = k * TLAT - 1, (k + 1) * TLAT + 1
        s0, s1 = max(lo, 0), min(hi, 8192)
        xl[s0 - lo:s1 - lo] = x[b, s0:s1]
        vl[s0 - lo:s1 - lo] = 1
        ck = k % 2
        xc = np.zeros((R1_TCTX + 2, 2048), np.float32)
        vc = np.zeros(R1_TCTX + 2, np.float32)
        lo, hi = ck * R1_TCTX - 1, (ck + 1) * R1_TCTX + 1
        s0, s1 = max(lo, 0), min(hi, 256)
        xc[s0 - lo:s1 - lo] = ctx[b, s0:s1]
        vc[s0 - lo:s1 - lo] = 1
        maps.append({"x_lat": to_fm(xl), "valid_lat": vl, "mod_lat": vec_fm(mods_l[b].reshape(6, 2048)),
                     "x_ctx": to_fm(xc), "valid_ctx": vc, "mod_ctx": vec_fm(mods_l[2].reshape(6, 2048)),
                     "normg": vec_fm(inp["norm_g"][1]), "mu": vec_fm(inp["rwkv_mu"][0]), "w_rkv": wb["rwkv_rkv"],
                     "w_la": inp["rwkv_w_lora_a"][0], "w_lb": inp["rwkv_w_lora_b"][0], "a_la": inp["rwkv_a_lora_a"][0],
                     "a_lb": inp["rwkv_a_lora_b"][0], "g_la": inp["rwkv_g_lora_a"][0], "g_lb": inp["rwkv_g_lora_b"][0],
                     "dirvec": vec_fm(inp["rwkv_dir_vec"][0]), "r_k": vec_fm(inp["rwkv_r_k"][0].reshape(2048)), "rmask": rmask})
    return maps


def _r3_maps(x, ctx, yl, yc, r1, mods_l, inp, li, wb):
    maps = []
    fmc = lambda a: np.ascontiguousarray(a.reshape(a.shape[0], 16, 128).transpose(1, 2, 0))
    for i in range(8):
        b, k = i // 4, i % 4
        ls = slice(k * 2048, (k + 1) * 2048)
        cs = slice(k * 64, (k + 1) * 64)
        cc = r1[b * 4 + (k // 2)]
        co = (k % 2) * 64
        maps.append({"x_lat": to_fm(x[b, ls]), "x_ctx": to_fm(ctx[b, cs]),
                     "y_lat": np.stack([fmc(yl[b][0][ls]), fmc(yl[b][1][ls])]),
                     "y_ctx": np.stack([fmc(yc[b][0][cs]), fmc(yc[b][1][cs])]),
                     "bonus_lat": r1[i]["bonus_lat"], "g_lat": r1[i]["g_lat"],
                     "bonus_ctx": np.ascontiguousarray(cc["bonus_ctx"][:, :, co:co + 64]),
                     "g_ctx": np.ascontiguousarray(cc["g_ctx"][:, :, co:co + 64]),
                     "mod_lat": vec_fm(mods_l[b].reshape(6, 2048)), "mod_ctx": vec_fm(mods_l[2].reshape(6, 2048)),
                     "normg": vec_fm(inp["norm_g"][li]), "lnx": vec_fm(inp["rwkv_ln_x"][0]), "w_o": wb["rwkv_wo"],
                     "w_in": wb["ffn_in%d" % li], "w_out": wb["ffn_out%d" % li]})
    return maps


def _run_rwkv_layer(x, ctx, mods_l, inp, li, wb):
    nc1 = build_r1()
    res1 = run_bass_kernel_spmd(nc1, _r1_maps(x, ctx, mods_l, inp, wb), core_ids=list(range(8)))
    r1 = [{k: np.asarray(v) for k, v in r.items()} for r in res1.results]
    nc2 = build_r2(NCH)
    res2 = run_bass_kernel_spmd(nc2, r2_maps_from_r1(r1, r2_consts()), core_ids=list(range(8)))
    r2res = [{"y": np.asarray(r["y"])} for r in res2.results]
    yl, yc = r3_y_from_r2(r2res)
    nc3 = build_r3()
    res3 = run_bass_kernel_spmd(nc3, _r3_maps(x, ctx, yl, yc, r1, mods_l, inp, li, wb), core_ids=list(range(8)))
    xo = np.zeros_like(x)
    co = np.zeros_like(ctx)
    for i in range(8):
        b, k = i // 4, i % 4
        xo[b, k * 2048:(k + 1) * 2048] = from_fm(res3.results[i]["out_lat"])
        co[b, k * 64:(k + 1) * 64] = from_fm(res3.results[i]["out_ctx"])
    return xo, co


def kernel(**inputs):
    inp = {k: np.ascontiguousarray(np.asarray(v, dtype=np.float32)) for k, v in inputs.items()}
    x, ctx = inp["x"], inp["ctx"]
    mods, wb = run_l0(inp)
    x, ctx = run_pool_layer(x, ctx, mods[:, 0], inp["norm_g"][0], inp["pool_scale"][0], inp["pool_w"][0],
                            wb["ffn_in0"], wb["ffn_out0"], True)
    x, ctx = _run_rwkv_layer(x, ctx, mods[:, 1], inp, 1, wb)
    a1 = run_a1(x, ctx, mods[:, 2], inp["norm_g"][2], wb["diff_qkv"], inp["diff_qk_g"][0])
    a1 = [{k: np.asarray(v) for k, v in r.items()} for r in a1]
    lambda_init = 0.8 - 0.6 * math.exp(-0.3 * 2)
    x = run_a2(a1, x, mods[:, 2], inp["norm_g"][2], inp["diff_lambda"][0], inp["diff_subln_g"][0], wb["diff_wo"],
               wb["ffn_in2"], wb["ffn_out2"], lambda_init)
    x, _ = run_pool_layer(x, None, mods[:, 3], inp["norm_g"][3], inp["pool_scale"][1], inp["pool_w"][1],
                          wb["ffn_in3"], wb["ffn_out3"], False)
    return x.astype(np.float32)
```
